# Optimizing a Trainium2 kernel written in Bass

```python
import math
import jax
import jax.numpy as jnp
from jax import lax
import numpy as np

D_MODEL = 1024
BATCH = 4
SEQ = 4096
DEPTH = 2
DEC_BATCH = 32
DEC_SEQ = 1
PAST_LEN = 16384
PAGE_SIZE = 128

HEAD_DIM = 128
HEADS_PER_GROUP = 4
N_ATT_GROUPS = 3
WINDOWS = (128, 512, 2048)
DILATIONS = (1, 4, 16)
ROPE_THETA = 10000.0
ATT_QKV = N_ATT_GROUPS * 3 * HEADS_PER_GROUP * HEAD_DIM
ATT_OUT = HEADS_PER_GROUP * HEAD_DIM
D_INNER = 2 * D_MODEL
SSM_HEAD_DIM = 64
SSM_HEADS = D_INNER // SSM_HEAD_DIM
SSM_GROUPS = 4
HEADS_PER_SSM_GROUP = SSM_HEADS // SSM_GROUPS
D_STATE = 128
CONV_W = 4
CONV_DIM = D_INNER + 2 * SSM_GROUPS * D_STATE
CHUNK = 128
D_FF = ((8 * D_MODEL // 3 + 127) // 128) * 128
FFN_RES = 0.5
IN_COLS = ATT_QKV + D_INNER + CONV_DIM + SSM_HEADS + 2 * D_MODEL
EPS = 1e-6

kernel_name = "dilated_ssd_macaron_decoder_step"


def rmsnorm(x, g):
    xf = x.astype(jnp.float32)
    xf = xf * lax.rsqrt(jnp.mean(xf * xf, axis=-1, keepdims=True) + EPS)
    return (xf * g.astype(jnp.float32)).astype(x.dtype)


def rope(x, pos):
    half = HEAD_DIM // 2
    inv = ROPE_THETA ** (-jnp.arange(half, dtype=jnp.float32) / half)
    ang = pos.astype(jnp.float32)[:, None] * inv[None, :]
    cos = jnp.cos(ang)[None, :, None, :]
    sin = jnp.sin(ang)[None, :, None, :]
    xf = x.astype(jnp.float32)
    x1, x2 = xf[..., :half], xf[..., half:]
    return jnp.concatenate([x1 * cos - x2 * sin, x2 * cos + x1 * sin], axis=-1).astype(x.dtype)


def modulation(c, w, bias):
    mod = jax.nn.silu(c) @ w + bias
    return mod.reshape(c.shape[0], 3, 3, D_MODEL)


def modulate(x, g, mod_k):
    return rmsnorm(x, g) * (1 + mod_k[:, 1][:, None]) + mod_k[:, 0][:, None]


def residual(x, out, g_post, mod_k, weight):
    return x + weight * mod_k[:, 2][:, None] * rmsnorm(out, g_post)


def ffn_sublayer(x, mod_k, g_pre, g_post, w1, w2):
    u = modulate(x, g_pre, mod_k) @ w1
    f = (jax.nn.silu(u[..., :D_FF]) * u[..., D_FF:]) @ w2
    return residual(x, f, g_post, mod_k, FFN_RES)


def dilated_attn_prompt(q, k, v, window, dil):
    b, s, h, dh = q.shape
    nw = window // dil
    blk = nw
    span = dil * blk
    sp = -(-s // span) * span
    padn = sp - s
    L = sp // dil
    nb = L // blk

    def pad(t):
        return jnp.pad(t, ((0, 0), (0, padn), (0, 0), (0, 0)))

    def kv_blocks(t):
        t = pad(t).reshape(b, L, dil, h, dh)
        t = jnp.pad(t, ((0, 0), (blk, 0), (0, 0), (0, 0), (0, 0)))
        prev = t[:, :L].reshape(b, nb, blk, dil, h, dh)
        cur = t[:, blk:].reshape(b, nb, blk, dil, h, dh)
        return jnp.concatenate([prev, cur], axis=2)

    qb = pad(q).reshape(b, nb, blk, dil, h, dh)
    kb = kv_blocks(k)
    vb = kv_blocks(v)
    sc = jnp.einsum('bnqrhd,bnkrhd->bnrhqk', qb, kb).astype(jnp.float32) * (HEAD_DIM ** -0.5)
    iq = jnp.arange(blk)[:, None]
    ik = jnp.arange(2 * blk)[None, :]
    rel = iq + blk - ik
    mk = (jnp.arange(nb)[:, None, None] - 1) * blk + ik[None]
    valid = (rel >= 0)[None] & (rel <= nw)[None] & (mk >= 0)
    sc = jnp.where(valid[None, :, None, None], sc, -jnp.inf)
    m = jnp.max(sc, axis=-1)
    p = jnp.exp(sc - m[..., None])
    den = jnp.sum(p, axis=-1)
    den_t = jnp.transpose(den, (0, 1, 4, 2, 3))
    o = jnp.einsum('bnrhqk,bnkrhd->bnqrhd', p, vb.astype(jnp.float32)) / den_t[..., None]
    o = o.reshape(b, sp, h, dh)[:, :s]
    m = jnp.transpose(m, (0, 1, 4, 2, 3)).reshape(b, sp, h)[:, :s]
    den = den_t.reshape(b, sp, h)[:, :s]
    return o, m, den


def dilated_attn_sample(q, k_all, v_all, buf_len, window, dil):
    t = q.shape[1]
    nw = window // dil
    idx = buf_len + jnp.arange(t)[:, None] - dil * jnp.arange(nw + 1)[None, :]
    valid = idx >= 0
    idx = jnp.maximum(idx, 0)
    kg = k_all[:, idx]
    vg = v_all[:, idx]
    sc = jnp.einsum('bthd,btjhd->bthj', q, kg).astype(jnp.float32) * (HEAD_DIM ** -0.5)
    sc = jnp.where(valid[None, :, None, :], sc, -jnp.inf)
    m = jnp.max(sc, axis=-1)
    p = jnp.exp(sc - m[..., None])
    den = jnp.sum(p, axis=-1)
    o = jnp.einsum('bthj,btjhd->bthd', p, vg.astype(jnp.float32)) / den[..., None]
    return o, m, den


def merge_dilations(os, ms, dens):
    o = jnp.stack(os)
    m = jnp.stack(ms)
    d = jnp.stack(dens)
    w = d * jnp.exp(m - jnp.max(m, axis=0, keepdims=True))
    return jnp.sum(w[..., None] * o, axis=0) / jnp.sum(w, axis=0)[..., None]


def project_mixer(h, w_in):
    u = h @ w_in
    b, t, _ = u.shape
    att = u[..., :ATT_QKV].reshape(b, t, N_ATT_GROUPS, 3, HEADS_PER_GROUP, HEAD_DIM)
    o = ATT_QKV
    z = u[..., o:o + D_INNER]
    o += D_INNER
    xbc = u[..., o:o + CONV_DIM]
    o += CONV_DIM
    dt_raw = u[..., o:o + SSM_HEADS]
    o += SSM_HEADS
    g_att = u[..., o:o + D_MODEL]
    g_ssm = u[..., o + D_MODEL:]
    return att, z, xbc, dt_raw, g_att, g_ssm


def causal_conv(xpad, w, bias, t):
    return sum(xpad[:, j:j + t] * w[j] for j in range(CONV_W)) + bias


def ssm_inputs(xbc_conv, dt_raw, dt_bias, a_log):
    xc = jax.nn.silu(xbc_conv.astype(jnp.float32))
    b, t, _ = xc.shape
    gn = SSM_GROUPS * D_STATE
    xs = xc[..., :D_INNER].reshape(b, t, SSM_HEADS, SSM_HEAD_DIM)
    bm = xc[..., D_INNER:D_INNER + gn].reshape(b, t, SSM_GROUPS, D_STATE)
    cm = xc[..., D_INNER + gn:].reshape(b, t, SSM_GROUPS, D_STATE)
    dt = jax.nn.softplus(dt_raw.astype(jnp.float32) + dt_bias.astype(jnp.float32))
    a = -jnp.exp(a_log.astype(jnp.float32))
    return xs, bm, cm, dt, a


def ssd_chunked(xs, dt, a, bm, cm):
    b, s = xs.shape[:2]
    nc = s // CHUNK
    G, E, P, N = SSM_GROUPS, HEADS_PER_SSM_GROUP, SSM_HEAD_DIM, D_STATE
    xr = (xs * dt[..., None]).reshape(b, nc, CHUNK, G, E, P)
    ar = (dt * a).reshape(b, nc, CHUNK, G, E)
    br = bm.reshape(b, nc, CHUNK, G, N)
    cr = cm.reshape(b, nc, CHUNK, G, N)
    acs = jnp.cumsum(ar, axis=2)
    diff = acs[:, :, :, None] - acs[:, :, None]
    causal = jnp.tril(jnp.ones((CHUNK, CHUNK), dtype=bool))[:, :, None, None]
    lmat = jnp.exp(jnp.where(causal, diff, -jnp.inf))
    cb = jnp.einsum('bclgn,bcsgn->bclsg', cr, br)
    y_diag = jnp.einsum('bclsg,bclsge,bcsgep->bclgep', cb, lmat, xr)
    decay_states = jnp.exp(acs[:, :, -1:] - acs)
    chunk_states = jnp.einsum('bclgn,bclge,bclgep->bcgepn', br, decay_states, xr)
    chunk_decay = jnp.exp(acs[:, :, -1])

    def chunk_step(hc, inp):
        dec, st = inp
        return dec[..., None, None] * hc + st, hc

    h0 = jnp.zeros((b, G, E, P, N), jnp.float32)
    h_fin, prev = lax.scan(chunk_step, h0, (jnp.moveaxis(chunk_decay, 1, 0), jnp.moveaxis(chunk_states, 1, 0)))
    prev = jnp.moveaxis(prev, 0, 1)
    y_off = jnp.einsum('bclgn,bcgepn,bclge->bclgep', cr, prev, jnp.exp(acs))
    y = (y_diag + y_off).reshape(b, s, SSM_HEADS, P)
    return y, h_fin.reshape(b, SSM_HEADS, P, N)


def ssd_recurrent(xs, dt, a, bm, cm, h0):
    b, t = xs.shape[:2]
    G, E, P, N = SSM_GROUPS, HEADS_PER_SSM_GROUP, SSM_HEAD_DIM, D_STATE
    hc0 = h0.astype(jnp.float32).reshape(b, G, E, P, N)
    xr = xs.reshape(b, t, G, E, P)
    dtr = dt.reshape(b, t, G, E)
    ar = a.reshape(G, E)

    def step(hc, inp):
        x_t, dt_t, b_t, c_t = inp
        hc = jnp.exp(dt_t * ar)[..., None, None] * hc + jnp.einsum('bgep,bgn->bgepn', x_t * dt_t[..., None], b_t)
        return hc, jnp.einsum('bgepn,bgn->bgep', hc, c_t)

    tm = lambda z: jnp.moveaxis(z, 1, 0)
    h_fin, ys = lax.scan(step, hc0, (tm(xr), tm(dtr), tm(bm), tm(cm)))
    return jnp.moveaxis(ys, 0, 1).reshape(b, t, SSM_HEADS, P), h_fin.reshape(b, SSM_HEADS, P, N)


def ssm_output(y, xs, z, d_skip, norm_ssm):
    b, t = y.shape[:2]
    y = y + d_skip.astype(jnp.float32)[:, None] * xs
    y = y.reshape(b, t, D_INNER) * jax.nn.silu(z.astype(jnp.float32))
    yg = y.reshape(b, t, SSM_GROUPS, D_INNER // SSM_GROUPS)
    yg = yg * lax.rsqrt(jnp.mean(yg * yg, axis=-1, keepdims=True) + EPS)
    return (yg.reshape(b, t, D_INNER) * norm_ssm.astype(jnp.float32)).astype(z.dtype)


def merge_branches(att_o, y_ssm, g_att, g_ssm, prm):
    b, t = att_o.shape[:2]
    a_br = att_o.reshape(b, t, ATT_OUT).astype(g_att.dtype) @ prm['w_br_att']
    s_br = y_ssm @ prm['w_br_ssm']
    return (jax.nn.sigmoid(g_att) * a_br + jax.nn.sigmoid(g_ssm) * s_br) @ prm['w_out']


def mixer_prompt(h, pos, prm):
    att, z, xbc, dt_raw, g_att, g_ssm = project_mixer(h, prm['w_in'])
    s = h.shape[1]
    os, ms, ds, kv_new = [], [], [], []
    for g in range(N_ATT_GROUPS):
        q = rope(att[:, :, g, 0], pos)
        k = rope(att[:, :, g, 1], pos)
        v = att[:, :, g, 2]
        o, m, d = dilated_attn_prompt(q, k, v, WINDOWS[g], DILATIONS[g])
        os.append(o)
        ms.append(m)
        ds.append(d)
        keep = min(WINDOWS[g], s)
        kv_new.append(jnp.stack([k[:, s - keep:], v[:, s - keep:]], axis=2))
    att_o = merge_dilations(os, ms, ds)
    xpad = jnp.pad(xbc, ((0, 0), (CONV_W - 1, 0), (0, 0)))
    xs, bm, cm, dt, a = ssm_inputs(causal_conv(xpad, prm['conv_w'], prm['conv_b'], s), dt_raw, prm['dt_bias'], prm['a_log'])
    y, h_fin = ssd_chunked(xs, dt, a, bm, cm)
    y_ssm = ssm_output(y, xs, z, prm['d_skip'], prm['norm_ssm'])
    out = merge_branches(att_o, y_ssm, g_att, g_ssm, prm)
    return out, kv_new, h_fin, xbc[:, s - (CONV_W - 1):]


def mixer_sample(h, pos, kv_bufs, ssm_h, conv_buf, prm):
    att, z, xbc, dt_raw, g_att, g_ssm = project_mixer(h, prm['w_in'])
    t = h.shape[1]
    os, ms, ds, kv_new = [], [], [], []
    for g in range(N_ATT_GROUPS):
        q = rope(att[:, :, g, 0], pos)
        k = rope(att[:, :, g, 1], pos)
        v = att[:, :, g, 2]
        buf = kv_bufs[g]
        k_all = jnp.concatenate([buf[:, :, 0].astype(k.dtype), k], axis=1)
        v_all = jnp.concatenate([buf[:, :, 1].astype(v.dtype), v], axis=1)
        o, m, d = dilated_attn_sample(q, k_all, v_all, buf.shape[1], WINDOWS[g], DILATIONS[g])
        os.append(o)
        ms.append(m)
        ds.append(d)
        kv_new.append(jnp.stack([k, v], axis=2))
    att_o = merge_dilations(os, ms, ds)
    xcat = jnp.concatenate([conv_buf.astype(xbc.dtype), xbc], axis=1)
    xs, bm, cm, dt, a = ssm_inputs(causal_conv(xcat, prm['conv_w'], prm['conv_b'], t), dt_raw, prm['dt_bias'], prm['a_log'])
    y, h_new = ssd_recurrent(xs, dt, a, bm, cm, ssm_h)
    y_ssm = ssm_output(y, xs, z, prm['d_skip'], prm['norm_ssm'])
    out = merge_branches(att_o, y_ssm, g_att, g_ssm, prm)
    return out, kv_new, h_new, xcat[:, -(CONV_W - 1):]


def setup_inputs(seed: int = 0) -> dict:
    key = jax.random.key(seed)
    ks = jax.random.split(key, 28)

    def nrm(k, shape, scale):
        return scale * jax.random.normal(k, shape, jnp.float32)

    lens = [min(w, PAST_LEN) for w in WINDOWS]
    kv_shape = lambda L: (DEPTH, DEC_BATCH, L, 2, HEADS_PER_GROUP, HEAD_DIM)
    dt0 = jnp.exp(jax.random.uniform(ks[18], (DEPTH, SSM_HEADS), jnp.float32, math.log(1e-3), math.log(1e-1)))
    return {
        "x_prompt": nrm(ks[0], (BATCH, SEQ, D_MODEL), 1.0),
        "x_sample": nrm(ks[1], (DEC_BATCH, DEC_SEQ, D_MODEL), 1.0),
        "cache_kv_g0": nrm(ks[2], kv_shape(lens[0]), 1.0),
        "cache_kv_g1": nrm(ks[3], kv_shape(lens[1]), 1.0),
        "cache_kv_g2": nrm(ks[4], kv_shape(lens[2]), 1.0),
        "state_ssm": nrm(ks[5], (DEPTH, DEC_BATCH, SSM_HEADS, SSM_HEAD_DIM, D_STATE), 0.1),
        "state_conv": nrm(ks[6], (DEPTH, DEC_BATCH, CONV_W - 1, CONV_DIM), 1.0),
        "c_prompt": nrm(ks[7], (BATCH, D_MODEL), 1.0),
        "c_sample": nrm(ks[8], (DEC_BATCH, D_MODEL), 1.0),
        "w_mod": nrm(ks[9], (DEPTH, D_MODEL, 9 * D_MODEL), D_MODEL ** -0.5),
        "b_mod": nrm(ks[10], (DEPTH, 9 * D_MODEL), 0.02),
        "norm_pre": 1.0 + nrm(ks[11], (DEPTH, 3, D_MODEL), 0.05),
        "norm_post": 1.0 + nrm(ks[12], (DEPTH, 3, D_MODEL), 0.05),
        "w_ff_in": nrm(ks[13], (DEPTH, 2, D_MODEL, 2 * D_FF), D_MODEL ** -0.5),
        "w_ff_out": nrm(ks[14], (DEPTH, 2, D_FF, D_MODEL), D_FF ** -0.5),
        "w_in": nrm(ks[15], (DEPTH, D_MODEL, IN_COLS), D_MODEL ** -0.5),
        "conv_w": nrm(ks[16], (DEPTH, CONV_W, CONV_DIM), CONV_W ** -0.5),
        "conv_b": nrm(ks[17], (DEPTH, CONV_DIM), 0.02),
        "dt_bias": dt0 + jnp.log(-jnp.expm1(-dt0)),
        "a_log": jnp.log(jax.random.uniform(ks[19], (DEPTH, SSM_HEADS), jnp.float32, 1.0, 16.0)),
        "d_skip": 1.0 + nrm(ks[20], (DEPTH, SSM_HEADS), 0.1),
        "norm_ssm": 1.0 + nrm(ks[21], (DEPTH, D_INNER), 0.05),
        "w_br_att": nrm(ks[22], (DEPTH, ATT_OUT, D_MODEL), ATT_OUT ** -0.5),
        "w_br_ssm": nrm(ks[23], (DEPTH, D_INNER, D_MODEL), D_INNER ** -0.5),
        "w_out": nrm(ks[24], (DEPTH, D_MODEL, D_MODEL), D_MODEL ** -0.5),
    }


def reference(x_prompt, x_sample, cache_kv_g0, cache_kv_g1, cache_kv_g2, state_ssm, state_conv,
              c_prompt, c_sample, w_mod, b_mod, norm_pre, norm_post, w_ff_in, w_ff_out, w_in,
              conv_w, conv_b, dt_bias, a_log, d_skip, norm_ssm, w_br_att, w_br_ssm, w_out):
    pos_p = jnp.arange(x_prompt.shape[1])
    pos_s = PAST_LEN + jnp.arange(x_sample.shape[1])
    xp, xs = x_prompt, x_sample
    kvp = [[], [], []]
    kvs = [[], [], []]
    ssm_p, ssm_s, conv_p, conv_s = [], [], [], []
    for l in range(DEPTH):
        prm = {'w_in': w_in[l], 'conv_w': conv_w[l], 'conv_b': conv_b[l], 'dt_bias': dt_bias[l],
               'a_log': a_log[l], 'd_skip': d_skip[l], 'norm_ssm': norm_ssm[l],
               'w_br_att': w_br_att[l], 'w_br_ssm': w_br_ssm[l], 'w_out': w_out[l]}
        mod_p = modulation(c_prompt, w_mod[l], b_mod[l])
        mod_s = modulation(c_sample, w_mod[l], b_mod[l])
        xp = ffn_sublayer(xp, mod_p[:, 0], norm_pre[l, 0], norm_post[l, 0], w_ff_in[l, 0], w_ff_out[l, 0])
        xs = ffn_sublayer(xs, mod_s[:, 0], norm_pre[l, 0], norm_post[l, 0], w_ff_in[l, 0], w_ff_out[l, 0])
        out_p, kv_new_p, h_p, cv_p = mixer_prompt(modulate(xp, norm_pre[l, 1], mod_p[:, 1]), pos_p, prm)
        xp = residual(xp, out_p, norm_post[l, 1], mod_p[:, 1], 1.0)
        bufs = (cache_kv_g0[l], cache_kv_g1[l], cache_kv_g2[l])
        out_s, kv_new_s, h_s, cv_s = mixer_sample(modulate(xs, norm_pre[l, 1], mod_s[:, 1]), pos_s, bufs,
                                                  state_ssm[l], state_conv[l], prm)
        xs = residual(xs, out_s, norm_post[l, 1], mod_s[:, 1], 1.0)
        xp = ffn_sublayer(xp, mod_p[:, 2], norm_pre[l, 2], norm_post[l, 2], w_ff_in[l, 1], w_ff_out[l, 1])
        xs = ffn_sublayer(xs, mod_s[:, 2], norm_pre[l, 2], norm_post[l, 2], w_ff_in[l, 1], w_ff_out[l, 1])
        for g in range(N_ATT_GROUPS):
            kvp[g].append(kv_new_p[g])
            kvs[g].append(kv_new_s[g])
        ssm_p.append(h_p)
        ssm_s.append(h_s)
        conv_p.append(cv_p)
        conv_s.append(cv_s)
    kv_g0_prompt = jnp.stack(kvp[0])
    kv_g0_sample = jnp.stack(kvs[0])
    kv_g1_prompt = jnp.stack(kvp[1])
    kv_g1_sample = jnp.stack(kvs[1])
    kv_g2_prompt = jnp.stack(kvp[2])
    kv_g2_sample = jnp.stack(kvs[2])
    return (xp, xs, kv_g0_prompt, kv_g0_sample, kv_g1_prompt, kv_g1_sample, kv_g2_prompt, kv_g2_sample,
            jnp.stack(ssm_p), jnp.stack(ssm_s), jnp.stack(conv_p), jnp.stack(conv_s))
```

```python
import contextlib
import os as _os
import numpy as np
import concourse.bass as bass
import concourse.mybir as mybir
from concourse.bass_utils import run_bass_kernel_spmd

F32 = mybir.dt.float32
BF16 = mybir.dt.bfloat16
AF = mybir.ActivationFunctionType
ALU = mybir.AluOpType
AX = mybir.AxisListType

D = 1024
KC = 8
SEQ = 4096
T = 2048
NGRP = SEQ // T
DFF = 2816
NJ = DFF // 128
INC = 11808
EPS = 1e-6
NS = 4
NCOL = 1 + NS
WIN = (128, 512, 2048)
DIL = (1, 4, 16)
PAST = 16384
SCALE = 128 ** -0.5
RES_W = (0.5, 1.0, 0.5)


class Op:
    __slots__ = ("eng", "fn", "deps", "chan", "chan_cnt", "sig", "idx", "pos", "is_dma", "waits", "grp", "grp_last")


class Sched:
    ENG = ("pe", "act", "dve", "pool", "sp")

    def __init__(self, nc):
        self.nc = nc
        self.ops = []
        self.last_w = {}
        self.readers = {}
        self.per_eng = {e: [] for e in self.ENG}
        self.chan_count = {}
        self.stack = contextlib.ExitStack()
        self.nbytes = 0
        self.bar = {}
        self.bar_start = 0
        self.chan_last = {}
        self.gid = 0

    def sb(self, name, shape, dt):
        t = self.stack.enter_context(self.nc.sbuf_tensor(name, list(shape), dt))
        n = 1
        for s in shape[1:]:
            n *= s
        self.nbytes += n * (4 if dt == F32 else 2)
        return t

    def ps(self, name, shape, dt=F32):
        return self.stack.enter_context(self.nc.psum_tensor(name, list(shape), dt))

    def add(self, eng, fn, r=(), w=(), chan=None, grp=None):
        op = Op()
        op.eng = eng
        op.fn = fn
        op.idx = len(self.ops)
        op.is_dma = chan is not None
        op.chan = chan
        op.sig = None
        deps = {}
        for k in r:
            j = self.last_w.get(k)
            if j is not None:
                deps[j] = True
        for k in w:
            j = self.last_w.get(k)
            if j is not None:
                deps.setdefault(j, False)
            for j in self.readers.get(k, ()):
                deps.setdefault(j, False)
        for k in w:
            self.last_w[k] = op.idx
            self.readers[k] = []
        for k in r:
            self.readers.setdefault(k, []).append(op.idx)
        if self.bar.get(eng):
            for j in self.bar.pop(eng):
                deps[j] = True
        deps.pop(op.idx, None)
        op.deps = deps
        if op.is_dma:
            c = self.chan_count.get(chan, 0) + 1
            self.chan_count[chan] = c
            op.chan_cnt = c
            if grp is None:
                self.gid += 1
                grp = ("u", self.gid)
            op.grp = grp
            op.grp_last = op
            prev = self.chan_last.get(chan)
            if prev is not None and not _os.environ.get("K_NOGRP"):
                if prev.grp == grp:
                    q = prev
                    members = [q]
                    for o2 in reversed(self.ops):
                        if o2.is_dma and o2.chan == chan and o2.grp == grp and o2 is not q:
                            members.append(o2)
                        elif o2.is_dma and o2.chan == chan and o2.grp != grp:
                            break
                    for o2 in members:
                        o2.grp_last = op
                    for j, v in prev.deps.items():
                        if self.ops[j].is_dma and self.ops[j].chan == chan:
                            deps[j] = True
                else:
                    deps[prev.idx] = True
            self.chan_last[chan] = op
        op.pos = len(self.per_eng[eng])
        self.per_eng[eng].append(op)
        self.ops.append(op)
        return op

    def barrier(self):
        lastops = [self.per_eng[e][-1].idx for e in self.ENG if self.per_eng[e]]
        dmas = [op.idx for op in self.ops[self.bar_start:] if op.is_dma]
        pend = lastops + dmas
        for e in self.ENG:
            self.bar[e] = list(self.bar.get(e, [])) + pend
        self.bar_start = len(self.ops)

    def dma(self, q, out, in_, r=(), w=(), chan=None, grp=None, **kw):
        assert chan is not None
        return self.add(q, lambda e: e.dma_start(out=out, in_=in_, **kw), r=r, w=w, chan=chan, grp=grp)

    def emit(self, final_waits=()):
        nc = self.nc
        ops = self.ops
        need = [False] * len(ops)
        waits_of = [None] * len(ops)
        for op in ops:
            wl = []
            for j, raw in op.deps.items():
                y = ops[j]
                if y.is_dma:
                    if op.is_dma and y.chan == op.chan and y.grp == op.grp:
                        continue
                    wl.append(j)
                elif y.eng != op.eng or op.is_dma:
                    need[j] = True
                    wl.append(j)
                else:
                    if op.eng == "pe":
                        continue
                    if raw and (op.pos - y.pos) <= 3:
                        need[j] = True
                        wl.append(j)
            waits_of[op.idx] = wl
        cnt = {e: 0 for e in self.ENG}
        for e in self.ENG:
            for op in self.per_eng[e]:
                if not op.is_dma and need[op.idx]:
                    cnt[e] += 1
                    op.sig = cnt[e]
        st = self.stack
        esem = {e: st.enter_context(nc.semaphore("s_" + e)) for e in self.ENG}
        csem = {c: st.enter_context(nc.semaphore("c_" + c)) for c in self.chan_count}
        handles = {"pe": "tensor", "act": "scalar", "dve": "vector", "pool": "gpsimd", "sp": "sync"}
        nwait = [0]

        self.sim = {e: [] for e in self.ENG}

        def run_engine(ename, eng):
            seen = {}
            for op in self.per_eng[ename]:
                req = {}
                for j in waits_of[op.idx]:
                    y = ops[j]
                    if y.is_dma:
                        key, val = ("c", y.chan), 16 * y.grp_last.chan_cnt
                    else:
                        key, val = ("e", y.eng), y.sig
                    if req.get(key, 0) < val:
                        req[key] = val
                for key, val in req.items():
                    if seen.get(key, 0) >= val:
                        continue
                    seen[key] = val
                    sem = csem[key[1]] if key[0] == "c" else esem[key[1]]
                    eng.wait_ge(sem, val)
                    nwait[0] += 1
                self.sim[ename].append((op.idx, list(req.items()), ("c", op.chan) if op.is_dma else (("e", ename) if op.sig is not None else None)))
                ins = op.fn(eng)
                if op.is_dma:
                    ins.then_inc(csem[op.chan], 16)
                elif op.sig is not None:
                    ins.then_inc(esem[ename], 1)
            if ename == "sp":
                for c in final_waits:
                    eng.wait_ge(csem[c], 16 * self.chan_count[c])

        block = st.enter_context(nc.Block())
        for ename in self.ENG:
            getattr(block, handles[ename])(lambda eng, ename=ename: run_engine(ename, eng))
        self.stats = dict(n_ops={e: len(v) for e, v in self.per_eng.items()}, n_wait=nwait[0],
                          n_sem=len(esem) + len(csem))


def fap(t, off, dims, p0=0, pn=None):
    base = t[:]
    pstep = base.ap[0][0]
    if pn is None:
        pn = base.ap[0][1] - p0
    return bass.AP(base.tensor, base.offset + p0 * pstep + off, [[pstep, pn]] + [list(d) for d in dims])


class Builder:
    def __init__(self, depth=2, do_prompt=True, do_sample=True, stop_after=None, nsteps=None):
        self.nsteps = nsteps
        self.stepi = 0
        self.depth = depth
        self.do_prompt = do_prompt
        self.do_sample = do_sample
        self.stop_after = stop_after
        nc = bass.Bass("TRN2", target_bir_lowering=False)
        self.nc = nc
        self.S = Sched(nc)
        self.dram_in = {}
        self.dram_out = {}
        self.out_chans = []
        self.uid = 0

    def din(self, name, shape, dt=F32):
        ap = self.nc.dram_tensor(name, list(shape), dt, kind="ExternalInput").ap()
        self.dram_in[name] = ap
        return ap

    def dout(self, name, shape, dt=F32):
        ap = self.nc.dram_tensor(name, list(shape), dt, kind="ExternalOutput").ap()
        self.dram_out[name] = ap
        return ap

    def dscr(self, name, shape, dt=F32):
        return self.nc.dram_tensor(name, list(shape), dt, kind="Internal").ap()

    def u(self, p):
        self.uid += 1
        return "%s%d" % (p, self.uid)

    def declare(self):
        L = self.depth
        self.xT_in = self.din("xT_in", [KC, 128, SEQ])
        self.cT = self.din("cT", [128, KC, NCOL])
        self.w_mod = self.din("w_mod", [L, D, 9 * D])
        self.bmodT = self.din("bmodT", [L, 128, 72])
        self.gpreT = self.din("gpreT", [L, 128, 3, KC])
        self.gpostT = self.din("gpostT", [L, 128, 3, KC])
        self.w_ff_in = self.din("w_ff_in", [L, 2, D, 2 * DFF])
        self.w_ff_out = self.din("w_ff_out", [L, 2, DFF, D])
        self.yT_out = self.dout("yT_out", [KC, 128, SEQ])
        self.dbg = bool(_os.environ.get("K_DBG"))
        self.xres = (self.dout if self.dbg else self.dscr)("xres", [KC, 128, SEQ])
        self.cbf_in = self.din("cbf_in", [128, 896])
        self.cf32_in = self.din("cf32_in", [128, 288])
        self.cosT = self.din("cosT", [128, SEQ + 1])
        self.sinT = self.din("sinT", [128, SEQ + 1])
        self.w_in = self.din("w_in", [L, D, INC])
        self.conv_wT = self.din("conv_wT", [L, 128, 24, 4])
        self.conv_bT = self.din("conv_bT", [L, 128, 24])
        self.dt_bias = self.din("dt_bias", [L, 32])
        self.a_log = self.din("a_log", [L, 32])
        self.d_skip = self.din("d_skip", [L, 32])
        self.norm_ssm = self.din("norm_ssm", [L, 2048])
        self.w_br_att = self.din("w_br_att", [L, 512, D])
        self.w_br_ssm = self.din("w_br_ssm", [L, 2048, D])
        self.w_out = self.din("w_out", [L, D, D])
        self.kT_out = [self.dout("kT_out%d" % g, [L, 4, 128, WIN[g]]) for g in range(3)]
        self.vcm_out = [self.dout("vcm_out%d" % g, [L, 4, DIL[g], 128, 128]) for g in range(3)]
        self.ssm_out = self.dout("ssm_out", [L, 128, 2048])
        self.conv_out = self.dout("conv_out", [L, 128, 24, 3])
        self.khist = [self.dscr("khist%d" % g, [4, 128, WIN[g]], BF16) for g in range(3)]
        self.vhist = [self.dscr("vhist%d" % g, [4, 128, DIL[g] * 128], BF16) for g in range(3)]
        self.yscr = (self.dout if self.dbg else self.dscr)("yscr", [16, 128, T], BF16)
        if self.dbg:
            self.atto_dbg = self.dout("atto_dbg", [128, 4, T], BF16)
        if self.do_sample:
            self.s_declare()

    def consts(self):
        S = self.S
        nc = self.nc
        self.ones_bf = S.sb("ones_bf", [128, 128], BF16)
        S.add("pool", lambda e: e.memset(self.ones_bf[:], 1.0), w=["ones_bf"])
        self.psall = S.ps("psall", [128, 4096])
        self.psall_bf = self.psall[:].bitcast(BF16)
        self.bank = [self.psall[:, i * 512:(i + 1) * 512] for i in range(8)]
        self.bank_bf = [self.psall_bf[:, i * 1024:(i + 1) * 1024] for i in range(8)]
        self.cbf = S.sb("cbf", [128, 896], BF16)
        S.dma("pool", self.cbf[:], self.cbf_in, w=["cbf"], chan="misc2")
        self.ident_bf = self.cbf[:, 0:128]
        self.mask2 = self.cbf[:, 128:384]
        self.negm4 = self.cbf[:, 384:896]
        self.cf32 = S.sb("cf32", [128, 288], F32)
        S.dma("sp", self.cf32[:], self.cf32_in, w=["cf32"], chan="misc")
        self.tri = self.cf32[:, 0:128]
        self.sel127 = self.cf32[:, 128:256]
        self.identS = self.cf32[0:64, 256:288]
        self.Sst = S.sb("Sst", [128, 2048], F32)
        self.convhist = S.sb("convhist", [128, 24, 3], F32)
        self.convw = S.sb("convw", [128, 24, 4], F32)
        self.convb = S.sb("convb", [128, 24], F32)
        self.dtb_bc = S.sb("dtb_bc", [128, 32], F32)
        self.a_bc = S.sb("a_bc", [128, 32], F32)
        self.dsk_bc = S.sb("dsk_bc", [128, 32], F32)

        self.NSLOT = 3
        self.wring = [S.sb("wslot%d" % i, [128, 4096], BF16) for i in range(self.NSLOT)]
        self.wnext = 0
        self.modT = S.sb("modT", [128, 72, NCOL], F32)
        self.Aall = S.sb("Aall", [128, 3, KC, NCOL], F32)
        self.Gall = S.sb("Gall", [128, 3, KC, NCOL], F32)
        self.scT = S.sb("scT", [128, KC, NCOL], BF16)
        self.cTs = S.sb("cTs", [128, KC, NCOL], F32)
        self.bmod_sb = S.sb("bmod_sb", [128, 72], F32)
        self.gpre_sb = S.sb("gpre_sb", [128, 3, KC], F32)
        self.gpost_sb = S.sb("gpost_sb", [128, 3, KC], F32)
        if self.do_sample:
            self.s_consts()
        self.ARENA_B = (int(self.nc.sbuf_bytes_remaining) - 128) // 256 * 256
        self.amax = 0
        self.arena = S.sb("arena", [128, self.ARENA_B // 4], F32)
        self.arena16 = self.arena[:].bitcast(BF16)
        self.aptr = 0
        self.xt_i = 0

    def alloc(self, shape, dt):
        n = 1
        for v in shape:
            n *= v
        nb = n * (4 if dt == F32 else 2)
        off = self.aptr
        self.aptr = (off + nb + 63) // 64 * 64
        assert self.aptr <= self.ARENA_B, ("arena overflow", self.aptr, self.ARENA_B)
        self.amax = max(self.amax, self.aptr)
        if dt == F32:
            ap = self.arena[:, off // 4:off // 4 + n]
        else:
            ap = self.arena16[:, off // 2:off // 2 + n]
        if len(shape) == 2:
            return ap.rearrange("p (a b) -> p a b", a=shape[0])
        if len(shape) == 3:
            return ap.rearrange("p (a b c) -> p a b c", a=shape[0], b=shape[1])
        return ap

    def phase(self, mark=0):
        self.S.barrier()
        self.aptr = mark

    def norm_bufs(self):
        self.xt = [self.alloc([KC, 512], F32) for i in range(2)]
        self.sq = self.alloc([KC, 512], BF16)
        self.rstd = self.alloc([512], F32)
        self.tmp = self.alloc([KC, 512], F32)

    def ffn_bufs(self):
        self.phase(0)
        self.hT = self.alloc([KC, 1024], BF16)
        self.norm_bufs()
        self.aT = self.alloc([NJ, 1024], BF16)
        self.sg = [self.alloc([512], BF16) for i in range(2)]
        self.fT = self.alloc([KC, 1024], F32)
        self.sqf = self.alloc([KC, 1024], BF16)

    def wslot(self, loads, tag):
        S = self.S
        i = self.wnext
        self.wnext = (i + 1) % self.NSLOT
        slot = self.wring[i]
        key = "wslot%d" % i
        S.gid += 1
        grp = ("w", S.gid)
        for (off, dims, src) in loads:
            S.dma("pool", fap(slot, off, dims), src, w=[key], chan=key, grp=grp)
        return slot, key

    def modulation(self, l):
        S = self.S
        mm = self.bank[7]
        if l == 0:
            S.dma("sp", self.cTs[:], self.cT, w=["cTs"], chan="misc")
            S.add("act", lambda e: e.activation(out=self.scT[:], in_=self.cTs[:], func=AF.Silu), r=["cTs"], w=["scT"])
        S.dma("sp", self.bmod_sb[:], self.bmodT[l], w=["bmod_sb"], chan="misc")
        S.dma("sp", self.gpre_sb[:], self.gpreT[l], w=["gpre_sb"], chan="misc")
        S.dma("sp", self.gpost_sb[:], self.gpostT[l], w=["gpost_sb"], chan="misc")
        wv = self.w_mod[l].rearrange("(kc p) n -> p kc n", p=128)
        for blk in range(18):
            slot, key = self.wslot([(0, [[512, KC], [1, 512]], wv[:, :, blk * 512:(blk + 1) * 512])], "mod")
            for cc in range(4):
                ch = blk * 4 + cc
                for kc in range(KC):
                    S.add("pe", lambda e, ch=ch, kc=kc, cc=cc, slot=slot: e.matmul(
                        fap(mm, ch * NCOL, [[1, NCOL]]), lhsT=fap(slot, kc * 512 + cc * 128, [[1, 128]]),
                        rhs=self.scT[:, kc, :], start=(kc == 0), stop=(kc == KC - 1)),
                        r=[key, "scT"], w=["bank7"])
        S.add("dve", lambda e: e.tensor_tensor(
            out=self.modT[:], in0=fap(mm, 0, [[NCOL, 72], [1, NCOL]]),
            in1=self.bmod_sb[:].unsqueeze(2).to_broadcast([128, 72, NCOL]), op=ALU.add),
            r=["bank7", "bmod_sb"], w=["modT"])
        for k in range(3):
            S.add("dve", lambda e, k=k: e.scalar_tensor_tensor(
                out=self.Aall[:, k], in0=self.modT[:, k * 24 + 8:k * 24 + 16, :], scalar=1.0,
                in1=self.gpre_sb[:, k, :].unsqueeze(2).to_broadcast([128, KC, NCOL]),
                op0=ALU.add, op1=ALU.mult), r=["modT", "gpre_sb"], w=["Aall"])
            S.add("dve", lambda e, k=k: e.scalar_tensor_tensor(
                out=self.Gall[:, k], in0=self.modT[:, k * 24 + 16:k * 24 + 24, :], scalar=float(RES_W[k]),
                in1=self.gpost_sb[:, k, :].unsqueeze(2).to_broadcast([128, KC, NCOL]),
                op0=ALU.mult, op1=ALU.mult), r=["modT", "gpost_sb"], w=["Gall"])

    def rstd_from_sq(self, sq_ap_fn, nch, denom, rkeys):
        _rstd = self.rstd
        S = self.S
        st = self.bank[6]
        for c in range(nch):
            S.add("pe", lambda e, c=c: e.matmul(st[:, :], lhsT=self.ones_bf[:, :], rhs=sq_ap_fn(c),
                                                  start=(c == 0), stop=(c == nch - 1)),
                  r=list(rkeys) + ["ones_bf"], w=["bank6"])
        S.add("dve", lambda e: e.tensor_scalar(out=_rstd[:], in0=st[:, :], scalar1=1.0 / denom, scalar2=EPS,
                                                 op0=ALU.mult, op1=ALU.add), r=["bank6"], w=["rstd"])
        S.add("act", lambda e: e.activation(out=_rstd[:], in_=_rstd[:], func=AF.Sqrt), r=["rstd"], w=["rstd"])
        S.add("dve", lambda e: e.reciprocal(out=_rstd[:], in_=_rstd[:]), r=["rstd"], w=["rstd"])

    def load_x(self, src, tok0):
        S = self.S
        i = self.xt_i
        self.xt_i ^= 1
        xt = self.xt[i]
        key = "xt%d" % i
        S.dma("sp", xt[:], src[:, :, tok0:tok0 + 512].rearrange("c p t -> p c t"),
              r=["xdram"], w=[key], chan=key)
        return xt, key

    def prenorm(self, src, tok0, ntok, k, hoff):
        _hT = self.hT
        _sq = self.sq
        _rstd = self.rstd
        _tmp = self.tmp
        S = self.S
        for tt in range(ntok // 512):
            xt, xkey = self.load_x(src, tok0 + tt * 512)
            S.add("act", lambda e, xt=xt: e.activation(out=_sq[:], in_=xt[:], func=AF.Square),
                  r=[xkey], w=["sq"])
            self.rstd_from_sq(lambda c: _sq[:, c, :], KC, D, ["sq"])
            S.add("dve", lambda e, xt=xt: e.tensor_tensor(
                out=_tmp[:], in0=xt[:], in1=_rstd[:].unsqueeze(1).to_broadcast([128, KC, 512]),
                op=ALU.mult), r=[xkey, "rstd"], w=["tmp"])
            for c in range(KC):
                o0 = hoff + tt * 512
                S.add("act", lambda e, c=c, o0=o0: e.activation(
                    out=_hT[:, c, o0:o0 + 512], in_=_tmp[:, c, :], func=AF.Identity,
                    scale=self.Aall[:, k, c, 0:1], bias=self.modT[:, k * 24 + c, 0:1]),
                    r=["tmp", "Aall", "modT"], w=["hT"])

    def postnorm_residual(self, f_ap, sq_fn, k, src, dst, tok0, fkeys, out_chan=None):
        _rstd = self.rstd
        _tmp = self.tmp
        S = self.S
        self.rstd_from_sq(sq_fn, KC, D, fkeys)
        xt, xkey = self.load_x(src, tok0)
        S.add("dve", lambda e: e.tensor_tensor(
            out=_tmp[:], in0=f_ap, in1=_rstd[:].unsqueeze(1).to_broadcast([128, KC, 512]),
            op=ALU.mult), r=list(fkeys) + ["rstd"], w=["tmp"])
        for c in range(KC):
            S.add("dve", lambda e, c=c, xt=xt: e.scalar_tensor_tensor(
                out=xt[:, c, :], in0=_tmp[:, c, :], scalar=self.Gall[:, k, c, 0:1], in1=xt[:, c, :],
                op0=ALU.mult, op1=ALU.add), r=["tmp", "Gall", xkey], w=[xkey])
        ch = out_chan or (xkey + "o")
        S.dma("sp", dst[:, :, tok0:tok0 + 512].rearrange("c p t -> p c t"), xt[:], r=[xkey], w=["xdram"], chan=ch)
        if out_chan and out_chan not in self.out_chans:
            self.out_chans.append(out_chan)

    def ffn(self, l, i, k, src, dst, g0, is_out):
        self.ffn_bufs()
        _hT = self.hT
        _aT = self.aT
        _sg = self.sg
        _fT = self.fT
        _sqf = self.sqf
        S = self.S
        w1 = self.w_ff_in[l, i].rearrange("(kc p) n -> p kc n", p=128)
        w2 = self.w_ff_out[l, i].rearrange("(j p) n -> p j n", p=128)
        pa = 0
        for half in range(2):
            t0 = g0 + half * 1024
            self.prenorm(src, t0, 1024, k, 0)
            for jb in range(NJ // 2):
                slot, key = self.wslot([
                    (0, [[256, KC], [1, 256]], w1[:, :, jb * 256:(jb + 1) * 256]),
                    (2048, [[256, KC], [1, 256]], w1[:, :, DFF + jb * 256:DFF + (jb + 1) * 256])], "w1")
                for jj in range(2):
                    j = jb * 2 + jj
                    for tt in range(2):
                        A = self.bank[pa]
                        B = self.bank[2 + pa]
                        ka, kb = "bank%d" % pa, "bank%d" % (2 + pa)
                        sg = _sg[pa]
                        sk = "sg%d" % pa
                        pa ^= 1
                        for kc in range(KC):
                            S.add("pe", lambda e, A=A, kc=kc, jj=jj, tt=tt, slot=slot: e.matmul(
                                A[:, :], lhsT=fap(slot, kc * 256 + jj * 128, [[1, 128]]),
                                rhs=_hT[:, kc, tt * 512:(tt + 1) * 512], start=(kc == 0), stop=(kc == KC - 1)),
                                r=[key, "hT"], w=[ka])
                        for kc in range(KC):
                            S.add("pe", lambda e, B=B, kc=kc, jj=jj, tt=tt, slot=slot: e.matmul(
                                B[:, :], lhsT=fap(slot, 2048 + kc * 256 + jj * 128, [[1, 128]]),
                                rhs=_hT[:, kc, tt * 512:(tt + 1) * 512], start=(kc == 0), stop=(kc == KC - 1)),
                                r=[key, "hT"], w=[kb])
                        S.add("act", lambda e, A=A, sg=sg: e.activation(out=sg[:], in_=A[:, :], func=AF.Silu),
                              r=[ka], w=[sk])
                        S.add("dve", lambda e, B=B, sg=sg, j=j, tt=tt: e.tensor_tensor(
                            out=_aT[:, j, tt * 512:(tt + 1) * 512], in0=sg[:], in1=B[:, :], op=ALU.mult),
                            r=[sk, kb], w=["aT"])
            pf = 0
            for m in range(KC):
                slot, key = self.wslot([(0, [[128, NJ], [1, 128]], w2[:, :, m * 128:(m + 1) * 128])], "w2")
                for tt in range(2):
                    Fp = self.bank[4 + pf]
                    kf = "bank%d" % (4 + pf)
                    pf ^= 1
                    for j in range(NJ):
                        S.add("pe", lambda e, Fp=Fp, j=j, tt=tt, slot=slot: e.matmul(
                            Fp[:, :], lhsT=fap(slot, j * 128, [[1, 128]]),
                            rhs=_aT[:, j, tt * 512:(tt + 1) * 512], start=(j == 0), stop=(j == NJ - 1)),
                            r=[key, "aT"], w=[kf])
                    S.add("act", lambda e, Fp=Fp, m=m, tt=tt: e.activation(
                        out=_fT[:, m, tt * 512:(tt + 1) * 512], in_=Fp[:, :], func=AF.Copy), r=[kf], w=["fT%d" % tt])
                    S.add("act", lambda e, Fp=Fp, m=m, tt=tt: e.activation(
                        out=_sqf[:, m, tt * 512:(tt + 1) * 512], in_=Fp[:, :], func=AF.Square), r=[kf], w=["sqf%d" % tt])
            for tt in range(2):
                self.postnorm_residual(_fT[:, :, tt * 512:(tt + 1) * 512],
                                       lambda c, tt=tt: _sqf[:, c, tt * 512:(tt + 1) * 512],
                                       k, src, dst, t0 + tt * 512, ["fT%d" % tt, "sqf%d" % tt],
                                       out_chan=("yout" if is_out else None))

    def proj_tiles(self, slot, woff, wkstride, key, ntile, banks, cb, extra_r=()):
        _hT = self.hT
        S = self.S
        for tt in range(ntile):
            bi = banks[tt % len(banks)]
            Bk = self.bank[bi]
            bk = "bank%d" % bi
            for kc in range(KC):
                S.add("pe", lambda e, Bk=Bk, kc=kc, tt=tt: e.matmul(
                    Bk[:, :], lhsT=fap(slot, woff + kc * wkstride, [[1, 128]]),
                    rhs=_hT[:, kc, tt * 512:(tt + 1) * 512], start=(kc == 0), stop=(kc == KC - 1)),
                    r=[key, "hT"] + list(extra_r), w=[bk])
            cb(tt, Bk, bk)

    def win_cols(self, l, c0, n):
        return self.w_in[l].rearrange("(kc p) n -> p kc n", p=128)[:, :, c0:c0 + n]

    def attention(self, l, g0):
        _hT = self.hT
        _attoT = self.attoT
        S = self.S
        last_grp = (g0 + T == SEQ) and not _os.environ.get("K_NOKVOUT")
        cosF = self.alloc([T], F32)
        sinS = self.alloc([T], F32)
        S.dma("sp", cosF[:], self.cosT[:, g0:g0 + T], w=["cosF"], chan="misc")
        S.dma("sp", sinS[:], self.sinT[:, g0:g0 + T], w=["sinS"], chan="misc")
        qb = self.alloc([3, T], BF16)
        kb = [self.alloc([WIN[g] + T], BF16) for g in range(3)]
        vb = [self.alloc([DIL[g] + T // 128, 128], BF16) for g in range(3)]
        ND = self.alloc([2, T], F32)
        Pt = [self.alloc([256], BF16) for i in range(2)]
        r1 = [self.alloc([512], F32) for i in range(2)]
        r2 = [self.alloc([512], F32) for i in range(2)]
        kst = [self.alloc([512], F32) for i in range(2)]
        vst = [self.alloc([512], F32) for i in range(2)]
        rden = self.alloc([T], F32)
        cnt = {"rt": 0, "vs": 0, "pt": 0, "ps": 0, "po": 0}
        for h in range(4):
            if g0 > 0 and not _os.environ.get("K_NOHISTLD"):
                for g in range(3):
                    S.dma("sp", kb[g][:, 0:WIN[g]], self.khist[g][h], r=["khist%d" % g], w=["kb%d" % g], chan="hist")
                    S.dma("sp", vb[g][:, 0:DIL[g], :], self.vhist[g][h].rearrange("p (a b) -> p a b", b=128),
                          r=["vhist%d" % g], w=["vb%d" % g], chan="hist")
            for g in range(3):
                d, span = DIL[g], WIN[g]
                cb0 = g * 1536 + h * 128
                for qk in range(2):
                    c0 = cb0 + qk * 512
                    wsrc = self.win_cols(l, c0, 128)
                    slot, key = self.wslot([
                        (0, [[128, KC], [1, 128]], wsrc),
                        (1024, [[128, KC], [1, 64]], wsrc[:, :, 64:128]),
                        (1024 + 64, [[128, KC], [1, 64]], wsrc[:, :, 0:64])], "wqk")

                    def rope_cb(tt, Bk, bk, slot=slot, key=key, qk=qk, g=g, h=h, span=span):
                        bi2 = 2 + (tt % 2)
                        B2 = self.bank[bi2]
                        b2k = "bank%d" % bi2
                        for kc in range(KC):
                            S.add("pe", lambda e, kc=kc: e.matmul(
                                B2[:, :], lhsT=fap(slot, 1024 + kc * 128, [[1, 128]]),
                                rhs=_hT[:, kc, tt * 512:(tt + 1) * 512], start=(kc == 0), stop=(kc == KC - 1)),
                                r=[key, "hT"], w=[b2k])
                        i = cnt["rt"] % 2
                        cnt["rt"] += 1
                        S.add("dve", lambda e: e.tensor_tensor(out=r1[i][:], in0=Bk[:, :], in1=cosF[:, tt * 512:(tt + 1) * 512],
                                                                 op=ALU.mult), r=[bk, "cosF"], w=["r1_%d" % i])
                        S.add("dve", lambda e: e.tensor_tensor(out=r2[i][:], in0=B2[:, :], in1=sinS[:, tt * 512:(tt + 1) * 512],
                                                                 op=ALU.mult), r=[b2k, "sinS"], w=["r2_%d" % i])
                        if qk == 0:
                            S.add("pool", lambda e: e.tensor_tensor(out=qb[:, g, tt * 512:(tt + 1) * 512], in0=r1[i][:],
                                                                      in1=r2[i][:], op=ALU.add),
                                  r=["r1_%d" % i, "r2_%d" % i], w=["qb"])
                        else:
                            S.add("pool", lambda e: e.tensor_tensor(out=kst[i][:], in0=r1[i][:], in1=r2[i][:], op=ALU.add),
                                  r=["r1_%d" % i, "r2_%d" % i], w=["kst%d" % i])
                            S.add("act", lambda e: e.activation(out=kb[g][:, span + tt * 512:span + (tt + 1) * 512],
                                                                  in_=kst[i][:], func=AF.Copy),
                                  r=["kst%d" % i], w=["kb%d" % g])
                            if last_grp:
                                lo = max(g0 + tt * 512, SEQ - span)
                                hi = g0 + (tt + 1) * 512
                                if lo < hi:
                                    S.dma("sp", self.kT_out[g][l, h, :, lo - (SEQ - span):hi - (SEQ - span)],
                                          kst[i][:, lo - (g0 + tt * 512):512], r=["kst%d" % i], w=["kTout"], chan="kvout")
                    self.proj_tiles(slot, 0, 128, key, T // 512, [0, 1], rope_cb)
                c0 = cb0 + 1024
                slot, key = self.wslot([(0, [[128, KC], [1, 128]], self.win_cols(l, c0, 128))], "wv")
                tiles = [(n, r) for n in range(T // span) for r in range(d)]
                for t4 in range(0, len(tiles), 4):
                    bi = (t4 // 4) % 2
                    Bk = self.bank[bi]
                    bk = "bank%d" % bi
                    for q4 in range(4):
                        n, r = tiles[t4 + q4]
                        for kc in range(KC):
                            S.add("pe", lambda e, Bk=Bk, kc=kc, q4=q4, n=n, r=r, d=d, span=span, slot=slot: e.matmul(
                                Bk[:, q4 * 128:(q4 + 1) * 128],
                                lhsT=fap(_hT, kc * T + n * span + r, [[d, 128]]),
                                rhs=fap(slot, kc * 128, [[1, 128]]), start=(kc == 0), stop=(kc == KC - 1)),
                                r=[key, "hT"], w=[bk])
                    ti0 = d + t4
                    S.add("act", lambda e, Bk=Bk, ti0=ti0, g=g: e.activation(
                        out=vb[g][:, ti0:ti0 + 4, :], in_=fap(Bk, 0, [[128, 4], [1, 128]]), func=AF.Copy),
                        r=[bk], w=["vb%d" % g])
                    if last_grp and tiles[t4 + 3][0] == T // span - 1:
                        i = cnt["vs"] % 2
                        cnt["vs"] += 1
                        S.add("act", lambda e, Bk=Bk, i=i: e.activation(out=vst[i][:], in_=Bk[:, :], func=AF.Copy), r=[bk], w=["vst%d" % i])
                        for q4 in range(4):
                            n, r = tiles[t4 + q4]
                            if n == T // span - 1:
                                S.dma("sp", self.vcm_out[g][l, h, r], vst[i][:, q4 * 128:(q4 + 1) * 128],
                                      r=["vst%d" % i], w=["vout"], chan="kvout")
            for g in range(3):
                d, span = DIL[g], WIN[g]
                for n in range(T // span):
                    for r in range(d):
                        has_prev = (g0 > 0) or (n > 0)
                        c_lo = 0 if has_prev else 128
                        qap = fap(qb, g * T + n * span + r, [[d, 128]])
                        kcur = fap(kb[g], span + n * span + r, [[d, 128]])
                        kprev = fap(kb[g], n * span + r, [[d, 128]])
                        vcur = vb[g][:, d + n * d + r, :]
                        vprev = vb[g][:, n * d + r, :]
                        si = 4 + cnt["ps"] % 2
                        cnt["ps"] += 1
                        oi = 6 + cnt["po"] % 2
                        cnt["po"] += 1
                        pi = cnt["pt"] % 2
                        cnt["pt"] += 1
                        Sb, Ob, P = self.bank[si], self.bank[oi], Pt[pi]
                        sk, ok, pk = "bank%d" % si, "bank%d" % oi, "Pt%d" % pi
                        if has_prev:
                            S.add("pe", lambda e, Sb=Sb, kprev=kprev, qap=qap: e.matmul(
                                Sb[:, 0:128], lhsT=kprev, rhs=qap, start=True, stop=True), r=["kb%d" % g, "qb"], w=[sk])
                        S.add("pe", lambda e, Sb=Sb, kcur=kcur, qap=qap: e.matmul(
                            Sb[:, 128:256], lhsT=kcur, rhs=qap, start=True, stop=True), r=["kb%d" % g, "qb"], w=[sk])
                        S.add("act", lambda e, Sb=Sb, P=P, c_lo=c_lo: e.activation(
                            out=P[:, c_lo:256], in_=Sb[:, c_lo:256], func=AF.Exp, scale=float(SCALE)), r=[sk], w=[pk])
                        S.add("pool", lambda e, P=P, c_lo=c_lo: e.tensor_tensor(
                            out=P[:, c_lo:256], in0=P[:, c_lo:256], in1=self.mask2[:, c_lo:256], op=ALU.mult),
                            r=[pk, "cbf"], w=[pk])
                        for half, lh in ((0, None), (1, self.ones_bf)):
                            oc = half * 128
                            if has_prev:
                                S.add("pe", lambda e, Ob=Ob, P=P, oc=oc, lh=lh, vprev=vprev: e.matmul(
                                    Ob[:, oc:oc + 128], lhsT=(vprev if lh is None else lh[:, :]), rhs=P[:, 0:128],
                                    start=True, stop=False), r=[pk, "vb%d" % g, "ones_bf"], w=[ok])
                            S.add("pe", lambda e, Ob=Ob, P=P, oc=oc, lh=lh, vcur=vcur, has_prev=has_prev: e.matmul(
                                Ob[:, oc:oc + 128], lhsT=(vcur if lh is None else lh[:, :]), rhs=P[:, 128:256],
                                start=(not has_prev), stop=True), r=[pk, "vb%d" % g, "ones_bf"], w=[ok])
                        nd = fap(ND, n * span + r, [[T, 2], [d, 128]])
                        osrc = fap(Ob, 0, [[128, 2], [1, 128]])
                        if g == 0:
                            S.add("dve", lambda e, nd=nd, osrc=osrc: e.tensor_copy(out=nd, in_=osrc), r=[ok], w=["ND"])
                        else:
                            S.add("dve", lambda e, nd=nd, osrc=osrc: e.tensor_tensor(out=nd, in0=osrc, in1=nd, op=ALU.add),
                                  r=[ok, "ND"], w=["ND"])
            S.add("dve", lambda e: e.reciprocal(out=rden[:], in_=ND[:, 1, :]), r=["ND"], w=["rden"])
            S.add("dve", lambda e, h=h: e.tensor_tensor(out=_attoT[:, h, :], in0=ND[:, 0, :], in1=rden[:], op=ALU.mult),
                  r=["ND", "rden"], w=["attoT"])
            if not last_grp:
                for g in range(3):
                    S.dma("sp", self.khist[g][h], kb[g][:, T:T + WIN[g]], r=["kb%d" % g], w=["khist%d" % g], chan="hist")
                    S.dma("sp", self.vhist[g][h].rearrange("p (a b) -> p a b", b=128),
                          vb[g][:, T // 128:T // 128 + DIL[g], :], r=["vb%d" % g], w=["vhist%d" % g], chan="hist")
        if last_grp and "kvout" not in self.out_chans:
            self.out_chans.append("kvout")
        if self.dbg and g0 == 0 and l == 0:
            S.dma("sp", self.atto_dbg, _attoT[:], r=["attoT"], w=["attodbg"], chan="dbg")
            self.out_chans.append("dbg")

    def ssd_params(self, l):
        S = self.S
        S.dma("sp", self.convw[:], self.conv_wT[l], w=["convw"], chan="misc")
        S.dma("sp", self.convb[:], self.conv_bT[l], w=["convb"], chan="misc")
        S.dma("sp", self.dtb_bc[:], self.dt_bias[l].partition_broadcast(128), w=["dtb_bc"], chan="misc")
        S.dma("sp", self.a_bc[:], self.a_log[l].partition_broadcast(128), w=["a_bc"], chan="misc")
        S.dma("sp", self.dsk_bc[:], self.d_skip[l].partition_broadcast(128), w=["dsk_bc"], chan="misc")
        S.add("act", lambda e: e.activation(out=self.a_bc[:], in_=self.a_bc[:], func=AF.Exp), r=["a_bc"], w=["a_bc"])
        S.add("dve", lambda e: e.tensor_scalar(out=self.a_bc[:], in0=self.a_bc[:], scalar1=-1.0, scalar2=None, op0=ALU.mult),
              r=["a_bc"], w=["a_bc"])
        S.add("dve", lambda e: e.memset(self.Sst[:], 0.0), w=["Sst"])
        S.add("dve", lambda e: e.memset(self.convhist[:], 0.0), w=["convhist"])

    def ssd(self, l, g0):
        _hT = self.hT
        S = self.S
        last_grp = (g0 + T == SEQ)
        NB = T // 128
        dtT = self.alloc([NB, 32], F32)
        dA2 = self.alloc([NB, 64], F32)
        acs = self.alloc([NB, 32], F32)
        eacs = self.alloc([NB, 32], F32)
        dtw = self.alloc([NB, 32], F32)
        dec = self.alloc([NB, 32], F32)
        L1 = self.alloc([T], F32)
        acsT = self.alloc([T], F32)
        R1 = self.alloc([8, 128], F32)
        raw = self.alloc([T + 8], F32)
        acc = self.alloc([T], F32)
        xTc = self.alloc([4, T], BF16)
        BT = self.alloc([T], BF16)
        CT = self.alloc([T], BF16)
        yT = self.alloc([4, T], BF16)
        Sbf = self.alloc([512], BF16)
        Xtm = [self.alloc([512], BF16) for i in range(2)]
        Xdt = [self.alloc([512], BF16) for i in range(2)]
        Xw = [self.alloc([512], BF16) for i in range(2)]
        XD = [self.alloc([512], BF16) for i in range(2)]
        Btm = [self.alloc([128], BF16) for i in range(2)]
        CBm = [self.alloc([128], BF16) for i in range(2)]
        _Lt = self.alloc([8, 128], BF16)
        Lt = [_Lt, _Lt]
        _MG = self.alloc([8, 128], BF16)
        MG = [_MG, _MG]
        _sz = self.alloc([512], F32)
        sz = [_sz, _sz]
        _t1 = self.alloc([512], F32)
        t1 = [_t1, _t1]
        _yg = self.alloc([512], F32)
        yg = [_yg, _yg]
        nssm = self.alloc([512], F32)
        yn = [self.alloc([512], BF16) for i in range(2)]
        ss = self.alloc([4], F32)
        slot, key = self.wslot([(0, [[32, KC], [1, 32]], self.win_cols(l, 9728, 32))], "wdt")
        b0 = self.bank[0]
        for blk in range(NB):
            for kc in range(KC):
                S.add("pe", lambda e, blk=blk, kc=kc, slot=slot: e.matmul(
                    b0[:, blk * 32:(blk + 1) * 32], lhsT=_hT[:, kc, blk * 128:(blk + 1) * 128],
                    rhs=fap(slot, kc * 32, [[1, 32]]), start=(kc == 0), stop=(kc == KC - 1)), r=[key, "hT"], w=["bank0"])
        S.add("dve", lambda e: e.tensor_tensor(out=dtT[:], in0=fap(b0, 0, [[32, NB], [1, 32]]),
                                                 in1=self.dtb_bc[:].unsqueeze(1).to_broadcast([128, NB, 32]), op=ALU.add),
              r=["bank0", "dtb_bc"], w=["dtT"])
        S.add("act", lambda e: e.activation(out=dtT[:], in_=dtT[:], func=AF.Exp), r=["dtT"], w=["dtT"])
        S.add("act", lambda e: e.activation(out=dtT[:], in_=dtT[:], func=AF.Ln, bias=1.0), r=["dtT"], w=["dtT"])
        if self.stop_after == "ssd_p1":
            return
        for hf in range(2):
            S.add("dve", lambda e, hf=hf: e.tensor_tensor(
                out=dA2[:, :, hf * 32:(hf + 1) * 32], in0=dtT[:],
                in1=self.a_bc[:].unsqueeze(1).to_broadcast([128, NB, 32]), op=ALU.mult), r=["dtT", "a_bc"], w=["dA2"])
        S.add("dve", lambda e: e.memset(L1[0:32, :], 1.0), w=["L1a"])
        for b4 in range(NB // 4):
            bi = 1 + b4 % 2
            Bk = self.bank[bi]
            bk = "bank%d" % bi
            for q in range(4):
                blk = b4 * 4 + q
                S.add("pe", lambda e, Bk=Bk, q=q, blk=blk: e.matmul(
                    Bk[0:64, q * 128:(q + 1) * 128], lhsT=dA2[:, blk, :], rhs=self.tri[:, :], start=True, stop=True),
                    r=["dA2", "cf32"], w=[bk])
            S.add("act", lambda e, Bk=Bk, b4=b4: e.activation(out=acsT[0:32, b4 * 512:(b4 + 1) * 512], in_=Bk[0:32, :],
                                                               func=AF.Copy), r=[bk], w=["acsT"])
            S.add("act", lambda e, Bk=Bk, b4=b4: e.activation(out=L1[32:64, b4 * 512:(b4 + 1) * 512], in_=Bk[32:64, :],
                                                               func=AF.Copy, scale=-1.0), r=[bk], w=["L1b"])
        if self.stop_after == "ssd_p2":
            return
        b3 = self.bank[3]
        for blk in range(NB):
            S.add("pe", lambda e, blk=blk: e.matmul(b3[:, blk * 32:(blk + 1) * 32], lhsT=acsT[0:32, blk * 128:(blk + 1) * 128],
                                                    rhs=self.identS[0:32, 0:32], start=True, stop=True),
                  r=["acsT", "cf32"], w=["bank3"])
        S.add("dve", lambda e: e.tensor_copy(out=acs[:], in_=fap(b3, 0, [[32, NB], [1, 32]])), r=["bank3"], w=["acs"])
        if self.stop_after == "ssd_p3":
            return
        b4_ = self.bank[4]
        BD = self.alloc([NB, 32], F32)
        S.add("dve", lambda e: e.tensor_tensor(
            out=BD[0:32], in0=fap(acsT, 127, [[128, NB]], pn=32).unsqueeze(2).to_broadcast([32, NB, 32]),
            in1=self.identS[0:32, 0:32].unsqueeze(1).to_broadcast([32, NB, 32]), op=ALU.mult), r=["acsT", "cf32"], w=["BD"])
        S.add("pe", lambda e: e.matmul(b4_[:, :], lhsT=L1[0:32, 0:128], rhs=fap(BD, 0, [[1, NB * 32]], pn=32),
                                       start=True, stop=True), r=["BD", "L1a"], w=["bank4"])
        if self.stop_after == "ssd_p4":
            return
        S.add("act", lambda e: e.activation(out=dec[:], in_=fap(b4_, 0, [[32, NB], [1, 32]]), func=AF.Exp), r=["bank4"], w=["dec"])
        if self.stop_after == "ssd_p5":
            return
        S.add("act", lambda e: e.activation(out=dtw[:], in_=fap(b4_, 0, [[32, NB], [1, 32]]), func=AF.Copy), r=["bank4"], w=["dtw"])
        S.add("pool", lambda e: e.tensor_tensor(out=dtw[:], in0=dtw[:], in1=acs[:], op=ALU.subtract), r=["dtw", "acs"], w=["dtw"])
        if self.stop_after == "ssd_p6":
            return
        S.add("act", lambda e: e.activation(out=dtw[:], in_=dtw[:], func=AF.Exp), r=["dtw"], w=["dtw"])
        S.add("dve", lambda e: e.tensor_tensor(out=dtw[:], in0=dtw[:], in1=dtT[:], op=ALU.mult), r=["dtw", "dtT"], w=["dtw"])
        if self.stop_after == "ssd_p7":
            return
        S.add("act", lambda e: e.activation(out=eacs[:], in_=acs[:], func=AF.Exp), r=["acs"], w=["eacs"])
        if self.stop_after == "ssd_pre":
            return
        for G in range(4):
            chunks = [(6656 + G * 512 + q * 128, G * 4 + q, ("x", q)) for q in range(4)]
            chunks += [(6656 + 2048 + G * 128, 16 + G, ("B", 0)), (6656 + 2560 + G * 128, 20 + G, ("C", 0))]
            for (c0, ci, (kind, q)) in chunks:
                slot, key = self.wslot([(0, [[128, KC], [1, 128]], self.win_cols(l, c0, 128))], "wx")

                def ev(tt, Bk, bk):
                    S.add("act", lambda e: e.activation(out=raw[:, 3 + tt * 512:3 + (tt + 1) * 512], in_=Bk[:, :], func=AF.Copy),
                          r=[bk], w=["raw"])
                S.add("act", lambda e, ci=ci: e.activation(out=raw[:, 0:3], in_=self.convhist[:, ci, :], func=AF.Copy),
                      r=["convhist"], w=["raw"])
                self.proj_tiles(slot, 0, 128, key, T // 512, [5, 6], ev)
                S.add("dve", lambda e, ci=ci: e.tensor_scalar(
                    out=acc[:], in0=raw[:, 3:3 + T], scalar1=self.convw[:, ci, 3:4], scalar2=self.convb[:, ci:ci + 1],
                    op0=ALU.mult, op1=ALU.add), r=["raw", "convw", "convb"], w=["acc"])
                for j in (2, 1, 0):
                    S.add("dve", lambda e, ci=ci, j=j: e.scalar_tensor_tensor(
                        out=acc[:], in0=raw[:, j:j + T], scalar=self.convw[:, ci, j:j + 1], in1=acc[:],
                        op0=ALU.mult, op1=ALU.add), r=["raw", "convw", "acc"], w=["acc"])
                dst = xTc[:, q, :] if kind == "x" else (BT[:] if kind == "B" else CT[:])
                dk = {"x": "xTc", "B": "BT", "C": "CT"}[kind]
                S.add("act", lambda e, dst=dst: e.activation(out=dst, in_=acc[:], func=AF.Silu), r=["acc"], w=[dk])
                S.add("act", lambda e, ci=ci: e.activation(out=self.convhist[:, ci, :], in_=raw[:, T:T + 3], func=AF.Copy),
                      r=["raw"], w=["convhist"])
            if self.stop_after == "ssd_conv":
                continue
            S.dma("sp", nssm[:], self.norm_ssm[l, G * 512:(G + 1) * 512].partition_broadcast(128), w=["nssm"], chan="misc")
            S.add("dve", lambda e, G=G: e.tensor_copy(
                out=R1[32:64], in_=self.identS[32:64, 8 * G:8 * G + 8].unsqueeze(2).to_broadcast([32, 8, 128])),
                r=["cf32"], w=["R1b"])
            Sg = self.Sst[:, G * 512:(G + 1) * 512]
            S.add("act", lambda e, Sg=Sg: e.activation(out=Sbf[:], in_=Sg, func=AF.Copy), r=["Sst"], w=["Sbf"])
            wz, wzk = self.wslot([(0, [[512, KC], [1, 512]], self.win_cols(l, 4608 + G * 512, 512))], "wz")
            for c in range(NB if self.stop_after != "ssd_c1" else 1):
                i = c % 2
                tk = lambda nm: ("%s" % nm) if nm in ("Lt", "MG", "sz", "t1", "yg") else "%s%d" % (nm, i)
                tok = slice(c * 128, (c + 1) * 128)
                hs = slice(8 * G, 8 * G + 8)
                trb = self.bank_bf[7]
                for q in range(4):
                    S.add("pe", lambda e, q=q, tok=tok: e.transpose(out=trb[:, q * 128:(q + 1) * 128], in_=xTc[:, q, tok],
                                                                    identity=self.ident_bf), r=["xTc", "cbf"], w=["bank7"])
                S.add("pe", lambda e, tok=tok: e.transpose(out=trb[:, 512:640], in_=BT[:, tok], identity=self.ident_bf),
                      r=["BT", "cbf"], w=["bank7"])
                S.add("act", lambda e, i=i: e.activation(out=Xtm[i][:], in_=trb[:, 0:512], func=AF.Copy), r=["bank7"], w=[tk("Xtm")])
                S.add("act", lambda e, i=i: e.activation(out=Btm[i][:], in_=trb[:, 512:640], func=AF.Copy), r=["bank7"], w=[tk("Btm")])
                bc = lambda t_, c=c, hs=hs: t_[:, c, hs].unsqueeze(2).to_broadcast([128, 8, 64])
                x3 = lambda t_: t_[:].rearrange("p (e q) -> p e q", e=8)
                S.add("pool", lambda e, i=i, bc=bc, x3=x3: e.tensor_tensor(out=x3(Xdt[i]), in0=x3(Xtm[i]), in1=bc(dtT), op=ALU.mult),
                      r=[tk("Xtm"), "dtT"], w=[tk("Xdt")])
                S.add("pool", lambda e, i=i, bc=bc, x3=x3: e.tensor_tensor(out=x3(Xw[i]), in0=x3(Xtm[i]), in1=bc(dtw), op=ALU.mult),
                      r=[tk("Xtm"), "dtw"], w=[tk("Xw")])
                S.add("dve", lambda e, i=i, x3=x3, hs=hs: e.tensor_tensor(
                    out=x3(XD[i]), in0=x3(Xtm[i]), in1=self.dsk_bc[:, hs].unsqueeze(2).to_broadcast([128, 8, 64]), op=ALU.mult),
                    r=[tk("Xtm"), "dsk_bc"], w=[tk("XD")])
                b0_ = self.bank[0]
                S.add("pe", lambda e, tok=tok: e.matmul(b0_[:, 0:128], lhsT=BT[:, tok], rhs=CT[:, tok], start=True, stop=True),
                      r=["BT", "CT"], w=["bank0"])
                S.add("dve", lambda e, i=i: e.tensor_tensor(out=CBm[i][:], in0=b0_[:, 0:128], in1=self.tri[:, :], op=ALU.mult),
                      r=["bank0", "cf32"], w=[tk("CBm")])
                S.add("dve", lambda e, tok=tok, G=G: e.tensor_tensor(
                    out=R1[0:32], in0=acsT[0:32, tok].unsqueeze(1).to_broadcast([32, 8, 128]),
                    in1=self.identS[0:32, 8 * G:8 * G + 8].unsqueeze(2).to_broadcast([32, 8, 128]), op=ALU.mult),
                    r=["acsT", "cf32"], w=["R1a"])
                for hh in range(2):
                    Bh = self.bank[1 + hh]
                    S.add("pe", lambda e, Bh=Bh: e.matmul(Bh[:, :], lhsT=self.ident_bf, rhs=self.negm4, start=True, stop=False),
                          r=["cbf"], w=["bank%d" % (1 + hh)])
                    S.add("pe", lambda e, Bh=Bh, hh=hh, tok=tok: e.matmul(
                        Bh[:, :], lhsT=L1[0:64, tok], rhs=fap(R1, hh * 512, [[1, 512]], pn=64), start=False, stop=True),
                        r=["L1a", "L1b", "R1a", "R1b"], w=["bank%d" % (1 + hh)])
                S.add("act", lambda e, i=i: e.activation(out=fap(Lt[i], 0, [[1, 1024]]), in_=self.psall[:, 512:1536], func=AF.Exp),
                      r=["bank1", "bank2"], w=[tk("Lt")])
                S.add("dve", lambda e, i=i: e.tensor_tensor(out=MG[i][:], in0=Lt[i][:],
                                                             in1=CBm[i][:].unsqueeze(1).to_broadcast([128, 8, 128]), op=ALU.mult),
                      r=[tk("Lt"), tk("CBm")], w=[tk("MG")])
                bY, bO, bC, bZ = self.bank[3], self.bank[4], self.bank[5], self.bank[6]
                S.add("pe", lambda e, i=i: e.matmul(bY[:, :], lhsT=self.ident_bf, rhs=XD[i][:], start=True, stop=False),
                      r=["cbf", tk("XD")], w=["bank3"])
                for hd in range(8):
                    S.add("pe", lambda e, i=i, hd=hd: e.matmul(
                        bY[:, hd * 64:(hd + 1) * 64], lhsT=MG[i][:, hd, :], rhs=Xdt[i][:, hd * 64:(hd + 1) * 64],
                        start=False, stop=(hd == 7), skip_group_check=True), r=[tk("MG"), tk("Xdt")], w=["bank3"])
                S.add("pe", lambda e, tok=tok: e.matmul(bO[:, :], lhsT=CT[:, tok], rhs=Sbf[:], start=True, stop=True),
                      r=["CT", "Sbf"], w=["bank4"])
                S.add("pe", lambda e, i=i: e.matmul(bC[:, :], lhsT=Btm[i][:], rhs=Xw[i][:], start=True, stop=True),
                      r=[tk("Btm"), tk("Xw")], w=["bank5"])
                for kc in range(KC):
                    S.add("pe", lambda e, kc=kc, tok=tok, wz=wz: e.matmul(
                        bZ[:, :], lhsT=_hT[:, kc, tok], rhs=fap(wz, kc * 512, [[1, 512]]), start=(kc == 0), stop=(kc == KC - 1)),
                        r=["hT", wzk], w=["bank6"])
                S.add("act", lambda e, i=i: e.activation(out=sz[i][:], in_=bZ[:, :], func=AF.Silu), r=["bank6"], w=[tk("sz")])
                S.add("dve", lambda e, i=i, bc=bc, x3=x3: e.tensor_tensor(
                    out=x3(t1[i]), in0=fap(bO, 0, [[64, 8], [1, 64]]), in1=bc(eacs), op=ALU.mult), r=["bank4", "eacs"], w=[tk("t1")])
                S.add("dve", lambda e, i=i: e.tensor_tensor(out=t1[i][:], in0=bY[:, :], in1=t1[i][:], op=ALU.add),
                      r=["bank3", tk("t1")], w=[tk("t1")])
                S.add("pool", lambda e, i=i: e.tensor_tensor(out=yg[i][:], in0=t1[i][:], in1=sz[i][:], op=ALU.mult),
                      r=[tk("t1"), tk("sz")], w=[tk("yg")])
                S.add("act", lambda e, i=i: e.activation(out=t1[i][:], in_=yg[i][:], func=AF.Square), r=[tk("yg")], w=[tk("t1")])
                S.add("dve", lambda e, i=i: e.reduce_sum(out=ss[:, 0:1], in_=t1[i][:], axis=AX.X), r=[tk("t1")], w=["ss"])
                S.add("dve", lambda e: e.tensor_scalar(out=ss[:, 1:2], in0=ss[:, 0:1], scalar1=1.0 / 512, scalar2=EPS,
                                                        op0=ALU.mult, op1=ALU.add), r=["ss"], w=["ss1"])
                S.add("act", lambda e: e.activation(out=ss[:, 2:3], in_=ss[:, 1:2], func=AF.Sqrt), r=["ss1"], w=["ss2"])
                S.add("dve", lambda e: e.reciprocal(out=ss[:, 3:4], in_=ss[:, 2:3]), r=["ss2"], w=["ss3"])
                S.add("dve", lambda e, i=i, G=G: e.scalar_tensor_tensor(
                    out=yn[i][:], in0=yg[i][:], scalar=ss[:, 3:4], in1=nssm[:],
                    op0=ALU.mult, op1=ALU.mult), r=[tk("yg"), "ss3", "nssm"], w=[tk("yn")])
                trb2 = self.bank_bf[0]
                for q in range(4):
                    S.add("pe", lambda e, i=i, q=q: e.transpose(out=trb2[:, 512 + q * 128:512 + (q + 1) * 128],
                                                                in_=yn[i][:, q * 128:(q + 1) * 128], identity=self.ident_bf),
                          r=[tk("yn"), "cbf"], w=["bank0"])
                S.add("act", lambda e, c=c: e.activation(out=fap(yT, c * 128, [[T, 4], [1, 128]]),
                                                         in_=fap(trb2, 512, [[128, 4], [1, 128]]), func=AF.Copy),
                      r=["bank0"], w=["yT"])
                S.add("dve", lambda e, Sg=Sg, bc=bc: e.tensor_tensor(
                    out=Sg.rearrange("p (e q) -> p e q", e=8), in0=Sg.rearrange("p (e q) -> p e q", e=8), in1=bc(dec),
                    op=ALU.mult), r=["Sst", "dec"], w=["Sst"])
                S.add("dve", lambda e, Sg=Sg: e.tensor_tensor(out=Sg, in0=bC[:, :], in1=Sg, op=ALU.add), r=["Sst", "bank5"], w=["Sst"])
                S.add("act", lambda e, Sg=Sg: e.activation(out=Sbf[:], in_=Sg, func=AF.Copy), r=["Sst"], w=["Sbf"])
            S.dma("sp", self.yscr[4 * G:4 * G + 4].rearrange("c p t -> p c t"), yT[:], r=["yT"], w=["yscr"], chan="yscr")
        if last_grp:
            S.dma("sp", self.ssm_out[l], self.Sst[:], r=["Sst"], w=["ssmout"], chan="stout")
            S.dma("sp", self.conv_out[l], self.convhist[:], r=["convhist"], w=["convout"], chan="stout")
            if "stout" not in self.out_chans:
                self.out_chans.append("stout")

    def tail(self, l, g0):
        _hT = self.hT
        _sqf = self.sqf
        _attoT = self.attoT
        S = self.S
        self.norm_bufs()
        ysT = self.alloc([16, 512], BF16)
        mT = self.alloc([KC, 512], BF16)
        sga = self.alloc([512], F32)
        sgs = self.alloc([512], F32)
        ta = self.alloc([512], F32)
        tsb = self.alloc([512], F32)
        oT = self.alloc([KC, 512], F32)
        wa = self.w_br_att[l].rearrange("(k p) n -> p k n", p=128)
        ws = self.w_br_ssm[l].rearrange("(k p) n -> p k n", p=128)
        wo = self.w_out[l].rearrange("(k p) n -> p k n", p=128)
        for tt in range(T // 512):
            tsl = slice(tt * 512, (tt + 1) * 512)
            S.dma("sp", ysT[:], self.yscr[:, :, tsl].rearrange("c p t -> p c t"), r=["yscr"], w=["ysT"], chan="ysT")
            for m in range(KC):
                ms = slice(m * 128, (m + 1) * 128)
                s1, k1 = self.wslot([(0, [[128, 4], [1, 128]], wa[:, :, ms]),
                                     (512, [[128, KC], [1, 128]], self.win_cols(l, 9760 + m * 128, 128)),
                                     (1536, [[128, KC], [1, 128]], self.win_cols(l, 10784 + m * 128, 128))], "wt1")
                s2, k2 = self.wslot([(0, [[128, 16], [1, 128]], ws[:, :, ms])], "wt2")
                pa_, ps_, pga, pgs = self.bank[0], self.bank[1], self.bank[2], self.bank[3]
                for kk in range(4):
                    S.add("pe", lambda e, kk=kk, tsl=tsl, s1=s1: e.matmul(pa_[:, :], lhsT=fap(s1, kk * 128, [[1, 128]]),
                                                                   rhs=_attoT[:, kk, tsl], start=(kk == 0), stop=(kk == 3)),
                          r=[k1, "attoT"], w=["bank0"])
                for kk in range(16):
                    S.add("pe", lambda e, kk=kk, s2=s2: e.matmul(ps_[:, :], lhsT=fap(s2, kk * 128, [[1, 128]]), rhs=ysT[:, kk, :],
                                                          start=(kk == 0), stop=(kk == 15)), r=[k2, "ysT"], w=["bank1"])
                for kc in range(KC):
                    S.add("pe", lambda e, kc=kc, tsl=tsl, s1=s1: e.matmul(pga[:, :], lhsT=fap(s1, 512 + kc * 128, [[1, 128]]),
                                                                   rhs=_hT[:, kc, tsl], start=(kc == 0), stop=(kc == KC - 1)),
                          r=[k1, "hT"], w=["bank2"])
                for kc in range(KC):
                    S.add("pe", lambda e, kc=kc, tsl=tsl, s1=s1: e.matmul(pgs[:, :], lhsT=fap(s1, 1536 + kc * 128, [[1, 128]]),
                                                                   rhs=_hT[:, kc, tsl], start=(kc == 0), stop=(kc == KC - 1)),
                          r=[k1, "hT"], w=["bank3"])
                S.add("act", lambda e: e.activation(out=sga[:], in_=pga[:, :], func=AF.Sigmoid), r=["bank2"], w=["sga"])
                S.add("act", lambda e: e.activation(out=sgs[:], in_=pgs[:, :], func=AF.Sigmoid), r=["bank3"], w=["sgs"])
                S.add("dve", lambda e: e.tensor_tensor(out=ta[:], in0=pa_[:, :], in1=sga[:], op=ALU.mult), r=["bank0", "sga"], w=["ta"])
                S.add("dve", lambda e: e.tensor_tensor(out=tsb[:], in0=ps_[:, :], in1=sgs[:], op=ALU.mult), r=["bank1", "sgs"], w=["tsb"])
                S.add("pool", lambda e, m=m: e.tensor_tensor(out=mT[:, m, :], in0=ta[:], in1=tsb[:], op=ALU.add),
                      r=["ta", "tsb"], w=["mT"])
            for m2 in range(KC):
                s3, k3 = self.wslot([(0, [[128, KC], [1, 128]], wo[:, :, m2 * 128:(m2 + 1) * 128])], "wo")
                bi = 4 + m2 % 2
                Bo = self.bank[bi]
                for kc in range(KC):
                    S.add("pe", lambda e, kc=kc, Bo=Bo, s3=s3: e.matmul(Bo[:, :], lhsT=fap(s3, kc * 128, [[1, 128]]), rhs=mT[:, kc, :],
                                                                 start=(kc == 0), stop=(kc == KC - 1)), r=[k3, "mT"], w=["bank%d" % bi])
                S.add("act", lambda e, Bo=Bo, m2=m2: e.activation(out=oT[:, m2, :], in_=Bo[:, :], func=AF.Copy), r=["bank%d" % bi], w=["oT"])
                S.add("act", lambda e, Bo=Bo, m2=m2: e.activation(out=_sqf[:, m2, :], in_=Bo[:, :], func=AF.Square),
                      r=["bank%d" % bi], w=["sqo"])
            self.postnorm_residual(oT[:], lambda c: _sqf[:, c, :], 1, self.xres, self.xres, g0 + tt * 512, ["oT", "sqo"])

    def s_declare(self):
        L = self.depth
        self.xsT_in = self.din("xsT_in", [128, KC, NS])
        self.scst_in = self.din("scst_in", [128, 1172])
        self.rope_s = self.din("rope_s", [NS, 256])
        self.cache = [self.din("cache%d" % g, [L, NS, WIN[g], 1024]) for g in range(3)]
        self.st_ssm = self.din("st_ssm", [L, NS, 2048, 128])
        self.st_conv = self.din("st_conv", [L, NS, 3, 3072])
        self.convw_rep = self.din("convw_rep", [L, NS, 4, 3072])
        self.convb_rep = self.din("convb_rep", [L, NS, 3072])
        self.dtb_rep = self.din("dtb_rep", [L, NS, 32])
        self.alog_rep = self.din("alog_rep", [L, NS, 32])
        self.dsk_rep = self.din("dsk_rep", [L, NS, 32])
        self.nssm_rep = self.din("nssm_rep", [L, NS, 2048])
        self.ys_out = self.dout("ys_out", [128, KC, NS])
        self.kvs_out = [self.dout("kvs_out%d" % g, [L, NS, 2, 512]) for g in range(3)]
        self.ssms_out = self.dout("ssms_out", [L, NS, 2048, 128])
        self.convs_out = self.dout("convs_out", [L, NS, 3, 3072])

    def s_consts(self):
        S = self.S
        self.xsT = S.sb("xsT", [128, KC, NS], F32)
        S.dma("sp", self.xsT[:], self.xsT_in, w=["xsT"], chan="misc")
        self.ones_f = S.sb("ones_f", [128, 4], F32)
        S.add("pool", lambda e: e.memset(self.ones_f[:], 1.0), w=["ones_f"])

    def s_load_consts(self):
        S = self.S
        self.scst = self.alloc([1172], F32)
        S.dma("sp", self.scst, self.scst_in, w=["scst"], chan="s_ld")
        self.identF = self.scst[:, 0:128]
        self.selB = lambda b: self.scst[0:NS, 128 + b * 128:128 + (b + 1) * 128]
        self.selcol = lambda b: self.scst[0:4, 640 + b * NS:640 + (b + 1) * NS]
        self.I4 = self.scst[0:4, 656:660]
        self.bdmask = self.scst[0:4, 660:1172]
        self.ropes = self.alloc([256], F32)
        S.dma("sp", self.ropes[0:NS, :], self.rope_s, w=["ropes"], chan="s_ld")

    def s_rstd(self, ps_ap, denom, rs, rkey):
        S = self.S
        S.add("act", lambda e: e.activation(out=rs, in_=ps_ap, func=AF.Copy), r=[rkey], w=["s_rs"])
        S.add("dve", lambda e: e.tensor_scalar(out=rs, in0=rs, scalar1=1.0 / denom, scalar2=EPS, op0=ALU.mult, op1=ALU.add),
              r=["s_rs"], w=["s_rs"])
        S.add("act", lambda e: e.activation(out=rs, in_=rs, func=AF.Sqrt), r=["s_rs"], w=["s_rs"])
        S.add("dve", lambda e: e.reciprocal(out=rs, in_=rs), r=["s_rs"], w=["s_rs"])

    def s_norm_stat(self, src3, srckeys):
        S = self.S
        sq = self.alloc([KC * NS], BF16)
        rs = self.alloc([NS], F32)
        st = self.bank[6]
        S.add("act", lambda e: e.activation(out=sq.rearrange("p (c n) -> p c n", n=NS), in_=src3, func=AF.Square),
              r=list(srckeys), w=["s_sq"])
        for c in range(KC):
            S.add("pe", lambda e, c=c: e.matmul(st[:, 0:NS], lhsT=self.ones_bf[:, :], rhs=sq[:, c * NS:(c + 1) * NS],
                                                  start=(c == 0), stop=(c == KC - 1)), r=["s_sq", "ones_bf"], w=["bank6"])
        self.s_rstd(st[:, 0:NS], D, rs, "bank6")
        return rs

    def s_prenorm(self, k):
        S = self.S
        rs = self.s_norm_stat(self.xsT[:], ["xsT"])
        tmp = self.alloc([KC * NS], F32)
        tmp3 = tmp.rearrange("p (c n) -> p c n", n=NS)
        hs = self.alloc([KC * NS], BF16)
        hs3 = hs.rearrange("p (c n) -> p c n", n=NS)
        S.add("dve", lambda e: e.tensor_tensor(out=tmp3, in0=self.xsT[:], in1=rs.unsqueeze(1).to_broadcast([128, KC, NS]),
                                                 op=ALU.mult), r=["xsT", "s_rs"], w=["s_tmp"])
        S.add("dve", lambda e: e.tensor_tensor(out=tmp3, in0=tmp3, in1=self.Aall[:, k, :, 1:NCOL], op=ALU.mult),
              r=["s_tmp", "Aall"], w=["s_tmp"])
        S.add("pool", lambda e: e.tensor_tensor(out=hs3, in0=tmp3, in1=self.modT[:, k * 24:k * 24 + 8, 1:NCOL], op=ALU.add),
              r=["s_tmp", "modT"], w=["s_hs"])
        return hs3

    def s_postnorm(self, f3, fkeys, k):
        S = self.S
        rs = self.s_norm_stat(f3, fkeys)
        tmp = self.alloc([KC * NS], F32)
        tmp3 = tmp.rearrange("p (c n) -> p c n", n=NS)
        S.add("dve", lambda e: e.tensor_tensor(out=tmp3, in0=f3, in1=rs.unsqueeze(1).to_broadcast([128, KC, NS]),
                                                 op=ALU.mult), r=list(fkeys) + ["s_rs"], w=["s_tmp2"])
        S.add("dve", lambda e: e.tensor_tensor(out=tmp3, in0=tmp3, in1=self.Gall[:, k, :, 1:NCOL], op=ALU.mult),
              r=["s_tmp2", "Gall"], w=["s_tmp2"])
        S.add("pool", lambda e: e.tensor_tensor(out=self.xsT[:], in0=self.xsT[:], in1=tmp3, op=ALU.add),
              r=["s_tmp2", "xsT"], w=["xsT"])

    def s_ffn(self, l, i, k):
        S = self.S
        self.phase(0)
        hs3 = self.s_prenorm(k)
        w1 = self.w_ff_in[l, i].rearrange("(kc p) n -> p kc n", p=128)
        w2 = self.w_ff_out[l, i].rearrange("(j p) n -> p j n", p=128)
        bu = self.bank[0]
        for jb in range(NJ // 2):
            slot, key = self.wslot([
                (0, [[256, KC], [1, 256]], w1[:, :, jb * 256:(jb + 1) * 256]),
                (2048, [[256, KC], [1, 256]], w1[:, :, DFF + jb * 256:DFF + (jb + 1) * 256])], "w1s")
            for jj in range(2):
                j = jb * 2 + jj
                for half in range(2):
                    col = (half * NJ + j) * NS
                    for kc in range(KC):
                        S.add("pe", lambda e, kc=kc, jj=jj, half=half, col=col, slot=slot: e.matmul(
                            bu[:, col:col + NS], lhsT=fap(slot, half * 2048 + kc * 256 + jj * 128, [[1, 128]]),
                            rhs=hs3[:, kc, :], start=(kc == 0), stop=(kc == KC - 1)), r=[key, "s_hs"], w=["bank0"])
        sg = self.alloc([NJ * NS], F32)
        up = self.alloc([NJ * NS], F32)
        aT = self.alloc([NJ * NS], BF16)
        S.add("act", lambda e: e.activation(out=sg, in_=bu[:, 0:NJ * NS], func=AF.Silu), r=["bank0"], w=["s_sg"])
        S.add("act", lambda e: e.activation(out=up, in_=bu[:, NJ * NS:2 * NJ * NS], func=AF.Copy), r=["bank0"], w=["s_up"])
        S.add("dve", lambda e: e.tensor_tensor(out=aT, in0=sg, in1=up, op=ALU.mult), r=["s_sg", "s_up"], w=["s_aT"])
        bf_ = self.bank[1]
        for m in range(KC):
            slot, key = self.wslot([(0, [[128, NJ], [1, 128]], w2[:, :, m * 128:(m + 1) * 128])], "w2s")
            for j in range(NJ):
                S.add("pe", lambda e, j=j, m=m, slot=slot: e.matmul(
                    bf_[:, m * NS:(m + 1) * NS], lhsT=fap(slot, j * 128, [[1, 128]]), rhs=aT[:, j * NS:(j + 1) * NS],
                    start=(j == 0), stop=(j == NJ - 1)), r=[key, "s_aT"], w=["bank1"])
        fT = self.alloc([KC * NS], F32)
        S.add("act", lambda e: e.activation(out=fT, in_=bf_[:, 0:KC * NS], func=AF.Copy), r=["bank1"], w=["s_fT"])
        self.s_postnorm(fT.rearrange("p (c n) -> p c n", n=NS), ["s_fT"], k)

    def s_transpose_to_fm(self, tok_ap, nch, out_tile, bank_i, rkeys, okey):
        S = self.S
        Bk = self.bank[bank_i]
        bk = "bank%d" % bank_i
        for c in range(nch):
            S.add("pe", lambda e, c=c: e.matmul(Bk[:, c * NS:(c + 1) * NS], lhsT=tok_ap[:, c * 128:(c + 1) * 128],
                                                  rhs=self.I4[0:NS, 0:NS], start=True, stop=True),
                  r=list(rkeys) + ["scst"], w=[bk])
        S.add("act", lambda e: e.activation(out=out_tile, in_=Bk[:, 0:nch * NS], func=AF.Copy), r=[bk], w=[okey])

    def s_mixer(self, l):
        S = self.S
        self.phase(0)
        hs3 = self.s_prenorm(1)
        U = self.alloc([INC], F32)
        Us = lambda a, n: U[0:NS, a:a + n]
        abr = self.alloc([D], F32)
        self.s_load_consts()
        mark = self.aptr
        nblk = (INC + 511) // 512
        for blk in range(nblk):
            c0 = blk * 512
            n = min(512, INC - c0)
            slot, key = self.wslot([(0, [[n, KC], [1, n]], self.win_cols(l, c0, n))], "wins")
            bi = blk % 2
            Bk = self.bank[bi]
            bk = "bank%d" % bi
            for kc in range(KC):
                S.add("pe", lambda e, kc=kc, n=n, Bk=Bk, slot=slot: e.matmul(
                    Bk[0:NS, 0:n], lhsT=hs3[:, kc, :], rhs=fap(slot, kc * n, [[1, n]]), start=(kc == 0), stop=(kc == KC - 1)),
                    r=[key, "s_hs"], w=[bk])
            S.add("act", lambda e, c0=c0, n=n, Bk=Bk: e.activation(out=Us(c0, n), in_=Bk[0:NS, 0:n], func=AF.Copy),
                  r=[bk], w=["U"])
        QK = self.alloc([3 * 1024], F32)
        t1 = self.alloc([1024], F32)
        t2 = self.alloc([1024], F32)
        v3 = lambda ap, d=128: ap.rearrange("p (a d) -> p a d", d=d)
        cosF = self.ropes[0:NS, 0:128]
        sinS = self.ropes[0:NS, 128:256]
        for g in range(3):
            x = v3(Us(g * 1536, 1024))
            S.add("dve", lambda e, x=x: e.tensor_tensor(out=v3(t1[0:NS, :]), in0=x, in1=cosF.unsqueeze(1).to_broadcast([NS, 8, 128]),
                                                         op=ALU.mult), r=["U", "ropes"], w=["s_t1"])
            S.add("pool", lambda e, x=x: e.tensor_tensor(out=v3(t2[0:NS, :])[:, :, 0:64], in0=x[:, :, 64:128],
                                                          in1=sinS[:, 0:64].unsqueeze(1).to_broadcast([NS, 8, 64]), op=ALU.mult),
                  r=["U", "ropes"], w=["s_t2a"])
            S.add("pool", lambda e, x=x: e.tensor_tensor(out=v3(t2[0:NS, :])[:, :, 64:128], in0=x[:, :, 0:64],
                                                          in1=sinS[:, 64:128].unsqueeze(1).to_broadcast([NS, 8, 64]), op=ALU.mult),
                  r=["U", "ropes"], w=["s_t2b"])
            S.add("dve", lambda e, g=g: e.tensor_tensor(out=QK[0:NS, g * 1024:(g + 1) * 1024], in0=t1[0:NS, :], in1=t2[0:NS, :],
                                                         op=ALU.add), r=["s_t1", "s_t2a", "s_t2b"], w=["QK"])
            S.dma("sp", self.kvs_out[g][l, :, 0, :], QK[0:NS, g * 1024 + 512:(g + 1) * 1024], r=["QK"], w=["kvs_o"], chan="s_out")
            S.dma("sp", self.kvs_out[g][l, :, 1, :], Us(g * 1536 + 1024, 512), r=["U"], w=["kvs_o"], chan="s_out")
        if "s_out" not in self.out_chans:
            self.out_chans.append("s_out")
        P0 = self.alloc([512], F32)
        s0 = self.alloc([12], F32)
        p0 = self.alloc([12], F32)
        N0 = self.alloc([512], F32)
        D0 = self.alloc([4], F32)
        for g in range(3):
            S.add("pool", lambda e, g=g: e.tensor_tensor(out=P0[0:NS, :], in0=QK[0:NS, g * 1024:g * 1024 + 512],
                                                          in1=QK[0:NS, g * 1024 + 512:(g + 1) * 1024], op=ALU.mult),
                  r=["QK"], w=["s_P0"])
            S.add("dve", lambda e, g=g: e.reduce_sum(out=s0[0:NS, g * 4:(g + 1) * 4], in_=v3(P0[0:NS, :]), axis=AX.X),
                  r=["s_P0"], w=["s_s0"])
        S.add("act", lambda e: e.activation(out=p0[0:NS, :], in_=s0[0:NS, :], func=AF.Exp, scale=float(SCALE)), r=["s_s0"], w=["s_p0"])
        for g in range(3):
            vg = v3(Us(g * 1536 + 1024, 512))
            pb = p0[0:NS, g * 4:(g + 1) * 4].unsqueeze(2).to_broadcast([NS, 4, 128])
            if g == 0:
                S.add("dve", lambda e, vg=vg, pb=pb: e.tensor_tensor(out=v3(N0[0:NS, :]), in0=vg, in1=pb, op=ALU.mult),
                      r=["U", "s_p0"], w=["s_N0"])
            else:
                S.add("pool", lambda e, vg=vg, pb=pb: e.tensor_tensor(out=v3(P0[0:NS, :]), in0=vg, in1=pb, op=ALU.mult),
                      r=["U", "s_p0"], w=["s_P0"])
                S.add("dve", lambda e: e.tensor_tensor(out=N0[0:NS, :], in0=N0[0:NS, :], in1=P0[0:NS, :], op=ALU.add),
                      r=["s_N0", "s_P0"], w=["s_N0"])
        S.add("dve", lambda e: e.tensor_tensor(out=D0[0:NS, :], in0=p0[0:NS, 0:4], in1=p0[0:NS, 4:8], op=ALU.add), r=["s_p0"], w=["s_D0"])
        S.add("dve", lambda e: e.tensor_tensor(out=D0[0:NS, :], in0=D0[0:NS, :], in1=p0[0:NS, 8:12], op=ALU.add), r=["s_p0", "s_D0"], w=["s_D0"])
        KV = [self.alloc([1024], F32) for i in range(2)]
        prod = self.alloc([512], F32)
        sc = self.alloc([4], F32)
        pp = [self.alloc([4], F32) for i in range(2)]
        numS = self.alloc([NS * 512], F32)
        denS = self.alloc([NS], F32)
        cnt = 0
        for b in range(NS):
            nb_, db_ = 4 + 2 * (b % 2), 5 + 2 * (b % 2)
            numP, denP = self.bank[nb_], self.bank[db_]
            for g in range(3):
                i = cnt % 2
                cnt += 1
                kv = KV[i]
                kvk = "s_KV%d" % i
                src = self.cache[g][l, b].rearrange("(j d) c -> j d c", d=DIL[g])[:, 0, :]
                S.dma("sp", kv, src, w=[kvk], chan=kvk)
                qi = 2 + i
                qP = self.bank[qi]
                S.add("pe", lambda e, b=b, g=g, qP=qP: e.matmul(qP[:, :], lhsT=self.selB(b), rhs=QK[0:NS, g * 1024:g * 1024 + 512],
                                                                start=True, stop=True), r=["QK", "scst"], w=["bank%d" % qi])
                S.add("dve", lambda e, kv=kv, qP=qP: e.tensor_tensor(out=prod, in0=kv[:, 0:512], in1=qP[:, :], op=ALU.mult),
                      r=[kvk, "bank%d" % qi], w=["s_prod"])
                S.add("dve", lambda e: e.reduce_sum(out=sc, in_=v3(prod), axis=AX.X), r=["s_prod"], w=["s_sc"])
                p_ = pp[i]
                pk = "s_pp%d" % i
                S.add("act", lambda e, p_=p_: e.activation(out=p_, in_=sc, func=AF.Exp, scale=float(SCALE)), r=["s_sc"], w=[pk])
                S.add("pe", lambda e, p_=p_, kv=kv, g=g, numP=numP: e.matmul(numP[0:4, :], lhsT=p_, rhs=kv[:, 512:1024],
                                                                            start=(g == 0), stop=(g == 2)),
                      r=[pk, kvk], w=["bank%d" % nb_])
                S.add("pe", lambda e, p_=p_, g=g, denP=denP: e.matmul(denP[0:4, 0:1], lhsT=p_, rhs=self.ones_f[:, 0:1],
                                                                     start=(g == 0), stop=(g == 2)),
                      r=[pk, "ones_f"], w=["bank%d" % db_])
            S.add("act", lambda e, b=b, numP=numP: e.activation(out=numS[0:4, b * 512:(b + 1) * 512], in_=numP[0:4, :], func=AF.Copy),
                  r=["bank%d" % nb_], w=["s_numS"])
            S.add("act", lambda e, b=b, denP=denP: e.activation(out=denS[0:4, b:b + 1], in_=denP[0:4, 0:1], func=AF.Copy),
                  r=["bank%d" % db_], w=["s_denS"])
        S.add("pool", lambda e: e.tensor_tensor(out=numS[0:4, :].rearrange("p (b n) -> p b n", n=512),
                                                 in0=numS[0:4, :].rearrange("p (b n) -> p b n", n=512),
                                                 in1=self.bdmask.unsqueeze(1).to_broadcast([4, NS, 512]), op=ALU.mult),
              r=["s_numS", "scst"], w=["s_numS"])
        ncP, dcP = self.bank[0], self.bank[1]
        for b in range(NS):
            S.add("pe", lambda e, b=b: e.matmul(ncP[0:NS, :], lhsT=self.selcol(b), rhs=numS[0:4, b * 512:(b + 1) * 512],
                                                  start=(b == 0), stop=(b == NS - 1)), r=["s_numS", "scst"], w=["bank0"])
        S.add("pe", lambda e: e.matmul(dcP[0:NS, 0:4], lhsT=denS[0:4, 0:NS], rhs=self.I4, start=True, stop=True),
              r=["s_denS", "scst"], w=["bank1"])
        Dt = self.alloc([4], F32)
        Nt = self.alloc([512], F32)
        atto = self.alloc([512], F32)
        S.add("act", lambda e: e.activation(out=Dt[0:NS, :], in_=dcP[0:NS, 0:4], func=AF.Copy), r=["bank1"], w=["s_Dt"])
        S.add("dve", lambda e: e.tensor_tensor(out=Dt[0:NS, :], in0=Dt[0:NS, :], in1=D0[0:NS, :], op=ALU.add), r=["s_Dt", "s_D0"], w=["s_Dt"])
        S.add("dve", lambda e: e.reciprocal(out=Dt[0:NS, :], in_=Dt[0:NS, :]), r=["s_Dt"], w=["s_Dt"])
        S.add("act", lambda e: e.activation(out=Nt[0:NS, :], in_=ncP[0:NS, :], func=AF.Copy), r=["bank0"], w=["s_Nt"])
        S.add("dve", lambda e: e.tensor_tensor(out=Nt[0:NS, :], in0=Nt[0:NS, :], in1=N0[0:NS, :], op=ALU.add), r=["s_Nt", "s_N0"], w=["s_Nt"])
        S.add("dve", lambda e: e.tensor_tensor(out=v3(atto[0:NS, :]), in0=v3(Nt[0:NS, :]),
                                                 in1=Dt[0:NS, :].unsqueeze(2).to_broadcast([NS, 4, 128]), op=ALU.mult),
              r=["s_Nt", "s_Dt"], w=["s_atto"])
        attoT = self.alloc([4 * NS], BF16)
        self.s_transpose_to_fm(atto[0:NS, :], 4, attoT, 2, ["s_atto"], "s_attoT")
        wa = self.w_br_att[l].rearrange("(k p) n -> p k n", p=128)
        for cb in range(2):
            slot, key = self.wslot([(0, [[512, 4], [1, 512]], wa[:, :, cb * 512:(cb + 1) * 512])], "was")
            Bk = self.bank[3]
            for kk in range(4):
                S.add("pe", lambda e, kk=kk, slot=slot: e.matmul(Bk[0:NS, :], lhsT=attoT[:, kk * NS:(kk + 1) * NS],
                                                                 rhs=fap(slot, kk * 512, [[1, 512]]), start=(kk == 0), stop=(kk == 3)),
                      r=[key, "s_attoT"], w=["bank3"])
            S.add("act", lambda e, cb=cb: e.activation(out=abr[0:NS, cb * 512:(cb + 1) * 512], in_=Bk[0:NS, :], func=AF.Copy),
                  r=["bank3"], w=["s_abr"])
        self.phase(mark)
        xc = self.alloc([3072], F32)
        cw = self.alloc([4 * 512], F32)
        cs = self.alloc([3 * 512], F32)
        cbi = self.alloc([512], F32)
        acc = self.alloc([512], F32)
        ct = self.alloc([512], F32)
        for q in range(6):
            c0 = q * 512
            S.dma("sp", cw[0:NS, :].rearrange("p (j n) -> p j n", n=512), self.convw_rep[l, :, :, c0:c0 + 512], w=["s_cw"], chan="s_ld")
            S.dma("sp", cs[0:NS, :].rearrange("p (j n) -> p j n", n=512), self.st_conv[l, :, :, c0:c0 + 512], w=["s_cs"], chan="s_ld")
            S.dma("sp", cbi[0:NS, :], self.convb_rep[l, :, c0:c0 + 512], w=["s_cb"], chan="s_ld")
            S.dma("sp", self.convs_out[l, :, 0:2, c0:c0 + 512], cs[0:NS, 512:1536].rearrange("p (j n) -> p j n", n=512),
                  r=["s_cs"], w=["convs_o"], chan="s_out")
            S.dma("sp", self.convs_out[l, :, 2, c0:c0 + 512], Us(6656 + c0, 512), r=["U"], w=["convs_o"], chan="s_out")
            S.add("dve", lambda e, c0=c0: e.tensor_tensor(out=acc[0:NS, :], in0=Us(6656 + c0, 512), in1=cw[0:NS, 1536:2048], op=ALU.mult),
                  r=["U", "s_cw"], w=["s_acc"])
            S.add("dve", lambda e: e.tensor_tensor(out=acc[0:NS, :], in0=acc[0:NS, :], in1=cbi[0:NS, :], op=ALU.add),
                  r=["s_acc", "s_cb"], w=["s_acc"])
            for j in range(3):
                S.add("pool", lambda e, j=j: e.tensor_tensor(out=ct[0:NS, :], in0=cs[0:NS, j * 512:(j + 1) * 512],
                                                              in1=cw[0:NS, j * 512:(j + 1) * 512], op=ALU.mult),
                      r=["s_cs", "s_cw"], w=["s_ct"])
                S.add("dve", lambda e: e.tensor_tensor(out=acc[0:NS, :], in0=acc[0:NS, :], in1=ct[0:NS, :], op=ALU.add),
                      r=["s_acc", "s_ct"], w=["s_acc"])
            S.add("act", lambda e, c0=c0: e.activation(out=xc[0:NS, c0:c0 + 512], in_=acc[0:NS, :], func=AF.Silu), r=["s_acc"], w=["s_xc"])
        dt = self.alloc([32], F32)
        av = self.alloc([32], F32)
        dA = self.alloc([32], F32)
        dsk = self.alloc([32], F32)
        S.dma("sp", dt[0:NS, :], self.dtb_rep[l], w=["s_dt"], chan="s_ld")
        S.dma("sp", av[0:NS, :], self.alog_rep[l], w=["s_av"], chan="s_ld")
        S.dma("sp", dsk[0:NS, :], self.dsk_rep[l], w=["s_dsk"], chan="s_ld")
        S.add("dve", lambda e: e.tensor_tensor(out=dt[0:NS, :], in0=dt[0:NS, :], in1=Us(9728, 32), op=ALU.add), r=["s_dt", "U"], w=["s_dt"])
        S.add("act", lambda e: e.activation(out=dt[0:NS, :], in_=dt[0:NS, :], func=AF.Exp), r=["s_dt"], w=["s_dt"])
        S.add("act", lambda e: e.activation(out=dt[0:NS, :], in_=dt[0:NS, :], func=AF.Ln, bias=1.0), r=["s_dt"], w=["s_dt"])
        S.add("act", lambda e: e.activation(out=av[0:NS, :], in_=av[0:NS, :], func=AF.Exp), r=["s_av"], w=["s_av"])
        S.add("dve", lambda e: e.tensor_tensor(out=dA[0:NS, :], in0=dt[0:NS, :], in1=av[0:NS, :], op=ALU.mult), r=["s_dt", "s_av"], w=["s_dA"])
        S.add("act", lambda e: e.activation(out=dA[0:NS, :], in_=dA[0:NS, :], func=AF.Exp, scale=-1.0), r=["s_dA"], w=["s_dA"])
        xdt = self.alloc([2048], F32)
        dAe = self.alloc([2048], F32)
        v64 = lambda ap: ap.rearrange("p (a d) -> p a d", d=64)
        S.add("dve", lambda e: e.tensor_tensor(out=v64(xdt[0:NS, :]), in0=v64(xc[0:NS, 0:2048]),
                                                 in1=dt[0:NS, :].unsqueeze(2).to_broadcast([NS, 32, 64]), op=ALU.mult),
              r=["s_xc", "s_dt"], w=["s_xdt"])
        S.add("pool", lambda e: e.tensor_copy(out=v64(dAe[0:NS, :]), in_=dA[0:NS, :].unsqueeze(2).to_broadcast([NS, 32, 64])),
              r=["s_dA"], w=["s_dAe"])
        xdtT = self.alloc([16 * NS], F32)
        dAT = self.alloc([16 * NS], F32)
        self.s_transpose_to_fm(xdt[0:NS, :], 16, xdtT, 2, ["s_xdt"], "s_xdtT")
        self.s_transpose_to_fm(dAe[0:NS, :], 16, dAT, 3, ["s_dAe"], "s_dAT")
        Hb = [self.alloc([2048], F32) for i in range(2)]
        outer = self.alloc([2048], F32)
        Bbc = self.alloc([512], F32)
        Cbc = self.alloc([512], F32)
        yTa = self.alloc([NS * 16], F32)
        xdT3 = xdtT.rearrange("p (c n) -> p c n", n=NS)
        dAT3 = dAT.rearrange("p (c n) -> p c n", n=NS)
        for b in range(NS):
            H = Hb[b % 2]
            hk = "s_H%d" % (b % 2)
            H3 = v3(H)
            S.dma("sp", H3, self.st_ssm[l, b].rearrange("(c p) n -> p c n", p=128), w=[hk], chan=hk)
            S.add("pe", lambda e, b=b: e.matmul(self.bank[4][:, :], lhsT=self.selB(b), rhs=xc[0:NS, 2048:2560], start=True, stop=True),
                  r=["s_xc", "scst"], w=["bank4"])
            S.add("pe", lambda e, b=b: e.matmul(self.bank[5][:, :], lhsT=self.selB(b), rhs=xc[0:NS, 2560:3072], start=True, stop=True),
                  r=["s_xc", "scst"], w=["bank5"])
            S.add("act", lambda e: e.activation(out=Bbc, in_=self.bank[4][:, :], func=AF.Copy), r=["bank4"], w=["s_Bbc"])
            S.add("act", lambda e: e.activation(out=Cbc, in_=self.bank[5][:, :], func=AF.Copy), r=["bank5"], w=["s_Cbc"])
            S.add("dve", lambda e, b=b, H3=H3: e.tensor_tensor(out=H3, in0=H3, in1=dAT3[:, :, b:b + 1].to_broadcast([128, 16, 128]),
                                                                op=ALU.mult), r=[hk, "s_dAT"], w=[hk])
            for G in range(4):
                S.add("pool", lambda e, b=b, G=G: e.tensor_tensor(
                    out=v3(outer)[:, 4 * G:4 * G + 4, :], in0=xdT3[:, 4 * G:4 * G + 4, b:b + 1].to_broadcast([128, 4, 128]),
                    in1=Bbc[:, G * 128:(G + 1) * 128].unsqueeze(1).to_broadcast([128, 4, 128]), op=ALU.mult),
                    r=["s_xdtT", "s_Bbc"], w=["s_outer"])
            S.add("dve", lambda e, H=H: e.tensor_tensor(out=H, in0=H, in1=outer, op=ALU.add), r=[hk, "s_outer"], w=[hk])
            S.dma("sp", self.ssms_out[l, b].rearrange("(c p) n -> p c n", p=128), H3, r=[hk], w=["ssms_o"], chan="s_out")
            for G in range(4):
                S.add("pool", lambda e, G=G, H3=H3: e.tensor_tensor(
                    out=v3(outer)[:, 4 * G:4 * G + 4, :], in0=H3[:, 4 * G:4 * G + 4, :],
                    in1=Cbc[:, G * 128:(G + 1) * 128].unsqueeze(1).to_broadcast([128, 4, 128]), op=ALU.mult),
                    r=[hk, "s_Cbc"], w=["s_outer"])
            S.add("dve", lambda e, b=b: e.reduce_sum(out=yTa[:, b * 16:(b + 1) * 16], in_=v3(outer), axis=AX.X),
                  r=["s_outer"], w=["s_yTa"])
        ytok = self.alloc([2048], F32)
        for c in range(16):
            bi = 4 + c // 4
            Bk = self.bank[bi]
            S.add("pe", lambda e, c=c, Bk=Bk: e.matmul(Bk[0:NS, (c % 4) * 128:(c % 4 + 1) * 128],
                                                       lhsT=fap(yTa, c, [[16, NS]]), rhs=self.identF, start=True, stop=True),
                  r=["s_yTa", "scst"], w=["bank%d" % bi])
            if c % 4 == 3:
                S.add("act", lambda e, bi=bi, Bk=Bk: e.activation(out=ytok[0:NS, (bi - 4) * 512:(bi - 3) * 512], in_=Bk[0:NS, :], func=AF.Copy),
                      r=["bank%d" % bi], w=["s_ytok"])
        yt = ytok[0:NS, :]
        tq = self.alloc([2048], F32)
        tqs = tq[0:NS, :]
        S.add("pool", lambda e: e.tensor_tensor(out=v64(tqs), in0=v64(xc[0:NS, 0:2048]), in1=dsk[0:NS, :].unsqueeze(2).to_broadcast([NS, 32, 64]),
                                                 op=ALU.mult), r=["s_xc", "s_dsk"], w=["s_tq"])
        S.add("dve", lambda e: e.tensor_tensor(out=yt, in0=yt, in1=tqs, op=ALU.add), r=["s_ytok", "s_tq"], w=["s_ytok"])
        S.add("act", lambda e: e.activation(out=tqs, in_=Us(4608, 2048), func=AF.Silu), r=["U", "s_ytok"], w=["s_tq"])
        S.add("dve", lambda e: e.tensor_tensor(out=yt, in0=yt, in1=tqs, op=ALU.mult), r=["s_ytok", "s_tq"], w=["s_ytok"])
        S.add("act", lambda e: e.activation(out=tqs, in_=yt, func=AF.Square), r=["s_ytok"], w=["s_tq"])
        ss = self.alloc([4], F32)
        sss = ss[0:NS, :]
        v512 = lambda ap: ap.rearrange("p (a d) -> p a d", d=512)
        S.add("dve", lambda e: e.reduce_sum(out=sss, in_=v512(tqs), axis=AX.X), r=["s_tq"], w=["s_ss"])
        S.add("dve", lambda e: e.tensor_scalar(out=sss, in0=sss, scalar1=1.0 / 512, scalar2=EPS, op0=ALU.mult, op1=ALU.add),
              r=["s_ss"], w=["s_ss"])
        S.add("act", lambda e: e.activation(out=sss, in_=sss, func=AF.Sqrt), r=["s_ss"], w=["s_ss"])
        S.add("dve", lambda e: e.reciprocal(out=sss, in_=sss), r=["s_ss"], w=["s_ss"])
        S.add("dve", lambda e: e.tensor_tensor(out=v512(yt), in0=v512(yt), in1=sss.unsqueeze(2).to_broadcast([NS, 4, 512]), op=ALU.mult),
              r=["s_ytok", "s_ss"], w=["s_ytok"])
        S.dma("sp", tqs, self.nssm_rep[l], r=["s_ss"], w=["s_tq"], chan="s_ld")
        S.add("dve", lambda e: e.tensor_tensor(out=yt, in0=yt, in1=tqs, op=ALU.mult), r=["s_ytok", "s_tq"], w=["s_ytok"])
        ysT = self.alloc([16 * NS], BF16)
        self.s_transpose_to_fm(yt, 16, ysT, 2, ["s_ytok"], "s_ysT")
        sbr = self.alloc([D], F32)
        ws = self.w_br_ssm[l].rearrange("(k p) n -> p k n", p=128)
        for cb in range(4):
            slot, key = self.wslot([(0, [[256, 16], [1, 256]], ws[:, :, cb * 256:(cb + 1) * 256])], "wss")
            Bk = self.bank[3]
            for kk in range(16):
                S.add("pe", lambda e, kk=kk, slot=slot: e.matmul(Bk[0:NS, 0:256], lhsT=ysT[:, kk * NS:(kk + 1) * NS],
                                                                 rhs=fap(slot, kk * 256, [[1, 256]]), start=(kk == 0), stop=(kk == 15)),
                      r=[key, "s_ysT"], w=["bank3"])
            S.add("act", lambda e, cb=cb: e.activation(out=sbr[0:NS, cb * 256:(cb + 1) * 256], in_=Bk[0:NS, 0:256], func=AF.Copy),
                  r=["bank3"], w=["s_sbr"])
        sga = self.alloc([D], F32)
        sgs = self.alloc([D], F32)
        S.add("act", lambda e: e.activation(out=sga[0:NS, :], in_=Us(9760, D), func=AF.Sigmoid), r=["U"], w=["s_sga"])
        S.add("act", lambda e: e.activation(out=sgs[0:NS, :], in_=Us(10784, D), func=AF.Sigmoid), r=["U"], w=["s_sgs"])
        S.add("dve", lambda e: e.tensor_tensor(out=sga[0:NS, :], in0=sga[0:NS, :], in1=abr[0:NS, :], op=ALU.mult), r=["s_sga", "s_abr"], w=["s_sga"])
        S.add("pool", lambda e: e.tensor_tensor(out=sgs[0:NS, :], in0=sgs[0:NS, :], in1=sbr[0:NS, :], op=ALU.mult), r=["s_sgs", "s_sbr"], w=["s_sgs"])
        S.add("dve", lambda e: e.tensor_tensor(out=sga[0:NS, :], in0=sga[0:NS, :], in1=sgs[0:NS, :], op=ALU.add), r=["s_sga", "s_sgs"], w=["s_sga"])
        mT = self.alloc([KC * NS], BF16)
        self.s_transpose_to_fm(sga[0:NS, :], KC, mT, 2, ["s_sga"], "s_mT")
        wo = self.w_out[l].rearrange("(k p) n -> p k n", p=128)
        Bo = self.bank[0]
        for m2 in range(KC):
            slot, key = self.wslot([(0, [[128, KC], [1, 128]], wo[:, :, m2 * 128:(m2 + 1) * 128])], "wos")
            for kc in range(KC):
                S.add("pe", lambda e, kc=kc, m2=m2, slot=slot: e.matmul(Bo[:, m2 * NS:(m2 + 1) * NS], lhsT=fap(slot, kc * 128, [[1, 128]]),
                                                                        rhs=mT[:, kc * NS:(kc + 1) * NS], start=(kc == 0), stop=(kc == KC - 1)),
                      r=[key, "s_mT"], w=["bank0"])
        oT = self.alloc([KC * NS], F32)
        S.add("act", lambda e: e.activation(out=oT, in_=Bo[:, 0:KC * NS], func=AF.Copy), r=["bank0"], w=["s_oT"])
        self.s_postnorm(oT.rearrange("p (c n) -> p c n", n=NS), ["s_oT"], 1)

    def s_finish(self):
        self.S.dma("sp", self.ys_out, self.xsT[:], r=["xsT"], w=["ys_o"], chan="s_out")
        if "s_out" not in self.out_chans:
            self.out_chans.append("s_out")

    def step(self):
        if self.nsteps is not None and self.stepi >= self.nsteps:
            return False
        self.stepi += 1
        return True

    def mixer(self, l, g0):
        sa = self.stop_after
        if not self.step():
            return
        self.phase(0)
        self.hT = self.alloc([KC, T], BF16)
        self.attoT = self.alloc([4, T], BF16)
        mark = self.aptr
        self.norm_bufs()
        self.prenorm(self.xres, g0, T, 1, 0)
        if sa == "prenorm":
            return
        self.phase(mark)
        if sa != "noattn":
            self.attention(l, g0)
        if sa == "attn":
            return
        if not self.step():
            return
        self.phase(mark)
        self.ssd(l, g0)
        if sa in ("ssd", "ssd_pre", "ssd_conv", "ssd_c1", "ssd_p1", "ssd_p2", "ssd_p3", "ssd_p4", "ssd_p5", "ssd_p6", "ssd_p7"):
            return
        if not self.step():
            return
        self.phase(mark)
        self.sqf = self.alloc([KC, 512], BF16)
        self.tail(l, g0)

    def build(self):
        self.declare()
        self.consts()
        S = self.S
        for l in range(self.depth):
            self.modulation(l)
            if self.do_sample:
                self.s_ffn(l, 0, 0)
                self.s_mixer(l)
                self.s_ffn(l, 1, 2)
            for g in range(NGRP if self.do_prompt else 0):
                g0 = g * T
                src = self.xT_in if l == 0 else self.xres
                if self.step():
                    self.ffn(l, 0, 0, src, self.xres, g0, False)
                if g == 0:
                    self.ssd_params(l)
                self.mixer(l, g0)
                if self.stop_after:
                    break
                last = (l == self.depth - 1)
                if self.step():
                    self.ffn(l, 1, 2, self.xres, self.yT_out if last else self.xres, g0, last)
            if self.stop_after:
                break
        if self.do_sample and not self.stop_after:
            self.s_finish()
        S.emit(final_waits=self.out_chans)
        return self.nc


def pm(v):
    v = np.asarray(v)
    c = v.shape[-1] // 128
    v = v.reshape(v.shape[:-1] + (c, 128))
    return np.ascontiguousarray(np.moveaxis(v, -1, 0))


_CACHE = {}


def _consts():
    j = np.arange(128)[:, None]
    i = np.arange(128)[None, :]
    ident = (j == i).astype(np.float32)
    mask2 = np.concatenate([(j >= i), (j <= i)], axis=1).astype(np.float32)
    negm = np.where(j > i, -30000.0, 0.0).astype(np.float32)
    cbf = np.concatenate([ident, mask2, np.tile(negm, (1, 4))], axis=1)
    tri = (j <= i).astype(np.float32)
    sel127 = np.zeros((128, 128), np.float32)
    sel127[127, :] = 1.0
    identS = np.zeros((128, 32), np.float32)
    identS[0:32] = np.eye(32)
    identS[32:64] = np.eye(32)
    cf32 = np.concatenate([tri, sel127, identS], axis=1)
    half = 64
    inv = (np.float32(10000.0) ** (-(np.arange(half, dtype=np.float32) / np.float32(half)))).astype(np.float32)
    pos = np.concatenate([np.arange(SEQ), [PAST]]).astype(np.float32)
    ang = (pos[None, :] * inv[:, None]).astype(np.float32)
    cos = np.cos(ang).astype(np.float32)
    sin = np.sin(ang).astype(np.float32)
    cosT = np.concatenate([cos, cos], axis=0)
    sinT = np.concatenate([-sin, sin], axis=0)
    return cbf, cf32, np.ascontiguousarray(cosT), np.ascontiguousarray(sinT)


def _sconsts():
    sc = np.zeros((128, 1172), np.float32)
    sc[:, 0:128] = np.eye(128, dtype=np.float32)
    for b in range(NS):
        sc[b, 128 + b * 128:128 + (b + 1) * 128] = 1.0
        sc[0:4, 640 + b * NS + b] = 1.0
    sc[0:4, 656:660] = np.eye(4, dtype=np.float32)
    for h in range(4):
        sc[h, 660 + h * 128:660 + (h + 1) * 128] = 1.0
    return sc


def kernel(**inp):
    f = lambda k: np.asarray(inp[k], dtype=np.float32)
    x_prompt = f("x_prompt")
    depth = 2
    if "nc" not in _CACHE:
        b = Builder(depth=depth)
        _CACHE["nc"] = b.build()
        _CACHE["b"] = b
    nc = _CACHE["nc"]
    cbf, cf32, cosT, sinT = _consts()
    shared = {}
    shared["w_mod"] = f("w_mod")
    shared["bmodT"] = np.ascontiguousarray(f("b_mod").reshape(depth, 72, 128).transpose(0, 2, 1))
    shared["gpreT"] = np.ascontiguousarray(f("norm_pre").reshape(depth, 3, KC, 128).transpose(0, 3, 1, 2))
    shared["gpostT"] = np.ascontiguousarray(f("norm_post").reshape(depth, 3, KC, 128).transpose(0, 3, 1, 2))
    shared["w_ff_in"] = f("w_ff_in")
    shared["w_ff_out"] = f("w_ff_out")
    shared["cbf_in"] = cbf
    shared["cf32_in"] = cf32
    shared["cosT"] = cosT
    shared["sinT"] = sinT
    shared["w_in"] = f("w_in")
    shared["conv_wT"] = np.ascontiguousarray(f("conv_w").reshape(depth, 4, 24, 128).transpose(0, 3, 2, 1))
    shared["conv_bT"] = np.ascontiguousarray(f("conv_b").reshape(depth, 24, 128).transpose(0, 2, 1))
    for k in ("dt_bias", "a_log", "d_skip", "norm_ssm", "w_br_att", "w_br_ssm", "w_out"):
        shared[k] = f(k)
    rep = lambda a: np.ascontiguousarray(np.broadcast_to(a[:, None], (a.shape[0], NS) + a.shape[1:]))
    shared["scst_in"] = _sconsts()
    shared["rope_s"] = np.ascontiguousarray(np.broadcast_to(
        np.concatenate([cosT[:, SEQ], sinT[:, SEQ]])[None, :], (NS, 256)))
    shared["convw_rep"] = rep(f("conv_w"))
    shared["convb_rep"] = rep(f("conv_b"))
    shared["dtb_rep"] = rep(f("dt_bias"))
    shared["alog_rep"] = rep(f("a_log"))
    shared["dsk_rep"] = rep(f("d_skip"))
    shared["nssm_rep"] = rep(f("norm_ssm"))
    caches = [f("cache_kv_g%d" % g) for g in range(3)]
    st_ssm = f("state_ssm")
    st_conv = f("state_conv")
    x_sample = f("x_sample")
    in_maps = []
    for core in range(8):
        bidx = core % 4
        m = dict(shared)
        sl = slice(core * NS, (core + 1) * NS)
        m["xsT_in"] = np.ascontiguousarray(x_sample[sl, 0, :].T.reshape(KC, 128, NS).transpose(1, 0, 2))
        for g in range(3):
            m["cache%d" % g] = np.ascontiguousarray(caches[g][:, sl]).reshape(depth, NS, WIN[g], 1024)
        m["st_ssm"] = np.ascontiguousarray(st_ssm[:, sl]).reshape(depth, NS, 2048, 128)
        m["st_conv"] = np.ascontiguousarray(st_conv[:, sl])
        m["xT_in"] = np.ascontiguousarray(x_prompt[bidx].T).reshape(KC, 128, SEQ)
        cc = np.concatenate([f("c_prompt")[bidx:bidx + 1], f("c_sample")[core * NS:(core + 1) * NS]], axis=0)
        m["cT"] = np.ascontiguousarray(cc.T.reshape(KC, 128, NCOL).transpose(1, 0, 2))
        m = {k: v for k, v in m.items() if k in _CACHE["b"].dram_in}
        in_maps.append(m)
    res = run_bass_kernel_spmd(nc, in_maps, core_ids=list(range(8)))
    r = res.results
    B = 4
    y_prompt = np.stack([r[b]["yT_out"].reshape(D, SEQ).T for b in range(B)], axis=0)
    outs = {"y_prompt": y_prompt}
    kvp = []
    for g in range(3):
        d, keep = DIL[g], WIN[g]
        arr = np.zeros((depth, B, keep, 2, 4, 128), np.float32)
        for b in range(B):
            kT = r[b]["kT_out%d" % g]
            arr[:, b, :, 0] = kT.transpose(0, 3, 1, 2)
            vc = r[b]["vcm_out%d" % g]
            v = vc.transpose(0, 3, 2, 1, 4).reshape(depth, keep, 4, 128)
            arr[:, b, :, 1] = v
        kvp.append(arr)
    ssm_p = np.stack([r[b]["ssm_out"].transpose(0, 2, 1).reshape(depth, 32, 64, 128) for b in range(B)], axis=1)
    conv_p = np.stack([r[b]["conv_out"].transpose(0, 3, 2, 1).reshape(depth, 3, 3072) for b in range(B)], axis=1)
    y_sample = np.concatenate([r[c]["ys_out"].transpose(2, 1, 0).reshape(NS, 1, D) for c in range(8)], axis=0)
    kvs = [np.concatenate([r[c]["kvs_out%d" % g].reshape(depth, NS, 1, 2, 4, 128) for c in range(8)], axis=1) for g in range(3)]
    ssm_s = np.concatenate([r[c]["ssms_out"].reshape(depth, NS, 32, 64, 128) for c in range(8)], axis=1)
    conv_s = np.concatenate([r[c]["convs_out"] for c in range(8)], axis=1)
    return (y_prompt, y_sample, kvp[0], kvs[0], kvp[1], kvs[1], kvp[2], kvs[2], ssm_p, ssm_s, conv_p, conv_s)
```

```python
import contextlib
import os as _os
import numpy as np
import concourse.bass as bass
import concourse.mybir as mybir
from concourse.bass_utils import run_bass_kernel_spmd

F32 = mybir.dt.float32
BF16 = mybir.dt.bfloat16
AF = mybir.ActivationFunctionType
ALU = mybir.AluOpType
AX = mybir.AxisListType

D = 1024
KC = 8
SEQ = 4096
T = 2048
NGRP = SEQ // T
DFF = 2816
NJ = DFF // 128
INC = 11808
EPS = 1e-6
NS = 4
NCOL = 1 + NS
WIN = (128, 512, 2048)
DIL = (1, 4, 16)
PAST = 16384
SCALE = 128 ** -0.5
RES_W = (0.5, 1.0, 0.5)


class Op:
    __slots__ = ("eng", "fn", "deps", "chan", "chan_cnt", "sig", "idx", "pos", "is_dma", "waits", "grp", "grp_last")


class Sched:
    ENG = ("pe", "act", "dve", "pool", "sp")

    def __init__(self, nc):
        self.nc = nc
        self.ops = []
        self.last_w = {}
        self.readers = {}
        self.per_eng = {e: [] for e in self.ENG}
        self.chan_count = {}
        self.stack = contextlib.ExitStack()
        self.nbytes = 0
        self.bar = {}
        self.bar_start = 0
        self.chan_last = {}
        self.gid = 0

    def sb(self, name, shape, dt):
        t = self.stack.enter_context(self.nc.sbuf_tensor(name, list(shape), dt))
        n = 1
        for s in shape[1:]:
            n *= s
        self.nbytes += n * (4 if dt == F32 else 2)
        return t

    def ps(self, name, shape, dt=F32):
        return self.stack.enter_context(self.nc.psum_tensor(name, list(shape), dt))

    def add(self, eng, fn, r=(), w=(), chan=None, grp=None):
        op = Op()
        op.eng = eng
        op.fn = fn
        op.idx = len(self.ops)
        op.is_dma = chan is not None
        op.chan = chan
        op.sig = None
        deps = {}
        for k in r:
            j = self.last_w.get(k)
            if j is not None:
                deps[j] = True
        for k in w:
            j = self.last_w.get(k)
            if j is not None:
                deps.setdefault(j, False)
            for j in self.readers.get(k, ()):
                deps.setdefault(j, False)
        for k in w:
            self.last_w[k] = op.idx
            self.readers[k] = []
        for k in r:
            self.readers.setdefault(k, []).append(op.idx)
        if self.bar.get(eng):
            for j in self.bar.pop(eng):
                deps[j] = True
        deps.pop(op.idx, None)
        op.deps = deps
        if op.is_dma:
            c = self.chan_count.get(chan, 0) + 1
            self.chan_count[chan] = c
            op.chan_cnt = c
            if grp is None:
                self.gid += 1
                grp = ("u", self.gid)
            op.grp = grp
            op.grp_last = op
            prev = self.chan_last.get(chan)
            if prev is not None and not _os.environ.get("K_NOGRP"):
                if prev.grp == grp:
                    q = prev
                    members = [q]
                    for o2 in reversed(self.ops):
                        if o2.is_dma and o2.chan == chan and o2.grp == grp and o2 is not q:
                            members.append(o2)
                        elif o2.is_dma and o2.chan == chan and o2.grp != grp:
                            break
                    for o2 in members:
                        o2.grp_last = op
                    for j, v in prev.deps.items():
                        if self.ops[j].is_dma and self.ops[j].chan == chan:
                            deps[j] = True
                else:
                    deps[prev.idx] = True
            self.chan_last[chan] = op
        op.pos = len(self.per_eng[eng])
        self.per_eng[eng].append(op)
        self.ops.append(op)
        return op

    def barrier(self):
        lastops = [self.per_eng[e][-1].idx for e in self.ENG if self.per_eng[e]]
        dmas = [op.idx for op in self.ops[self.bar_start:] if op.is_dma]
        pend = lastops + dmas
        for e in self.ENG:
            self.bar[e] = list(self.bar.get(e, [])) + pend
        self.bar_start = len(self.ops)

    def dma(self, q, out, in_, r=(), w=(), chan=None, grp=None, **kw):
        assert chan is not None
        return self.add(q, lambda e: e.dma_start(out=out, in_=in_, **kw), r=r, w=w, chan=chan, grp=grp)

    def emit(self, final_waits=()):
        nc = self.nc
        ops = self.ops
        need = [False] * len(ops)
        waits_of = [None] * len(ops)
        for op in ops:
            wl = []
            for j, raw in op.deps.items():
                y = ops[j]
                if y.is_dma:
                    if op.is_dma and y.chan == op.chan and y.grp == op.grp:
                        continue
                    wl.append(j)
                elif y.eng != op.eng or op.is_dma:
                    need[j] = True
                    wl.append(j)
                else:
                    if op.eng == "pe":
                        continue
                    if raw and (op.pos - y.pos) <= 3:
                        need[j] = True
                        wl.append(j)
            waits_of[op.idx] = wl
        cnt = {e: 0 for e in self.ENG}
        for e in self.ENG:
            for op in self.per_eng[e]:
                if not op.is_dma and need[op.idx]:
                    cnt[e] += 1
                    op.sig = cnt[e]
        st = self.stack
        esem = {e: st.enter_context(nc.semaphore("s_" + e)) for e in self.ENG}
        csem = {c: st.enter_context(nc.semaphore("c_" + c)) for c in self.chan_count}
        handles = {"pe": "tensor", "act": "scalar", "dve": "vector", "pool": "gpsimd", "sp": "sync"}
        nwait = [0]

        self.sim = {e: [] for e in self.ENG}

        def run_engine(ename, eng):
            seen = {}
            for op in self.per_eng[ename]:
                req = {}
                for j in waits_of[op.idx]:
                    y = ops[j]
                    if y.is_dma:
                        key, val = ("c", y.chan), 16 * y.grp_last.chan_cnt
                    else:
                        key, val = ("e", y.eng), y.sig
                    if req.get(key, 0) < val:
                        req[key] = val
                for key, val in req.items():
                    if seen.get(key, 0) >= val:
                        continue
                    seen[key] = val
                    sem = csem[key[1]] if key[0] == "c" else esem[key[1]]
                    eng.wait_ge(sem, val)
                    nwait[0] += 1
                self.sim[ename].append((op.idx, list(req.items()), ("c", op.chan) if op.is_dma else (("e", ename) if op.sig is not None else None)))
                ins = op.fn(eng)
                if op.is_dma:
                    ins.then_inc(csem[op.chan], 16)
                elif op.sig is not None:
                    ins.then_inc(esem[ename], 1)
            if ename == "sp":
                for c in final_waits:
                    eng.wait_ge(csem[c], 16 * self.chan_count[c])

        block = st.enter_context(nc.Block())
        for ename in self.ENG:
            getattr(block, handles[ename])(lambda eng, ename=ename: run_engine(ename, eng))
        self.stats = dict(n_ops={e: len(v) for e, v in self.per_eng.items()}, n_wait=nwait[0],
                          n_sem=len(esem) + len(csem))


def fap(t, off, dims, p0=0, pn=None):
    base = t[:]
    pstep = base.ap[0][0]
    if pn is None:
        pn = base.ap[0][1] - p0
    return bass.AP(base.tensor, base.offset + p0 * pstep + off, [[pstep, pn]] + [list(d) for d in dims])


class Builder:
    def __init__(self, depth=2, do_prompt=True, do_sample=True, stop_after=None, nsteps=None):
        self.nsteps = nsteps
        self.stepi = 0
        self.depth = depth
        self.do_prompt = do_prompt
        self.do_sample = do_sample
        self.stop_after = stop_after
        nc = bass.Bass("TRN2", target_bir_lowering=False)
        self.nc = nc
        self.S = Sched(nc)
        self.dram_in = {}
        self.dram_out = {}
        self.out_chans = []
        self.uid = 0

    def din(self, name, shape, dt=F32):
        ap = self.nc.dram_tensor(name, list(shape), dt, kind="ExternalInput").ap()
        self.dram_in[name] = ap
        return ap

    def dout(self, name, shape, dt=F32):
        ap = self.nc.dram_tensor(name, list(shape), dt, kind="ExternalOutput").ap()
        self.dram_out[name] = ap
        return ap

    def dscr(self, name, shape, dt=F32):
        return self.nc.dram_tensor(name, list(shape), dt, kind="Internal").ap()

    def u(self, p):
        self.uid += 1
        return "%s%d" % (p, self.uid)

    def declare(self):
        L = self.depth
        self.xT_in = self.din("xT_in", [KC, 128, SEQ])
        self.cT = self.din("cT", [128, KC, NCOL])
        self.w_mod = self.din("w_mod", [L, D, 9 * D])
        self.bmodT = self.din("bmodT", [L, 128, 72])
        self.gpreT = self.din("gpreT", [L, 128, 3, KC])
        self.gpostT = self.din("gpostT", [L, 128, 3, KC])
        self.w_ff_in = self.din("w_ff_in", [L, 2, D, 2 * DFF])
        self.w_ff_out = self.din("w_ff_out", [L, 2, DFF, D])
        self.yT_out = self.dout("yT_out", [KC, 128, SEQ])
        self.dbg = bool(_os.environ.get("K_DBG"))
        self.xres = (self.dout if self.dbg else self.dscr)("xres", [KC, 128, SEQ])
        self.cbf_in = self.din("cbf_in", [128, 896])
        self.cf32_in = self.din("cf32_in", [128, 288])
        self.cosT = self.din("cosT", [128, SEQ + 1])
        self.sinT = self.din("sinT", [128, SEQ + 1])
        self.w_in = self.din("w_in", [L, D, INC])
        self.conv_wT = self.din("conv_wT", [L, 128, 24, 4])
        self.conv_bT = self.din("conv_bT", [L, 128, 24])
        self.dt_bias = self.din("dt_bias", [L, 32])
        self.a_log = self.din("a_log", [L, 32])
        self.d_skip = self.din("d_skip", [L, 32])
        self.norm_ssm = self.din("norm_ssm", [L, 2048])
        self.w_br_att = self.din("w_br_att", [L, 512, D])
        self.w_br_ssm = self.din("w_br_ssm", [L, 2048, D])
        self.w_out = self.din("w_out", [L, D, D])
        self.kT_out = [self.dout("kT_out%d" % g, [L, 4, 128, WIN[g]]) for g in range(3)]
        self.vcm_out = [self.dout("vcm_out%d" % g, [L, 4, DIL[g], 128, 128]) for g in range(3)]
        self.ssm_out = self.dout("ssm_out", [L, 128, 2048])
        self.conv_out = self.dout("conv_out", [L, 128, 24, 3])
        self.khist = [self.dscr("khist%d" % g, [4, 128, WIN[g]], BF16) for g in range(3)]
        self.vhist = [self.dscr("vhist%d" % g, [4, 128, DIL[g] * 128], BF16) for g in range(3)]
        self.yscr = (self.dout if self.dbg else self.dscr)("yscr", [16, 128, T], BF16)
        if self.dbg:
            self.atto_dbg = self.dout("atto_dbg", [128, 4, T], BF16)
        if self.do_sample:
            self.s_declare()

    def consts(self):
        S = self.S
        nc = self.nc
        self.ones_bf = S.sb("ones_bf", [128, 128], BF16)
        S.add("pool", lambda e: e.memset(self.ones_bf[:], 1.0), w=["ones_bf"])
        self.psall = S.ps("psall", [128, 4096])
        self.psall_bf = self.psall[:].bitcast(BF16)
        self.bank = [self.psall[:, i * 512:(i + 1) * 512] for i in range(8)]
        self.bank_bf = [self.psall_bf[:, i * 1024:(i + 1) * 1024] for i in range(8)]
        self.cbf = S.sb("cbf", [128, 896], BF16)
        S.dma("pool", self.cbf[:], self.cbf_in, w=["cbf"], chan="misc2")
        self.ident_bf = self.cbf[:, 0:128]
        self.mask2 = self.cbf[:, 128:384]
        self.negm4 = self.cbf[:, 384:896]
        self.cf32 = S.sb("cf32", [128, 288], F32)
        S.dma("sp", self.cf32[:], self.cf32_in, w=["cf32"], chan="misc")
        self.tri = self.cf32[:, 0:128]
        self.sel127 = self.cf32[:, 128:256]
        self.identS = self.cf32[0:64, 256:288]
        self.Sst = S.sb("Sst", [128, 2048], F32)
        self.convhist = S.sb("convhist", [128, 24, 3], F32)
        self.convw = S.sb("convw", [128, 24, 4], F32)
        self.convb = S.sb("convb", [128, 24], F32)
        self.dtb_bc = S.sb("dtb_bc", [128, 32], F32)
        self.a_bc = S.sb("a_bc", [128, 32], F32)
        self.dsk_bc = S.sb("dsk_bc", [128, 32], F32)

        self.NSLOT = 3
        self.wring = [S.sb("wslot%d" % i, [128, 4096], BF16) for i in range(self.NSLOT)]
        self.wnext = 0
        self.modT = S.sb("modT", [128, 72, NCOL], F32)
        self.Aall = S.sb("Aall", [128, 3, KC, NCOL], F32)
        self.Gall = S.sb("Gall", [128, 3, KC, NCOL], F32)
        self.scT = S.sb("scT", [128, KC, NCOL], BF16)
        self.cTs = S.sb("cTs", [128, KC, NCOL], F32)
        self.bmod_sb = S.sb("bmod_sb", [128, 72], F32)
        self.gpre_sb = S.sb("gpre_sb", [128, 3, KC], F32)
        self.gpost_sb = S.sb("gpost_sb", [128, 3, KC], F32)
        if self.do_sample:
            self.s_consts()
        self.ARENA_B = (int(self.nc.sbuf_bytes_remaining) - 128) // 256 * 256
        self.amax = 0
        self.arena = S.sb("arena", [128, self.ARENA_B // 4], F32)
        self.arena16 = self.arena[:].bitcast(BF16)
        self.aptr = 0
        self.xt_i = 0

    def alloc(self, shape, dt):
        n = 1
        for v in shape:
            n *= v
        nb = n * (4 if dt == F32 else 2)
        off = self.aptr
        self.aptr = (off + nb + 63) // 64 * 64
        assert self.aptr <= self.ARENA_B, ("arena overflow", self.aptr, self.ARENA_B)
        self.amax = max(self.amax, self.aptr)
        if dt == F32:
            ap = self.arena[:, off // 4:off // 4 + n]
        else:
            ap = self.arena16[:, off // 2:off // 2 + n]
        if len(shape) == 2:
            return ap.rearrange("p (a b) -> p a b", a=shape[0])
        if len(shape) == 3:
            return ap.rearrange("p (a b c) -> p a b c", a=shape[0], b=shape[1])
        return ap

    def phase(self, mark=0):
        self.S.barrier()
        self.aptr = mark

    def norm_bufs(self):
        self.xt = [self.alloc([KC, 512], F32) for i in range(2)]
        self.sq = self.alloc([KC, 512], BF16)
        self.rstd = self.alloc([512], F32)
        self.tmp = self.alloc([KC, 512], F32)

    def ffn_bufs(self):
        self.phase(0)
        self.hT = self.alloc([KC, 1024], BF16)
        self.norm_bufs()
        self.aT = self.alloc([NJ, 1024], BF16)
        self.sg = [self.alloc([512], BF16) for i in range(2)]
        self.fT = self.alloc([KC, 1024], F32)
        self.sqf = self.alloc([KC, 1024], BF16)

    def wslot(self, loads, tag):
        S = self.S
        i = self.wnext
        self.wnext = (i + 1) % self.NSLOT
        slot = self.wring[i]
        key = "wslot%d" % i
        S.gid += 1
        grp = ("w", S.gid)
        for (off, dims, src) in loads:
            S.dma("pool", fap(slot, off, dims), src, w=[key], chan=key, grp=grp)
        return slot, key

    def modulation(self, l):
        S = self.S
        mm = self.bank[7]
        if l == 0:
            S.dma("sp", self.cTs[:], self.cT, w=["cTs"], chan="misc")
            S.add("act", lambda e: e.activation(out=self.scT[:], in_=self.cTs[:], func=AF.Silu), r=["cTs"], w=["scT"])
        S.dma("sp", self.bmod_sb[:], self.bmodT[l], w=["bmod_sb"], chan="misc")
        S.dma("sp", self.gpre_sb[:], self.gpreT[l], w=["gpre_sb"], chan="misc")
        S.dma("sp", self.gpost_sb[:], self.gpostT[l], w=["gpost_sb"], chan="misc")
        wv = self.w_mod[l].rearrange("(kc p) n -> p kc n", p=128)
        for blk in range(18):
            slot, key = self.wslot([(0, [[512, KC], [1, 512]], wv[:, :, blk * 512:(blk + 1) * 512])], "mod")
            for cc in range(4):
                ch = blk * 4 + cc
                for kc in range(KC):
                    S.add("pe", lambda e, ch=ch, kc=kc, cc=cc, slot=slot: e.matmul(
                        fap(mm, ch * NCOL, [[1, NCOL]]), lhsT=fap(slot, kc * 512 + cc * 128, [[1, 128]]),
                        rhs=self.scT[:, kc, :], start=(kc == 0), stop=(kc == KC - 1)),
                        r=[key, "scT"], w=["bank7"])
        S.add("dve", lambda e: e.tensor_tensor(
            out=self.modT[:], in0=fap(mm, 0, [[NCOL, 72], [1, NCOL]]),
            in1=self.bmod_sb[:].unsqueeze(2).to_broadcast([128, 72, NCOL]), op=ALU.add),
            r=["bank7", "bmod_sb"], w=["modT"])
        for k in range(3):
            S.add("dve", lambda e, k=k: e.scalar_tensor_tensor(
                out=self.Aall[:, k], in0=self.modT[:, k * 24 + 8:k * 24 + 16, :], scalar=1.0,
                in1=self.gpre_sb[:, k, :].unsqueeze(2).to_broadcast([128, KC, NCOL]),
                op0=ALU.add, op1=ALU.mult), r=["modT", "gpre_sb"], w=["Aall"])
            S.add("dve", lambda e, k=k: e.scalar_tensor_tensor(
                out=self.Gall[:, k], in0=self.modT[:, k * 24 + 16:k * 24 + 24, :], scalar=float(RES_W[k]),
                in1=self.gpost_sb[:, k, :].unsqueeze(2).to_broadcast([128, KC, NCOL]),
                op0=ALU.mult, op1=ALU.mult), r=["modT", "gpost_sb"], w=["Gall"])

    def rstd_from_sq(self, sq_ap_fn, nch, denom, rkeys):
        _rstd = self.rstd
        S = self.S
        st = self.bank[6]
        for c in range(nch):
            S.add("pe", lambda e, c=c: e.matmul(st[:, :], lhsT=self.ones_bf[:, :], rhs=sq_ap_fn(c),
                                                  start=(c == 0), stop=(c == nch - 1)),
                  r=list(rkeys) + ["ones_bf"], w=["bank6"])
        S.add("dve", lambda e: e.tensor_scalar(out=_rstd[:], in0=st[:, :], scalar1=1.0 / denom, scalar2=EPS,
                                                 op0=ALU.mult, op1=ALU.add), r=["bank6"], w=["rstd"])
        S.add("act", lambda e: e.activation(out=_rstd[:], in_=_rstd[:], func=AF.Sqrt), r=["rstd"], w=["rstd"])
        S.add("dve", lambda e: e.reciprocal(out=_rstd[:], in_=_rstd[:]), r=["rstd"], w=["rstd"])

    def load_x(self, src, tok0):
        S = self.S
        i = self.xt_i
        self.xt_i ^= 1
        xt = self.xt[i]
        key = "xt%d" % i
        S.dma("sp", xt[:], src[:, :, tok0:tok0 + 512].rearrange("c p t -> p c t"),
              r=["xdram"], w=[key], chan=key)
        return xt, key

    def prenorm(self, src, tok0, ntok, k, hoff):
        _hT = self.hT
        _sq = self.sq
        _rstd = self.rstd
        _tmp = self.tmp
        S = self.S
        for tt in range(ntok // 512):
            xt, xkey = self.load_x(src, tok0 + tt * 512)
            S.add("act", lambda e, xt=xt: e.activation(out=_sq[:], in_=xt[:], func=AF.Square),
                  r=[xkey], w=["sq"])
            self.rstd_from_sq(lambda c: _sq[:, c, :], KC, D, ["sq"])
            S.add("dve", lambda e, xt=xt: e.tensor_tensor(
                out=_tmp[:], in0=xt[:], in1=_rstd[:].unsqueeze(1).to_broadcast([128, KC, 512]),
                op=ALU.mult), r=[xkey, "rstd"], w=["tmp"])
            for c in range(KC):
                o0 = hoff + tt * 512
                S.add("act", lambda e, c=c, o0=o0: e.activation(
                    out=_hT[:, c, o0:o0 + 512], in_=_tmp[:, c, :], func=AF.Identity,
                    scale=self.Aall[:, k, c, 0:1], bias=self.modT[:, k * 24 + c, 0:1]),
                    r=["tmp", "Aall", "modT"], w=["hT"])

    def postnorm_residual(self, f_ap, sq_fn, k, src, dst, tok0, fkeys, out_chan=None):
        _rstd = self.rstd
        _tmp = self.tmp
        S = self.S
        self.rstd_from_sq(sq_fn, KC, D, fkeys)
        xt, xkey = self.load_x(src, tok0)
        S.add("dve", lambda e: e.tensor_tensor(
            out=_tmp[:], in0=f_ap, in1=_rstd[:].unsqueeze(1).to_broadcast([128, KC, 512]),
            op=ALU.mult), r=list(fkeys) + ["rstd"], w=["tmp"])
        for c in range(KC):
            S.add("dve", lambda e, c=c, xt=xt: e.scalar_tensor_tensor(
                out=xt[:, c, :], in0=_tmp[:, c, :], scalar=self.Gall[:, k, c, 0:1], in1=xt[:, c, :],
                op0=ALU.mult, op1=ALU.add), r=["tmp", "Gall", xkey], w=[xkey])
        ch = out_chan or (xkey + "o")
        S.dma("sp", dst[:, :, tok0:tok0 + 512].rearrange("c p t -> p c t"), xt[:], r=[xkey], w=["xdram"], chan=ch)
        if out_chan and out_chan not in self.out_chans:
            self.out_chans.append(out_chan)

    def ffn(self, l, i, k, src, dst, g0, is_out):
        self.ffn_bufs()
        _hT = self.hT
        _aT = self.aT
        _sg = self.sg
        _fT = self.fT
        _sqf = self.sqf
        S = self.S
        w1 = self.w_ff_in[l, i].rearrange("(kc p) n -> p kc n", p=128)
        w2 = self.w_ff_out[l, i].rearrange("(j p) n -> p j n", p=128)
        pa = 0
        for half in range(2):
            t0 = g0 + half * 1024
            self.prenorm(src, t0, 1024, k, 0)
            for jb in range(NJ // 2):
                slot, key = self.wslot([
                    (0, [[256, KC], [1, 256]], w1[:, :, jb * 256:(jb + 1) * 256]),
                    (2048, [[256, KC], [1, 256]], w1[:, :, DFF + jb * 256:DFF + (jb + 1) * 256])], "w1")
                for jj in range(2):
                    j = jb * 2 + jj
                    for tt in range(2):
                        A = self.bank[pa]
                        B = self.bank[2 + pa]
                        ka, kb = "bank%d" % pa, "bank%d" % (2 + pa)
                        sg = _sg[pa]
                        sk = "sg%d" % pa
                        pa ^= 1
                        for kc in range(KC):
                            S.add("pe", lambda e, A=A, kc=kc, jj=jj, tt=tt, slot=slot: e.matmul(
                                A[:, :], lhsT=fap(slot, kc * 256 + jj * 128, [[1, 128]]),
                                rhs=_hT[:, kc, tt * 512:(tt + 1) * 512], start=(kc == 0), stop=(kc == KC - 1)),
                                r=[key, "hT"], w=[ka])
                        for kc in range(KC):
                            S.add("pe", lambda e, B=B, kc=kc, jj=jj, tt=tt, slot=slot: e.matmul(
                                B[:, :], lhsT=fap(slot, 2048 + kc * 256 + jj * 128, [[1, 128]]),
                                rhs=_hT[:, kc, tt * 512:(tt + 1) * 512], start=(kc == 0), stop=(kc == KC - 1)),
                                r=[key, "hT"], w=[kb])
                        S.add("act", lambda e, A=A, sg=sg: e.activation(out=sg[:], in_=A[:, :], func=AF.Silu),
                              r=[ka], w=[sk])
                        S.add("dve", lambda e, B=B, sg=sg, j=j, tt=tt: e.tensor_tensor(
                            out=_aT[:, j, tt * 512:(tt + 1) * 512], in0=sg[:], in1=B[:, :], op=ALU.mult),
                            r=[sk, kb], w=["aT"])
            pf = 0
            for m in range(KC):
                slot, key = self.wslot([(0, [[128, NJ], [1, 128]], w2[:, :, m * 128:(m + 1) * 128])], "w2")
                for tt in range(2):
                    Fp = self.bank[4 + pf]
                    kf = "bank%d" % (4 + pf)
                    pf ^= 1
                    for j in range(NJ):
                        S.add("pe", lambda e, Fp=Fp, j=j, tt=tt, slot=slot: e.matmul(
                            Fp[:, :], lhsT=fap(slot, j * 128, [[1, 128]]),
                            rhs=_aT[:, j, tt * 512:(tt + 1) * 512], start=(j == 0), stop=(j == NJ - 1)),
                            r=[key, "aT"], w=[kf])
                    S.add("act", lambda e, Fp=Fp, m=m, tt=tt: e.activation(
                        out=_fT[:, m, tt * 512:(tt + 1) * 512], in_=Fp[:, :], func=AF.Copy), r=[kf], w=["fT%d" % tt])
                    S.add("act", lambda e, Fp=Fp, m=m, tt=tt: e.activation(
                        out=_sqf[:, m, tt * 512:(tt + 1) * 512], in_=Fp[:, :], func=AF.Square), r=[kf], w=["sqf%d" % tt])
            for tt in range(2):
                self.postnorm_residual(_fT[:, :, tt * 512:(tt + 1) * 512],
                                       lambda c, tt=tt: _sqf[:, c, tt * 512:(tt + 1) * 512],
                                       k, src, dst, t0 + tt * 512, ["fT%d" % tt, "sqf%d" % tt],
                                       out_chan=("yout" if is_out else None))

    def proj_tiles(self, slot, woff, wkstride, key, ntile, banks, cb, extra_r=()):
        _hT = self.hT
        S = self.S
        for tt in range(ntile):
            bi = banks[tt % len(banks)]
            Bk = self.bank[bi]
            bk = "bank%d" % bi
            for kc in range(KC):
                S.add("pe", lambda e, Bk=Bk, kc=kc, tt=tt: e.matmul(
                    Bk[:, :], lhsT=fap(slot, woff + kc * wkstride, [[1, 128]]),
                    rhs=_hT[:, kc, tt * 512:(tt + 1) * 512], start=(kc == 0), stop=(kc == KC - 1)),
                    r=[key, "hT"] + list(extra_r), w=[bk])
            cb(tt, Bk, bk)

    def win_cols(self, l, c0, n):
        return self.w_in[l].rearrange("(kc p) n -> p kc n", p=128)[:, :, c0:c0 + n]

    def attention(self, l, g0):
        _hT = self.hT
        _attoT = self.attoT
        S = self.S
        last_grp = (g0 + T == SEQ) and not _os.environ.get("K_NOKVOUT")
        cosF = self.alloc([T], F32)
        sinS = self.alloc([T], F32)
        S.dma("sp", cosF[:], self.cosT[:, g0:g0 + T], w=["cosF"], chan="misc")
        S.dma("sp", sinS[:], self.sinT[:, g0:g0 + T], w=["sinS"], chan="misc")
        qb = self.alloc([3, T], BF16)
        kb = [self.alloc([WIN[g] + T], BF16) for g in range(3)]
        vb = [self.alloc([DIL[g] + T // 128, 128], BF16) for g in range(3)]
        ND = self.alloc([2, T], F32)
        Pt = [self.alloc([256], BF16) for i in range(2)]
        r1 = [self.alloc([512], F32) for i in range(2)]
        r2 = [self.alloc([512], F32) for i in range(2)]
        kst = [self.alloc([512], F32) for i in range(2)]
        vst = [self.alloc([512], F32) for i in range(2)]
        rden = self.alloc([T], F32)
        cnt = {"rt": 0, "vs": 0, "pt": 0, "ps": 0, "po": 0}
        for h in range(4):
            if g0 > 0 and not _os.environ.get("K_NOHISTLD"):
                for g in range(3):
                    S.dma("sp", kb[g][:, 0:WIN[g]], self.khist[g][h], r=["khist%d" % g], w=["kb%d" % g], chan="hist")
                    S.dma("sp", vb[g][:, 0:DIL[g], :], self.vhist[g][h].rearrange("p (a b) -> p a b", b=128),
                          r=["vhist%d" % g], w=["vb%d" % g], chan="hist")
            for g in range(3):
                d, span = DIL[g], WIN[g]
                cb0 = g * 1536 + h * 128
                for qk in range(2):
                    c0 = cb0 + qk * 512
                    wsrc = self.win_cols(l, c0, 128)
                    slot, key = self.wslot([
                        (0, [[128, KC], [1, 128]], wsrc),
                        (1024, [[128, KC], [1, 64]], wsrc[:, :, 64:128]),
                        (1024 + 64, [[128, KC], [1, 64]], wsrc[:, :, 0:64])], "wqk")

                    def rope_cb(tt, Bk, bk, slot=slot, key=key, qk=qk, g=g, h=h, span=span):
                        bi2 = 2 + (tt % 2)
                        B2 = self.bank[bi2]
                        b2k = "bank%d" % bi2
                        for kc in range(KC):
                            S.add("pe", lambda e, kc=kc: e.matmul(
                                B2[:, :], lhsT=fap(slot, 1024 + kc * 128, [[1, 128]]),
                                rhs=_hT[:, kc, tt * 512:(tt + 1) * 512], start=(kc == 0), stop=(kc == KC - 1)),
                                r=[key, "hT"], w=[b2k])
                        i = cnt["rt"] % 2
                        cnt["rt"] += 1
                        S.add("dve", lambda e: e.tensor_tensor(out=r1[i][:], in0=Bk[:, :], in1=cosF[:, tt * 512:(tt + 1) * 512],
                                                                 op=ALU.mult), r=[bk, "cosF"], w=["r1_%d" % i])
                        S.add("dve", lambda e: e.tensor_tensor(out=r2[i][:], in0=B2[:, :], in1=sinS[:, tt * 512:(tt + 1) * 512],
                                                                 op=ALU.mult), r=[b2k, "sinS"], w=["r2_%d" % i])
                        if qk == 0:
                            S.add("pool", lambda e: e.tensor_tensor(out=qb[:, g, tt * 512:(tt + 1) * 512], in0=r1[i][:],
                                                                      in1=r2[i][:], op=ALU.add),
                                  r=["r1_%d" % i, "r2_%d" % i], w=["qb"])
                        else:
                            S.add("pool", lambda e: e.tensor_tensor(out=kst[i][:], in0=r1[i][:], in1=r2[i][:], op=ALU.add),
                                  r=["r1_%d" % i, "r2_%d" % i], w=["kst%d" % i])
                            S.add("act", lambda e: e.activation(out=kb[g][:, span + tt * 512:span + (tt + 1) * 512],
                                                                  in_=kst[i][:], func=AF.Copy),
                                  r=["kst%d" % i], w=["kb%d" % g])
                            if last_grp:
                                lo = max(g0 + tt * 512, SEQ - span)
                                hi = g0 + (tt + 1) * 512
                                if lo < hi:
                                    S.dma("sp", self.kT_out[g][l, h, :, lo - (SEQ - span):hi - (SEQ - span)],
                                          kst[i][:, lo - (g0 + tt * 512):512], r=["kst%d" % i], w=["kTout"], chan="kvout")
                    self.proj_tiles(slot, 0, 128, key, T // 512, [0, 1], rope_cb)
                c0 = cb0 + 1024
                slot, key = self.wslot([(0, [[128, KC], [1, 128]], self.win_cols(l, c0, 128))], "wv")
                tiles = [(n, r) for n in range(T // span) for r in range(d)]
                for t4 in range(0, len(tiles), 4):
                    bi = (t4 // 4) % 2
                    Bk = self.bank[bi]
                    bk = "bank%d" % bi
                    for q4 in range(4):
                        n, r = tiles[t4 + q4]
                        for kc in range(KC):
                            S.add("pe", lambda e, Bk=Bk, kc=kc, q4=q4, n=n, r=r, d=d, span=span, slot=slot: e.matmul(
                                Bk[:, q4 * 128:(q4 + 1) * 128],
                                lhsT=fap(_hT, kc * T + n * span + r, [[d, 128]]),
                                rhs=fap(slot, kc * 128, [[1, 128]]), start=(kc == 0), stop=(kc == KC - 1)),
                                r=[key, "hT"], w=[bk])
                    ti0 = d + t4
                    S.add("act", lambda e, Bk=Bk, ti0=ti0, g=g: e.activation(
                        out=vb[g][:, ti0:ti0 + 4, :], in_=fap(Bk, 0, [[128, 4], [1, 128]]), func=AF.Copy),
                        r=[bk], w=["vb%d" % g])
                    if last_grp and tiles[t4 + 3][0] == T // span - 1:
                        i = cnt["vs"] % 2
                        cnt["vs"] += 1
                        S.add("act", lambda e, Bk=Bk, i=i: e.activation(out=vst[i][:], in_=Bk[:, :], func=AF.Copy), r=[bk], w=["vst%d" % i])
                        for q4 in range(4):
                            n, r = tiles[t4 + q4]
                            if n == T // span - 1:
                                S.dma("sp", self.vcm_out[g][l, h, r], vst[i][:, q4 * 128:(q4 + 1) * 128],
                                      r=["vst%d" % i], w=["vout"], chan="kvout")
            for g in range(3):
                d, span = DIL[g], WIN[g]
                for n in range(T // span):
                    for r in range(d):
                        has_prev = (g0 > 0) or (n > 0)
                        c_lo = 0 if has_prev else 128
                        qap = fap(qb, g * T + n * span + r, [[d, 128]])
                        kcur = fap(kb[g], span + n * span + r, [[d, 128]])
                        kprev = fap(kb[g], n * span + r, [[d, 128]])
                        vcur = vb[g][:, d + n * d + r, :]
                        vprev = vb[g][:, n * d + r, :]
                        si = 4 + cnt["ps"] % 2
                        cnt["ps"] += 1
                        oi = 6 + cnt["po"] % 2
                        cnt["po"] += 1
                        pi = cnt["pt"] % 2
                        cnt["pt"] += 1
                        Sb, Ob, P = self.bank[si], self.bank[oi], Pt[pi]
                        sk, ok, pk = "bank%d" % si, "bank%d" % oi, "Pt%d" % pi
                        if has_prev:
                            S.add("pe", lambda e, Sb=Sb, kprev=kprev, qap=qap: e.matmul(
                                Sb[:, 0:128], lhsT=kprev, rhs=qap, start=True, stop=True), r=["kb%d" % g, "qb"], w=[sk])
                        S.add("pe", lambda e, Sb=Sb, kcur=kcur, qap=qap: e.matmul(
                            Sb[:, 128:256], lhsT=kcur, rhs=qap, start=True, stop=True), r=["kb%d" % g, "qb"], w=[sk])
                        S.add("act", lambda e, Sb=Sb, P=P, c_lo=c_lo: e.activation(
                            out=P[:, c_lo:256], in_=Sb[:, c_lo:256], func=AF.Exp, scale=float(SCALE)), r=[sk], w=[pk])
                        S.add("pool", lambda e, P=P, c_lo=c_lo: e.tensor_tensor(
                            out=P[:, c_lo:256], in0=P[:, c_lo:256], in1=self.mask2[:, c_lo:256], op=ALU.mult),
                            r=[pk, "cbf"], w=[pk])
                        for half, lh in ((0, None), (1, self.ones_bf)):
                            oc = half * 128
                            if has_prev:
                                S.add("pe", lambda e, Ob=Ob, P=P, oc=oc, lh=lh, vprev=vprev: e.matmul(
                                    Ob[:, oc:oc + 128], lhsT=(vprev if lh is None else lh[:, :]), rhs=P[:, 0:128],
                                    start=True, stop=False), r=[pk, "vb%d" % g, "ones_bf"], w=[ok])
                            S.add("pe", lambda e, Ob=Ob, P=P, oc=oc, lh=lh, vcur=vcur, has_prev=has_prev: e.matmul(
                                Ob[:, oc:oc + 128], lhsT=(vcur if lh is None else lh[:, :]), rhs=P[:, 128:256],
                                start=(not has_prev), stop=True), r=[pk, "vb%d" % g, "ones_bf"], w=[ok])
                        nd = fap(ND, n * span + r, [[T, 2], [d, 128]])
                        osrc = fap(Ob, 0, [[128, 2], [1, 128]])
                        if g == 0:
                            S.add("dve", lambda e, nd=nd, osrc=osrc: e.tensor_copy(out=nd, in_=osrc), r=[ok], w=["ND"])
                        else:
                            S.add("dve", lambda e, nd=nd, osrc=osrc: e.tensor_tensor(out=nd, in0=osrc, in1=nd, op=ALU.add),
                                  r=[ok, "ND"], w=["ND"])
            S.add("dve", lambda e: e.reciprocal(out=rden[:], in_=ND[:, 1, :]), r=["ND"], w=["rden"])
            S.add("dve", lambda e, h=h: e.tensor_tensor(out=_attoT[:, h, :], in0=ND[:, 0, :], in1=rden[:], op=ALU.mult),
                  r=["ND", "rden"], w=["attoT"])
            if not last_grp:
                for g in range(3):
                    S.dma("sp", self.khist[g][h], kb[g][:, T:T + WIN[g]], r=["kb%d" % g], w=["khist%d" % g], chan="hist")
                    S.dma("sp", self.vhist[g][h].rearrange("p (a b) -> p a b", b=128),
                          vb[g][:, T // 128:T // 128 + DIL[g], :], r=["vb%d" % g], w=["vhist%d" % g], chan="hist")
        if last_grp and "kvout" not in self.out_chans:
            self.out_chans.append("kvout")
        if self.dbg and g0 == 0 and l == 0:
            S.dma("sp", self.atto_dbg, _attoT[:], r=["attoT"], w=["attodbg"], chan="dbg")
            self.out_chans.append("dbg")

    def ssd_params(self, l):
        S = self.S
        S.dma("sp", self.convw[:], self.conv_wT[l], w=["convw"], chan="misc")
        S.dma("sp", self.convb[:], self.conv_bT[l], w=["convb"], chan="misc")
        S.dma("sp", self.dtb_bc[:], self.dt_bias[l].partition_broadcast(128), w=["dtb_bc"], chan="misc")
        S.dma("sp", self.a_bc[:], self.a_log[l].partition_broadcast(128), w=["a_bc"], chan="misc")
        S.dma("sp", self.dsk_bc[:], self.d_skip[l].partition_broadcast(128), w=["dsk_bc"], chan="misc")
        S.add("act", lambda e: e.activation(out=self.a_bc[:], in_=self.a_bc[:], func=AF.Exp), r=["a_bc"], w=["a_bc"])
        S.add("dve", lambda e: e.tensor_scalar(out=self.a_bc[:], in0=self.a_bc[:], scalar1=-1.0, scalar2=None, op0=ALU.mult),
              r=["a_bc"], w=["a_bc"])
        S.add("dve", lambda e: e.memset(self.Sst[:], 0.0), w=["Sst"])
        S.add("dve", lambda e: e.memset(self.convhist[:], 0.0), w=["convhist"])

    def ssd(self, l, g0):
        _hT = self.hT
        S = self.S
        last_grp = (g0 + T == SEQ)
        NB = T // 128
        dtT = self.alloc([NB, 32], F32)
        dA2 = self.alloc([NB, 64], F32)
        acs = self.alloc([NB, 32], F32)
        eacs = self.alloc([NB, 32], F32)
        dtw = self.alloc([NB, 32], F32)
        dec = self.alloc([NB, 32], F32)
        L1 = self.alloc([T], F32)
        acsT = self.alloc([T], F32)
        R1 = self.alloc([8, 128], F32)
        raw = self.alloc([T + 8], F32)
        acc = self.alloc([T], F32)
        xTc = self.alloc([4, T], BF16)
        BT = self.alloc([T], BF16)
        CT = self.alloc([T], BF16)
        yT = self.alloc([4, T], BF16)
        Sbf = self.alloc([512], BF16)
        Xtm = [self.alloc([512], BF16) for i in range(2)]
        Xdt = [self.alloc([512], BF16) for i in range(2)]
        Xw = [self.alloc([512], BF16) for i in range(2)]
        XD = [self.alloc([512], BF16) for i in range(2)]
        Btm = [self.alloc([128], BF16) for i in range(2)]
        CBm = [self.alloc([128], BF16) for i in range(2)]
        _Lt = self.alloc([8, 128], BF16)
        Lt = [_Lt, _Lt]
        _MG = self.alloc([8, 128], BF16)
        MG = [_MG, _MG]
        _sz = self.alloc([512], F32)
        sz = [_sz, _sz]
        _t1 = self.alloc([512], F32)
        t1 = [_t1, _t1]
        _yg = self.alloc([512], F32)
        yg = [_yg, _yg]
        nssm = self.alloc([512], F32)
        yn = [self.alloc([512], BF16) for i in range(2)]
        ss = self.alloc([4], F32)
        slot, key = self.wslot([(0, [[32, KC], [1, 32]], self.win_cols(l, 9728, 32))], "wdt")
        b0 = self.bank[0]
        for blk in range(NB):
            for kc in range(KC):
                S.add("pe", lambda e, blk=blk, kc=kc, slot=slot: e.matmul(
                    b0[:, blk * 32:(blk + 1) * 32], lhsT=_hT[:, kc, blk * 128:(blk + 1) * 128],
                    rhs=fap(slot, kc * 32, [[1, 32]]), start=(kc == 0), stop=(kc == KC - 1)), r=[key, "hT"], w=["bank0"])
        S.add("dve", lambda e: e.tensor_tensor(out=dtT[:], in0=fap(b0, 0, [[32, NB], [1, 32]]),
                                                 in1=self.dtb_bc[:].unsqueeze(1).to_broadcast([128, NB, 32]), op=ALU.add),
              r=["bank0", "dtb_bc"], w=["dtT"])
        S.add("act", lambda e: e.activation(out=dtT[:], in_=dtT[:], func=AF.Exp), r=["dtT"], w=["dtT"])
        S.add("act", lambda e: e.activation(out=dtT[:], in_=dtT[:], func=AF.Ln, bias=1.0), r=["dtT"], w=["dtT"])
        if self.stop_after == "ssd_p1":
            return
        for hf in range(2):
            S.add("dve", lambda e, hf=hf: e.tensor_tensor(
                out=dA2[:, :, hf * 32:(hf + 1) * 32], in0=dtT[:],
                in1=self.a_bc[:].unsqueeze(1).to_broadcast([128, NB, 32]), op=ALU.mult), r=["dtT", "a_bc"], w=["dA2"])
        S.add("dve", lambda e: e.memset(L1[0:32, :], 1.0), w=["L1a"])
        for b4 in range(NB // 4):
            bi = 1 + b4 % 2
            Bk = self.bank[bi]
            bk = "bank%d" % bi
            for q in range(4):
                blk = b4 * 4 + q
                S.add("pe", lambda e, Bk=Bk, q=q, blk=blk: e.matmul(
                    Bk[0:64, q * 128:(q + 1) * 128], lhsT=dA2[:, blk, :], rhs=self.tri[:, :], start=True, stop=True),
                    r=["dA2", "cf32"], w=[bk])
            S.add("act", lambda e, Bk=Bk, b4=b4: e.activation(out=acsT[0:32, b4 * 512:(b4 + 1) * 512], in_=Bk[0:32, :],
                                                               func=AF.Copy), r=[bk], w=["acsT"])
            S.add("act", lambda e, Bk=Bk, b4=b4: e.activation(out=L1[32:64, b4 * 512:(b4 + 1) * 512], in_=Bk[32:64, :],
                                                               func=AF.Copy, scale=-1.0), r=[bk], w=["L1b"])
        if self.stop_after == "ssd_p2":
            return
        b3 = self.bank[3]
        for blk in range(NB):
            S.add("pe", lambda e, blk=blk: e.matmul(b3[:, blk * 32:(blk + 1) * 32], lhsT=acsT[0:32, blk * 128:(blk + 1) * 128],
                                                    rhs=self.identS[0:32, 0:32], start=True, stop=True),
                  r=["acsT", "cf32"], w=["bank3"])
        S.add("dve", lambda e: e.tensor_copy(out=acs[:], in_=fap(b3, 0, [[32, NB], [1, 32]])), r=["bank3"], w=["acs"])
        if self.stop_after == "ssd_p3":
            return
        b4_ = self.bank[4]
        BD = self.alloc([NB, 32], F32)
        S.add("dve", lambda e: e.tensor_tensor(
            out=BD[0:32], in0=fap(acsT, 127, [[128, NB]], pn=32).unsqueeze(2).to_broadcast([32, NB, 32]),
            in1=self.identS[0:32, 0:32].unsqueeze(1).to_broadcast([32, NB, 32]), op=ALU.mult), r=["acsT", "cf32"], w=["BD"])
        S.add("pe", lambda e: e.matmul(b4_[:, :], lhsT=L1[0:32, 0:128], rhs=fap(BD, 0, [[1, NB * 32]], pn=32),
                                       start=True, stop=True), r=["BD", "L1a"], w=["bank4"])
        if self.stop_after == "ssd_p4":
            return
        S.add("act", lambda e: e.activation(out=dec[:], in_=fap(b4_, 0, [[32, NB], [1, 32]]), func=AF.Exp), r=["bank4"], w=["dec"])
        if self.stop_after == "ssd_p5":
            return
        S.add("act", lambda e: e.activation(out=dtw[:], in_=fap(b4_, 0, [[32, NB], [1, 32]]), func=AF.Copy), r=["bank4"], w=["dtw"])
        S.add("pool", lambda e: e.tensor_tensor(out=dtw[:], in0=dtw[:], in1=acs[:], op=ALU.subtract), r=["dtw", "acs"], w=["dtw"])
        if self.stop_after == "ssd_p6":
            return
        S.add("act", lambda e: e.activation(out=dtw[:], in_=dtw[:], func=AF.Exp), r=["dtw"], w=["dtw"])
        S.add("dve", lambda e: e.tensor_tensor(out=dtw[:], in0=dtw[:], in1=dtT[:], op=ALU.mult), r=["dtw", "dtT"], w=["dtw"])
        if self.stop_after == "ssd_p7":
            return
        S.add("act", lambda e: e.activation(out=eacs[:], in_=acs[:], func=AF.Exp), r=["acs"], w=["eacs"])
        if self.stop_after == "ssd_pre":
            return
        for G in range(4):
            chunks = [(6656 + G * 512 + q * 128, G * 4 + q, ("x", q)) for q in range(4)]
            chunks += [(6656 + 2048 + G * 128, 16 + G, ("B", 0)), (6656 + 2560 + G * 128, 20 + G, ("C", 0))]
            for (c0, ci, (kind, q)) in chunks:
                slot, key = self.wslot([(0, [[128, KC], [1, 128]], self.win_cols(l, c0, 128))], "wx")

                def ev(tt, Bk, bk):
                    S.add("act", lambda e: e.activation(out=raw[:, 3 + tt * 512:3 + (tt + 1) * 512], in_=Bk[:, :], func=AF.Copy),
                          r=[bk], w=["raw"])
                S.add("act", lambda e, ci=ci: e.activation(out=raw[:, 0:3], in_=self.convhist[:, ci, :], func=AF.Copy),
                      r=["convhist"], w=["raw"])
                self.proj_tiles(slot, 0, 128, key, T // 512, [5, 6], ev)
                S.add("dve", lambda e, ci=ci: e.tensor_scalar(
                    out=acc[:], in0=raw[:, 3:3 + T], scalar1=self.convw[:, ci, 3:4], scalar2=self.convb[:, ci:ci + 1],
                    op0=ALU.mult, op1=ALU.add), r=["raw", "convw", "convb"], w=["acc"])
                for j in (2, 1, 0):
                    S.add("dve", lambda e, ci=ci, j=j: e.scalar_tensor_tensor(
                        out=acc[:], in0=raw[:, j:j + T], scalar=self.convw[:, ci, j:j + 1], in1=acc[:],
                        op0=ALU.mult, op1=ALU.add), r=["raw", "convw", "acc"], w=["acc"])
                dst = xTc[:, q, :] if kind == "x" else (BT[:] if kind == "B" else CT[:])
                dk = {"x": "xTc", "B": "BT", "C": "CT"}[kind]
                S.add("act", lambda e, dst=dst: e.activation(out=dst, in_=acc[:], func=AF.Silu), r=["acc"], w=[dk])
                S.add("act", lambda e, ci=ci: e.activation(out=self.convhist[:, ci, :], in_=raw[:, T:T + 3], func=AF.Copy),
                      r=["raw"], w=["convhist"])
            if self.stop_after == "ssd_conv":
                continue
            S.dma("sp", nssm[:], self.norm_ssm[l, G * 512:(G + 1) * 512].partition_broadcast(128), w=["nssm"], chan="misc")
            S.add("dve", lambda e, G=G: e.tensor_copy(
                out=R1[32:64], in_=self.identS[32:64, 8 * G:8 * G + 8].unsqueeze(2).to_broadcast([32, 8, 128])),
                r=["cf32"], w=["R1b"])
            Sg = self.Sst[:, G * 512:(G + 1) * 512]
            S.add("act", lambda e, Sg=Sg: e.activation(out=Sbf[:], in_=Sg, func=AF.Copy), r=["Sst"], w=["Sbf"])
            wz, wzk = self.wslot([(0, [[512, KC], [1, 512]], self.win_cols(l, 4608 + G * 512, 512))], "wz")
            for c in range(NB if self.stop_after != "ssd_c1" else 1):
                i = c % 2
                tk = lambda nm: ("%s" % nm) if nm in ("Lt", "MG", "sz", "t1", "yg") else "%s%d" % (nm, i)
                tok = slice(c * 128, (c + 1) * 128)
                hs = slice(8 * G, 8 * G + 8)
                trb = self.bank_bf[7]
                for q in range(4):
                    S.add("pe", lambda e, q=q, tok=tok: e.transpose(out=trb[:, q * 128:(q + 1) * 128], in_=xTc[:, q, tok],
                                                                    identity=self.ident_bf), r=["xTc", "cbf"], w=["bank7"])
                S.add("pe", lambda e, tok=tok: e.transpose(out=trb[:, 512:640], in_=BT[:, tok], identity=self.ident_bf),
                      r=["BT", "cbf"], w=["bank7"])
                S.add("act", lambda e, i=i: e.activation(out=Xtm[i][:], in_=trb[:, 0:512], func=AF.Copy), r=["bank7"], w=[tk("Xtm")])
                S.add("act", lambda e, i=i: e.activation(out=Btm[i][:], in_=trb[:, 512:640], func=AF.Copy), r=["bank7"], w=[tk("Btm")])
                bc = lambda t_, c=c, hs=hs: t_[:, c, hs].unsqueeze(2).to_broadcast([128, 8, 64])
                x3 = lambda t_: t_[:].rearrange("p (e q) -> p e q", e=8)
                S.add("pool", lambda e, i=i, bc=bc, x3=x3: e.tensor_tensor(out=x3(Xdt[i]), in0=x3(Xtm[i]), in1=bc(dtT), op=ALU.mult),
                      r=[tk("Xtm"), "dtT"], w=[tk("Xdt")])
                S.add("pool", lambda e, i=i, bc=bc, x3=x3: e.tensor_tensor(out=x3(Xw[i]), in0=x3(Xtm[i]), in1=bc(dtw), op=ALU.mult),
                      r=[tk("Xtm"), "dtw"], w=[tk("Xw")])
                S.add("dve", lambda e, i=i, x3=x3, hs=hs: e.tensor_tensor(
                    out=x3(XD[i]), in0=x3(Xtm[i]), in1=self.dsk_bc[:, hs].unsqueeze(2).to_broadcast([128, 8, 64]), op=ALU.mult),
                    r=[tk("Xtm"), "dsk_bc"], w=[tk("XD")])
                b0_ = self.bank[0]
                S.add("pe", lambda e, tok=tok: e.matmul(b0_[:, 0:128], lhsT=BT[:, tok], rhs=CT[:, tok], start=True, stop=True),
                      r=["BT", "CT"], w=["bank0"])
                S.add("dve", lambda e, i=i: e.tensor_tensor(out=CBm[i][:], in0=b0_[:, 0:128], in1=self.tri[:, :], op=ALU.mult),
                      r=["bank0", "cf32"], w=[tk("CBm")])
                S.add("dve", lambda e, tok=tok, G=G: e.tensor_tensor(
                    out=R1[0:32], in0=acsT[0:32, tok].unsqueeze(1).to_broadcast([32, 8, 128]),
                    in1=self.identS[0:32, 8 * G:8 * G + 8].unsqueeze(2).to_broadcast([32, 8, 128]), op=ALU.mult),
                    r=["acsT", "cf32"], w=["R1a"])
                for hh in range(2):
                    Bh = self.bank[1 + hh]
                    S.add("pe", lambda e, Bh=Bh: e.matmul(Bh[:, :], lhsT=self.ident_bf, rhs=self.negm4, start=True, stop=False),
                          r=["cbf"], w=["bank%d" % (1 + hh)])
                    S.add("pe", lambda e, Bh=Bh, hh=hh, tok=tok: e.matmul(
                        Bh[:, :], lhsT=L1[0:64, tok], rhs=fap(R1, hh * 512, [[1, 512]], pn=64), start=False, stop=True),
                        r=["L1a", "L1b", "R1a", "R1b"], w=["bank%d" % (1 + hh)])
                S.add("act", lambda e, i=i: e.activation(out=fap(Lt[i], 0, [[1, 1024]]), in_=self.psall[:, 512:1536], func=AF.Exp),
                      r=["bank1", "bank2"], w=[tk("Lt")])
                S.add("dve", lambda e, i=i: e.tensor_tensor(out=MG[i][:], in0=Lt[i][:],
                                                             in1=CBm[i][:].unsqueeze(1).to_broadcast([128, 8, 128]), op=ALU.mult),
                      r=[tk("Lt"), tk("CBm")], w=[tk("MG")])
                bY, bO, bC, bZ = self.bank[3], self.bank[4], self.bank[5], self.bank[6]
                S.add("pe", lambda e, i=i: e.matmul(bY[:, :], lhsT=self.ident_bf, rhs=XD[i][:], start=True, stop=False),
                      r=["cbf", tk("XD")], w=["bank3"])
                for hd in range(8):
                    S.add("pe", lambda e, i=i, hd=hd: e.matmul(
                        bY[:, hd * 64:(hd + 1) * 64], lhsT=MG[i][:, hd, :], rhs=Xdt[i][:, hd * 64:(hd + 1) * 64],
                        start=False, stop=(hd == 7), skip_group_check=True), r=[tk("MG"), tk("Xdt")], w=["bank3"])
                S.add("pe", lambda e, tok=tok: e.matmul(bO[:, :], lhsT=CT[:, tok], rhs=Sbf[:], start=True, stop=True),
                      r=["CT", "Sbf"], w=["bank4"])
                S.add("pe", lambda e, i=i: e.matmul(bC[:, :], lhsT=Btm[i][:], rhs=Xw[i][:], start=True, stop=True),
                      r=[tk("Btm"), tk("Xw")], w=["bank5"])
                for kc in range(KC):
                    S.add("pe", lambda e, kc=kc, tok=tok, wz=wz: e.matmul(
                        bZ[:, :], lhsT=_hT[:, kc, tok], rhs=fap(wz, kc * 512, [[1, 512]]), start=(kc == 0), stop=(kc == KC - 1)),
                        r=["hT", wzk], w=["bank6"])
                S.add("act", lambda e, i=i: e.activation(out=sz[i][:], in_=bZ[:, :], func=AF.Silu), r=["bank6"], w=[tk("sz")])
                S.add("dve", lambda e, i=i, bc=bc, x3=x3: e.tensor_tensor(
                    out=x3(t1[i]), in0=fap(bO, 0, [[64, 8], [1, 64]]), in1=bc(eacs), op=ALU.mult), r=["bank4", "eacs"], w=[tk("t1")])
                S.add("dve", lambda e, i=i: e.tensor_tensor(out=t1[i][:], in0=bY[:, :], in1=t1[i][:], op=ALU.add),
                      r=["bank3", tk("t1")], w=[tk("t1")])
                S.add("pool", lambda e, i=i: e.tensor_tensor(out=yg[i][:], in0=t1[i][:], in1=sz[i][:], op=ALU.mult),
                      r=[tk("t1"), tk("sz")], w=[tk("yg")])
                S.add("act", lambda e, i=i: e.activation(out=t1[i][:], in_=yg[i][:], func=AF.Square), r=[tk("yg")], w=[tk("t1")])
                S.add("dve", lambda e, i=i: e.reduce_sum(out=ss[:, 0:1], in_=t1[i][:], axis=AX.X), r=[tk("t1")], w=["ss"])
                S.add("dve", lambda e: e.tensor_scalar(out=ss[:, 1:2], in0=ss[:, 0:1], scalar1=1.0 / 512, scalar2=EPS,
                                                        op0=ALU.mult, op1=ALU.add), r=["ss"], w=["ss1"])
                S.add("act", lambda e: e.activation(out=ss[:, 2:3], in_=ss[:, 1:2], func=AF.Sqrt), r=["ss1"], w=["ss2"])
                S.add("dve", lambda e: e.reciprocal(out=ss[:, 3:4], in_=ss[:, 2:3]), r=["ss2"], w=["ss3"])
                S.add("dve", lambda e, i=i, G=G: e.scalar_tensor_tensor(
                    out=yn[i][:], in0=yg[i][:], scalar=ss[:, 3:4], in1=nssm[:],
                    op0=ALU.mult, op1=ALU.mult), r=[tk("yg"), "ss3", "nssm"], w=[tk("yn")])
                trb2 = self.bank_bf[0]
                for q in range(4):
                    S.add("pe", lambda e, i=i, q=q: e.transpose(out=trb2[:, 512 + q * 128:512 + (q + 1) * 128],
                                                                in_=yn[i][:, q * 128:(q + 1) * 128], identity=self.ident_bf),
                          r=[tk("yn"), "cbf"], w=["bank0"])
                S.add("act", lambda e, c=c: e.activation(out=fap(yT, c * 128, [[T, 4], [1, 128]]),
                                                         in_=fap(trb2, 512, [[128, 4], [1, 128]]), func=AF.Copy),
                      r=["bank0"], w=["yT"])
                S.add("dve", lambda e, Sg=Sg, bc=bc: e.tensor_tensor(
                    out=Sg.rearrange("p (e q) -> p e q", e=8), in0=Sg.rearrange("p (e q) -> p e q", e=8), in1=bc(dec),
                    op=ALU.mult), r=["Sst", "dec"], w=["Sst"])
                S.add("dve", lambda e, Sg=Sg: e.tensor_tensor(out=Sg, in0=bC[:, :], in1=Sg, op=ALU.add), r=["Sst", "bank5"], w=["Sst"])
                S.add("act", lambda e, Sg=Sg: e.activation(out=Sbf[:], in_=Sg, func=AF.Copy), r=["Sst"], w=["Sbf"])
            S.dma("sp", self.yscr[4 * G:4 * G + 4].rearrange("c p t -> p c t"), yT[:], r=["yT"], w=["yscr"], chan="yscr")
        if last_grp:
            S.dma("sp", self.ssm_out[l], self.Sst[:], r=["Sst"], w=["ssmout"], chan="stout")
            S.dma("sp", self.conv_out[l], self.convhist[:], r=["convhist"], w=["convout"], chan="stout")
            if "stout" not in self.out_chans:
                self.out_chans.append("stout")

    def tail(self, l, g0):
        _hT = self.hT
        _sqf = self.sqf
        _attoT = self.attoT
        S = self.S
        self.norm_bufs()
        ysT = self.alloc([16, 512], BF16)
        mT = self.alloc([KC, 512], BF16)
        sga = self.alloc([512], F32)
        sgs = self.alloc([512], F32)
        ta = self.alloc([512], F32)
        tsb = self.alloc([512], F32)
        oT = self.alloc([KC, 512], F32)
        wa = self.w_br_att[l].rearrange("(k p) n -> p k n", p=128)
        ws = self.w_br_ssm[l].rearrange("(k p) n -> p k n", p=128)
        wo = self.w_out[l].rearrange("(k p) n -> p k n", p=128)
        for tt in range(T // 512):
            tsl = slice(tt * 512, (tt + 1) * 512)
            S.dma("sp", ysT[:], self.yscr[:, :, tsl].rearrange("c p t -> p c t"), r=["yscr"], w=["ysT"], chan="ysT")
            for m in range(KC):
                ms = slice(m * 128, (m + 1) * 128)
                s1, k1 = self.wslot([(0, [[128, 4], [1, 128]], wa[:, :, ms]),
                                     (512, [[128, KC], [1, 128]], self.win_cols(l, 9760 + m * 128, 128)),
                                     (1536, [[128, KC], [1, 128]], self.win_cols(l, 10784 + m * 128, 128))], "wt1")
                s2, k2 = self.wslot([(0, [[128, 16], [1, 128]], ws[:, :, ms])], "wt2")
                pa_, ps_, pga, pgs = self.bank[0], self.bank[1], self.bank[2], self.bank[3]
                for kk in range(4):
                    S.add("pe", lambda e, kk=kk, tsl=tsl, s1=s1: e.matmul(pa_[:, :], lhsT=fap(s1, kk * 128, [[1, 128]]),
                                                                   rhs=_attoT[:, kk, tsl], start=(kk == 0), stop=(kk == 3)),
                          r=[k1, "attoT"], w=["bank0"])
                for kk in range(16):
                    S.add("pe", lambda e, kk=kk, s2=s2: e.matmul(ps_[:, :], lhsT=fap(s2, kk * 128, [[1, 128]]), rhs=ysT[:, kk, :],
                                                          start=(kk == 0), stop=(kk == 15)), r=[k2, "ysT"], w=["bank1"])
                for kc in range(KC):
                    S.add("pe", lambda e, kc=kc, tsl=tsl, s1=s1: e.matmul(pga[:, :], lhsT=fap(s1, 512 + kc * 128, [[1, 128]]),
                                                                   rhs=_hT[:, kc, tsl], start=(kc == 0), stop=(kc == KC - 1)),
                          r=[k1, "hT"], w=["bank2"])
                for kc in range(KC):
                    S.add("pe", lambda e, kc=kc, tsl=tsl, s1=s1: e.matmul(pgs[:, :], lhsT=fap(s1, 1536 + kc * 128, [[1, 128]]),
                                                                   rhs=_hT[:, kc, tsl], start=(kc == 0), stop=(kc == KC - 1)),
                          r=[k1, "hT"], w=["bank3"])
                S.add("act", lambda e: e.activation(out=sga[:], in_=pga[:, :], func=AF.Sigmoid), r=["bank2"], w=["sga"])
                S.add("act", lambda e: e.activation(out=sgs[:], in_=pgs[:, :], func=AF.Sigmoid), r=["bank3"], w=["sgs"])
                S.add("dve", lambda e: e.tensor_tensor(out=ta[:], in0=pa_[:, :], in1=sga[:], op=ALU.mult), r=["bank0", "sga"], w=["ta"])
                S.add("dve", lambda e: e.tensor_tensor(out=tsb[:], in0=ps_[:, :], in1=sgs[:], op=ALU.mult), r=["bank1", "sgs"], w=["tsb"])
                S.add("pool", lambda e, m=m: e.tensor_tensor(out=mT[:, m, :], in0=ta[:], in1=tsb[:], op=ALU.add),
                      r=["ta", "tsb"], w=["mT"])
            for m2 in range(KC):
                s3, k3 = self.wslot([(0, [[128, KC], [1, 128]], wo[:, :, m2 * 128:(m2 + 1) * 128])], "wo")
                bi = 4 + m2 % 2
                Bo = self.bank[bi]
                for kc in range(KC):
                    S.add("pe", lambda e, kc=kc, Bo=Bo, s3=s3: e.matmul(Bo[:, :], lhsT=fap(s3, kc * 128, [[1, 128]]), rhs=mT[:, kc, :],
                                                                 start=(kc == 0), stop=(kc == KC - 1)), r=[k3, "mT"], w=["bank%d" % bi])
                S.add("act", lambda e, Bo=Bo, m2=m2: e.activation(out=oT[:, m2, :], in_=Bo[:, :], func=AF.Copy), r=["bank%d" % bi], w=["oT"])
                S.add("act", lambda e, Bo=Bo, m2=m2: e.activation(out=_sqf[:, m2, :], in_=Bo[:, :], func=AF.Square),
                      r=["bank%d" % bi], w=["sqo"])
            self.postnorm_residual(oT[:], lambda c: _sqf[:, c, :], 1, self.xres, self.xres, g0 + tt * 512, ["oT", "sqo"])

    def s_declare(self):
        L = self.depth
        self.xsT_in = self.din("xsT_in", [128, KC, NS])
        self.scst_in = self.din("scst_in", [128, 1172])
        self.rope_s = self.din("rope_s", [NS, 256])
        self.cache = [self.din("cache%d" % g, [L, NS, WIN[g], 1024]) for g in range(3)]
        self.st_ssm = self.din("st_ssm", [L, NS, 2048, 128])
        self.st_conv = self.din("st_conv", [L, NS, 3, 3072])
        self.convw_rep = self.din("convw_rep", [L, NS, 4, 3072])
        self.convb_rep = self.din("convb_rep", [L, NS, 3072])
        self.dtb_rep = self.din("dtb_rep", [L, NS, 32])
        self.alog_rep = self.din("alog_rep", [L, NS, 32])
        self.dsk_rep = self.din("dsk_rep", [L, NS, 32])
        self.nssm_rep = self.din("nssm_rep", [L, NS, 2048])
        self.ys_out = self.dout("ys_out", [128, KC, NS])
        self.kvs_out = [self.dout("kvs_out%d" % g, [L, NS, 2, 512]) for g in range(3)]
        self.ssms_out = self.dout("ssms_out", [L, NS, 2048, 128])
        self.convs_out = self.dout("convs_out", [L, NS, 3, 3072])

    def s_consts(self):
        S = self.S
        self.xsT = S.sb("xsT", [128, KC, NS], F32)
        S.dma("sp", self.xsT[:], self.xsT_in, w=["xsT"], chan="misc")
        self.ones_f = S.sb("ones_f", [128, 4], F32)
        S.add("pool", lambda e: e.memset(self.ones_f[:], 1.0), w=["ones_f"])

    def s_load_consts(self):
        S = self.S
        self.scst = self.alloc([1172], F32)
        S.dma("sp", self.scst, self.scst_in, w=["scst"], chan="s_ld")
        self.identF = self.scst[:, 0:128]
        self.selB = lambda b: self.scst[0:NS, 128 + b * 128:128 + (b + 1) * 128]
        self.selcol = lambda b: self.scst[0:4, 640 + b * NS:640 + (b + 1) * NS]
        self.I4 = self.scst[0:4, 656:660]
        self.bdmask = self.scst[0:4, 660:1172]
        self.ropes = self.alloc([256], F32)
        S.dma("sp", self.ropes[0:NS, :], self.rope_s, w=["ropes"], chan="s_ld")

    def s_rstd(self, ps_ap, denom, rs, rkey):
        S = self.S
        S.add("act", lambda e: e.activation(out=rs, in_=ps_ap, func=AF.Copy), r=[rkey], w=["s_rs"])
        S.add("dve", lambda e: e.tensor_scalar(out=rs, in0=rs, scalar1=1.0 / denom, scalar2=EPS, op0=ALU.mult, op1=ALU.add),
              r=["s_rs"], w=["s_rs"])
        S.add("act", lambda e: e.activation(out=rs, in_=rs, func=AF.Sqrt), r=["s_rs"], w=["s_rs"])
        S.add("dve", lambda e: e.reciprocal(out=rs, in_=rs), r=["s_rs"], w=["s_rs"])

    def s_norm_stat(self, src3, srckeys):
        S = self.S
        sq = self.alloc([KC * NS], BF16)
        rs = self.alloc([NS], F32)
        st = self.bank[6]
        S.add("act", lambda e: e.activation(out=sq.rearrange("p (c n) -> p c n", n=NS), in_=src3, func=AF.Square),
              r=list(srckeys), w=["s_sq"])
        for c in range(KC):
            S.add("pe", lambda e, c=c: e.matmul(st[:, 0:NS], lhsT=self.ones_bf[:, :], rhs=sq[:, c * NS:(c + 1) * NS],
                                                  start=(c == 0), stop=(c == KC - 1)), r=["s_sq", "ones_bf"], w=["bank6"])
        self.s_rstd(st[:, 0:NS], D, rs, "bank6")
        return rs

    def s_prenorm(self, k):
        S = self.S
        rs = self.s_norm_stat(self.xsT[:], ["xsT"])
        tmp = self.alloc([KC * NS], F32)
        tmp3 = tmp.rearrange("p (c n) -> p c n", n=NS)
        hs = self.alloc([KC * NS], BF16)
        hs3 = hs.rearrange("p (c n) -> p c n", n=NS)
        S.add("dve", lambda e: e.tensor_tensor(out=tmp3, in0=self.xsT[:], in1=rs.unsqueeze(1).to_broadcast([128, KC, NS]),
                                                 op=ALU.mult), r=["xsT", "s_rs"], w=["s_tmp"])
        S.add("dve", lambda e: e.tensor_tensor(out=tmp3, in0=tmp3, in1=self.Aall[:, k, :, 1:NCOL], op=ALU.mult),
              r=["s_tmp", "Aall"], w=["s_tmp"])
        S.add("pool", lambda e: e.tensor_tensor(out=hs3, in0=tmp3, in1=self.modT[:, k * 24:k * 24 + 8, 1:NCOL], op=ALU.add),
              r=["s_tmp", "modT"], w=["s_hs"])
        return hs3

    def s_postnorm(self, f3, fkeys, k):
        S = self.S
        rs = self.s_norm_stat(f3, fkeys)
        tmp = self.alloc([KC * NS], F32)
        tmp3 = tmp.rearrange("p (c n) -> p c n", n=NS)
        S.add("dve", lambda e: e.tensor_tensor(out=tmp3, in0=f3, in1=rs.unsqueeze(1).to_broadcast([128, KC, NS]),
                                                 op=ALU.mult), r=list(fkeys) + ["s_rs"], w=["s_tmp2"])
        S.add("dve", lambda e: e.tensor_tensor(out=tmp3, in0=tmp3, in1=self.Gall[:, k, :, 1:NCOL], op=ALU.mult),
              r=["s_tmp2", "Gall"], w=["s_tmp2"])
        S.add("pool", lambda e: e.tensor_tensor(out=self.xsT[:], in0=self.xsT[:], in1=tmp3, op=ALU.add),
              r=["s_tmp2", "xsT"], w=["xsT"])

    def s_ffn(self, l, i, k):
        S = self.S
        self.phase(0)
        hs3 = self.s_prenorm(k)
        w1 = self.w_ff_in[l, i].rearrange("(kc p) n -> p kc n", p=128)
        w2 = self.w_ff_out[l, i].rearrange("(j p) n -> p j n", p=128)
        bu = self.bank[0]
        for jb in range(NJ // 2):
            slot, key = self.wslot([
                (0, [[256, KC], [1, 256]], w1[:, :, jb * 256:(jb + 1) * 256]),
                (2048, [[256, KC], [1, 256]], w1[:, :, DFF + jb * 256:DFF + (jb + 1) * 256])], "w1s")
            for jj in range(2):
                j = jb * 2 + jj
                for half in range(2):
                    col = (half * NJ + j) * NS
                    for kc in range(KC):
                        S.add("pe", lambda e, kc=kc, jj=jj, half=half, col=col, slot=slot: e.matmul(
                            bu[:, col:col + NS], lhsT=fap(slot, half * 2048 + kc * 256 + jj * 128, [[1, 128]]),
                            rhs=hs3[:, kc, :], start=(kc == 0), stop=(kc == KC - 1)), r=[key, "s_hs"], w=["bank0"])
        sg = self.alloc([NJ * NS], F32)
        up = self.alloc([NJ * NS], F32)
        aT = self.alloc([NJ * NS], BF16)
        S.add("act", lambda e: e.activation(out=sg, in_=bu[:, 0:NJ * NS], func=AF.Silu), r=["bank0"], w=["s_sg"])
        S.add("act", lambda e: e.activation(out=up, in_=bu[:, NJ * NS:2 * NJ * NS], func=AF.Copy), r=["bank0"], w=["s_up"])
        S.add("dve", lambda e: e.tensor_tensor(out=aT, in0=sg, in1=up, op=ALU.mult), r=["s_sg", "s_up"], w=["s_aT"])
        bf_ = self.bank[1]
        for m in range(KC):
            slot, key = self.wslot([(0, [[128, NJ], [1, 128]], w2[:, :, m * 128:(m + 1) * 128])], "w2s")
            for j in range(NJ):
                S.add("pe", lambda e, j=j, m=m, slot=slot: e.matmul(
                    bf_[:, m * NS:(m + 1) * NS], lhsT=fap(slot, j * 128, [[1, 128]]), rhs=aT[:, j * NS:(j + 1) * NS],
                    start=(j == 0), stop=(j == NJ - 1)), r=[key, "s_aT"], w=["bank1"])
        fT = self.alloc([KC * NS], F32)
        S.add("act", lambda e: e.activation(out=fT, in_=bf_[:, 0:KC * NS], func=AF.Copy), r=["bank1"], w=["s_fT"])
        self.s_postnorm(fT.rearrange("p (c n) -> p c n", n=NS), ["s_fT"], k)

    def s_transpose_to_fm(self, tok_ap, nch, out_tile, bank_i, rkeys, okey):
        S = self.S
        Bk = self.bank[bank_i]
        bk = "bank%d" % bank_i
        for c in range(nch):
            S.add("pe", lambda e, c=c: e.matmul(Bk[:, c * NS:(c + 1) * NS], lhsT=tok_ap[:, c * 128:(c + 1) * 128],
                                                  rhs=self.I4[0:NS, 0:NS], start=True, stop=True),
                  r=list(rkeys) + ["scst"], w=[bk])
        S.add("act", lambda e: e.activation(out=out_tile, in_=Bk[:, 0:nch * NS], func=AF.Copy), r=[bk], w=[okey])

    def s_mixer(self, l):
        S = self.S
        self.phase(0)
        hs3 = self.s_prenorm(1)
        U = self.alloc([INC], F32)
        Us = lambda a, n: U[0:NS, a:a + n]
        abr = self.alloc([D], F32)
        self.s_load_consts()
        mark = self.aptr
        nblk = (INC + 511) // 512
        for blk in range(nblk):
            c0 = blk * 512
            n = min(512, INC - c0)
            slot, key = self.wslot([(0, [[n, KC], [1, n]], self.win_cols(l, c0, n))], "wins")
            bi = blk % 2
            Bk = self.bank[bi]
            bk = "bank%d" % bi
            for kc in range(KC):
                S.add("pe", lambda e, kc=kc, n=n, Bk=Bk, slot=slot: e.matmul(
                    Bk[0:NS, 0:n], lhsT=hs3[:, kc, :], rhs=fap(slot, kc * n, [[1, n]]), start=(kc == 0), stop=(kc == KC - 1)),
                    r=[key, "s_hs"], w=[bk])
            S.add("act", lambda e, c0=c0, n=n, Bk=Bk: e.activation(out=Us(c0, n), in_=Bk[0:NS, 0:n], func=AF.Copy),
                  r=[bk], w=["U"])
        QK = self.alloc([3 * 1024], F32)
        t1 = self.alloc([1024], F32)
        t2 = self.alloc([1024], F32)
        v3 = lambda ap, d=128: ap.rearrange("p (a d) -> p a d", d=d)
        cosF = self.ropes[0:NS, 0:128]
        sinS = self.ropes[0:NS, 128:256]
        for g in range(3):
            x = v3(Us(g * 1536, 1024))
            S.add("dve", lambda e, x=x: e.tensor_tensor(out=v3(t1[0:NS, :]), in0=x, in1=cosF.unsqueeze(1).to_broadcast([NS, 8, 128]),
                                                         op=ALU.mult), r=["U", "ropes"], w=["s_t1"])
            S.add("pool", lambda e, x=x: e.tensor_tensor(out=v3(t2[0:NS, :])[:, :, 0:64], in0=x[:, :, 64:128],
                                                          in1=sinS[:, 0:64].unsqueeze(1).to_broadcast([NS, 8, 64]), op=ALU.mult),
                  r=["U", "ropes"], w=["s_t2a"])
            S.add("pool", lambda e, x=x: e.tensor_tensor(out=v3(t2[0:NS, :])[:, :, 64:128], in0=x[:, :, 0:64],
                                                          in1=sinS[:, 64:128].unsqueeze(1).to_broadcast([NS, 8, 64]), op=ALU.mult),
                  r=["U", "ropes"], w=["s_t2b"])
            S.add("dve", lambda e, g=g: e.tensor_tensor(out=QK[0:NS, g * 1024:(g + 1) * 1024], in0=t1[0:NS, :], in1=t2[0:NS, :],
                                                         op=ALU.add), r=["s_t1", "s_t2a", "s_t2b"], w=["QK"])
            S.dma("sp", self.kvs_out[g][l, :, 0, :], QK[0:NS, g * 1024 + 512:(g + 1) * 1024], r=["QK"], w=["kvs_o"], chan="s_out")
            S.dma("sp", self.kvs_out[g][l, :, 1, :], Us(g * 1536 + 1024, 512), r=["U"], w=["kvs_o"], chan="s_out")
        if "s_out" not in self.out_chans:
            self.out_chans.append("s_out")
        P0 = self.alloc([512], F32)
        s0 = self.alloc([12], F32)
        p0 = self.alloc([12], F32)
        N0 = self.alloc([512], F32)
        D0 = self.alloc([4], F32)
        for g in range(3):
            S.add("pool", lambda e, g=g: e.tensor_tensor(out=P0[0:NS, :], in0=QK[0:NS, g * 1024:g * 1024 + 512],
                                                          in1=QK[0:NS, g * 1024 + 512:(g + 1) * 1024], op=ALU.mult),
                  r=["QK"], w=["s_P0"])
            S.add("dve", lambda e, g=g: e.reduce_sum(out=s0[0:NS, g * 4:(g + 1) * 4], in_=v3(P0[0:NS, :]), axis=AX.X),
                  r=["s_P0"], w=["s_s0"])
        S.add("act", lambda e: e.activation(out=p0[0:NS, :], in_=s0[0:NS, :], func=AF.Exp, scale=float(SCALE)), r=["s_s0"], w=["s_p0"])
        for g in range(3):
            vg = v3(Us(g * 1536 + 1024, 512))
            pb = p0[0:NS, g * 4:(g + 1) * 4].unsqueeze(2).to_broadcast([NS, 4, 128])
            if g == 0:
                S.add("dve", lambda e, vg=vg, pb=pb: e.tensor_tensor(out=v3(N0[0:NS, :]), in0=vg, in1=pb, op=ALU.mult),
                      r=["U", "s_p0"], w=["s_N0"])
            else:
                S.add("pool", lambda e, vg=vg, pb=pb: e.tensor_tensor(out=v3(P0[0:NS, :]), in0=vg, in1=pb, op=ALU.mult),
                      r=["U", "s_p0"], w=["s_P0"])
                S.add("dve", lambda e: e.tensor_tensor(out=N0[0:NS, :], in0=N0[0:NS, :], in1=P0[0:NS, :], op=ALU.add),
                      r=["s_N0", "s_P0"], w=["s_N0"])
        S.add("dve", lambda e: e.tensor_tensor(out=D0[0:NS, :], in0=p0[0:NS, 0:4], in1=p0[0:NS, 4:8], op=ALU.add), r=["s_p0"], w=["s_D0"])
        S.add("dve", lambda e: e.tensor_tensor(out=D0[0:NS, :], in0=D0[0:NS, :], in1=p0[0:NS, 8:12], op=ALU.add), r=["s_p0", "s_D0"], w=["s_D0"])
        KV = [self.alloc([1024], F32) for i in range(2)]
        prod = self.alloc([512], F32)
        sc = self.alloc([4], F32)
        pp = [self.alloc([4], F32) for i in range(2)]
        numS = self.alloc([NS * 512], F32)
        denS = self.alloc([NS], F32)
        cnt = 0
        for b in range(NS):
            nb_, db_ = 4 + 2 * (b % 2), 5 + 2 * (b % 2)
            numP, denP = self.bank[nb_], self.bank[db_]
            for g in range(3):
                i = cnt % 2
                cnt += 1
                kv = KV[i]
                kvk = "s_KV%d" % i
                src = self.cache[g][l, b].rearrange("(j d) c -> j d c", d=DIL[g])[:, 0, :]
                S.dma("sp", kv, src, w=[kvk], chan=kvk)
                qi = 2 + i
                qP = self.bank[qi]
                S.add("pe", lambda e, b=b, g=g, qP=qP: e.matmul(qP[:, :], lhsT=self.selB(b), rhs=QK[0:NS, g * 1024:g * 1024 + 512],
                                                                start=True, stop=True), r=["QK", "scst"], w=["bank%d" % qi])
                S.add("dve", lambda e, kv=kv, qP=qP: e.tensor_tensor(out=prod, in0=kv[:, 0:512], in1=qP[:, :], op=ALU.mult),
                      r=[kvk, "bank%d" % qi], w=["s_prod"])
                S.add("dve", lambda e: e.reduce_sum(out=sc, in_=v3(prod), axis=AX.X), r=["s_prod"], w=["s_sc"])
                p_ = pp[i]
                pk = "s_pp%d" % i
                S.add("act", lambda e, p_=p_: e.activation(out=p_, in_=sc, func=AF.Exp, scale=float(SCALE)), r=["s_sc"], w=[pk])
                S.add("pe", lambda e, p_=p_, kv=kv, g=g, numP=numP: e.matmul(numP[0:4, :], lhsT=p_, rhs=kv[:, 512:1024],
                                                                            start=(g == 0), stop=(g == 2)),
                      r=[pk, kvk], w=["bank%d" % nb_])
                S.add("pe", lambda e, p_=p_, g=g, denP=denP: e.matmul(denP[0:4, 0:1], lhsT=p_, rhs=self.ones_f[:, 0:1],
                                                                     start=(g == 0), stop=(g == 2)),
                      r=[pk, "ones_f"], w=["bank%d" % db_])
            S.add("act", lambda e, b=b, numP=numP: e.activation(out=numS[0:4, b * 512:(b + 1) * 512], in_=numP[0:4, :], func=AF.Copy),
                  r=["bank%d" % nb_], w=["s_numS"])
            S.add("act", lambda e, b=b, denP=denP: e.activation(out=denS[0:4, b:b + 1], in_=denP[0:4, 0:1], func=AF.Copy),
                  r=["bank%d" % db_], w=["s_denS"])
        S.add("pool", lambda e: e.tensor_tensor(out=numS[0:4, :].rearrange("p (b n) -> p b n", n=512),
                                                 in0=numS[0:4, :].rearrange("p (b n) -> p b n", n=512),
                                                 in1=self.bdmask.unsqueeze(1).to_broadcast([4, NS, 512]), op=ALU.mult),
              r=["s_numS", "scst"], w=["s_numS"])
        ncP, dcP = self.bank[0], self.bank[1]
        for b in range(NS):
            S.add("pe", lambda e, b=b: e.matmul(ncP[0:NS, :], lhsT=self.selcol(b), rhs=numS[0:4, b * 512:(b + 1) * 512],
                                                  start=(b == 0), stop=(b == NS - 1)), r=["s_numS", "scst"], w=["bank0"])
        S.add("pe", lambda e: e.matmul(dcP[0:NS, 0:4], lhsT=denS[0:4, 0:NS], rhs=self.I4, start=True, stop=True),
              r=["s_denS", "scst"], w=["bank1"])
        Dt = self.alloc([4], F32)
        Nt = self.alloc([512], F32)
        atto = self.alloc([512], F32)
        S.add("act", lambda e: e.activation(out=Dt[0:NS, :], in_=dcP[0:NS, 0:4], func=AF.Copy), r=["bank1"], w=["s_Dt"])
        S.add("dve", lambda e: e.tensor_tensor(out=Dt[0:NS, :], in0=Dt[0:NS, :], in1=D0[0:NS, :], op=ALU.add), r=["s_Dt", "s_D0"], w=["s_Dt"])
        S.add("dve", lambda e: e.reciprocal(out=Dt[0:NS, :], in_=Dt[0:NS, :]), r=["s_Dt"], w=["s_Dt"])
        S.add("act", lambda e: e.activation(out=Nt[0:NS, :], in_=ncP[0:NS, :], func=AF.Copy), r=["bank0"], w=["s_Nt"])
        S.add("dve", lambda e: e.tensor_tensor(out=Nt[0:NS, :], in0=Nt[0:NS, :], in1=N0[0:NS, :], op=ALU.add), r=["s_Nt", "s_N0"], w=["s_Nt"])
        S.add("dve", lambda e: e.tensor_tensor(out=v3(atto[0:NS, :]), in0=v3(Nt[0:NS, :]),
                                                 in1=Dt[0:NS, :].unsqueeze(2).to_broadcast([NS, 4, 128]), op=ALU.mult),
              r=["s_Nt", "s_Dt"], w=["s_atto"])
        attoT = self.alloc([4 * NS], BF16)
        self.s_transpose_to_fm(atto[0:NS, :], 4, attoT, 2, ["s_atto"], "s_attoT")
        wa = self.w_br_att[l].rearrange("(k p) n -> p k n", p=128)
        for cb in range(2):
            slot, key = self.wslot([(0, [[512, 4], [1, 512]], wa[:, :, cb * 512:(cb + 1) * 512])], "was")
            Bk = self.bank[3]
            for kk in range(4):
                S.add("pe", lambda e, kk=kk, slot=slot: e.matmul(Bk[0:NS, :], lhsT=attoT[:, kk * NS:(kk + 1) * NS],
                                                                 rhs=fap(slot, kk * 512, [[1, 512]]), start=(kk == 0), stop=(kk == 3)),
                      r=[key, "s_attoT"], w=["bank3"])
            S.add("act", lambda e, cb=cb: e.activation(out=abr[0:NS, cb * 512:(cb + 1) * 512], in_=Bk[0:NS, :], func=AF.Copy),
                  r=["bank3"], w=["s_abr"])
        self.phase(mark)
        xc = self.alloc([3072], F32)
        cw = self.alloc([4 * 512], F32)
        cs = self.alloc([3 * 512], F32)
        cbi = self.alloc([512], F32)
        acc = self.alloc([512], F32)
        ct = self.alloc([512], F32)
        for q in range(6):
            c0 = q * 512
            S.dma("sp", cw[0:NS, :].rearrange("p (j n) -> p j n", n=512), self.convw_rep[l, :, :, c0:c0 + 512], w=["s_cw"], chan="s_ld")
            S.dma("sp", cs[0:NS, :].rearrange("p (j n) -> p j n", n=512), self.st_conv[l, :, :, c0:c0 + 512], w=["s_cs"], chan="s_ld")
            S.dma("sp", cbi[0:NS, :], self.convb_rep[l, :, c0:c0 + 512], w=["s_cb"], chan="s_ld")
            S.dma("sp", self.convs_out[l, :, 0:2, c0:c0 + 512], cs[0:NS, 512:1536].rearrange("p (j n) -> p j n", n=512),
                  r=["s_cs"], w=["convs_o"], chan="s_out")
            S.dma("sp", self.convs_out[l, :, 2, c0:c0 + 512], Us(6656 + c0, 512), r=["U"], w=["convs_o"], chan="s_out")
            S.add("dve", lambda e, c0=c0: e.tensor_tensor(out=acc[0:NS, :], in0=Us(6656 + c0, 512), in1=cw[0:NS, 1536:2048], op=ALU.mult),
                  r=["U", "s_cw"], w=["s_acc"])
            S.add("dve", lambda e: e.tensor_tensor(out=acc[0:NS, :], in0=acc[0:NS, :], in1=cbi[0:NS, :], op=ALU.add),
                  r=["s_acc", "s_cb"], w=["s_acc"])
            for j in range(3):
                S.add("pool", lambda e, j=j: e.tensor_tensor(out=ct[0:NS, :], in0=cs[0:NS, j * 512:(j + 1) * 512],
                                                              in1=cw[0:NS, j * 512:(j + 1) * 512], op=ALU.mult),
                      r=["s_cs", "s_cw"], w=["s_ct"])
                S.add("dve", lambda e: e.tensor_tensor(out=acc[0:NS, :], in0=acc[0:NS, :], in1=ct[0:NS, :], op=ALU.add),
                      r=["s_acc", "s_ct"], w=["s_acc"])
            S.add("act", lambda e, c0=c0: e.activation(out=xc[0:NS, c0:c0 + 512], in_=acc[0:NS, :], func=AF.Silu), r=["s_acc"], w=["s_xc"])
        dt = self.alloc([32], F32)
        av = self.alloc([32], F32)
        dA = self.alloc([32], F32)
        dsk = self.alloc([32], F32)
        S.dma("sp", dt[0:NS, :], self.dtb_rep[l], w=["s_dt"], chan="s_ld")
        S.dma("sp", av[0:NS, :], self.alog_rep[l], w=["s_av"], chan="s_ld")
        S.dma("sp", dsk[0:NS, :], self.dsk_rep[l], w=["s_dsk"], chan="s_ld")
        S.add("dve", lambda e: e.tensor_tensor(out=dt[0:NS, :], in0=dt[0:NS, :], in1=Us(9728, 32), op=ALU.add), r=["s_dt", "U"], w=["s_dt"])
        S.add("act", lambda e: e.activation(out=dt[0:NS, :], in_=dt[0:NS, :], func=AF.Exp), r=["s_dt"], w=["s_dt"])
        S.add("act", lambda e: e.activation(out=dt[0:NS, :], in_=dt[0:NS, :], func=AF.Ln, bias=1.0), r=["s_dt"], w=["s_dt"])
        S.add("act", lambda e: e.activation(out=av[0:NS, :], in_=av[0:NS, :], func=AF.Exp), r=["s_av"], w=["s_av"])
        S.add("dve", lambda e: e.tensor_tensor(out=dA[0:NS, :], in0=dt[0:NS, :], in1=av[0:NS, :], op=ALU.mult), r=["s_dt", "s_av"], w=["s_dA"])
        S.add("act", lambda e: e.activation(out=dA[0:NS, :], in_=dA[0:NS, :], func=AF.Exp, scale=-1.0), r=["s_dA"], w=["s_dA"])
        xdt = self.alloc([2048], F32)
        dAe = self.alloc([2048], F32)
        v64 = lambda ap: ap.rearrange("p (a d) -> p a d", d=64)
        S.add("dve", lambda e: e.tensor_tensor(out=v64(xdt[0:NS, :]), in0=v64(xc[0:NS, 0:2048]),
                                                 in1=dt[0:NS, :].unsqueeze(2).to_broadcast([NS, 32, 64]), op=ALU.mult),
              r=["s_xc", "s_dt"], w=["s_xdt"])
        S.add("pool", lambda e: e.tensor_copy(out=v64(dAe[0:NS, :]), in_=dA[0:NS, :].unsqueeze(2).to_broadcast([NS, 32, 64])),
              r=["s_dA"], w=["s_dAe"])
        xdtT = self.alloc([16 * NS], F32)
        dAT = self.alloc([16 * NS], F32)
        self.s_transpose_to_fm(xdt[0:NS, :], 16, xdtT, 2, ["s_xdt"], "s_xdtT")
        self.s_transpose_to_fm(dAe[0:NS, :], 16, dAT, 3, ["s_dAe"], "s_dAT")
        Hb = [self.alloc([2048], F32) for i in range(2)]
        outer = self.alloc([2048], F32)
        Bbc = self.alloc([512], F32)
        Cbc = self.alloc([512], F32)
        yTa = self.alloc([NS * 16], F32)
        xdT3 = xdtT.rearrange("p (c n) -> p c n", n=NS)
        dAT3 = dAT.rearrange("p (c n) -> p c n", n=NS)
        for b in range(NS):
            H = Hb[b % 2]
            hk = "s_H%d" % (b % 2)
            H3 = v3(H)
            S.dma("sp", H3, self.st_ssm[l, b].rearrange("(c p) n -> p c n", p=128), w=[hk], chan=hk)
            S.add("pe", lambda e, b=b: e.matmul(self.bank[4][:, :], lhsT=self.selB(b), rhs=xc[0:NS, 2048:2560], start=True, stop=True),
                  r=["s_xc", "scst"], w=["bank4"])
            S.add("pe", lambda e, b=b: e.matmul(self.bank[5][:, :], lhsT=self.selB(b), rhs=xc[0:NS, 2560:3072], start=True, stop=True),
                  r=["s_xc", "scst"], w=["bank5"])
            S.add("act", lambda e: e.activation(out=Bbc, in_=self.bank[4][:, :], func=AF.Copy), r=["bank4"], w=["s_Bbc"])
            S.add("act", lambda e: e.activation(out=Cbc, in_=self.bank[5][:, :], func=AF.Copy), r=["bank5"], w=["s_Cbc"])
            S.add("dve", lambda e, b=b, H3=H3: e.tensor_tensor(out=H3, in0=H3, in1=dAT3[:, :, b:b + 1].to_broadcast([128, 16, 128]),
                                                                op=ALU.mult), r=[hk, "s_dAT"], w=[hk])
            for G in range(4):
                S.add("pool", lambda e, b=b, G=G: e.tensor_tensor(
                    out=v3(outer)[:, 4 * G:4 * G + 4, :], in0=xdT3[:, 4 * G:4 * G + 4, b:b + 1].to_broadcast([128, 4, 128]),
                    in1=Bbc[:, G * 128:(G + 1) * 128].unsqueeze(1).to_broadcast([128, 4, 128]), op=ALU.mult),
                    r=["s_xdtT", "s_Bbc"], w=["s_outer"])
            S.add("dve", lambda e, H=H: e.tensor_tensor(out=H, in0=H, in1=outer, op=ALU.add), r=[hk, "s_outer"], w=[hk])
            S.dma("sp", self.ssms_out[l, b].rearrange("(c p) n -> p c n", p=128), H3, r=[hk], w=["ssms_o"], chan="s_out")
            for G in range(4):
                S.add("pool", lambda e, G=G, H3=H3: e.tensor_tensor(
                    out=v3(outer)[:, 4 * G:4 * G + 4, :], in0=H3[:, 4 * G:4 * G + 4, :],
                    in1=Cbc[:, G * 128:(G + 1) * 128].unsqueeze(1).to_broadcast([128, 4, 128]), op=ALU.mult),
                    r=[hk, "s_Cbc"], w=["s_outer"])
            S.add("dve", lambda e, b=b: e.reduce_sum(out=yTa[:, b * 16:(b + 1) * 16], in_=v3(outer), axis=AX.X),
                  r=["s_outer"], w=["s_yTa"])
        ytok = self.alloc([2048], F32)
        for c in range(16):
            bi = 4 + c // 4
            Bk = self.bank[bi]
            S.add("pe", lambda e, c=c, Bk=Bk: e.matmul(Bk[0:NS, (c % 4) * 128:(c % 4 + 1) * 128],
                                                       lhsT=fap(yTa, c, [[16, NS]]), rhs=self.identF, start=True, stop=True),
                  r=["s_yTa", "scst"], w=["bank%d" % bi])
            if c % 4 == 3:
                S.add("act", lambda e, bi=bi, Bk=Bk: e.activation(out=ytok[0:NS, (bi - 4) * 512:(bi - 3) * 512], in_=Bk[0:NS, :], func=AF.Copy),
                      r=["bank%d" % bi], w=["s_ytok"])
        yt = ytok[0:NS, :]
        tq = self.alloc([2048], F32)
        tqs = tq[0:NS, :]
        S.add("pool", lambda e: e.tensor_tensor(out=v64(tqs), in0=v64(xc[0:NS, 0:2048]), in1=dsk[0:NS, :].unsqueeze(2).to_broadcast([NS, 32, 64]),
                                                 op=ALU.mult), r=["s_xc", "s_dsk"], w=["s_tq"])
        S.add("dve", lambda e: e.tensor_tensor(out=yt, in0=yt, in1=tqs, op=ALU.add), r=["s_ytok", "s_tq"], w=["s_ytok"])
        S.add("act", lambda e: e.activation(out=tqs, in_=Us(4608, 2048), func=AF.Silu), r=["U", "s_ytok"], w=["s_tq"])
        S.add("dve", lambda e: e.tensor_tensor(out=yt, in0=yt, in1=tqs, op=ALU.mult), r=["s_ytok", "s_tq"], w=["s_ytok"])
        S.add("act", lambda e: e.activation(out=tqs, in_=yt, func=AF.Square), r=["s_ytok"], w=["s_tq"])
        ss = self.alloc([4], F32)
        sss = ss[0:NS, :]
        v512 = lambda ap: ap.rearrange("p (a d) -> p a d", d=512)
        S.add("dve", lambda e: e.reduce_sum(out=sss, in_=v512(tqs), axis=AX.X), r=["s_tq"], w=["s_ss"])
        S.add("dve", lambda e: e.tensor_scalar(out=sss, in0=sss, scalar1=1.0 / 512, scalar2=EPS, op0=ALU.mult, op1=ALU.add),
              r=["s_ss"], w=["s_ss"])
        S.add("act", lambda e: e.activation(out=sss, in_=sss, func=AF.Sqrt), r=["s_ss"], w=["s_ss"])
        S.add("dve", lambda e: e.reciprocal(out=sss, in_=sss), r=["s_ss"], w=["s_ss"])
        S.add("dve", lambda e: e.tensor_tensor(out=v512(yt), in0=v512(yt), in1=sss.unsqueeze(2).to_broadcast([NS, 4, 512]), op=ALU.mult),
              r=["s_ytok", "s_ss"], w=["s_ytok"])
        S.dma("sp", tqs, self.nssm_rep[l], r=["s_ss"], w=["s_tq"], chan="s_ld")
        S.add("dve", lambda e: e.tensor_tensor(out=yt, in0=yt, in1=tqs, op=ALU.mult), r=["s_ytok", "s_tq"], w=["s_ytok"])
        ysT = self.alloc([16 * NS], BF16)
        self.s_transpose_to_fm(yt, 16, ysT, 2, ["s_ytok"], "s_ysT")
        sbr = self.alloc([D], F32)
        ws = self.w_br_ssm[l].rearrange("(k p) n -> p k n", p=128)
        for cb in range(4):
            slot, key = self.wslot([(0, [[256, 16], [1, 256]], ws[:, :, cb * 256:(cb + 1) * 256])], "wss")
            Bk = self.bank[3]
            for kk in range(16):
                S.add("pe", lambda e, kk=kk, slot=slot: e.matmul(Bk[0:NS, 0:256], lhsT=ysT[:, kk * NS:(kk + 1) * NS],
                                                                 rhs=fap(slot, kk * 256, [[1, 256]]), start=(kk == 0), stop=(kk == 15)),
                      r=[key, "s_ysT"], w=["bank3"])
            S.add("act", lambda e, cb=cb: e.activation(out=sbr[0:NS, cb * 256:(cb + 1) * 256], in_=Bk[0:NS, 0:256], func=AF.Copy),
                  r=["bank3"], w=["s_sbr"])
        sga = self.alloc([D], F32)
        sgs = self.alloc([D], F32)
        S.add("act", lambda e: e.activation(out=sga[0:NS, :], in_=Us(9760, D), func=AF.Sigmoid), r=["U"], w=["s_sga"])
        S.add("act", lambda e: e.activation(out=sgs[0:NS, :], in_=Us(10784, D), func=AF.Sigmoid), r=["U"], w=["s_sgs"])
        S.add("dve", lambda e: e.tensor_tensor(out=sga[0:NS, :], in0=sga[0:NS, :], in1=abr[0:NS, :], op=ALU.mult), r=["s_sga", "s_abr"], w=["s_sga"])
        S.add("pool", lambda e: e.tensor_tensor(out=sgs[0:NS, :], in0=sgs[0:NS, :], in1=sbr[0:NS, :], op=ALU.mult), r=["s_sgs", "s_sbr"], w=["s_sgs"])
        S.add("dve", lambda e: e.tensor_tensor(out=sga[0:NS, :], in0=sga[0:NS, :], in1=sgs[0:NS, :], op=ALU.add), r=["s_sga", "s_sgs"], w=["s_sga"])
        mT = self.alloc([KC * NS], BF16)
        self.s_transpose_to_fm(sga[0:NS, :], KC, mT, 2, ["s_sga"], "s_mT")
        wo = self.w_out[l].rearrange("(k p) n -> p k n", p=128)
        Bo = self.bank[0]
        for m2 in range(KC):
            slot, key = self.wslot([(0, [[128, KC], [1, 128]], wo[:, :, m2 * 128:(m2 + 1) * 128])], "wos")
            for kc in range(KC):
                S.add("pe", lambda e, kc=kc, m2=m2, slot=slot: e.matmul(Bo[:, m2 * NS:(m2 + 1) * NS], lhsT=fap(slot, kc * 128, [[1, 128]]),
                                                                        rhs=mT[:, kc * NS:(kc + 1) * NS], start=(kc == 0), stop=(kc == KC - 1)),
                      r=[key, "s_mT"], w=["bank0"])
        oT = self.alloc([KC * NS], F32)
        S.add("act", lambda e: e.activation(out=oT, in_=Bo[:, 0:KC * NS], func=AF.Copy), r=["bank0"], w=["s_oT"])
        self.s_postnorm(oT.rearrange("p (c n) -> p c n", n=NS), ["s_oT"], 1)

    def s_finish(self):
        self.S.dma("sp", self.ys_out, self.xsT[:], r=["xsT"], w=["ys_o"], chan="s_out")
        if "s_out" not in self.out_chans:
            self.out_chans.append("s_out")

    def step(self):
        if self.nsteps is not None and self.stepi >= self.nsteps:
            return False
        self.stepi += 1
        return True

    def mixer(self, l, g0):
        sa = self.stop_after
        if not self.step():
            return
        self.phase(0)
        self.hT = self.alloc([KC, T], BF16)
        self.attoT = self.alloc([4, T], BF16)
        mark = self.aptr
        self.norm_bufs()
        self.prenorm(self.xres, g0, T, 1, 0)
        if sa == "prenorm":
            return
        self.phase(mark)
        if sa != "noattn":
            self.attention(l, g0)
        if sa == "attn":
            return
        if not self.step():
            return
        self.phase(mark)
        self.ssd(l, g0)
        if sa in ("ssd", "ssd_pre", "ssd_conv", "ssd_c1", "ssd_p1", "ssd_p2", "ssd_p3", "ssd_p4", "ssd_p5", "ssd_p6", "ssd_p7"):
            return
        if not self.step():
            return
        self.phase(mark)
        self.sqf = self.alloc([KC, 512], BF16)
        self.tail(l, g0)

    def build(self):
        self.declare()
        self.consts()
        S = self.S
        for l in range(self.depth):
            self.modulation(l)
            if self.do_sample:
                self.s_ffn(l, 0, 0)
                self.s_mixer(l)
                self.s_ffn(l, 1, 2)
            for g in range(NGRP if self.do_prompt else 0):
                g0 = g * T
                src = self.xT_in if l == 0 else self.xres
                if self.step():
                    self.ffn(l, 0, 0, src, self.xres, g0, False)
                if g == 0:
                    self.ssd_params(l)
                self.mixer(l, g0)
                if self.stop_after:
                    break
                last = (l == self.depth - 1)
                if self.step():
                    self.ffn(l, 1, 2, self.xres, self.yT_out if last else self.xres, g0, last)
            if self.stop_after:
                break
        if self.do_sample and not self.stop_after:
            self.s_finish()
        S.emit(final_waits=self.out_chans)
        return self.nc


def pm(v):
    v = np.asarray(v)
    c = v.shape[-1] // 128
    v = v.reshape(v.shape[:-1] + (c, 128))
    return np.ascontiguousarray(np.moveaxis(v, -1, 0))


_CACHE = {}


def _consts():
    j = np.arange(128)[:, None]
    i = np.arange(128)[None, :]
    ident = (j == i).astype(np.float32)
    mask2 = np.concatenate([(j >= i), (j <= i)], axis=1).astype(np.float32)
    negm = np.where(j > i, -30000.0, 0.0).astype(np.float32)
    cbf = np.concatenate([ident, mask2, np.tile(negm, (1, 4))], axis=1)
    tri = (j <= i).astype(np.float32)
    sel127 = np.zeros((128, 128), np.float32)
    sel127[127, :] = 1.0
    identS = np.zeros((128, 32), np.float32)
    identS[0:32] = np.eye(32)
    identS[32:64] = np.eye(32)
    cf32 = np.concatenate([tri, sel127, identS], axis=1)
    half = 64
    inv = (np.float32(10000.0) ** (-(np.arange(half, dtype=np.float32) / np.float32(half)))).astype(np.float32)
    pos = np.concatenate([np.arange(SEQ), [PAST]]).astype(np.float32)
    ang = (pos[None, :] * inv[:, None]).astype(np.float32)
    cos = np.cos(ang).astype(np.float32)
    sin = np.sin(ang).astype(np.float32)
    cosT = np.concatenate([cos, cos], axis=0)
    sinT = np.concatenate([-sin, sin], axis=0)
    return cbf, cf32, np.ascontiguousarray(cosT), np.ascontiguousarray(sinT)


def _sconsts():
    sc = np.zeros((128, 1172), np.float32)
    sc[:, 0:128] = np.eye(128, dtype=np.float32)
    for b in range(NS):
        sc[b, 128 + b * 128:128 + (b + 1) * 128] = 1.0
        sc[0:4, 640 + b * NS + b] = 1.0
    sc[0:4, 656:660] = np.eye(4, dtype=np.float32)
    for h in range(4):
        sc[h, 660 + h * 128:660 + (h + 1) * 128] = 1.0
    return sc


def kernel(**inp):
    f = lambda k: np.asarray(inp[k], dtype=np.float32)
    x_prompt = f("x_prompt")
    depth = 2
    if "nc" not in _CACHE:
        b = Builder(depth=depth)
        _CACHE["nc"] = b.build()
        _CACHE["b"] = b
    nc = _CACHE["nc"]
    cbf, cf32, cosT, sinT = _consts()
    shared = {}
    shared["w_mod"] = f("w_mod")
    shared["bmodT"] = np.ascontiguousarray(f("b_mod").reshape(depth, 72, 128).transpose(0, 2, 1))
    shared["gpreT"] = np.ascontiguousarray(f("norm_pre").reshape(depth, 3, KC, 128).transpose(0, 3, 1, 2))
    shared["gpostT"] = np.ascontiguousarray(f("norm_post").reshape(depth, 3, KC, 128).transpose(0, 3, 1, 2))
    shared["w_ff_in"] = f("w_ff_in")
    shared["w_ff_out"] = f("w_ff_out")
    shared["cbf_in"] = cbf
    shared["cf32_in"] = cf32
    shared["cosT"] = cosT
    shared["sinT"] = sinT
    shared["w_in"] = f("w_in")
    shared["conv_wT"] = np.ascontiguousarray(f("conv_w").reshape(depth, 4, 24, 128).transpose(0, 3, 2, 1))
    shared["conv_bT"] = np.ascontiguousarray(f("conv_b").reshape(depth, 24, 128).transpose(0, 2, 1))
    for k in ("dt_bias", "a_log", "d_skip", "norm_ssm", "w_br_att", "w_br_ssm", "w_out"):
        shared[k] = f(k)
    rep = lambda a: np.ascontiguousarray(np.broadcast_to(a[:, None], (a.shape[0], NS) + a.shape[1:]))
    shared["scst_in"] = _sconsts()
    shared["rope_s"] = np.ascontiguousarray(np.broadcast_to(
        np.concatenate([cosT[:, SEQ], sinT[:, SEQ]])[None, :], (NS, 256)))
    shared["convw_rep"] = rep(f("conv_w"))
    shared["convb_rep"] = rep(f("conv_b"))
    shared["dtb_rep"] = rep(f("dt_bias"))
    shared["alog_rep"] = rep(f("a_log"))
    shared["dsk_rep"] = rep(f("d_skip"))
    shared["nssm_rep"] = rep(f("norm_ssm"))
    caches = [f("cache_kv_g0"), f("cache_kv_g1"), f("cache_kv_g2")]
    st_ssm = f("state_ssm")
    st_conv = f("state_conv")
    x_sample = f("x_sample")
    in_maps = []
    for core in range(8):
        bidx = core % 4
        m = dict(shared)
        sl = slice(core * NS, (core + 1) * NS)
        m["xsT_in"] = np.ascontiguousarray(x_sample[sl, 0, :].T.reshape(KC, 128, NS).transpose(1, 0, 2))
        for g in range(3):
            m["cache%d" % g] = np.ascontiguousarray(caches[g][:, sl]).reshape(depth, NS, WIN[g], 1024)
        m["st_ssm"] = np.ascontiguousarray(st_ssm[:, sl]).reshape(depth, NS, 2048, 128)
        m["st_conv"] = np.ascontiguousarray(st_conv[:, sl])
        m["xT_in"] = np.ascontiguousarray(x_prompt[bidx].T).reshape(KC, 128, SEQ)
        cc = np.concatenate([f("c_prompt")[bidx:bidx + 1], f("c_sample")[core * NS:(core + 1) * NS]], axis=0)
        m["cT"] = np.ascontiguousarray(cc.T.reshape(KC, 128, NCOL).transpose(1, 0, 2))
        m = {k: v for k, v in m.items() if k in _CACHE["b"].dram_in}
        in_maps.append(m)
    res = run_bass_kernel_spmd(nc, in_maps, core_ids=list(range(8)))
    r = res.results
    B = 4
    y_prompt = np.stack([r[b]["yT_out"].reshape(D, SEQ).T for b in range(B)], axis=0)
    outs = {"y_prompt": y_prompt}
    kvp = []
    for g in range(3):
        d, keep = DIL[g], WIN[g]
        arr = np.zeros((depth, B, keep, 2, 4, 128), np.float32)
        for b in range(B):
            kT = r[b]["kT_out%d" % g]
            arr[:, b, :, 0] = kT.transpose(0, 3, 1, 2)
            vc = r[b]["vcm_out%d" % g]
            v = vc.transpose(0, 3, 2, 1, 4).reshape(depth, keep, 4, 128)
            arr[:, b, :, 1] = v
        kvp.append(arr)
    ssm_p = np.stack([r[b]["ssm_out"].transpose(0, 2, 1).reshape(depth, 32, 64, 128) for b in range(B)], axis=1)
    conv_p = np.stack([r[b]["conv_out"].transpose(0, 3, 2, 1).reshape(depth, 3, 3072) for b in range(B)], axis=1)
    y_sample = np.concatenate([r[c]["ys_out"].transpose(2, 1, 0).reshape(NS, 1, D) for c in range(8)], axis=0)
    kvs = [np.concatenate([r[c]["kvs_out%d" % g].reshape(depth, NS, 1, 2, 4, 128) for c in range(8)], axis=1) for g in range(3)]
    ssm_s = np.concatenate([r[c]["ssms_out"].reshape(depth, NS, 32, 64, 128) for c in range(8)], axis=1)
    conv_s = np.concatenate([r[c]["convs_out"] for c in range(8)], axis=1)
    return (y_prompt, y_sample, kvp[0], kvs[0], kvp[1], kvs[1], kvp[2], kvs[2], ssm_p, ssm_s, conv_p, conv_s)
```

```python
import contextlib
import os as _os
import numpy as np
import concourse.bass as bass
import concourse.mybir as mybir
from concourse.bass_utils import run_bass_kernel_spmd

F32 = mybir.dt.float32
BF16 = mybir.dt.bfloat16
AF = mybir.ActivationFunctionType
ALU = mybir.AluOpType
AX = mybir.AxisListType

D = 1024
KC = 8
SEQ = 4096
T = 2048
NGRP = SEQ // T
DFF = 2816
NJ = DFF // 128
INC = 11808
EPS = 1e-6
NS = 4
NCOL = 1 + NS
WIN = (128, 512, 2048)
DIL = (1, 4, 16)
PAST = 16384
SCALE = 128 ** -0.5
RES_W = (0.5, 1.0, 0.5)


class Op:
    __slots__ = ("eng", "fn", "deps", "chan", "chan_cnt", "sig", "idx", "pos", "is_dma", "waits", "grp", "grp_last")


class Sched:
    ENG = ("pe", "act", "dve", "pool", "sp")

    def __init__(self, nc):
        self.nc = nc
        self.ops = []
        self.last_w = {}
        self.readers = {}
        self.per_eng = {e: [] for e in self.ENG}
        self.chan_count = {}
        self.stack = contextlib.ExitStack()
        self.nbytes = 0
        self.bar = {}
        self.bar_start = 0
        self.chan_last = {}
        self.gid = 0

    def sb(self, name, shape, dt):
        t = self.stack.enter_context(self.nc.sbuf_tensor(name, list(shape), dt))
        n = 1
        for s in shape[1:]:
            n *= s
        self.nbytes += n * (4 if dt == F32 else 2)
        return t

    def ps(self, name, shape, dt=F32):
        return self.stack.enter_context(self.nc.psum_tensor(name, list(shape), dt))

    def add(self, eng, fn, r=(), w=(), chan=None, grp=None):
        op = Op()
        op.eng = eng
        op.fn = fn
        op.idx = len(self.ops)
        op.is_dma = chan is not None
        op.chan = chan
        op.sig = None
        deps = {}
        for k in r:
            j = self.last_w.get(k)
            if j is not None:
                deps[j] = True
        for k in w:
            j = self.last_w.get(k)
            if j is not None:
                deps.setdefault(j, False)
            for j in self.readers.get(k, ()):
                deps.setdefault(j, False)
        for k in w:
            self.last_w[k] = op.idx
            self.readers[k] = []
        for k in r:
            self.readers.setdefault(k, []).append(op.idx)
        if self.bar.get(eng) and not (chan is not None and str(chan).startswith("wslot")):
            for j in self.bar.pop(eng):
                deps[j] = True
        deps.pop(op.idx, None)
        op.deps = deps
        if op.is_dma:
            c = self.chan_count.get(chan, 0) + 1
            self.chan_count[chan] = c
            op.chan_cnt = c
            if grp is None:
                self.gid += 1
                grp = ("u", self.gid)
            op.grp = grp
            op.grp_last = op
            prev = self.chan_last.get(chan)
            if prev is not None and not _os.environ.get("K_NOGRP"):
                if prev.grp == grp:
                    q = prev
                    members = [q]
                    for o2 in reversed(self.ops):
                        if o2.is_dma and o2.chan == chan and o2.grp == grp and o2 is not q:
                            members.append(o2)
                        elif o2.is_dma and o2.chan == chan and o2.grp != grp:
                            break
                    for o2 in members:
                        o2.grp_last = op
                    for j, v in prev.deps.items():
                        if self.ops[j].is_dma and self.ops[j].chan == chan:
                            deps[j] = True
                else:
                    deps[prev.idx] = True
            self.chan_last[chan] = op
        op.pos = len(self.per_eng[eng])
        self.per_eng[eng].append(op)
        self.ops.append(op)
        return op

    def barrier(self):
        lastops = [self.per_eng[e][-1].idx for e in self.ENG if self.per_eng[e]]
        dmas = [op.idx for op in self.ops[self.bar_start:] if op.is_dma]
        pend = lastops + dmas
        for e in self.ENG:
            self.bar[e] = list(self.bar.get(e, [])) + pend
        self.bar_start = len(self.ops)

    def dma(self, q, out, in_, r=(), w=(), chan=None, grp=None, **kw):
        assert chan is not None
        return self.add(q, lambda e: e.dma_start(out=out, in_=in_, **kw), r=r, w=w, chan=chan, grp=grp)

    def emit(self, final_waits=()):
        nc = self.nc
        ops = self.ops
        need = [False] * len(ops)
        waits_of = [None] * len(ops)
        for op in ops:
            wl = []
            for j, raw in op.deps.items():
                y = ops[j]
                if y.is_dma:
                    if op.is_dma and y.chan == op.chan and y.grp == op.grp:
                        continue
                    wl.append(j)
                elif y.eng != op.eng or op.is_dma:
                    need[j] = True
                    wl.append(j)
                else:
                    if op.eng == "pe":
                        continue
                    if raw and (op.pos - y.pos) <= 3:
                        need[j] = True
                        wl.append(j)
            waits_of[op.idx] = wl
        cnt = {e: 0 for e in self.ENG}
        for e in self.ENG:
            for op in self.per_eng[e]:
                if not op.is_dma and need[op.idx]:
                    cnt[e] += 1
                    op.sig = cnt[e]
        st = self.stack
        esem = {e: st.enter_context(nc.semaphore("s_" + e)) for e in self.ENG}
        csem = {c: st.enter_context(nc.semaphore("c_" + c)) for c in self.chan_count}
        handles = {"pe": "tensor", "act": "scalar", "dve": "vector", "pool": "gpsimd", "sp": "sync"}
        nwait = [0]

        self.sim = {e: [] for e in self.ENG}

        def run_engine(ename, eng):
            seen = {}
            for op in self.per_eng[ename]:
                req = {}
                for j in waits_of[op.idx]:
                    y = ops[j]
                    if y.is_dma:
                        key, val = ("c", y.chan), 16 * y.grp_last.chan_cnt
                    else:
                        key, val = ("e", y.eng), y.sig
                    if req.get(key, 0) < val:
                        req[key] = val
                for key, val in req.items():
                    if seen.get(key, 0) >= val:
                        continue
                    seen[key] = val
                    sem = csem[key[1]] if key[0] == "c" else esem[key[1]]
                    eng.wait_ge(sem, val)
                    nwait[0] += 1
                self.sim[ename].append((op.idx, list(req.items()), ("c", op.chan) if op.is_dma else (("e", ename) if op.sig is not None else None)))
                ins = op.fn(eng)
                if op.is_dma:
                    ins.then_inc(csem[op.chan], 16)
                elif op.sig is not None:
                    ins.then_inc(esem[ename], 1)
            if ename == "sp":
                for c in final_waits:
                    eng.wait_ge(csem[c], 16 * self.chan_count[c])

        block = st.enter_context(nc.Block())
        for ename in self.ENG:
            getattr(block, handles[ename])(lambda eng, ename=ename: run_engine(ename, eng))
        self.stats = dict(n_ops={e: len(v) for e, v in self.per_eng.items()}, n_wait=nwait[0],
                          n_sem=len(esem) + len(csem))


def fap(t, off, dims, p0=0, pn=None):
    base = t[:]
    pstep = base.ap[0][0]
    if pn is None:
        pn = base.ap[0][1] - p0
    return bass.AP(base.tensor, base.offset + p0 * pstep + off, [[pstep, pn]] + [list(d) for d in dims])


class Builder:
    def __init__(self, depth=2, do_prompt=True, do_sample=True, stop_after=None, nsteps=None):
        self.nsteps = nsteps
        self.stepi = 0
        self.depth = depth
        self.do_prompt = do_prompt
        self.do_sample = do_sample
        self.stop_after = stop_after
        nc = bass.Bass("TRN2", target_bir_lowering=False)
        self.nc = nc
        self.S = Sched(nc)
        self.dram_in = {}
        self.dram_out = {}
        self.out_chans = []
        self.uid = 0

    def din(self, name, shape, dt=F32):
        ap = self.nc.dram_tensor(name, list(shape), dt, kind="ExternalInput").ap()
        self.dram_in[name] = ap
        return ap

    def dout(self, name, shape, dt=F32):
        ap = self.nc.dram_tensor(name, list(shape), dt, kind="ExternalOutput").ap()
        self.dram_out[name] = ap
        return ap

    def dscr(self, name, shape, dt=F32):
        return self.nc.dram_tensor(name, list(shape), dt, kind="Internal").ap()

    def u(self, p):
        self.uid += 1
        return "%s%d" % (p, self.uid)

    def declare(self):
        L = self.depth
        self.xT_in = self.din("xT_in", [KC, 128, SEQ])
        self.cT = self.din("cT", [128, KC, NCOL])
        self.w_mod = self.din("w_mod", [L, D, 9 * D])
        self.bmodT = self.din("bmodT", [L, 128, 72])
        self.gpreT = self.din("gpreT", [L, 128, 3, KC])
        self.gpostT = self.din("gpostT", [L, 128, 3, KC])
        self.w_ff_in = self.din("w_ff_in", [L, 2, D, 2 * DFF])
        self.w_ff_out = self.din("w_ff_out", [L, 2, DFF, D])
        self.yT_out = self.dout("yT_out", [KC, 128, SEQ])
        self.dbg = bool(_os.environ.get("K_DBG"))
        self.xres = (self.dout if self.dbg else self.dscr)("xres", [KC, 128, SEQ])
        self.cbf_in = self.din("cbf_in", [128, 896])
        self.cf32_in = self.din("cf32_in", [128, 288])
        self.cosT = self.din("cosT", [128, SEQ + 1])
        self.sinT = self.din("sinT", [128, SEQ + 1])
        self.w_in = self.din("w_in", [L, D, INC])
        self.conv_wT = self.din("conv_wT", [L, 128, 24, 4])
        self.conv_bT = self.din("conv_bT", [L, 128, 24])
        self.dt_bias = self.din("dt_bias", [L, 32])
        self.a_log = self.din("a_log", [L, 32])
        self.d_skip = self.din("d_skip", [L, 32])
        self.norm_ssm = self.din("norm_ssm", [L, 2048])
        self.w_br_att = self.din("w_br_att", [L, 512, D])
        self.w_br_ssm = self.din("w_br_ssm", [L, 2048, D])
        self.w_out = self.din("w_out", [L, D, D])
        self.kT_out = [self.dout("kT_out%d" % g, [L, 4, 128, WIN[g]]) for g in range(3)]
        self.vcm_out = [self.dout("vcm_out%d" % g, [L, 4, DIL[g], 128, 128]) for g in range(3)]
        self.ssm_out = self.dout("ssm_out", [L, 128, 2048])
        self.conv_out = self.dout("conv_out", [L, 128, 24, 3])
        self.khist = [self.dscr("khist%d" % g, [4, 128, WIN[g]], BF16) for g in range(3)]
        self.vhist = [self.dscr("vhist%d" % g, [4, 128, DIL[g] * 128], BF16) for g in range(3)]
        self.yscr = (self.dout if self.dbg else self.dscr)("yscr", [16, 128, T], BF16)
        if self.dbg:
            self.atto_dbg = self.dout("atto_dbg", [128, 4, T], BF16)
        if self.do_sample:
            self.s_declare()

    def consts(self):
        S = self.S
        nc = self.nc
        self.ones_bf = S.sb("ones_bf", [128, 128], BF16)
        S.add("pool", lambda e: e.memset(self.ones_bf[:], 1.0), w=["ones_bf"])
        self.psall = S.ps("psall", [128, 4096])
        self.psall_bf = self.psall[:].bitcast(BF16)
        self.bank = [self.psall[:, i * 512:(i + 1) * 512] for i in range(8)]
        self.bank_bf = [self.psall_bf[:, i * 1024:(i + 1) * 1024] for i in range(8)]
        self.cbf = S.sb("cbf", [128, 896], BF16)
        S.dma("pool", self.cbf[:], self.cbf_in, w=["cbf"], chan="misc2")
        self.ident_bf = self.cbf[:, 0:128]
        self.mask2 = self.cbf[:, 128:384]
        self.negm4 = self.cbf[:, 384:896]
        self.cf32 = S.sb("cf32", [128, 288], F32)
        S.dma("sp", self.cf32[:], self.cf32_in, w=["cf32"], chan="misc")
        self.tri = self.cf32[:, 0:128]
        self.sel127 = self.cf32[:, 128:256]
        self.identS = self.cf32[0:64, 256:288]
        self.Sst = S.sb("Sst", [128, 2048], F32)
        self.convhist = S.sb("convhist", [128, 24, 3], F32)
        self.convw = S.sb("convw", [128, 24, 4], F32)
        self.convb = S.sb("convb", [128, 24], F32)
        self.dtb_bc = S.sb("dtb_bc", [128, 32], F32)
        self.a_bc = S.sb("a_bc", [128, 32], F32)
        self.dsk_bc = S.sb("dsk_bc", [128, 32], F32)

        self.NSLOT = 3
        self.wring = [S.sb("wslot%d" % i, [128, 4096], BF16) for i in range(self.NSLOT)]
        self.wnext = 0
        self.modT = S.sb("modT", [128, 72, NCOL], F32)
        self.Aall = S.sb("Aall", [128, 3, KC, NCOL], F32)
        self.Gall = S.sb("Gall", [128, 3, KC, NCOL], F32)
        self.scT = S.sb("scT", [128, KC, NCOL], BF16)
        self.cTs = S.sb("cTs", [128, KC, NCOL], F32)
        self.bmod_sb = S.sb("bmod_sb", [128, 72], F32)
        self.gpre_sb = S.sb("gpre_sb", [128, 3, KC], F32)
        self.gpost_sb = S.sb("gpost_sb", [128, 3, KC], F32)
        if self.do_sample:
            self.s_consts()
        self.ARENA_B = (int(self.nc.sbuf_bytes_remaining) - 128) // 256 * 256
        self.amax = 0
        self.arena = S.sb("arena", [128, self.ARENA_B // 4], F32)
        self.arena16 = self.arena[:].bitcast(BF16)
        self.aptr = 0
        self.xt_i = 0

    def alloc(self, shape, dt):
        n = 1
        for v in shape:
            n *= v
        nb = n * (4 if dt == F32 else 2)
        off = self.aptr
        self.aptr = (off + nb + 63) // 64 * 64
        assert self.aptr <= self.ARENA_B, ("arena overflow", self.aptr, self.ARENA_B)
        self.amax = max(self.amax, self.aptr)
        if dt == F32:
            ap = self.arena[:, off // 4:off // 4 + n]
        else:
            ap = self.arena16[:, off // 2:off // 2 + n]
        if len(shape) == 2:
            return ap.rearrange("p (a b) -> p a b", a=shape[0])
        if len(shape) == 3:
            return ap.rearrange("p (a b c) -> p a b c", a=shape[0], b=shape[1])
        return ap

    def phase(self, mark=0):
        self.S.barrier()
        self.aptr = mark

    def norm_bufs(self):
        self.xt = [self.alloc([KC, 512], F32) for i in range(2)]
        self.sq = self.alloc([KC, 512], BF16)
        self.rstd = self.alloc([512], F32)
        self.tmp = self.alloc([KC, 512], F32)

    def ffn_bufs(self):
        self.phase(0)
        self.hT = self.alloc([KC, 1024], BF16)
        self.norm_bufs()
        self.aT = self.alloc([NJ, 1024], BF16)
        self.sg = [self.alloc([512], BF16) for i in range(2)]
        self.fT = self.alloc([KC, 1024], F32)
        self.sqf = self.alloc([KC, 1024], BF16)

    def wslot(self, loads, tag):
        S = self.S
        i = self.wnext
        self.wnext = (i + 1) % self.NSLOT
        slot = self.wring[i]
        key = "wslot%d" % i
        S.gid += 1
        grp = ("w", S.gid)
        for (off, dims, src) in loads:
            S.dma("pool", fap(slot, off, dims), src, w=[key], chan=key, grp=grp)
        return slot, key

    def modulation(self, l):
        S = self.S
        mm = self.bank[7]
        if l == 0:
            S.dma("sp", self.cTs[:], self.cT, w=["cTs"], chan="misc")
            S.add("act", lambda e: e.activation(out=self.scT[:], in_=self.cTs[:], func=AF.Silu), r=["cTs"], w=["scT"])
        S.dma("sp", self.bmod_sb[:], self.bmodT[l], w=["bmod_sb"], chan="misc")
        S.dma("sp", self.gpre_sb[:], self.gpreT[l], w=["gpre_sb"], chan="misc")
        S.dma("sp", self.gpost_sb[:], self.gpostT[l], w=["gpost_sb"], chan="misc")
        wv = self.w_mod[l].rearrange("(kc p) n -> p kc n", p=128)
        for blk in range(18):
            slot, key = self.wslot([(0, [[512, KC], [1, 512]], wv[:, :, blk * 512:(blk + 1) * 512])], "mod")
            for cc in range(4):
                ch = blk * 4 + cc
                for kc in range(KC):
                    S.add("pe", lambda e, ch=ch, kc=kc, cc=cc, slot=slot: e.matmul(
                        fap(mm, ch * NCOL, [[1, NCOL]]), lhsT=fap(slot, kc * 512 + cc * 128, [[1, 128]]),
                        rhs=self.scT[:, kc, :], start=(kc == 0), stop=(kc == KC - 1)),
                        r=[key, "scT"], w=["bank7"])
        S.add("dve", lambda e: e.tensor_tensor(
            out=self.modT[:], in0=fap(mm, 0, [[NCOL, 72], [1, NCOL]]),
            in1=self.bmod_sb[:].unsqueeze(2).to_broadcast([128, 72, NCOL]), op=ALU.add),
            r=["bank7", "bmod_sb"], w=["modT"])
        for k in range(3):
            S.add("dve", lambda e, k=k: e.scalar_tensor_tensor(
                out=self.Aall[:, k], in0=self.modT[:, k * 24 + 8:k * 24 + 16, :], scalar=1.0,
                in1=self.gpre_sb[:, k, :].unsqueeze(2).to_broadcast([128, KC, NCOL]),
                op0=ALU.add, op1=ALU.mult), r=["modT", "gpre_sb"], w=["Aall"])
            S.add("dve", lambda e, k=k: e.scalar_tensor_tensor(
                out=self.Gall[:, k], in0=self.modT[:, k * 24 + 16:k * 24 + 24, :], scalar=float(RES_W[k]),
                in1=self.gpost_sb[:, k, :].unsqueeze(2).to_broadcast([128, KC, NCOL]),
                op0=ALU.mult, op1=ALU.mult), r=["modT", "gpost_sb"], w=["Gall"])

    def rstd_from_sq(self, sq_ap_fn, nch, denom, rkeys):
        _rstd = self.rstd
        S = self.S
        st = self.bank[6]
        for c in range(nch):
            S.add("pe", lambda e, c=c: e.matmul(st[:, :], lhsT=self.ones_bf[:, :], rhs=sq_ap_fn(c),
                                                  start=(c == 0), stop=(c == nch - 1)),
                  r=list(rkeys) + ["ones_bf"], w=["bank6"])
        S.add("dve", lambda e: e.tensor_scalar(out=_rstd[:], in0=st[:, :], scalar1=1.0 / denom, scalar2=EPS,
                                                 op0=ALU.mult, op1=ALU.add), r=["bank6"], w=["rstd"])
        S.add("act", lambda e: e.activation(out=_rstd[:], in_=_rstd[:], func=AF.Sqrt), r=["rstd"], w=["rstd"])
        S.add("dve", lambda e: e.reciprocal(out=_rstd[:], in_=_rstd[:]), r=["rstd"], w=["rstd"])

    def load_x(self, src, tok0):
        S = self.S
        i = self.xt_i
        self.xt_i ^= 1
        xt = self.xt[i]
        key = "xt%d" % i
        S.dma("sp", xt[:], src[:, :, tok0:tok0 + 512].rearrange("c p t -> p c t"),
              r=["xdram"], w=[key], chan=key)
        return xt, key

    def prenorm(self, src, tok0, ntok, k, hoff):
        _hT = self.hT
        _sq = self.sq
        _rstd = self.rstd
        _tmp = self.tmp
        S = self.S
        for tt in range(ntok // 512):
            xt, xkey = self.load_x(src, tok0 + tt * 512)
            S.add("act", lambda e, xt=xt: e.activation(out=_sq[:], in_=xt[:], func=AF.Square),
                  r=[xkey], w=["sq"])
            self.rstd_from_sq(lambda c: _sq[:, c, :], KC, D, ["sq"])
            S.add("dve", lambda e, xt=xt: e.tensor_tensor(
                out=_tmp[:], in0=xt[:], in1=_rstd[:].unsqueeze(1).to_broadcast([128, KC, 512]),
                op=ALU.mult), r=[xkey, "rstd"], w=["tmp"])
            for c in range(KC):
                o0 = hoff + tt * 512
                S.add("act", lambda e, c=c, o0=o0: e.activation(
                    out=_hT[:, c, o0:o0 + 512], in_=_tmp[:, c, :], func=AF.Identity,
                    scale=self.Aall[:, k, c, 0:1], bias=self.modT[:, k * 24 + c, 0:1]),
                    r=["tmp", "Aall", "modT"], w=["hT"])

    def postnorm_residual(self, f_ap, sq_fn, k, src, dst, tok0, fkeys, out_chan=None):
        _rstd = self.rstd
        _tmp = self.tmp
        S = self.S
        self.rstd_from_sq(sq_fn, KC, D, fkeys)
        xt, xkey = self.load_x(src, tok0)
        S.add("dve", lambda e: e.tensor_tensor(
            out=_tmp[:], in0=f_ap, in1=_rstd[:].unsqueeze(1).to_broadcast([128, KC, 512]),
            op=ALU.mult), r=list(fkeys) + ["rstd"], w=["tmp"])
        for c in range(KC):
            S.add("dve", lambda e, c=c, xt=xt: e.scalar_tensor_tensor(
                out=xt[:, c, :], in0=_tmp[:, c, :], scalar=self.Gall[:, k, c, 0:1], in1=xt[:, c, :],
                op0=ALU.mult, op1=ALU.add), r=["tmp", "Gall", xkey], w=[xkey])
        ch = out_chan or (xkey + "o")
        S.dma("sp", dst[:, :, tok0:tok0 + 512].rearrange("c p t -> p c t"), xt[:], r=[xkey], w=["xdram"], chan=ch)
        if out_chan and out_chan not in self.out_chans:
            self.out_chans.append(out_chan)

    def ffn(self, l, i, k, src, dst, g0, is_out):
        self.ffn_bufs()
        _hT = self.hT
        _aT = self.aT
        _sg = self.sg
        _fT = self.fT
        _sqf = self.sqf
        S = self.S
        w1 = self.w_ff_in[l, i].rearrange("(kc p) n -> p kc n", p=128)
        w2 = self.w_ff_out[l, i].rearrange("(j p) n -> p j n", p=128)
        pa = 0
        for half in range(2):
            t0 = g0 + half * 1024
            self.prenorm(src, t0, 1024, k, 0)
            for jb in range(NJ // 2):
                slot, key = self.wslot([
                    (0, [[256, KC], [1, 256]], w1[:, :, jb * 256:(jb + 1) * 256]),
                    (2048, [[256, KC], [1, 256]], w1[:, :, DFF + jb * 256:DFF + (jb + 1) * 256])], "w1")
                for jj in range(2):
                    j = jb * 2 + jj
                    for tt in range(2):
                        A = self.bank[pa]
                        B = self.bank[2 + pa]
                        ka, kb = "bank%d" % pa, "bank%d" % (2 + pa)
                        sg = _sg[pa]
                        sk = "sg%d" % pa
                        pa ^= 1
                        for kc in range(KC):
                            S.add("pe", lambda e, A=A, kc=kc, jj=jj, tt=tt, slot=slot: e.matmul(
                                A[:, :], lhsT=fap(slot, kc * 256 + jj * 128, [[1, 128]]),
                                rhs=_hT[:, kc, tt * 512:(tt + 1) * 512], start=(kc == 0), stop=(kc == KC - 1)),
                                r=[key, "hT"], w=[ka])
                        for kc in range(KC):
                            S.add("pe", lambda e, B=B, kc=kc, jj=jj, tt=tt, slot=slot: e.matmul(
                                B[:, :], lhsT=fap(slot, 2048 + kc * 256 + jj * 128, [[1, 128]]),
                                rhs=_hT[:, kc, tt * 512:(tt + 1) * 512], start=(kc == 0), stop=(kc == KC - 1)),
                                r=[key, "hT"], w=[kb])
                        S.add("act", lambda e, A=A, sg=sg: e.activation(out=sg[:], in_=A[:, :], func=AF.Silu),
                              r=[ka], w=[sk])
                        S.add("dve", lambda e, B=B, sg=sg, j=j, tt=tt: e.tensor_tensor(
                            out=_aT[:, j, tt * 512:(tt + 1) * 512], in0=sg[:], in1=B[:, :], op=ALU.mult),
                            r=[sk, kb], w=["aT"])
            pf = 0
            for m in range(KC):
                slot, key = self.wslot([(0, [[128, NJ], [1, 128]], w2[:, :, m * 128:(m + 1) * 128])], "w2")
                for tt in range(2):
                    Fp = self.bank[4 + pf]
                    kf = "bank%d" % (4 + pf)
                    pf ^= 1
                    for j in range(NJ):
                        S.add("pe", lambda e, Fp=Fp, j=j, tt=tt, slot=slot: e.matmul(
                            Fp[:, :], lhsT=fap(slot, j * 128, [[1, 128]]),
                            rhs=_aT[:, j, tt * 512:(tt + 1) * 512], start=(j == 0), stop=(j == NJ - 1)),
                            r=[key, "aT"], w=[kf])
                    S.add("act", lambda e, Fp=Fp, m=m, tt=tt: e.activation(
                        out=_fT[:, m, tt * 512:(tt + 1) * 512], in_=Fp[:, :], func=AF.Copy), r=[kf], w=["fT%d" % tt])
                    S.add("act", lambda e, Fp=Fp, m=m, tt=tt: e.activation(
                        out=_sqf[:, m, tt * 512:(tt + 1) * 512], in_=Fp[:, :], func=AF.Square), r=[kf], w=["sqf%d" % tt])
            for tt in range(2):
                self.postnorm_residual(_fT[:, :, tt * 512:(tt + 1) * 512],
                                       lambda c, tt=tt: _sqf[:, c, tt * 512:(tt + 1) * 512],
                                       k, src, dst, t0 + tt * 512, ["fT%d" % tt, "sqf%d" % tt],
                                       out_chan=("yout" if is_out else None))

    def proj_tiles(self, slot, woff, wkstride, key, ntile, banks, cb, extra_r=()):
        _hT = self.hT
        S = self.S
        for tt in range(ntile):
            bi = banks[tt % len(banks)]
            Bk = self.bank[bi]
            bk = "bank%d" % bi
            for kc in range(KC):
                S.add("pe", lambda e, Bk=Bk, kc=kc, tt=tt: e.matmul(
                    Bk[:, :], lhsT=fap(slot, woff + kc * wkstride, [[1, 128]]),
                    rhs=_hT[:, kc, tt * 512:(tt + 1) * 512], start=(kc == 0), stop=(kc == KC - 1)),
                    r=[key, "hT"] + list(extra_r), w=[bk])
            cb(tt, Bk, bk)

    def win_cols(self, l, c0, n):
        return self.w_in[l].rearrange("(kc p) n -> p kc n", p=128)[:, :, c0:c0 + n]

    def attention(self, l, g0):
        _hT = self.hT
        _attoT = self.attoT
        S = self.S
        last_grp = (g0 + T == SEQ) and not _os.environ.get("K_NOKVOUT")
        cosF = self.alloc([T], F32)
        sinS = self.alloc([T], F32)
        S.dma("sp", cosF[:], self.cosT[:, g0:g0 + T], w=["cosF"], chan="misc")
        S.dma("sp", sinS[:], self.sinT[:, g0:g0 + T], w=["sinS"], chan="misc")
        qb = self.alloc([3, T], BF16)
        kb = [self.alloc([WIN[g] + T], BF16) for g in range(3)]
        vb = [self.alloc([DIL[g] + T // 128, 128], BF16) for g in range(3)]
        ND = self.alloc([2, T], F32)
        Pt = [self.alloc([256], BF16) for i in range(2)]
        r1 = [self.alloc([512], F32) for i in range(2)]
        r2 = [self.alloc([512], F32) for i in range(2)]
        kst = [self.alloc([512], F32) for i in range(2)]
        vst = [self.alloc([512], F32) for i in range(2)]
        rden = self.alloc([T], F32)
        cnt = {"rt": 0, "vs": 0, "pt": 0, "ps": 0, "po": 0}
        for h in range(4):
            if g0 > 0 and not _os.environ.get("K_NOHISTLD"):
                for g in range(3):
                    S.dma("sp", kb[g][:, 0:WIN[g]], self.khist[g][h], r=["khist%d" % g], w=["kb%d" % g], chan="hist")
                    S.dma("sp", vb[g][:, 0:DIL[g], :], self.vhist[g][h].rearrange("p (a b) -> p a b", b=128),
                          r=["vhist%d" % g], w=["vb%d" % g], chan="hist")
            for g in range(3):
                d, span = DIL[g], WIN[g]
                cb0 = g * 1536 + h * 128
                for qk in range(2):
                    c0 = cb0 + qk * 512
                    wsrc = self.win_cols(l, c0, 128)
                    slot, key = self.wslot([
                        (0, [[128, KC], [1, 128]], wsrc),
                        (1024, [[128, KC], [1, 64]], wsrc[:, :, 64:128]),
                        (1024 + 64, [[128, KC], [1, 64]], wsrc[:, :, 0:64])], "wqk")

                    def rope_cb(tt, Bk, bk, slot=slot, key=key, qk=qk, g=g, h=h, span=span):
                        bi2 = 2 + (tt % 2)
                        B2 = self.bank[bi2]
                        b2k = "bank%d" % bi2
                        for kc in range(KC):
                            S.add("pe", lambda e, kc=kc: e.matmul(
                                B2[:, :], lhsT=fap(slot, 1024 + kc * 128, [[1, 128]]),
                                rhs=_hT[:, kc, tt * 512:(tt + 1) * 512], start=(kc == 0), stop=(kc == KC - 1)),
                                r=[key, "hT"], w=[b2k])
                        i = cnt["rt"] % 2
                        cnt["rt"] += 1
                        S.add("dve", lambda e: e.tensor_tensor(out=r1[i][:], in0=Bk[:, :], in1=cosF[:, tt * 512:(tt + 1) * 512],
                                                                 op=ALU.mult), r=[bk, "cosF"], w=["r1_%d" % i])
                        S.add("dve", lambda e: e.tensor_tensor(out=r2[i][:], in0=B2[:, :], in1=sinS[:, tt * 512:(tt + 1) * 512],
                                                                 op=ALU.mult), r=[b2k, "sinS"], w=["r2_%d" % i])
                        if qk == 0:
                            S.add("pool", lambda e: e.tensor_tensor(out=qb[:, g, tt * 512:(tt + 1) * 512], in0=r1[i][:],
                                                                      in1=r2[i][:], op=ALU.add),
                                  r=["r1_%d" % i, "r2_%d" % i], w=["qb"])
                        else:
                            S.add("pool", lambda e: e.tensor_tensor(out=kst[i][:], in0=r1[i][:], in1=r2[i][:], op=ALU.add),
                                  r=["r1_%d" % i, "r2_%d" % i], w=["kst%d" % i])
                            S.add("act", lambda e: e.activation(out=kb[g][:, span + tt * 512:span + (tt + 1) * 512],
                                                                  in_=kst[i][:], func=AF.Copy),
                                  r=["kst%d" % i], w=["kb%d" % g])
                            if last_grp:
                                lo = max(g0 + tt * 512, SEQ - span)
                                hi = g0 + (tt + 1) * 512
                                if lo < hi:
                                    S.dma("sp", self.kT_out[g][l, h, :, lo - (SEQ - span):hi - (SEQ - span)],
                                          kst[i][:, lo - (g0 + tt * 512):512], r=["kst%d" % i], w=["kTout"], chan="kvout")
                    self.proj_tiles(slot, 0, 128, key, T // 512, [0, 1], rope_cb)
                c0 = cb0 + 1024
                slot, key = self.wslot([(0, [[128, KC], [1, 128]], self.win_cols(l, c0, 128))], "wv")
                tiles = [(n, r) for n in range(T // span) for r in range(d)]
                for t4 in range(0, len(tiles), 4):
                    bi = (t4 // 4) % 2
                    Bk = self.bank[bi]
                    bk = "bank%d" % bi
                    for q4 in range(4):
                        n, r = tiles[t4 + q4]
                        for kc in range(KC):
                            S.add("pe", lambda e, Bk=Bk, kc=kc, q4=q4, n=n, r=r, d=d, span=span, slot=slot: e.matmul(
                                Bk[:, q4 * 128:(q4 + 1) * 128],
                                lhsT=fap(_hT, kc * T + n * span + r, [[d, 128]]),
                                rhs=fap(slot, kc * 128, [[1, 128]]), start=(kc == 0), stop=(kc == KC - 1)),
                                r=[key, "hT"], w=[bk])
                    ti0 = d + t4
                    S.add("act", lambda e, Bk=Bk, ti0=ti0, g=g: e.activation(
                        out=vb[g][:, ti0:ti0 + 4, :], in_=fap(Bk, 0, [[128, 4], [1, 128]]), func=AF.Copy),
                        r=[bk], w=["vb%d" % g])
                    if last_grp and tiles[t4 + 3][0] == T // span - 1:
                        i = cnt["vs"] % 2
                        cnt["vs"] += 1
                        S.add("act", lambda e, Bk=Bk, i=i: e.activation(out=vst[i][:], in_=Bk[:, :], func=AF.Copy), r=[bk], w=["vst%d" % i])
                        for q4 in range(4):
                            n, r = tiles[t4 + q4]
                            if n == T // span - 1:
                                S.dma("sp", self.vcm_out[g][l, h, r], vst[i][:, q4 * 128:(q4 + 1) * 128],
                                      r=["vst%d" % i], w=["vout"], chan="kvout")
            for g in range(3):
                d, span = DIL[g], WIN[g]
                for n in range(T // span):
                    for r in range(d):
                        has_prev = (g0 > 0) or (n > 0)
                        c_lo = 0 if has_prev else 128
                        qap = fap(qb, g * T + n * span + r, [[d, 128]])
                        kcur = fap(kb[g], span + n * span + r, [[d, 128]])
                        kprev = fap(kb[g], n * span + r, [[d, 128]])
                        vcur = vb[g][:, d + n * d + r, :]
                        vprev = vb[g][:, n * d + r, :]
                        si = 4 + cnt["ps"] % 2
                        cnt["ps"] += 1
                        oi = 6 + cnt["po"] % 2
                        cnt["po"] += 1
                        pi = cnt["pt"] % 2
                        cnt["pt"] += 1
                        Sb, Ob, P = self.bank[si], self.bank[oi], Pt[pi]
                        sk, ok, pk = "bank%d" % si, "bank%d" % oi, "Pt%d" % pi
                        if has_prev:
                            S.add("pe", lambda e, Sb=Sb, kprev=kprev, qap=qap: e.matmul(
                                Sb[:, 0:128], lhsT=kprev, rhs=qap, start=True, stop=True), r=["kb%d" % g, "qb"], w=[sk])
                        S.add("pe", lambda e, Sb=Sb, kcur=kcur, qap=qap: e.matmul(
                            Sb[:, 128:256], lhsT=kcur, rhs=qap, start=True, stop=True), r=["kb%d" % g, "qb"], w=[sk])
                        S.add("act", lambda e, Sb=Sb, P=P, c_lo=c_lo: e.activation(
                            out=P[:, c_lo:256], in_=Sb[:, c_lo:256], func=AF.Exp, scale=float(SCALE)), r=[sk], w=[pk])
                        S.add("pool", lambda e, P=P, c_lo=c_lo: e.tensor_tensor(
                            out=P[:, c_lo:256], in0=P[:, c_lo:256], in1=self.mask2[:, c_lo:256], op=ALU.mult),
                            r=[pk, "cbf"], w=[pk])
                        for half, lh in ((0, None), (1, self.ones_bf)):
                            oc = half * 128
                            if has_prev:
                                S.add("pe", lambda e, Ob=Ob, P=P, oc=oc, lh=lh, vprev=vprev: e.matmul(
                                    Ob[:, oc:oc + 128], lhsT=(vprev if lh is None else lh[:, :]), rhs=P[:, 0:128],
                                    start=True, stop=False), r=[pk, "vb%d" % g, "ones_bf"], w=[ok])
                            S.add("pe", lambda e, Ob=Ob, P=P, oc=oc, lh=lh, vcur=vcur, has_prev=has_prev: e.matmul(
                                Ob[:, oc:oc + 128], lhsT=(vcur if lh is None else lh[:, :]), rhs=P[:, 128:256],
                                start=(not has_prev), stop=True), r=[pk, "vb%d" % g, "ones_bf"], w=[ok])
                        nd = fap(ND, n * span + r, [[T, 2], [d, 128]])
                        osrc = fap(Ob, 0, [[128, 2], [1, 128]])
                        if g == 0:
                            S.add("dve", lambda e, nd=nd, osrc=osrc: e.tensor_copy(out=nd, in_=osrc), r=[ok], w=["ND"])
                        else:
                            S.add("dve", lambda e, nd=nd, osrc=osrc: e.tensor_tensor(out=nd, in0=osrc, in1=nd, op=ALU.add),
                                  r=[ok, "ND"], w=["ND"])
            S.add("dve", lambda e: e.reciprocal(out=rden[:], in_=ND[:, 1, :]), r=["ND"], w=["rden"])
            S.add("dve", lambda e, h=h: e.tensor_tensor(out=_attoT[:, h, :], in0=ND[:, 0, :], in1=rden[:], op=ALU.mult),
                  r=["ND", "rden"], w=["attoT"])
            if not last_grp:
                for g in range(3):
                    S.dma("sp", self.khist[g][h], kb[g][:, T:T + WIN[g]], r=["kb%d" % g], w=["khist%d" % g], chan="hist")
                    S.dma("sp", self.vhist[g][h].rearrange("p (a b) -> p a b", b=128),
                          vb[g][:, T // 128:T // 128 + DIL[g], :], r=["vb%d" % g], w=["vhist%d" % g], chan="hist")
        if last_grp and "kvout" not in self.out_chans:
            self.out_chans.append("kvout")
        if self.dbg and g0 == 0 and l == 0:
            S.dma("sp", self.atto_dbg, _attoT[:], r=["attoT"], w=["attodbg"], chan="dbg")
            self.out_chans.append("dbg")

    def ssd_params(self, l):
        S = self.S
        S.dma("sp", self.convw[:], self.conv_wT[l], w=["convw"], chan="misc")
        S.dma("sp", self.convb[:], self.conv_bT[l], w=["convb"], chan="misc")
        S.dma("sp", self.dtb_bc[:], self.dt_bias[l].partition_broadcast(128), w=["dtb_bc"], chan="misc")
        S.dma("sp", self.a_bc[:], self.a_log[l].partition_broadcast(128), w=["a_bc"], chan="misc")
        S.dma("sp", self.dsk_bc[:], self.d_skip[l].partition_broadcast(128), w=["dsk_bc"], chan="misc")
        S.add("act", lambda e: e.activation(out=self.a_bc[:], in_=self.a_bc[:], func=AF.Exp), r=["a_bc"], w=["a_bc"])
        S.add("dve", lambda e: e.tensor_scalar(out=self.a_bc[:], in0=self.a_bc[:], scalar1=-1.0, scalar2=None, op0=ALU.mult),
              r=["a_bc"], w=["a_bc"])
        S.add("dve", lambda e: e.memset(self.Sst[:], 0.0), w=["Sst"])
        S.add("dve", lambda e: e.memset(self.convhist[:], 0.0), w=["convhist"])

    def ssd(self, l, g0):
        _hT = self.hT
        S = self.S
        last_grp = (g0 + T == SEQ)
        NB = T // 128
        dtT = self.alloc([NB, 32], F32)
        dA2 = self.alloc([NB, 64], F32)
        acs = self.alloc([NB, 32], F32)
        eacs = self.alloc([NB, 32], F32)
        dtw = self.alloc([NB, 32], F32)
        dec = self.alloc([NB, 32], F32)
        L1 = self.alloc([T], F32)
        acsT = self.alloc([T], F32)
        R1 = self.alloc([8, 128], F32)
        raw = self.alloc([T + 8], F32)
        acc = self.alloc([T], F32)
        xTc = self.alloc([4, T], BF16)
        BT = self.alloc([T], BF16)
        CT = self.alloc([T], BF16)
        yT = self.alloc([4, T], BF16)
        Sbf = self.alloc([512], BF16)
        Xtm = [self.alloc([512], BF16) for i in range(2)]
        Xdt = [self.alloc([512], BF16) for i in range(2)]
        Xw = [self.alloc([512], BF16) for i in range(2)]
        XD = [self.alloc([512], BF16) for i in range(2)]
        Btm = [self.alloc([128], BF16) for i in range(2)]
        CBm = [self.alloc([128], BF16) for i in range(2)]
        _Lt = self.alloc([8, 128], BF16)
        Lt = [_Lt, _Lt]
        _MG = self.alloc([8, 128], BF16)
        MG = [_MG, _MG]
        _sz = self.alloc([512], F32)
        sz = [_sz, _sz]
        _t1 = self.alloc([512], F32)
        t1 = [_t1, _t1]
        _yg = self.alloc([512], F32)
        yg = [_yg, _yg]
        nssm = self.alloc([512], F32)
        yn = [self.alloc([512], BF16) for i in range(2)]
        ss = self.alloc([4], F32)
        slot, key = self.wslot([(0, [[32, KC], [1, 32]], self.win_cols(l, 9728, 32))], "wdt")
        b0 = self.bank[0]
        for blk in range(NB):
            for kc in range(KC):
                S.add("pe", lambda e, blk=blk, kc=kc, slot=slot: e.matmul(
                    b0[:, blk * 32:(blk + 1) * 32], lhsT=_hT[:, kc, blk * 128:(blk + 1) * 128],
                    rhs=fap(slot, kc * 32, [[1, 32]]), start=(kc == 0), stop=(kc == KC - 1)), r=[key, "hT"], w=["bank0"])
        S.add("dve", lambda e: e.tensor_tensor(out=dtT[:], in0=fap(b0, 0, [[32, NB], [1, 32]]),
                                                 in1=self.dtb_bc[:].unsqueeze(1).to_broadcast([128, NB, 32]), op=ALU.add),
              r=["bank0", "dtb_bc"], w=["dtT"])
        S.add("act", lambda e: e.activation(out=dtT[:], in_=dtT[:], func=AF.Exp), r=["dtT"], w=["dtT"])
        S.add("act", lambda e: e.activation(out=dtT[:], in_=dtT[:], func=AF.Ln, bias=1.0), r=["dtT"], w=["dtT"])
        if self.stop_after == "ssd_p1":
            return
        for hf in range(2):
            S.add("dve", lambda e, hf=hf: e.tensor_tensor(
                out=dA2[:, :, hf * 32:(hf + 1) * 32], in0=dtT[:],
                in1=self.a_bc[:].unsqueeze(1).to_broadcast([128, NB, 32]), op=ALU.mult), r=["dtT", "a_bc"], w=["dA2"])
        S.add("dve", lambda e: e.memset(L1[0:32, :], 1.0), w=["L1a"])
        for b4 in range(NB // 4):
            bi = 1 + b4 % 2
            Bk = self.bank[bi]
            bk = "bank%d" % bi
            for q in range(4):
                blk = b4 * 4 + q
                S.add("pe", lambda e, Bk=Bk, q=q, blk=blk: e.matmul(
                    Bk[0:64, q * 128:(q + 1) * 128], lhsT=dA2[:, blk, :], rhs=self.tri[:, :], start=True, stop=True),
                    r=["dA2", "cf32"], w=[bk])
            S.add("act", lambda e, Bk=Bk, b4=b4: e.activation(out=acsT[0:32, b4 * 512:(b4 + 1) * 512], in_=Bk[0:32, :],
                                                               func=AF.Copy), r=[bk], w=["acsT"])
            S.add("act", lambda e, Bk=Bk, b4=b4: e.activation(out=L1[32:64, b4 * 512:(b4 + 1) * 512], in_=Bk[32:64, :],
                                                               func=AF.Copy, scale=-1.0), r=[bk], w=["L1b"])
        if self.stop_after == "ssd_p2":
            return
        b3 = self.bank[3]
        for blk in range(NB):
            S.add("pe", lambda e, blk=blk: e.matmul(b3[:, blk * 32:(blk + 1) * 32], lhsT=acsT[0:32, blk * 128:(blk + 1) * 128],
                                                    rhs=self.identS[0:32, 0:32], start=True, stop=True),
                  r=["acsT", "cf32"], w=["bank3"])
        S.add("dve", lambda e: e.tensor_copy(out=acs[:], in_=fap(b3, 0, [[32, NB], [1, 32]])), r=["bank3"], w=["acs"])
        if self.stop_after == "ssd_p3":
            return
        b4_ = self.bank[4]
        BD = self.alloc([NB, 32], F32)
        S.add("dve", lambda e: e.tensor_tensor(
            out=BD[0:32], in0=fap(acsT, 127, [[128, NB]], pn=32).unsqueeze(2).to_broadcast([32, NB, 32]),
            in1=self.identS[0:32, 0:32].unsqueeze(1).to_broadcast([32, NB, 32]), op=ALU.mult), r=["acsT", "cf32"], w=["BD"])
        S.add("pe", lambda e: e.matmul(b4_[:, :], lhsT=L1[0:32, 0:128], rhs=fap(BD, 0, [[1, NB * 32]], pn=32),
                                       start=True, stop=True), r=["BD", "L1a"], w=["bank4"])
        if self.stop_after == "ssd_p4":
            return
        S.add("act", lambda e: e.activation(out=dec[:], in_=fap(b4_, 0, [[32, NB], [1, 32]]), func=AF.Exp), r=["bank4"], w=["dec"])
        if self.stop_after == "ssd_p5":
            return
        S.add("act", lambda e: e.activation(out=dtw[:], in_=fap(b4_, 0, [[32, NB], [1, 32]]), func=AF.Copy), r=["bank4"], w=["dtw"])
        S.add("pool", lambda e: e.tensor_tensor(out=dtw[:], in0=dtw[:], in1=acs[:], op=ALU.subtract), r=["dtw", "acs"], w=["dtw"])
        if self.stop_after == "ssd_p6":
            return
        S.add("act", lambda e: e.activation(out=dtw[:], in_=dtw[:], func=AF.Exp), r=["dtw"], w=["dtw"])
        S.add("dve", lambda e: e.tensor_tensor(out=dtw[:], in0=dtw[:], in1=dtT[:], op=ALU.mult), r=["dtw", "dtT"], w=["dtw"])
        if self.stop_after == "ssd_p7":
            return
        S.add("act", lambda e: e.activation(out=eacs[:], in_=acs[:], func=AF.Exp), r=["acs"], w=["eacs"])
        if self.stop_after == "ssd_pre":
            return
        for G in range(4):
            chunks = [(6656 + G * 512 + q * 128, G * 4 + q, ("x", q)) for q in range(4)]
            chunks += [(6656 + 2048 + G * 128, 16 + G, ("B", 0)), (6656 + 2560 + G * 128, 20 + G, ("C", 0))]
            for (c0, ci, (kind, q)) in chunks:
                slot, key = self.wslot([(0, [[128, KC], [1, 128]], self.win_cols(l, c0, 128))], "wx")

                def ev(tt, Bk, bk):
                    S.add("act", lambda e: e.activation(out=raw[:, 3 + tt * 512:3 + (tt + 1) * 512], in_=Bk[:, :], func=AF.Copy),
                          r=[bk], w=["raw"])
                S.add("act", lambda e, ci=ci: e.activation(out=raw[:, 0:3], in_=self.convhist[:, ci, :], func=AF.Copy),
                      r=["convhist"], w=["raw"])
                self.proj_tiles(slot, 0, 128, key, T // 512, [5, 6], ev)
                S.add("dve", lambda e, ci=ci: e.tensor_scalar(
                    out=acc[:], in0=raw[:, 3:3 + T], scalar1=self.convw[:, ci, 3:4], scalar2=self.convb[:, ci:ci + 1],
                    op0=ALU.mult, op1=ALU.add), r=["raw", "convw", "convb"], w=["acc"])
                for j in (2, 1, 0):
                    S.add("dve", lambda e, ci=ci, j=j: e.scalar_tensor_tensor(
                        out=acc[:], in0=raw[:, j:j + T], scalar=self.convw[:, ci, j:j + 1], in1=acc[:],
                        op0=ALU.mult, op1=ALU.add), r=["raw", "convw", "acc"], w=["acc"])
                dst = xTc[:, q, :] if kind == "x" else (BT[:] if kind == "B" else CT[:])
                dk = {"x": "xTc", "B": "BT", "C": "CT"}[kind]
                S.add("act", lambda e, dst=dst: e.activation(out=dst, in_=acc[:], func=AF.Silu), r=["acc"], w=[dk])
                S.add("act", lambda e, ci=ci: e.activation(out=self.convhist[:, ci, :], in_=raw[:, T:T + 3], func=AF.Copy),
                      r=["raw"], w=["convhist"])
            if self.stop_after == "ssd_conv":
                continue
            S.dma("sp", nssm[:], self.norm_ssm[l, G * 512:(G + 1) * 512].partition_broadcast(128), w=["nssm"], chan="misc")
            S.add("dve", lambda e, G=G: e.tensor_copy(
                out=R1[32:64], in_=self.identS[32:64, 8 * G:8 * G + 8].unsqueeze(2).to_broadcast([32, 8, 128])),
                r=["cf32"], w=["R1b"])
            Sg = self.Sst[:, G * 512:(G + 1) * 512]
            S.add("act", lambda e, Sg=Sg: e.activation(out=Sbf[:], in_=Sg, func=AF.Copy), r=["Sst"], w=["Sbf"])
            wz, wzk = self.wslot([(0, [[512, KC], [1, 512]], self.win_cols(l, 4608 + G * 512, 512))], "wz")
            for c in range(NB if self.stop_after != "ssd_c1" else 1):
                i = c % 2
                tk = lambda nm: ("%s" % nm) if nm in ("Lt", "MG", "sz", "t1", "yg") else "%s%d" % (nm, i)
                tok = slice(c * 128, (c + 1) * 128)
                hs = slice(8 * G, 8 * G + 8)
                trb = self.bank_bf[7]
                for q in range(4):
                    S.add("pe", lambda e, q=q, tok=tok: e.transpose(out=trb[:, q * 128:(q + 1) * 128], in_=xTc[:, q, tok],
                                                                    identity=self.ident_bf), r=["xTc", "cbf"], w=["bank7"])
                S.add("pe", lambda e, tok=tok: e.transpose(out=trb[:, 512:640], in_=BT[:, tok], identity=self.ident_bf),
                      r=["BT", "cbf"], w=["bank7"])
                S.add("act", lambda e, i=i: e.activation(out=Xtm[i][:], in_=trb[:, 0:512], func=AF.Copy), r=["bank7"], w=[tk("Xtm")])
                S.add("act", lambda e, i=i: e.activation(out=Btm[i][:], in_=trb[:, 512:640], func=AF.Copy), r=["bank7"], w=[tk("Btm")])
                bc = lambda t_, c=c, hs=hs: t_[:, c, hs].unsqueeze(2).to_broadcast([128, 8, 64])
                x3 = lambda t_: t_[:].rearrange("p (e q) -> p e q", e=8)
                S.add("pool", lambda e, i=i, bc=bc, x3=x3: e.tensor_tensor(out=x3(Xdt[i]), in0=x3(Xtm[i]), in1=bc(dtT), op=ALU.mult),
                      r=[tk("Xtm"), "dtT"], w=[tk("Xdt")])
                S.add("pool", lambda e, i=i, bc=bc, x3=x3: e.tensor_tensor(out=x3(Xw[i]), in0=x3(Xtm[i]), in1=bc(dtw), op=ALU.mult),
                      r=[tk("Xtm"), "dtw"], w=[tk("Xw")])
                S.add("dve", lambda e, i=i, x3=x3, hs=hs: e.tensor_tensor(
                    out=x3(XD[i]), in0=x3(Xtm[i]), in1=self.dsk_bc[:, hs].unsqueeze(2).to_broadcast([128, 8, 64]), op=ALU.mult),
                    r=[tk("Xtm"), "dsk_bc"], w=[tk("XD")])
                b0_ = self.bank[0]
                S.add("pe", lambda e, tok=tok: e.matmul(b0_[:, 0:128], lhsT=BT[:, tok], rhs=CT[:, tok], start=True, stop=True),
                      r=["BT", "CT"], w=["bank0"])
                S.add("dve", lambda e, i=i: e.tensor_tensor(out=CBm[i][:], in0=b0_[:, 0:128], in1=self.tri[:, :], op=ALU.mult),
                      r=["bank0", "cf32"], w=[tk("CBm")])
                S.add("dve", lambda e, tok=tok, G=G: e.tensor_tensor(
                    out=R1[0:32], in0=acsT[0:32, tok].unsqueeze(1).to_broadcast([32, 8, 128]),
                    in1=self.identS[0:32, 8 * G:8 * G + 8].unsqueeze(2).to_broadcast([32, 8, 128]), op=ALU.mult),
                    r=["acsT", "cf32"], w=["R1a"])
                for hh in range(2):
                    Bh = self.bank[1 + hh]
                    S.add("pe", lambda e, Bh=Bh: e.matmul(Bh[:, :], lhsT=self.ident_bf, rhs=self.negm4, start=True, stop=False),
                          r=["cbf"], w=["bank%d" % (1 + hh)])
                    S.add("pe", lambda e, Bh=Bh, hh=hh, tok=tok: e.matmul(
                        Bh[:, :], lhsT=L1[0:64, tok], rhs=fap(R1, hh * 512, [[1, 512]], pn=64), start=False, stop=True),
                        r=["L1a", "L1b", "R1a", "R1b"], w=["bank%d" % (1 + hh)])
                S.add("act", lambda e, i=i: e.activation(out=fap(Lt[i], 0, [[1, 1024]]), in_=self.psall[:, 512:1536], func=AF.Exp),
                      r=["bank1", "bank2"], w=[tk("Lt")])
                S.add("dve", lambda e, i=i: e.tensor_tensor(out=MG[i][:], in0=Lt[i][:],
                                                             in1=CBm[i][:].unsqueeze(1).to_broadcast([128, 8, 128]), op=ALU.mult),
                      r=[tk("Lt"), tk("CBm")], w=[tk("MG")])
                bY, bO, bC, bZ = self.bank[3], self.bank[4], self.bank[5], self.bank[6]
                S.add("pe", lambda e, i=i: e.matmul(bY[:, :], lhsT=self.ident_bf, rhs=XD[i][:], start=True, stop=False),
                      r=["cbf", tk("XD")], w=["bank3"])
                for hd in range(8):
                    S.add("pe", lambda e, i=i, hd=hd: e.matmul(
                        bY[:, hd * 64:(hd + 1) * 64], lhsT=MG[i][:, hd, :], rhs=Xdt[i][:, hd * 64:(hd + 1) * 64],
                        start=False, stop=(hd == 7), skip_group_check=True), r=[tk("MG"), tk("Xdt")], w=["bank3"])
                S.add("pe", lambda e, tok=tok: e.matmul(bO[:, :], lhsT=CT[:, tok], rhs=Sbf[:], start=True, stop=True),
                      r=["CT", "Sbf"], w=["bank4"])
                S.add("pe", lambda e, i=i: e.matmul(bC[:, :], lhsT=Btm[i][:], rhs=Xw[i][:], start=True, stop=True),
                      r=[tk("Btm"), tk("Xw")], w=["bank5"])
                for kc in range(KC):
                    S.add("pe", lambda e, kc=kc, tok=tok, wz=wz: e.matmul(
                        bZ[:, :], lhsT=_hT[:, kc, tok], rhs=fap(wz, kc * 512, [[1, 512]]), start=(kc == 0), stop=(kc == KC - 1)),
                        r=["hT", wzk], w=["bank6"])
                S.add("act", lambda e, i=i: e.activation(out=sz[i][:], in_=bZ[:, :], func=AF.Silu), r=["bank6"], w=[tk("sz")])
                S.add("dve", lambda e, i=i, bc=bc, x3=x3: e.tensor_tensor(
                    out=x3(t1[i]), in0=fap(bO, 0, [[64, 8], [1, 64]]), in1=bc(eacs), op=ALU.mult), r=["bank4", "eacs"], w=[tk("t1")])
                S.add("dve", lambda e, i=i: e.tensor_tensor(out=t1[i][:], in0=bY[:, :], in1=t1[i][:], op=ALU.add),
                      r=["bank3", tk("t1")], w=[tk("t1")])
                S.add("pool", lambda e, i=i: e.tensor_tensor(out=yg[i][:], in0=t1[i][:], in1=sz[i][:], op=ALU.mult),
                      r=[tk("t1"), tk("sz")], w=[tk("yg")])
                S.add("act", lambda e, i=i: e.activation(out=t1[i][:], in_=yg[i][:], func=AF.Square), r=[tk("yg")], w=[tk("t1")])
                S.add("dve", lambda e, i=i: e.reduce_sum(out=ss[:, 0:1], in_=t1[i][:], axis=AX.X), r=[tk("t1")], w=["ss"])
                S.add("dve", lambda e: e.tensor_scalar(out=ss[:, 1:2], in0=ss[:, 0:1], scalar1=1.0 / 512, scalar2=EPS,
                                                        op0=ALU.mult, op1=ALU.add), r=["ss"], w=["ss1"])
                S.add("act", lambda e: e.activation(out=ss[:, 2:3], in_=ss[:, 1:2], func=AF.Sqrt), r=["ss1"], w=["ss2"])
                S.add("dve", lambda e: e.reciprocal(out=ss[:, 3:4], in_=ss[:, 2:3]), r=["ss2"], w=["ss3"])
                S.add("dve", lambda e, i=i, G=G: e.scalar_tensor_tensor(
                    out=yn[i][:], in0=yg[i][:], scalar=ss[:, 3:4], in1=nssm[:],
                    op0=ALU.mult, op1=ALU.mult), r=[tk("yg"), "ss3", "nssm"], w=[tk("yn")])
                trb2 = self.bank_bf[0]
                for q in range(4):
                    S.add("pe", lambda e, i=i, q=q: e.transpose(out=trb2[:, 512 + q * 128:512 + (q + 1) * 128],
                                                                in_=yn[i][:, q * 128:(q + 1) * 128], identity=self.ident_bf),
                          r=[tk("yn"), "cbf"], w=["bank0"])
                S.add("act", lambda e, c=c: e.activation(out=fap(yT, c * 128, [[T, 4], [1, 128]]),
                                                         in_=fap(trb2, 512, [[128, 4], [1, 128]]), func=AF.Copy),
                      r=["bank0"], w=["yT"])
                S.add("dve", lambda e, Sg=Sg, bc=bc: e.tensor_tensor(
                    out=Sg.rearrange("p (e q) -> p e q", e=8), in0=Sg.rearrange("p (e q) -> p e q", e=8), in1=bc(dec),
                    op=ALU.mult), r=["Sst", "dec"], w=["Sst"])
                S.add("dve", lambda e, Sg=Sg: e.tensor_tensor(out=Sg, in0=bC[:, :], in1=Sg, op=ALU.add), r=["Sst", "bank5"], w=["Sst"])
                S.add("act", lambda e, Sg=Sg: e.activation(out=Sbf[:], in_=Sg, func=AF.Copy), r=["Sst"], w=["Sbf"])
            S.dma("sp", self.yscr[4 * G:4 * G + 4].rearrange("c p t -> p c t"), yT[:], r=["yT"], w=["yscr"], chan="yscr")
        if last_grp:
            S.dma("sp", self.ssm_out[l], self.Sst[:], r=["Sst"], w=["ssmout"], chan="stout")
            S.dma("sp", self.conv_out[l], self.convhist[:], r=["convhist"], w=["convout"], chan="stout")
            if "stout" not in self.out_chans:
                self.out_chans.append("stout")

    def tail(self, l, g0):
        _hT = self.hT
        _sqf = self.sqf
        _attoT = self.attoT
        S = self.S
        self.norm_bufs()
        ysT = self.alloc([16, 512], BF16)
        mT = self.alloc([KC, 512], BF16)
        sga = self.alloc([512], F32)
        sgs = self.alloc([512], F32)
        ta = self.alloc([512], F32)
        tsb = self.alloc([512], F32)
        oT = self.alloc([KC, 512], F32)
        wa = self.w_br_att[l].rearrange("(k p) n -> p k n", p=128)
        ws = self.w_br_ssm[l].rearrange("(k p) n -> p k n", p=128)
        wo = self.w_out[l].rearrange("(k p) n -> p k n", p=128)
        for tt in range(T // 512):
            tsl = slice(tt * 512, (tt + 1) * 512)
            S.dma("sp", ysT[:], self.yscr[:, :, tsl].rearrange("c p t -> p c t"), r=["yscr"], w=["ysT"], chan="ysT")
            for m in range(KC):
                ms = slice(m * 128, (m + 1) * 128)
                s1, k1 = self.wslot([(0, [[128, 4], [1, 128]], wa[:, :, ms]),
                                     (512, [[128, KC], [1, 128]], self.win_cols(l, 9760 + m * 128, 128)),
                                     (1536, [[128, KC], [1, 128]], self.win_cols(l, 10784 + m * 128, 128))], "wt1")
                s2, k2 = self.wslot([(0, [[128, 16], [1, 128]], ws[:, :, ms])], "wt2")
                pa_, ps_, pga, pgs = self.bank[0], self.bank[1], self.bank[2], self.bank[3]
                for kk in range(4):
                    S.add("pe", lambda e, kk=kk, tsl=tsl, s1=s1: e.matmul(pa_[:, :], lhsT=fap(s1, kk * 128, [[1, 128]]),
                                                                   rhs=_attoT[:, kk, tsl], start=(kk == 0), stop=(kk == 3)),
                          r=[k1, "attoT"], w=["bank0"])
                for kk in range(16):
                    S.add("pe", lambda e, kk=kk, s2=s2: e.matmul(ps_[:, :], lhsT=fap(s2, kk * 128, [[1, 128]]), rhs=ysT[:, kk, :],
                                                          start=(kk == 0), stop=(kk == 15)), r=[k2, "ysT"], w=["bank1"])
                for kc in range(KC):
                    S.add("pe", lambda e, kc=kc, tsl=tsl, s1=s1: e.matmul(pga[:, :], lhsT=fap(s1, 512 + kc * 128, [[1, 128]]),
                                                                   rhs=_hT[:, kc, tsl], start=(kc == 0), stop=(kc == KC - 1)),
                          r=[k1, "hT"], w=["bank2"])
                for kc in range(KC):
                    S.add("pe", lambda e, kc=kc, tsl=tsl, s1=s1: e.matmul(pgs[:, :], lhsT=fap(s1, 1536 + kc * 128, [[1, 128]]),
                                                                   rhs=_hT[:, kc, tsl], start=(kc == 0), stop=(kc == KC - 1)),
                          r=[k1, "hT"], w=["bank3"])
                S.add("act", lambda e: e.activation(out=sga[:], in_=pga[:, :], func=AF.Sigmoid), r=["bank2"], w=["sga"])
                S.add("act", lambda e: e.activation(out=sgs[:], in_=pgs[:, :], func=AF.Sigmoid), r=["bank3"], w=["sgs"])
                S.add("dve", lambda e: e.tensor_tensor(out=ta[:], in0=pa_[:, :], in1=sga[:], op=ALU.mult), r=["bank0", "sga"], w=["ta"])
                S.add("dve", lambda e: e.tensor_tensor(out=tsb[:], in0=ps_[:, :], in1=sgs[:], op=ALU.mult), r=["bank1", "sgs"], w=["tsb"])
                S.add("pool", lambda e, m=m: e.tensor_tensor(out=mT[:, m, :], in0=ta[:], in1=tsb[:], op=ALU.add),
                      r=["ta", "tsb"], w=["mT"])
            for m2 in range(KC):
                s3, k3 = self.wslot([(0, [[128, KC], [1, 128]], wo[:, :, m2 * 128:(m2 + 1) * 128])], "wo")
                bi = 4 + m2 % 2
                Bo = self.bank[bi]
                for kc in range(KC):
                    S.add("pe", lambda e, kc=kc, Bo=Bo, s3=s3: e.matmul(Bo[:, :], lhsT=fap(s3, kc * 128, [[1, 128]]), rhs=mT[:, kc, :],
                                                                 start=(kc == 0), stop=(kc == KC - 1)), r=[k3, "mT"], w=["bank%d" % bi])
                S.add("act", lambda e, Bo=Bo, m2=m2: e.activation(out=oT[:, m2, :], in_=Bo[:, :], func=AF.Copy), r=["bank%d" % bi], w=["oT"])
                S.add("act", lambda e, Bo=Bo, m2=m2: e.activation(out=_sqf[:, m2, :], in_=Bo[:, :], func=AF.Square),
                      r=["bank%d" % bi], w=["sqo"])
            self.postnorm_residual(oT[:], lambda c: _sqf[:, c, :], 1, self.xres, self.xres, g0 + tt * 512, ["oT", "sqo"])

    def s_declare(self):
        L = self.depth
        self.xsT_in = self.din("xsT_in", [128, KC, NS])
        self.scst_in = self.din("scst_in", [128, 1172])
        self.rope_s = self.din("rope_s", [NS, 256])
        self.cache = [self.din("cache%d" % g, [L, NS, WIN[g], 1024]) for g in range(3)]
        self.st_ssm = self.din("st_ssm", [L, NS, 2048, 128])
        self.st_conv = self.din("st_conv", [L, NS, 3, 3072])
        self.convw_rep = self.din("convw_rep", [L, NS, 4, 3072])
        self.convb_rep = self.din("convb_rep", [L, NS, 3072])
        self.dtb_rep = self.din("dtb_rep", [L, NS, 32])
        self.alog_rep = self.din("alog_rep", [L, NS, 32])
        self.dsk_rep = self.din("dsk_rep", [L, NS, 32])
        self.nssm_rep = self.din("nssm_rep", [L, NS, 2048])
        self.ys_out = self.dout("ys_out", [128, KC, NS])
        self.kvs_out = [self.dout("kvs_out%d" % g, [L, NS, 2, 512]) for g in range(3)]
        self.ssms_out = self.dout("ssms_out", [L, NS, 2048, 128])
        self.convs_out = self.dout("convs_out", [L, NS, 3, 3072])

    def s_consts(self):
        S = self.S
        self.xsT = S.sb("xsT", [128, KC, NS], F32)
        S.dma("sp", self.xsT[:], self.xsT_in, w=["xsT"], chan="misc")
        self.ones_f = S.sb("ones_f", [128, 4], F32)
        S.add("pool", lambda e: e.memset(self.ones_f[:], 1.0), w=["ones_f"])

    def s_load_consts(self):
        S = self.S
        self.scst = self.alloc([1172], F32)
        S.dma("sp", self.scst, self.scst_in, w=["scst"], chan="s_ld")
        self.identF = self.scst[:, 0:128]
        self.selB = lambda b: self.scst[0:NS, 128 + b * 128:128 + (b + 1) * 128]
        self.selcol = lambda b: self.scst[0:4, 640 + b * NS:640 + (b + 1) * NS]
        self.I4 = self.scst[0:4, 656:660]
        self.bdmask = self.scst[0:4, 660:1172]
        self.ropes = self.alloc([256], F32)
        S.dma("sp", self.ropes[0:NS, :], self.rope_s, w=["ropes"], chan="s_ld")

    def s_rstd(self, ps_ap, denom, rs, rkey):
        S = self.S
        S.add("act", lambda e: e.activation(out=rs, in_=ps_ap, func=AF.Copy), r=[rkey], w=["s_rs"])
        S.add("dve", lambda e: e.tensor_scalar(out=rs, in0=rs, scalar1=1.0 / denom, scalar2=EPS, op0=ALU.mult, op1=ALU.add),
              r=["s_rs"], w=["s_rs"])
        S.add("act", lambda e: e.activation(out=rs, in_=rs, func=AF.Sqrt), r=["s_rs"], w=["s_rs"])
        S.add("dve", lambda e: e.reciprocal(out=rs, in_=rs), r=["s_rs"], w=["s_rs"])

    def s_norm_stat(self, src3, srckeys):
        S = self.S
        sq = self.alloc([KC * NS], BF16)
        rs = self.alloc([NS], F32)
        st = self.bank[6]
        S.add("act", lambda e: e.activation(out=sq.rearrange("p (c n) -> p c n", n=NS), in_=src3, func=AF.Square),
              r=list(srckeys), w=["s_sq"])
        for c in range(KC):
            S.add("pe", lambda e, c=c: e.matmul(st[:, 0:NS], lhsT=self.ones_bf[:, :], rhs=sq[:, c * NS:(c + 1) * NS],
                                                  start=(c == 0), stop=(c == KC - 1)), r=["s_sq", "ones_bf"], w=["bank6"])
        self.s_rstd(st[:, 0:NS], D, rs, "bank6")
        return rs

    def s_prenorm(self, k):
        S = self.S
        rs = self.s_norm_stat(self.xsT[:], ["xsT"])
        tmp = self.alloc([KC * NS], F32)
        tmp3 = tmp.rearrange("p (c n) -> p c n", n=NS)
        hs = self.alloc([KC * NS], BF16)
        hs3 = hs.rearrange("p (c n) -> p c n", n=NS)
        S.add("dve", lambda e: e.tensor_tensor(out=tmp3, in0=self.xsT[:], in1=rs.unsqueeze(1).to_broadcast([128, KC, NS]),
                                                 op=ALU.mult), r=["xsT", "s_rs"], w=["s_tmp"])
        S.add("dve", lambda e: e.tensor_tensor(out=tmp3, in0=tmp3, in1=self.Aall[:, k, :, 1:NCOL], op=ALU.mult),
              r=["s_tmp", "Aall"], w=["s_tmp"])
        S.add("pool", lambda e: e.tensor_tensor(out=hs3, in0=tmp3, in1=self.modT[:, k * 24:k * 24 + 8, 1:NCOL], op=ALU.add),
              r=["s_tmp", "modT"], w=["s_hs"])
        return hs3

    def s_postnorm(self, f3, fkeys, k):
        S = self.S
        rs = self.s_norm_stat(f3, fkeys)
        tmp = self.alloc([KC * NS], F32)
        tmp3 = tmp.rearrange("p (c n) -> p c n", n=NS)
        S.add("dve", lambda e: e.tensor_tensor(out=tmp3, in0=f3, in1=rs.unsqueeze(1).to_broadcast([128, KC, NS]),
                                                 op=ALU.mult), r=list(fkeys) + ["s_rs"], w=["s_tmp2"])
        S.add("dve", lambda e: e.tensor_tensor(out=tmp3, in0=tmp3, in1=self.Gall[:, k, :, 1:NCOL], op=ALU.mult),
              r=["s_tmp2", "Gall"], w=["s_tmp2"])
        S.add("pool", lambda e: e.tensor_tensor(out=self.xsT[:], in0=self.xsT[:], in1=tmp3, op=ALU.add),
              r=["s_tmp2", "xsT"], w=["xsT"])

    def s_ffn(self, l, i, k):
        S = self.S
        self.phase(0)
        hs3 = self.s_prenorm(k)
        w1 = self.w_ff_in[l, i].rearrange("(kc p) n -> p kc n", p=128)
        w2 = self.w_ff_out[l, i].rearrange("(j p) n -> p j n", p=128)
        bu = self.bank[0]
        for jb in range(NJ // 2):
            slot, key = self.wslot([
                (0, [[256, KC], [1, 256]], w1[:, :, jb * 256:(jb + 1) * 256]),
                (2048, [[256, KC], [1, 256]], w1[:, :, DFF + jb * 256:DFF + (jb + 1) * 256])], "w1s")
            for jj in range(2):
                j = jb * 2 + jj
                for half in range(2):
                    col = (half * NJ + j) * NS
                    for kc in range(KC):
                        S.add("pe", lambda e, kc=kc, jj=jj, half=half, col=col, slot=slot: e.matmul(
                            bu[:, col:col + NS], lhsT=fap(slot, half * 2048 + kc * 256 + jj * 128, [[1, 128]]),
                            rhs=hs3[:, kc, :], start=(kc == 0), stop=(kc == KC - 1)), r=[key, "s_hs"], w=["bank0"])
        sg = self.alloc([NJ * NS], F32)
        up = self.alloc([NJ * NS], F32)
        aT = self.alloc([NJ * NS], BF16)
        S.add("act", lambda e: e.activation(out=sg, in_=bu[:, 0:NJ * NS], func=AF.Silu), r=["bank0"], w=["s_sg"])
        S.add("act", lambda e: e.activation(out=up, in_=bu[:, NJ * NS:2 * NJ * NS], func=AF.Copy), r=["bank0"], w=["s_up"])
        S.add("dve", lambda e: e.tensor_tensor(out=aT, in0=sg, in1=up, op=ALU.mult), r=["s_sg", "s_up"], w=["s_aT"])
        bf_ = self.bank[1]
        for m in range(KC):
            slot, key = self.wslot([(0, [[128, NJ], [1, 128]], w2[:, :, m * 128:(m + 1) * 128])], "w2s")
            for j in range(NJ):
                S.add("pe", lambda e, j=j, m=m, slot=slot: e.matmul(
                    bf_[:, m * NS:(m + 1) * NS], lhsT=fap(slot, j * 128, [[1, 128]]), rhs=aT[:, j * NS:(j + 1) * NS],
                    start=(j == 0), stop=(j == NJ - 1)), r=[key, "s_aT"], w=["bank1"])
        fT = self.alloc([KC * NS], F32)
        S.add("act", lambda e: e.activation(out=fT, in_=bf_[:, 0:KC * NS], func=AF.Copy), r=["bank1"], w=["s_fT"])
        self.s_postnorm(fT.rearrange("p (c n) -> p c n", n=NS), ["s_fT"], k)

    def s_transpose_to_fm(self, tok_ap, nch, out_tile, bank_i, rkeys, okey):
        S = self.S
        Bk = self.bank[bank_i]
        bk = "bank%d" % bank_i
        for c in range(nch):
            S.add("pe", lambda e, c=c: e.matmul(Bk[:, c * NS:(c + 1) * NS], lhsT=tok_ap[:, c * 128:(c + 1) * 128],
                                                  rhs=self.I4[0:NS, 0:NS], start=True, stop=True),
                  r=list(rkeys) + ["scst"], w=[bk])
        S.add("act", lambda e: e.activation(out=out_tile, in_=Bk[:, 0:nch * NS], func=AF.Copy), r=[bk], w=[okey])

    def s_mixer(self, l):
        S = self.S
        self.phase(0)
        hs3 = self.s_prenorm(1)
        U = self.alloc([INC], F32)
        Us = lambda a, n: U[0:NS, a:a + n]
        abr = self.alloc([D], F32)
        self.s_load_consts()
        mark = self.aptr
        nblk = (INC + 511) // 512
        for blk in range(nblk):
            c0 = blk * 512
            n = min(512, INC - c0)
            slot, key = self.wslot([(0, [[n, KC], [1, n]], self.win_cols(l, c0, n))], "wins")
            bi = blk % 2
            Bk = self.bank[bi]
            bk = "bank%d" % bi
            for kc in range(KC):
                S.add("pe", lambda e, kc=kc, n=n, Bk=Bk, slot=slot: e.matmul(
                    Bk[0:NS, 0:n], lhsT=hs3[:, kc, :], rhs=fap(slot, kc * n, [[1, n]]), start=(kc == 0), stop=(kc == KC - 1)),
                    r=[key, "s_hs"], w=[bk])
            S.add("act", lambda e, c0=c0, n=n, Bk=Bk: e.activation(out=Us(c0, n), in_=Bk[0:NS, 0:n], func=AF.Copy),
                  r=[bk], w=["U"])
        QK = self.alloc([3 * 1024], F32)
        t1 = self.alloc([1024], F32)
        t2 = self.alloc([1024], F32)
        v3 = lambda ap, d=128: ap.rearrange("p (a d) -> p a d", d=d)
        cosF = self.ropes[0:NS, 0:128]
        sinS = self.ropes[0:NS, 128:256]
        for g in range(3):
            x = v3(Us(g * 1536, 1024))
            S.add("dve", lambda e, x=x: e.tensor_tensor(out=v3(t1[0:NS, :]), in0=x, in1=cosF.unsqueeze(1).to_broadcast([NS, 8, 128]),
                                                         op=ALU.mult), r=["U", "ropes"], w=["s_t1"])
            S.add("pool", lambda e, x=x: e.tensor_tensor(out=v3(t2[0:NS, :])[:, :, 0:64], in0=x[:, :, 64:128],
                                                          in1=sinS[:, 0:64].unsqueeze(1).to_broadcast([NS, 8, 64]), op=ALU.mult),
                  r=["U", "ropes"], w=["s_t2a"])
            S.add("pool", lambda e, x=x: e.tensor_tensor(out=v3(t2[0:NS, :])[:, :, 64:128], in0=x[:, :, 0:64],
                                                          in1=sinS[:, 64:128].unsqueeze(1).to_broadcast([NS, 8, 64]), op=ALU.mult),
                  r=["U", "ropes"], w=["s_t2b"])
            S.add("dve", lambda e, g=g: e.tensor_tensor(out=QK[0:NS, g * 1024:(g + 1) * 1024], in0=t1[0:NS, :], in1=t2[0:NS, :],
                                                         op=ALU.add), r=["s_t1", "s_t2a", "s_t2b"], w=["QK"])
            S.dma("sp", self.kvs_out[g][l, :, 0, :], QK[0:NS, g * 1024 + 512:(g + 1) * 1024], r=["QK"], w=["kvs_o"], chan="s_out")
            S.dma("sp", self.kvs_out[g][l, :, 1, :], Us(g * 1536 + 1024, 512), r=["U"], w=["kvs_o"], chan="s_out")
        if "s_out" not in self.out_chans:
            self.out_chans.append("s_out")
        P0 = self.alloc([512], F32)
        s0 = self.alloc([12], F32)
        p0 = self.alloc([12], F32)
        N0 = self.alloc([512], F32)
        D0 = self.alloc([4], F32)
        for g in range(3):
            S.add("pool", lambda e, g=g: e.tensor_tensor(out=P0[0:NS, :], in0=QK[0:NS, g * 1024:g * 1024 + 512],
                                                          in1=QK[0:NS, g * 1024 + 512:(g + 1) * 1024], op=ALU.mult),
                  r=["QK"], w=["s_P0"])
            S.add("dve", lambda e, g=g: e.reduce_sum(out=s0[0:NS, g * 4:(g + 1) * 4], in_=v3(P0[0:NS, :]), axis=AX.X),
                  r=["s_P0"], w=["s_s0"])
        S.add("act", lambda e: e.activation(out=p0[0:NS, :], in_=s0[0:NS, :], func=AF.Exp, scale=float(SCALE)), r=["s_s0"], w=["s_p0"])
        for g in range(3):
            vg = v3(Us(g * 1536 + 1024, 512))
            pb = p0[0:NS, g * 4:(g + 1) * 4].unsqueeze(2).to_broadcast([NS, 4, 128])
            if g == 0:
                S.add("dve", lambda e, vg=vg, pb=pb: e.tensor_tensor(out=v3(N0[0:NS, :]), in0=vg, in1=pb, op=ALU.mult),
                      r=["U", "s_p0"], w=["s_N0"])
            else:
                S.add("pool", lambda e, vg=vg, pb=pb: e.tensor_tensor(out=v3(P0[0:NS, :]), in0=vg, in1=pb, op=ALU.mult),
                      r=["U", "s_p0"], w=["s_P0"])
                S.add("dve", lambda e: e.tensor_tensor(out=N0[0:NS, :], in0=N0[0:NS, :], in1=P0[0:NS, :], op=ALU.add),
                      r=["s_N0", "s_P0"], w=["s_N0"])
        S.add("dve", lambda e: e.tensor_tensor(out=D0[0:NS, :], in0=p0[0:NS, 0:4], in1=p0[0:NS, 4:8], op=ALU.add), r=["s_p0"], w=["s_D0"])
        S.add("dve", lambda e: e.tensor_tensor(out=D0[0:NS, :], in0=D0[0:NS, :], in1=p0[0:NS, 8:12], op=ALU.add), r=["s_p0", "s_D0"], w=["s_D0"])
        KV = [self.alloc([1024], F32) for i in range(2)]
        prod = self.alloc([512], F32)
        sc = self.alloc([4], F32)
        pp = [self.alloc([4], F32) for i in range(2)]
        numS = self.alloc([NS * 512], F32)
        denS = self.alloc([NS], F32)
        cnt = 0
        for b in range(NS):
            nb_, db_ = 4 + 2 * (b % 2), 5 + 2 * (b % 2)
            numP, denP = self.bank[nb_], self.bank[db_]
            for g in range(3):
                i = cnt % 2
                cnt += 1
                kv = KV[i]
                kvk = "s_KV%d" % i
                src = self.cache[g][l, b].rearrange("(j d) c -> j d c", d=DIL[g])[:, 0, :]
                S.dma("sp", kv, src, w=[kvk], chan=kvk)
                qi = 2 + i
                qP = self.bank[qi]
                S.add("pe", lambda e, b=b, g=g, qP=qP: e.matmul(qP[:, :], lhsT=self.selB(b), rhs=QK[0:NS, g * 1024:g * 1024 + 512],
                                                                start=True, stop=True), r=["QK", "scst"], w=["bank%d" % qi])
                S.add("dve", lambda e, kv=kv, qP=qP: e.tensor_tensor(out=prod, in0=kv[:, 0:512], in1=qP[:, :], op=ALU.mult),
                      r=[kvk, "bank%d" % qi], w=["s_prod"])
                S.add("dve", lambda e: e.reduce_sum(out=sc, in_=v3(prod), axis=AX.X), r=["s_prod"], w=["s_sc"])
                p_ = pp[i]
                pk = "s_pp%d" % i
                S.add("act", lambda e, p_=p_: e.activation(out=p_, in_=sc, func=AF.Exp, scale=float(SCALE)), r=["s_sc"], w=[pk])
                S.add("pe", lambda e, p_=p_, kv=kv, g=g, numP=numP: e.matmul(numP[0:4, :], lhsT=p_, rhs=kv[:, 512:1024],
                                                                            start=(g == 0), stop=(g == 2)),
                      r=[pk, kvk], w=["bank%d" % nb_])
                S.add("pe", lambda e, p_=p_, g=g, denP=denP: e.matmul(denP[0:4, 0:1], lhsT=p_, rhs=self.ones_f[:, 0:1],
                                                                     start=(g == 0), stop=(g == 2)),
                      r=[pk, "ones_f"], w=["bank%d" % db_])
            S.add("act", lambda e, b=b, numP=numP: e.activation(out=numS[0:4, b * 512:(b + 1) * 512], in_=numP[0:4, :], func=AF.Copy),
                  r=["bank%d" % nb_], w=["s_numS"])
            S.add("act", lambda e, b=b, denP=denP: e.activation(out=denS[0:4, b:b + 1], in_=denP[0:4, 0:1], func=AF.Copy),
                  r=["bank%d" % db_], w=["s_denS"])
        S.add("pool", lambda e: e.tensor_tensor(out=numS[0:4, :].rearrange("p (b n) -> p b n", n=512),
                                                 in0=numS[0:4, :].rearrange("p (b n) -> p b n", n=512),
                                                 in1=self.bdmask.unsqueeze(1).to_broadcast([4, NS, 512]), op=ALU.mult),
              r=["s_numS", "scst"], w=["s_numS"])
        ncP, dcP = self.bank[0], self.bank[1]
        for b in range(NS):
            S.add("pe", lambda e, b=b: e.matmul(ncP[0:NS, :], lhsT=self.selcol(b), rhs=numS[0:4, b * 512:(b + 1) * 512],
                                                  start=(b == 0), stop=(b == NS - 1)), r=["s_numS", "scst"], w=["bank0"])
        S.add("pe", lambda e: e.matmul(dcP[0:NS, 0:4], lhsT=denS[0:4, 0:NS], rhs=self.I4, start=True, stop=True),
              r=["s_denS", "scst"], w=["bank1"])
        Dt = self.alloc([4], F32)
        Nt = self.alloc([512], F32)
        atto = self.alloc([512], F32)
        S.add("act", lambda e: e.activation(out=Dt[0:NS, :], in_=dcP[0:NS, 0:4], func=AF.Copy), r=["bank1"], w=["s_Dt"])
        S.add("dve", lambda e: e.tensor_tensor(out=Dt[0:NS, :], in0=Dt[0:NS, :], in1=D0[0:NS, :], op=ALU.add), r=["s_Dt", "s_D0"], w=["s_Dt"])
        S.add("dve", lambda e: e.reciprocal(out=Dt[0:NS, :], in_=Dt[0:NS, :]), r=["s_Dt"], w=["s_Dt"])
        S.add("act", lambda e: e.activation(out=Nt[0:NS, :], in_=ncP[0:NS, :], func=AF.Copy), r=["bank0"], w=["s_Nt"])
        S.add("dve", lambda e: e.tensor_tensor(out=Nt[0:NS, :], in0=Nt[0:NS, :], in1=N0[0:NS, :], op=ALU.add), r=["s_Nt", "s_N0"], w=["s_Nt"])
        S.add("dve", lambda e: e.tensor_tensor(out=v3(atto[0:NS, :]), in0=v3(Nt[0:NS, :]),
                                                 in1=Dt[0:NS, :].unsqueeze(2).to_broadcast([NS, 4, 128]), op=ALU.mult),
              r=["s_Nt", "s_Dt"], w=["s_atto"])
        attoT = self.alloc([4 * NS], BF16)
        self.s_transpose_to_fm(atto[0:NS, :], 4, attoT, 2, ["s_atto"], "s_attoT")
        wa = self.w_br_att[l].rearrange("(k p) n -> p k n", p=128)
        for cb in range(2):
            slot, key = self.wslot([(0, [[512, 4], [1, 512]], wa[:, :, cb * 512:(cb + 1) * 512])], "was")
            Bk = self.bank[3]
            for kk in range(4):
                S.add("pe", lambda e, kk=kk, slot=slot: e.matmul(Bk[0:NS, :], lhsT=attoT[:, kk * NS:(kk + 1) * NS],
                                                                 rhs=fap(slot, kk * 512, [[1, 512]]), start=(kk == 0), stop=(kk == 3)),
                      r=[key, "s_attoT"], w=["bank3"])
            S.add("act", lambda e, cb=cb: e.activation(out=abr[0:NS, cb * 512:(cb + 1) * 512], in_=Bk[0:NS, :], func=AF.Copy),
                  r=["bank3"], w=["s_abr"])
        self.phase(mark)
        xc = self.alloc([3072], F32)
        cw = self.alloc([4 * 512], F32)
        cs = self.alloc([3 * 512], F32)
        cbi = self.alloc([512], F32)
        acc = self.alloc([512], F32)
        ct = self.alloc([512], F32)
        for q in range(6):
            c0 = q * 512
            S.dma("sp", cw[0:NS, :].rearrange("p (j n) -> p j n", n=512), self.convw_rep[l, :, :, c0:c0 + 512], w=["s_cw"], chan="s_ld")
            S.dma("sp", cs[0:NS, :].rearrange("p (j n) -> p j n", n=512), self.st_conv[l, :, :, c0:c0 + 512], w=["s_cs"], chan="s_ld")
            S.dma("sp", cbi[0:NS, :], self.convb_rep[l, :, c0:c0 + 512], w=["s_cb"], chan="s_ld")
            S.dma("sp", self.convs_out[l, :, 0:2, c0:c0 + 512], cs[0:NS, 512:1536].rearrange("p (j n) -> p j n", n=512),
                  r=["s_cs"], w=["convs_o"], chan="s_out")
            S.dma("sp", self.convs_out[l, :, 2, c0:c0 + 512], Us(6656 + c0, 512), r=["U"], w=["convs_o"], chan="s_out")
            S.add("dve", lambda e, c0=c0: e.tensor_tensor(out=acc[0:NS, :], in0=Us(6656 + c0, 512), in1=cw[0:NS, 1536:2048], op=ALU.mult),
                  r=["U", "s_cw"], w=["s_acc"])
            S.add("dve", lambda e: e.tensor_tensor(out=acc[0:NS, :], in0=acc[0:NS, :], in1=cbi[0:NS, :], op=ALU.add),
                  r=["s_acc", "s_cb"], w=["s_acc"])
            for j in range(3):
                S.add("pool", lambda e, j=j: e.tensor_tensor(out=ct[0:NS, :], in0=cs[0:NS, j * 512:(j + 1) * 512],
                                                              in1=cw[0:NS, j * 512:(j + 1) * 512], op=ALU.mult),
                      r=["s_cs", "s_cw"], w=["s_ct"])
                S.add("dve", lambda e: e.tensor_tensor(out=acc[0:NS, :], in0=acc[0:NS, :], in1=ct[0:NS, :], op=ALU.add),
                      r=["s_acc", "s_ct"], w=["s_acc"])
            S.add("act", lambda e, c0=c0: e.activation(out=xc[0:NS, c0:c0 + 512], in_=acc[0:NS, :], func=AF.Silu), r=["s_acc"], w=["s_xc"])
        dt = self.alloc([32], F32)
        av = self.alloc([32], F32)
        dA = self.alloc([32], F32)
        dsk = self.alloc([32], F32)
        S.dma("sp", dt[0:NS, :], self.dtb_rep[l], w=["s_dt"], chan="s_ld")
        S.dma("sp", av[0:NS, :], self.alog_rep[l], w=["s_av"], chan="s_ld")
        S.dma("sp", dsk[0:NS, :], self.dsk_rep[l], w=["s_dsk"], chan="s_ld")
        S.add("dve", lambda e: e.tensor_tensor(out=dt[0:NS, :], in0=dt[0:NS, :], in1=Us(9728, 32), op=ALU.add), r=["s_dt", "U"], w=["s_dt"])
        S.add("act", lambda e: e.activation(out=dt[0:NS, :], in_=dt[0:NS, :], func=AF.Exp), r=["s_dt"], w=["s_dt"])
        S.add("act", lambda e: e.activation(out=dt[0:NS, :], in_=dt[0:NS, :], func=AF.Ln, bias=1.0), r=["s_dt"], w=["s_dt"])
        S.add("act", lambda e: e.activation(out=av[0:NS, :], in_=av[0:NS, :], func=AF.Exp), r=["s_av"], w=["s_av"])
        S.add("dve", lambda e: e.tensor_tensor(out=dA[0:NS, :], in0=dt[0:NS, :], in1=av[0:NS, :], op=ALU.mult), r=["s_dt", "s_av"], w=["s_dA"])
        S.add("act", lambda e: e.activation(out=dA[0:NS, :], in_=dA[0:NS, :], func=AF.Exp, scale=-1.0), r=["s_dA"], w=["s_dA"])
        xdt = self.alloc([2048], F32)
        dAe = self.alloc([2048], F32)
        v64 = lambda ap: ap.rearrange("p (a d) -> p a d", d=64)
        S.add("dve", lambda e: e.tensor_tensor(out=v64(xdt[0:NS, :]), in0=v64(xc[0:NS, 0:2048]),
                                                 in1=dt[0:NS, :].unsqueeze(2).to_broadcast([NS, 32, 64]), op=ALU.mult),
              r=["s_xc", "s_dt"], w=["s_xdt"])
        S.add("pool", lambda e: e.tensor_copy(out=v64(dAe[0:NS, :]), in_=dA[0:NS, :].unsqueeze(2).to_broadcast([NS, 32, 64])),
              r=["s_dA"], w=["s_dAe"])
        xdtT = self.alloc([16 * NS], F32)
        dAT = self.alloc([16 * NS], F32)
        self.s_transpose_to_fm(xdt[0:NS, :], 16, xdtT, 2, ["s_xdt"], "s_xdtT")
        self.s_transpose_to_fm(dAe[0:NS, :], 16, dAT, 3, ["s_dAe"], "s_dAT")
        Hb = [self.alloc([2048], F32) for i in range(2)]
        outer = self.alloc([2048], F32)
        Bbc = self.alloc([512], F32)
        Cbc = self.alloc([512], F32)
        yTa = self.alloc([NS * 16], F32)
        xdT3 = xdtT.rearrange("p (c n) -> p c n", n=NS)
        dAT3 = dAT.rearrange("p (c n) -> p c n", n=NS)
        for b in range(NS):
            H = Hb[b % 2]
            hk = "s_H%d" % (b % 2)
            H3 = v3(H)
            S.dma("sp", H3, self.st_ssm[l, b].rearrange("(c p) n -> p c n", p=128), w=[hk], chan=hk)
            S.add("pe", lambda e, b=b: e.matmul(self.bank[4][:, :], lhsT=self.selB(b), rhs=xc[0:NS, 2048:2560], start=True, stop=True),
                  r=["s_xc", "scst"], w=["bank4"])
            S.add("pe", lambda e, b=b: e.matmul(self.bank[5][:, :], lhsT=self.selB(b), rhs=xc[0:NS, 2560:3072], start=True, stop=True),
                  r=["s_xc", "scst"], w=["bank5"])
            S.add("act", lambda e: e.activation(out=Bbc, in_=self.bank[4][:, :], func=AF.Copy), r=["bank4"], w=["s_Bbc"])
            S.add("act", lambda e: e.activation(out=Cbc, in_=self.bank[5][:, :], func=AF.Copy), r=["bank5"], w=["s_Cbc"])
            S.add("dve", lambda e, b=b, H3=H3: e.tensor_tensor(out=H3, in0=H3, in1=dAT3[:, :, b:b + 1].to_broadcast([128, 16, 128]),
                                                                op=ALU.mult), r=[hk, "s_dAT"], w=[hk])
            for G in range(4):
                S.add("pool", lambda e, b=b, G=G: e.tensor_tensor(
                    out=v3(outer)[:, 4 * G:4 * G + 4, :], in0=xdT3[:, 4 * G:4 * G + 4, b:b + 1].to_broadcast([128, 4, 128]),
                    in1=Bbc[:, G * 128:(G + 1) * 128].unsqueeze(1).to_broadcast([128, 4, 128]), op=ALU.mult),
                    r=["s_xdtT", "s_Bbc"], w=["s_outer"])
            S.add("dve", lambda e, H=H: e.tensor_tensor(out=H, in0=H, in1=outer, op=ALU.add), r=[hk, "s_outer"], w=[hk])
            S.dma("sp", self.ssms_out[l, b].rearrange("(c p) n -> p c n", p=128), H3, r=[hk], w=["ssms_o"], chan="s_out")
            for G in range(4):
                S.add("pool", lambda e, G=G, H3=H3: e.tensor_tensor(
                    out=v3(outer)[:, 4 * G:4 * G + 4, :], in0=H3[:, 4 * G:4 * G + 4, :],
                    in1=Cbc[:, G * 128:(G + 1) * 128].unsqueeze(1).to_broadcast([128, 4, 128]), op=ALU.mult),
                    r=[hk, "s_Cbc"], w=["s_outer"])
            S.add("dve", lambda e, b=b: e.reduce_sum(out=yTa[:, b * 16:(b + 1) * 16], in_=v3(outer), axis=AX.X),
                  r=["s_outer"], w=["s_yTa"])
        ytok = self.alloc([2048], F32)
        for c in range(16):
            bi = 4 + c // 4
            Bk = self.bank[bi]
            S.add("pe", lambda e, c=c, Bk=Bk: e.matmul(Bk[0:NS, (c % 4) * 128:(c % 4 + 1) * 128],
                                                       lhsT=fap(yTa, c, [[16, NS]]), rhs=self.identF, start=True, stop=True),
                  r=["s_yTa", "scst"], w=["bank%d" % bi])
            if c % 4 == 3:
                S.add("act", lambda e, bi=bi, Bk=Bk: e.activation(out=ytok[0:NS, (bi - 4) * 512:(bi - 3) * 512], in_=Bk[0:NS, :], func=AF.Copy),
                      r=["bank%d" % bi], w=["s_ytok"])
        yt = ytok[0:NS, :]
        tq = self.alloc([2048], F32)
        tqs = tq[0:NS, :]
        S.add("pool", lambda e: e.tensor_tensor(out=v64(tqs), in0=v64(xc[0:NS, 0:2048]), in1=dsk[0:NS, :].unsqueeze(2).to_broadcast([NS, 32, 64]),
                                                 op=ALU.mult), r=["s_xc", "s_dsk"], w=["s_tq"])
        S.add("dve", lambda e: e.tensor_tensor(out=yt, in0=yt, in1=tqs, op=ALU.add), r=["s_ytok", "s_tq"], w=["s_ytok"])
        S.add("act", lambda e: e.activation(out=tqs, in_=Us(4608, 2048), func=AF.Silu), r=["U", "s_ytok"], w=["s_tq"])
        S.add("dve", lambda e: e.tensor_tensor(out=yt, in0=yt, in1=tqs, op=ALU.mult), r=["s_ytok", "s_tq"], w=["s_ytok"])
        S.add("act", lambda e: e.activation(out=tqs, in_=yt, func=AF.Square), r=["s_ytok"], w=["s_tq"])
        ss = self.alloc([4], F32)
        sss = ss[0:NS, :]
        v512 = lambda ap: ap.rearrange("p (a d) -> p a d", d=512)
        S.add("dve", lambda e: e.reduce_sum(out=sss, in_=v512(tqs), axis=AX.X), r=["s_tq"], w=["s_ss"])
        S.add("dve", lambda e: e.tensor_scalar(out=sss, in0=sss, scalar1=1.0 / 512, scalar2=EPS, op0=ALU.mult, op1=ALU.add),
              r=["s_ss"], w=["s_ss"])
        S.add("act", lambda e: e.activation(out=sss, in_=sss, func=AF.Sqrt), r=["s_ss"], w=["s_ss"])
        S.add("dve", lambda e: e.reciprocal(out=sss, in_=sss), r=["s_ss"], w=["s_ss"])
        S.add("dve", lambda e: e.tensor_tensor(out=v512(yt), in0=v512(yt), in1=sss.unsqueeze(2).to_broadcast([NS, 4, 512]), op=ALU.mult),
              r=["s_ytok", "s_ss"], w=["s_ytok"])
        S.dma("sp", tqs, self.nssm_rep[l], r=["s_ss"], w=["s_tq"], chan="s_ld")
        S.add("dve", lambda e: e.tensor_tensor(out=yt, in0=yt, in1=tqs, op=ALU.mult), r=["s_ytok", "s_tq"], w=["s_ytok"])
        ysT = self.alloc([16 * NS], BF16)
        self.s_transpose_to_fm(yt, 16, ysT, 2, ["s_ytok"], "s_ysT")
        sbr = self.alloc([D], F32)
        ws = self.w_br_ssm[l].rearrange("(k p) n -> p k n", p=128)
        for cb in range(4):
            slot, key = self.wslot([(0, [[256, 16], [1, 256]], ws[:, :, cb * 256:(cb + 1) * 256])], "wss")
            Bk = self.bank[3]
            for kk in range(16):
                S.add("pe", lambda e, kk=kk, slot=slot: e.matmul(Bk[0:NS, 0:256], lhsT=ysT[:, kk * NS:(kk + 1) * NS],
                                                                 rhs=fap(slot, kk * 256, [[1, 256]]), start=(kk == 0), stop=(kk == 15)),
                      r=[key, "s_ysT"], w=["bank3"])
            S.add("act", lambda e, cb=cb: e.activation(out=sbr[0:NS, cb * 256:(cb + 1) * 256], in_=Bk[0:NS, 0:256], func=AF.Copy),
                  r=["bank3"], w=["s_sbr"])
        sga = self.alloc([D], F32)
        sgs = self.alloc([D], F32)
        S.add("act", lambda e: e.activation(out=sga[0:NS, :], in_=Us(9760, D), func=AF.Sigmoid), r=["U"], w=["s_sga"])
        S.add("act", lambda e: e.activation(out=sgs[0:NS, :], in_=Us(10784, D), func=AF.Sigmoid), r=["U"], w=["s_sgs"])
        S.add("dve", lambda e: e.tensor_tensor(out=sga[0:NS, :], in0=sga[0:NS, :], in1=abr[0:NS, :], op=ALU.mult), r=["s_sga", "s_abr"], w=["s_sga"])
        S.add("pool", lambda e: e.tensor_tensor(out=sgs[0:NS, :], in0=sgs[0:NS, :], in1=sbr[0:NS, :], op=ALU.mult), r=["s_sgs", "s_sbr"], w=["s_sgs"])
        S.add("dve", lambda e: e.tensor_tensor(out=sga[0:NS, :], in0=sga[0:NS, :], in1=sgs[0:NS, :], op=ALU.add), r=["s_sga", "s_sgs"], w=["s_sga"])
        mT = self.alloc([KC * NS], BF16)
        self.s_transpose_to_fm(sga[0:NS, :], KC, mT, 2, ["s_sga"], "s_mT")
        wo = self.w_out[l].rearrange("(k p) n -> p k n", p=128)
        Bo = self.bank[0]
        for m2 in range(KC):
            slot, key = self.wslot([(0, [[128, KC], [1, 128]], wo[:, :, m2 * 128:(m2 + 1) * 128])], "wos")
            for kc in range(KC):
                S.add("pe", lambda e, kc=kc, m2=m2, slot=slot: e.matmul(Bo[:, m2 * NS:(m2 + 1) * NS], lhsT=fap(slot, kc * 128, [[1, 128]]),
                                                                        rhs=mT[:, kc * NS:(kc + 1) * NS], start=(kc == 0), stop=(kc == KC - 1)),
                      r=[key, "s_mT"], w=["bank0"])
        oT = self.alloc([KC * NS], F32)
        S.add("act", lambda e: e.activation(out=oT, in_=Bo[:, 0:KC * NS], func=AF.Copy), r=["bank0"], w=["s_oT"])
        self.s_postnorm(oT.rearrange("p (c n) -> p c n", n=NS), ["s_oT"], 1)

    def s_finish(self):
        self.S.dma("sp", self.ys_out, self.xsT[:], r=["xsT"], w=["ys_o"], chan="s_out")
        if "s_out" not in self.out_chans:
            self.out_chans.append("s_out")

    def step(self):
        if self.nsteps is not None and self.stepi >= self.nsteps:
            return False
        self.stepi += 1
        return True

    def mixer(self, l, g0):
        sa = self.stop_after
        if not self.step():
            return
        self.phase(0)
        self.hT = self.alloc([KC, T], BF16)
        self.attoT = self.alloc([4, T], BF16)
        mark = self.aptr
        self.norm_bufs()
        self.prenorm(self.xres, g0, T, 1, 0)
        if sa == "prenorm":
            return
        self.phase(mark)
        if sa != "noattn":
            self.attention(l, g0)
        if sa == "attn":
            return
        if not self.step():
            return
        self.phase(mark)
        self.ssd(l, g0)
        if sa in ("ssd", "ssd_pre", "ssd_conv", "ssd_c1", "ssd_p1", "ssd_p2", "ssd_p3", "ssd_p4", "ssd_p5", "ssd_p6", "ssd_p7"):
            return
        if not self.step():
            return
        self.phase(mark)
        self.sqf = self.alloc([KC, 512], BF16)
        self.tail(l, g0)

    def build(self):
        self.declare()
        self.consts()
        S = self.S
        for l in range(self.depth):
            self.modulation(l)
            if self.do_sample:
                self.s_ffn(l, 0, 0)
                self.s_mixer(l)
                self.s_ffn(l, 1, 2)
            for g in range(NGRP if self.do_prompt else 0):
                g0 = g * T
                src = self.xT_in if l == 0 else self.xres
                if self.step():
                    self.ffn(l, 0, 0, src, self.xres, g0, False)
                if g == 0:
                    self.ssd_params(l)
                self.mixer(l, g0)
                if self.stop_after:
                    break
                last = (l == self.depth - 1)
                if self.step():
                    self.ffn(l, 1, 2, self.xres, self.yT_out if last else self.xres, g0, last)
            if self.stop_after:
                break
        if self.do_sample and not self.stop_after:
            self.s_finish()
        S.emit(final_waits=self.out_chans)
        return self.nc


def pm(v):
    v = np.asarray(v)
    c = v.shape[-1] // 128
    v = v.reshape(v.shape[:-1] + (c, 128))
    return np.ascontiguousarray(np.moveaxis(v, -1, 0))


_CACHE = {}


def _consts():
    j = np.arange(128)[:, None]
    i = np.arange(128)[None, :]
    ident = (j == i).astype(np.float32)
    mask2 = np.concatenate([(j >= i), (j <= i)], axis=1).astype(np.float32)
    negm = np.where(j > i, -30000.0, 0.0).astype(np.float32)
    cbf = np.concatenate([ident, mask2, np.tile(negm, (1, 4))], axis=1)
    tri = (j <= i).astype(np.float32)
    sel127 = np.zeros((128, 128), np.float32)
    sel127[127, :] = 1.0
    identS = np.zeros((128, 32), np.float32)
    identS[0:32] = np.eye(32)
    identS[32:64] = np.eye(32)
    cf32 = np.concatenate([tri, sel127, identS], axis=1)
    half = 64
    inv = (np.float32(10000.0) ** (-(np.arange(half, dtype=np.float32) / np.float32(half)))).astype(np.float32)
    pos = np.concatenate([np.arange(SEQ), [PAST]]).astype(np.float32)
    ang = (pos[None, :] * inv[:, None]).astype(np.float32)
    cos = np.cos(ang).astype(np.float32)
    sin = np.sin(ang).astype(np.float32)
    cosT = np.concatenate([cos, cos], axis=0)
    sinT = np.concatenate([-sin, sin], axis=0)
    return cbf, cf32, np.ascontiguousarray(cosT), np.ascontiguousarray(sinT)


def _sconsts():
    sc = np.zeros((128, 1172), np.float32)
    sc[:, 0:128] = np.eye(128, dtype=np.float32)
    for b in range(NS):
        sc[b, 128 + b * 128:128 + (b + 1) * 128] = 1.0
        sc[0:4, 640 + b * NS + b] = 1.0
    sc[0:4, 656:660] = np.eye(4, dtype=np.float32)
    for h in range(4):
        sc[h, 660 + h * 128:660 + (h + 1) * 128] = 1.0
    return sc


def kernel(**inp):
    f = lambda k: np.asarray(inp[k], dtype=np.float32)
    x_prompt = f("x_prompt")
    depth = 2
    if "nc" not in _CACHE:
        b = Builder(depth=depth)
        _CACHE["nc"] = b.build()
        _CACHE["b"] = b
    nc = _CACHE["nc"]
    cbf, cf32, cosT, sinT = _consts()
    shared = {}
    shared["w_mod"] = f("w_mod")
    shared["bmodT"] = np.ascontiguousarray(f("b_mod").reshape(depth, 72, 128).transpose(0, 2, 1))
    shared["gpreT"] = np.ascontiguousarray(f("norm_pre").reshape(depth, 3, KC, 128).transpose(0, 3, 1, 2))
    shared["gpostT"] = np.ascontiguousarray(f("norm_post").reshape(depth, 3, KC, 128).transpose(0, 3, 1, 2))
    shared["w_ff_in"] = f("w_ff_in")
    shared["w_ff_out"] = f("w_ff_out")
    shared["cbf_in"] = cbf
    shared["cf32_in"] = cf32
    shared["cosT"] = cosT
    shared["sinT"] = sinT
    shared["w_in"] = f("w_in")
    shared["conv_wT"] = np.ascontiguousarray(f("conv_w").reshape(depth, 4, 24, 128).transpose(0, 3, 2, 1))
    shared["conv_bT"] = np.ascontiguousarray(f("conv_b").reshape(depth, 24, 128).transpose(0, 2, 1))
    for k in ("dt_bias", "a_log", "d_skip", "norm_ssm", "w_br_att", "w_br_ssm", "w_out"):
        shared[k] = f(k)
    rep = lambda a: np.ascontiguousarray(np.broadcast_to(a[:, None], (a.shape[0], NS) + a.shape[1:]))
    shared["scst_in"] = _sconsts()
    shared["rope_s"] = np.ascontiguousarray(np.broadcast_to(
        np.concatenate([cosT[:, SEQ], sinT[:, SEQ]])[None, :], (NS, 256)))
    shared["convw_rep"] = rep(f("conv_w"))
    shared["convb_rep"] = rep(f("conv_b"))
    shared["dtb_rep"] = rep(f("dt_bias"))
    shared["alog_rep"] = rep(f("a_log"))
    shared["dsk_rep"] = rep(f("d_skip"))
    shared["nssm_rep"] = rep(f("norm_ssm"))
    caches = [f("cache_kv_g0"), f("cache_kv_g1"), f("cache_kv_g2")]
    st_ssm = f("state_ssm")
    st_conv = f("state_conv")
    x_sample = f("x_sample")
    in_maps = []
    for core in range(8):
        bidx = core % 4
        m = dict(shared)
        sl = slice(core * NS, (core + 1) * NS)
        m["xsT_in"] = np.ascontiguousarray(x_sample[sl, 0, :].T.reshape(KC, 128, NS).transpose(1, 0, 2))
        for g in range(3):
            m["cache%d" % g] = np.ascontiguousarray(caches[g][:, sl]).reshape(depth, NS, WIN[g], 1024)
        m["st_ssm"] = np.ascontiguousarray(st_ssm[:, sl]).reshape(depth, NS, 2048, 128)
        m["st_conv"] = np.ascontiguousarray(st_conv[:, sl])
        m["xT_in"] = np.ascontiguousarray(x_prompt[bidx].T).reshape(KC, 128, SEQ)
        cc = np.concatenate([f("c_prompt")[bidx:bidx + 1], f("c_sample")[core * NS:(core + 1) * NS]], axis=0)
        m["cT"] = np.ascontiguousarray(cc.T.reshape(KC, 128, NCOL).transpose(1, 0, 2))
        m = {k: v for k, v in m.items() if k in _CACHE["b"].dram_in}
        in_maps.append(m)
    res = run_bass_kernel_spmd(nc, in_maps, core_ids=list(range(8)))
    r = res.results
    B = 4
    y_prompt = np.stack([r[b]["yT_out"].reshape(D, SEQ).T for b in range(B)], axis=0)
    outs = {"y_prompt": y_prompt}
    kvp = []
    for g in range(3):
        d, keep = DIL[g], WIN[g]
        arr = np.zeros((depth, B, keep, 2, 4, 128), np.float32)
        for b in range(B):
            kT = r[b]["kT_out%d" % g]
            arr[:, b, :, 0] = kT.transpose(0, 3, 1, 2)
            vc = r[b]["vcm_out%d" % g]
            v = vc.transpose(0, 3, 2, 1, 4).reshape(depth, keep, 4, 128)
            arr[:, b, :, 1] = v
        kvp.append(arr)
    ssm_p = np.stack([r[b]["ssm_out"].transpose(0, 2, 1).reshape(depth, 32, 64, 128) for b in range(B)], axis=1)
    conv_p = np.stack([r[b]["conv_out"].transpose(0, 3, 2, 1).reshape(depth, 3, 3072) for b in range(B)], axis=1)
    y_sample = np.concatenate([r[c]["ys_out"].transpose(2, 1, 0).reshape(NS, 1, D) for c in range(8)], axis=0)
    kvs = [np.concatenate([r[c]["kvs_out%d" % g].reshape(depth, NS, 1, 2, 4, 128) for c in range(8)], axis=1) for g in range(3)]
    ssm_s = np.concatenate([r[c]["ssms_out"].reshape(depth, NS, 32, 64, 128) for c in range(8)], axis=1)
    conv_s = np.concatenate([r[c]["convs_out"] for c in range(8)], axis=1)
    return (y_prompt, y_sample, kvp[0], kvs[0], kvp[1], kvs[1], kvp[2], kvs[2], ssm_p, ssm_s, conv_p, conv_s)
```

```python
import contextlib
import os as _os
import numpy as np
import concourse.bass as bass
import concourse.mybir as mybir
from concourse.bass_utils import run_bass_kernel_spmd

F32 = mybir.dt.float32
BF16 = mybir.dt.bfloat16
AF = mybir.ActivationFunctionType
ALU = mybir.AluOpType
AX = mybir.AxisListType

D = 1024
KC = 8
SEQ = 4096
T = 2048
NGRP = SEQ // T
DFF = 2816
NJ = DFF // 128
INC = 11808
EPS = 1e-6
NS = 4
NCOL = 1 + NS
WIN = (128, 512, 2048)
DIL = (1, 4, 16)
PAST = 16384
SCALE = 128 ** -0.5
RES_W = (0.5, 1.0, 0.5)


class Op:
    __slots__ = ("eng", "fn", "deps", "chan", "chan_cnt", "sig", "idx", "pos", "is_dma", "waits", "grp", "grp_last")


class Sched:
    ENG = ("pe", "act", "dve", "pool", "sp")

    def __init__(self, nc):
        self.nc = nc
        self.ops = []
        self.last_w = {}
        self.readers = {}
        self.per_eng = {e: [] for e in self.ENG}
        self.chan_count = {}
        self.stack = contextlib.ExitStack()
        self.nbytes = 0
        self.bar = {}
        self.bar_start = 0
        self.chan_last = {}
        self.gid = 0

    def sb(self, name, shape, dt):
        t = self.stack.enter_context(self.nc.sbuf_tensor(name, list(shape), dt))
        n = 1
        for s in shape[1:]:
            n *= s
        self.nbytes += n * (4 if dt == F32 else 2)
        return t

    def ps(self, name, shape, dt=F32):
        return self.stack.enter_context(self.nc.psum_tensor(name, list(shape), dt))

    def add(self, eng, fn, r=(), w=(), chan=None, grp=None):
        op = Op()
        op.eng = eng
        op.fn = fn
        op.idx = len(self.ops)
        op.is_dma = chan is not None
        op.chan = chan
        op.sig = None
        deps = {}
        for k in r:
            j = self.last_w.get(k)
            if j is not None:
                deps[j] = True
        for k in w:
            j = self.last_w.get(k)
            if j is not None:
                deps.setdefault(j, False)
            for j in self.readers.get(k, ()):
                deps.setdefault(j, False)
        for k in w:
            self.last_w[k] = op.idx
            self.readers[k] = []
        for k in r:
            self.readers.setdefault(k, []).append(op.idx)
        if self.bar.get(eng) and not (chan is not None and str(chan).startswith("wslot")):
            for j in self.bar.pop(eng):
                deps[j] = True
        deps.pop(op.idx, None)
        op.deps = deps
        if op.is_dma:
            c = self.chan_count.get(chan, 0) + 1
            self.chan_count[chan] = c
            op.chan_cnt = c
            if grp is None:
                self.gid += 1
                grp = ("u", self.gid)
            op.grp = grp
            op.grp_last = op
            prev = self.chan_last.get(chan)
            if prev is not None and not _os.environ.get("K_NOGRP"):
                if prev.grp == grp:
                    q = prev
                    members = [q]
                    for o2 in reversed(self.ops):
                        if o2.is_dma and o2.chan == chan and o2.grp == grp and o2 is not q:
                            members.append(o2)
                        elif o2.is_dma and o2.chan == chan and o2.grp != grp:
                            break
                    for o2 in members:
                        o2.grp_last = op
                    for j, v in prev.deps.items():
                        if self.ops[j].is_dma and self.ops[j].chan == chan:
                            deps[j] = True
                else:
                    deps[prev.idx] = True
            self.chan_last[chan] = op
        op.pos = len(self.per_eng[eng])
        self.per_eng[eng].append(op)
        self.ops.append(op)
        return op

    def barrier(self):
        lastops = [self.per_eng[e][-1].idx for e in self.ENG if self.per_eng[e]]
        dmas = [op.idx for op in self.ops[self.bar_start:] if op.is_dma]
        pend = lastops + dmas
        for e in self.ENG:
            self.bar[e] = list(self.bar.get(e, [])) + pend
        self.bar_start = len(self.ops)

    def dma(self, q, out, in_, r=(), w=(), chan=None, grp=None, **kw):
        assert chan is not None
        return self.add(q, lambda e: e.dma_start(out=out, in_=in_, **kw), r=r, w=w, chan=chan, grp=grp)

    def emit(self, final_waits=()):
        nc = self.nc
        ops = self.ops
        need = [False] * len(ops)
        waits_of = [None] * len(ops)
        for op in ops:
            wl = []
            for j, raw in op.deps.items():
                y = ops[j]
                if y.is_dma:
                    if op.is_dma and y.chan == op.chan and y.grp == op.grp:
                        continue
                    wl.append(j)
                elif y.eng != op.eng or op.is_dma:
                    need[j] = True
                    wl.append(j)
                else:
                    if op.eng == "pe":
                        continue
                    if raw and (op.pos - y.pos) <= 3:
                        need[j] = True
                        wl.append(j)
            waits_of[op.idx] = wl
        cnt = {e: 0 for e in self.ENG}
        for e in self.ENG:
            for op in self.per_eng[e]:
                if not op.is_dma and need[op.idx]:
                    cnt[e] += 1
                    op.sig = cnt[e]
        st = self.stack
        esem = {e: st.enter_context(nc.semaphore("s_" + e)) for e in self.ENG}
        csem = {c: st.enter_context(nc.semaphore("c_" + c)) for c in self.chan_count}
        handles = {"pe": "tensor", "act": "scalar", "dve": "vector", "pool": "gpsimd", "sp": "sync"}
        nwait = [0]

        self.sim = {e: [] for e in self.ENG}

        def run_engine(ename, eng):
            seen = {}
            for op in self.per_eng[ename]:
                req = {}
                for j in waits_of[op.idx]:
                    y = ops[j]
                    if y.is_dma:
                        key, val = ("c", y.chan), 16 * y.grp_last.chan_cnt
                    else:
                        key, val = ("e", y.eng), y.sig
                    if req.get(key, 0) < val:
                        req[key] = val
                for key, val in req.items():
                    if seen.get(key, 0) >= val:
                        continue
                    seen[key] = val
                    sem = csem[key[1]] if key[0] == "c" else esem[key[1]]
                    eng.wait_ge(sem, val)
                    nwait[0] += 1
                self.sim[ename].append((op.idx, list(req.items()), ("c", op.chan) if op.is_dma else (("e", ename) if op.sig is not None else None)))
                ins = op.fn(eng)
                if op.is_dma:
                    ins.then_inc(csem[op.chan], 16)
                elif op.sig is not None:
                    ins.then_inc(esem[ename], 1)
            if ename == "sp":
                for c in final_waits:
                    eng.wait_ge(csem[c], 16 * self.chan_count[c])

        block = st.enter_context(nc.Block())
        for ename in self.ENG:
            getattr(block, handles[ename])(lambda eng, ename=ename: run_engine(ename, eng))
        self.stats = dict(n_ops={e: len(v) for e, v in self.per_eng.items()}, n_wait=nwait[0],
                          n_sem=len(esem) + len(csem))


def fap(t, off, dims, p0=0, pn=None):
    base = t[:]
    pstep = base.ap[0][0]
    if pn is None:
        pn = base.ap[0][1] - p0
    return bass.AP(base.tensor, base.offset + p0 * pstep + off, [[pstep, pn]] + [list(d) for d in dims])


class Builder:
    def __init__(self, depth=2, do_prompt=True, do_sample=True, stop_after=None, nsteps=None):
        self.nsteps = nsteps
        self.stepi = 0
        self.depth = depth
        self.do_prompt = do_prompt
        self.do_sample = do_sample
        self.stop_after = stop_after
        nc = bass.Bass("TRN2", target_bir_lowering=False)
        self.nc = nc
        self.S = Sched(nc)
        self.dram_in = {}
        self.dram_out = {}
        self.out_chans = []
        self.uid = 0

    def din(self, name, shape, dt=F32):
        ap = self.nc.dram_tensor(name, list(shape), dt, kind="ExternalInput").ap()
        self.dram_in[name] = ap
        return ap

    def dout(self, name, shape, dt=F32):
        ap = self.nc.dram_tensor(name, list(shape), dt, kind="ExternalOutput").ap()
        self.dram_out[name] = ap
        return ap

    def dscr(self, name, shape, dt=F32):
        return self.nc.dram_tensor(name, list(shape), dt, kind="Internal").ap()

    def u(self, p):
        self.uid += 1
        return "%s%d" % (p, self.uid)

    def declare(self):
        L = self.depth
        self.xT_in = self.din("xT_in", [KC, 128, SEQ])
        self.cT = self.din("cT", [128, KC, NCOL])
        self.w_mod = self.din("w_mod", [L, D, 9 * D])
        self.bmodT = self.din("bmodT", [L, 128, 72])
        self.gpreT = self.din("gpreT", [L, 128, 3, KC])
        self.gpostT = self.din("gpostT", [L, 128, 3, KC])
        self.w_ff_in = self.din("w_ff_in", [L, 2, D, 2 * DFF])
        self.w_ff_out = self.din("w_ff_out", [L, 2, DFF, D])
        self.yT_out = self.dout("yT_out", [KC, 128, SEQ])
        self.dbg = bool(_os.environ.get("K_DBG"))
        self.xres = (self.dout if self.dbg else self.dscr)("xres", [KC, 128, SEQ])
        self.cbf_in = self.din("cbf_in", [128, 896])
        self.cf32_in = self.din("cf32_in", [128, 288])
        self.cosT = self.din("cosT", [128, SEQ + 1])
        self.sinT = self.din("sinT", [128, SEQ + 1])
        self.w_in = self.din("w_in", [L, D, INC])
        self.conv_wT = self.din("conv_wT", [L, 128, 24, 4])
        self.conv_bT = self.din("conv_bT", [L, 128, 24])
        self.dt_bias = self.din("dt_bias", [L, 32])
        self.a_log = self.din("a_log", [L, 32])
        self.d_skip = self.din("d_skip", [L, 32])
        self.norm_ssm = self.din("norm_ssm", [L, 2048])
        self.w_br_att = self.din("w_br_att", [L, 512, D])
        self.w_br_ssm = self.din("w_br_ssm", [L, 2048, D])
        self.w_out = self.din("w_out", [L, D, D])
        self.kT_out = [self.dout("kT_out%d" % g, [L, 4, 128, WIN[g]]) for g in range(3)]
        self.vcm_out = [self.dout("vcm_out%d" % g, [L, 4, DIL[g], 128, 128]) for g in range(3)]
        self.ssm_out = self.dout("ssm_out", [L, 128, 2048])
        self.conv_out = self.dout("conv_out", [L, 128, 24, 3])
        self.khist = [self.dscr("khist%d" % g, [4, 128, WIN[g]], BF16) for g in range(3)]
        self.vhist = [self.dscr("vhist%d" % g, [4, 128, DIL[g] * 128], BF16) for g in range(3)]
        self.yscr = (self.dout if self.dbg else self.dscr)("yscr", [16, 128, T], BF16)
        if self.dbg:
            self.atto_dbg = self.dout("atto_dbg", [128, 4, T], BF16)
        if self.do_sample:
            self.s_declare()

    def consts(self):
        S = self.S
        nc = self.nc
        self.ones_bf = S.sb("ones_bf", [128, 128], BF16)
        S.add("pool", lambda e: e.memset(self.ones_bf[:], 1.0), w=["ones_bf"])
        self.psall = S.ps("psall", [128, 4096])
        self.psall_bf = self.psall[:].bitcast(BF16)
        self.bank = [self.psall[:, i * 512:(i + 1) * 512] for i in range(8)]
        self.bank_bf = [self.psall_bf[:, i * 1024:(i + 1) * 1024] for i in range(8)]
        self.cbf = S.sb("cbf", [128, 896], BF16)
        S.dma("pool", self.cbf[:], self.cbf_in, w=["cbf"], chan="misc2")
        self.ident_bf = self.cbf[:, 0:128]
        self.mask2 = self.cbf[:, 128:384]
        self.negm4 = self.cbf[:, 384:896]
        self.cf32 = S.sb("cf32", [128, 288], F32)
        S.dma("sp", self.cf32[:], self.cf32_in, w=["cf32"], chan="misc")
        self.tri = self.cf32[:, 0:128]
        self.sel127 = self.cf32[:, 128:256]
        self.identS = self.cf32[0:64, 256:288]
        self.Sst = S.sb("Sst", [128, 2048], F32)
        self.convhist = S.sb("convhist", [128, 24, 3], F32)
        self.convw = S.sb("convw", [128, 24, 4], F32)
        self.convb = S.sb("convb", [128, 24], F32)
        self.dtb_bc = S.sb("dtb_bc", [128, 32], F32)
        self.a_bc = S.sb("a_bc", [128, 32], F32)
        self.dsk_bc = S.sb("dsk_bc", [128, 32], F32)

        self.NSLOT = 3
        self.wring = [S.sb("wslot%d" % i, [128, 4096], BF16) for i in range(self.NSLOT)]
        self.wnext = 0
        self.modT = S.sb("modT", [128, 72, NCOL], F32)
        self.Aall = S.sb("Aall", [128, 3, KC, NCOL], F32)
        self.Gall = S.sb("Gall", [128, 3, KC, NCOL], F32)
        self.scT = S.sb("scT", [128, KC, NCOL], BF16)
        self.cTs = S.sb("cTs", [128, KC, NCOL], F32)
        self.bmod_sb = S.sb("bmod_sb", [128, 72], F32)
        self.gpre_sb = S.sb("gpre_sb", [128, 3, KC], F32)
        self.gpost_sb = S.sb("gpost_sb", [128, 3, KC], F32)
        if self.do_sample:
            self.s_consts()
        self.ARENA_B = (int(self.nc.sbuf_bytes_remaining) - 128) // 256 * 256
        self.amax = 0
        self.arena = S.sb("arena", [128, self.ARENA_B // 4], F32)
        self.arena16 = self.arena[:].bitcast(BF16)
        self.aptr = 0
        self.xt_i = 0

    def alloc(self, shape, dt):
        n = 1
        for v in shape:
            n *= v
        nb = n * (4 if dt == F32 else 2)
        off = self.aptr
        self.aptr = (off + nb + 63) // 64 * 64
        assert self.aptr <= self.ARENA_B, ("arena overflow", self.aptr, self.ARENA_B)
        self.amax = max(self.amax, self.aptr)
        if dt == F32:
            ap = self.arena[:, off // 4:off // 4 + n]
        else:
            ap = self.arena16[:, off // 2:off // 2 + n]
        if len(shape) == 2:
            return ap.rearrange("p (a b) -> p a b", a=shape[0])
        if len(shape) == 3:
            return ap.rearrange("p (a b c) -> p a b c", a=shape[0], b=shape[1])
        return ap

    def phase(self, mark=0):
        self.S.barrier()
        self.aptr = mark

    def norm_bufs(self):
        self.xt = [self.alloc([KC, 512], F32) for i in range(2)]
        self.sq = self.alloc([KC, 512], BF16)
        self.rstd = self.alloc([512], F32)
        self.tmp = self.alloc([KC, 512], F32)

    def ffn_bufs(self):
        self.phase(0)
        self.hT = self.alloc([KC, 1024], BF16)
        self.norm_bufs()
        self.aT = self.alloc([NJ, 1024], BF16)
        self.sg = [self.alloc([512], BF16) for i in range(2)]
        self.fT = self.alloc([KC, 1024], F32)
        self.sqf = self.alloc([KC, 1024], BF16)

    def wslot(self, loads, tag):
        S = self.S
        i = self.wnext
        self.wnext = (i + 1) % self.NSLOT
        slot = self.wring[i]
        key = "wslot%d" % i
        S.gid += 1
        grp = ("w", S.gid)
        for (off, dims, src) in loads:
            S.dma("pool", fap(slot, off, dims), src, w=[key], chan=key, grp=grp)
        return slot, key

    def modulation(self, l):
        S = self.S
        mm = self.bank[7]
        if l == 0:
            S.dma("sp", self.cTs[:], self.cT, w=["cTs"], chan="misc")
            S.add("act", lambda e: e.activation(out=self.scT[:], in_=self.cTs[:], func=AF.Silu), r=["cTs"], w=["scT"])
        S.dma("sp", self.bmod_sb[:], self.bmodT[l], w=["bmod_sb"], chan="misc")
        S.dma("sp", self.gpre_sb[:], self.gpreT[l], w=["gpre_sb"], chan="misc")
        S.dma("sp", self.gpost_sb[:], self.gpostT[l], w=["gpost_sb"], chan="misc")
        wv = self.w_mod[l].rearrange("(kc p) n -> p kc n", p=128)
        for blk in range(18):
            slot, key = self.wslot([(0, [[512, KC], [1, 512]], wv[:, :, blk * 512:(blk + 1) * 512])], "mod")
            for cc in range(4):
                ch = blk * 4 + cc
                for kc in range(KC):
                    S.add("pe", lambda e, ch=ch, kc=kc, cc=cc, slot=slot: e.matmul(
                        fap(mm, ch * NCOL, [[1, NCOL]]), lhsT=fap(slot, kc * 512 + cc * 128, [[1, 128]]),
                        rhs=self.scT[:, kc, :], start=(kc == 0), stop=(kc == KC - 1)),
                        r=[key, "scT"], w=["bank7"])
        S.add("dve", lambda e: e.tensor_tensor(
            out=self.modT[:], in0=fap(mm, 0, [[NCOL, 72], [1, NCOL]]),
            in1=self.bmod_sb[:].unsqueeze(2).to_broadcast([128, 72, NCOL]), op=ALU.add),
            r=["bank7", "bmod_sb"], w=["modT"])
        for k in range(3):
            S.add("dve", lambda e, k=k: e.scalar_tensor_tensor(
                out=self.Aall[:, k], in0=self.modT[:, k * 24 + 8:k * 24 + 16, :], scalar=1.0,
                in1=self.gpre_sb[:, k, :].unsqueeze(2).to_broadcast([128, KC, NCOL]),
                op0=ALU.add, op1=ALU.mult), r=["modT", "gpre_sb"], w=["Aall"])
            S.add("dve", lambda e, k=k: e.scalar_tensor_tensor(
                out=self.Gall[:, k], in0=self.modT[:, k * 24 + 16:k * 24 + 24, :], scalar=float(RES_W[k]),
                in1=self.gpost_sb[:, k, :].unsqueeze(2).to_broadcast([128, KC, NCOL]),
                op0=ALU.mult, op1=ALU.mult), r=["modT", "gpost_sb"], w=["Gall"])

    def rstd_from_sq(self, sq_ap_fn, nch, denom, rkeys):
        _rstd = self.rstd
        S = self.S
        st = self.bank[6]
        for c in range(nch):
            S.add("pe", lambda e, c=c: e.matmul(st[:, :], lhsT=self.ones_bf[:, :], rhs=sq_ap_fn(c),
                                                  start=(c == 0), stop=(c == nch - 1)),
                  r=list(rkeys) + ["ones_bf"], w=["bank6"])
        S.add("dve", lambda e: e.tensor_scalar(out=_rstd[:], in0=st[:, :], scalar1=1.0 / denom, scalar2=EPS,
                                                 op0=ALU.mult, op1=ALU.add), r=["bank6"], w=["rstd"])
        S.add("act", lambda e: e.activation(out=_rstd[:], in_=_rstd[:], func=AF.Sqrt), r=["rstd"], w=["rstd"])
        S.add("dve", lambda e: e.reciprocal(out=_rstd[:], in_=_rstd[:]), r=["rstd"], w=["rstd"])

    def load_x(self, src, tok0):
        S = self.S
        i = self.xt_i
        self.xt_i ^= 1
        xt = self.xt[i]
        key = "xt%d" % i
        S.dma("sp", xt[:], src[:, :, tok0:tok0 + 512].rearrange("c p t -> p c t"),
              r=["xdram"], w=[key], chan=key)
        return xt, key

    def prenorm(self, src, tok0, ntok, k, hoff):
        _hT = self.hT
        _sq = self.sq
        _rstd = self.rstd
        _tmp = self.tmp
        S = self.S
        for tt in range(ntok // 512):
            xt, xkey = self.load_x(src, tok0 + tt * 512)
            S.add("act", lambda e, xt=xt: e.activation(out=_sq[:], in_=xt[:], func=AF.Square),
                  r=[xkey], w=["sq"])
            self.rstd_from_sq(lambda c: _sq[:, c, :], KC, D, ["sq"])
            S.add("dve", lambda e, xt=xt: e.tensor_tensor(
                out=_tmp[:], in0=xt[:], in1=_rstd[:].unsqueeze(1).to_broadcast([128, KC, 512]),
                op=ALU.mult), r=[xkey, "rstd"], w=["tmp"])
            for c in range(KC):
                o0 = hoff + tt * 512
                S.add("act", lambda e, c=c, o0=o0: e.activation(
                    out=_hT[:, c, o0:o0 + 512], in_=_tmp[:, c, :], func=AF.Identity,
                    scale=self.Aall[:, k, c, 0:1], bias=self.modT[:, k * 24 + c, 0:1]),
                    r=["tmp", "Aall", "modT"], w=["hT"])

    def postnorm_residual(self, f_ap, sq_fn, k, src, dst, tok0, fkeys, out_chan=None):
        _rstd = self.rstd
        _tmp = self.tmp
        S = self.S
        self.rstd_from_sq(sq_fn, KC, D, fkeys)
        xt, xkey = self.load_x(src, tok0)
        S.add("dve", lambda e: e.tensor_tensor(
            out=_tmp[:], in0=f_ap, in1=_rstd[:].unsqueeze(1).to_broadcast([128, KC, 512]),
            op=ALU.mult), r=list(fkeys) + ["rstd"], w=["tmp"])
        for c in range(KC):
            S.add("dve", lambda e, c=c, xt=xt: e.scalar_tensor_tensor(
                out=xt[:, c, :], in0=_tmp[:, c, :], scalar=self.Gall[:, k, c, 0:1], in1=xt[:, c, :],
                op0=ALU.mult, op1=ALU.add), r=["tmp", "Gall", xkey], w=[xkey])
        ch = out_chan or (xkey + "o")
        S.dma("sp", dst[:, :, tok0:tok0 + 512].rearrange("c p t -> p c t"), xt[:], r=[xkey], w=["xdram"], chan=ch)
        if out_chan and out_chan not in self.out_chans:
            self.out_chans.append(out_chan)

    def ffn(self, l, i, k, src, dst, g0, is_out):
        self.ffn_bufs()
        _hT = self.hT
        _aT = self.aT
        _sg = self.sg
        _fT = self.fT
        _sqf = self.sqf
        S = self.S
        w1 = self.w_ff_in[l, i].rearrange("(kc p) n -> p kc n", p=128)
        w2 = self.w_ff_out[l, i].rearrange("(j p) n -> p j n", p=128)
        pa = 0
        for half in range(2):
            t0 = g0 + half * 1024
            self.prenorm(src, t0, 1024, k, 0)
            for jb in range(NJ // 2):
                slot, key = self.wslot([
                    (0, [[256, KC], [1, 256]], w1[:, :, jb * 256:(jb + 1) * 256]),
                    (2048, [[256, KC], [1, 256]], w1[:, :, DFF + jb * 256:DFF + (jb + 1) * 256])], "w1")
                for jj in range(2):
                    j = jb * 2 + jj
                    for tt in range(2):
                        A = self.bank[pa]
                        B = self.bank[2 + pa]
                        ka, kb = "bank%d" % pa, "bank%d" % (2 + pa)
                        sg = _sg[pa]
                        sk = "sg%d" % pa
                        pa ^= 1
                        for kc in range(KC):
                            S.add("pe", lambda e, A=A, kc=kc, jj=jj, tt=tt, slot=slot: e.matmul(
                                A[:, :], lhsT=fap(slot, kc * 256 + jj * 128, [[1, 128]]),
                                rhs=_hT[:, kc, tt * 512:(tt + 1) * 512], start=(kc == 0), stop=(kc == KC - 1)),
                                r=[key, "hT"], w=[ka])
                        for kc in range(KC):
                            S.add("pe", lambda e, B=B, kc=kc, jj=jj, tt=tt, slot=slot: e.matmul(
                                B[:, :], lhsT=fap(slot, 2048 + kc * 256 + jj * 128, [[1, 128]]),
                                rhs=_hT[:, kc, tt * 512:(tt + 1) * 512], start=(kc == 0), stop=(kc == KC - 1)),
                                r=[key, "hT"], w=[kb])
                        S.add("act", lambda e, A=A, sg=sg: e.activation(out=sg[:], in_=A[:, :], func=AF.Silu),
                              r=[ka], w=[sk])
                        S.add("dve", lambda e, B=B, sg=sg, j=j, tt=tt: e.tensor_tensor(
                            out=_aT[:, j, tt * 512:(tt + 1) * 512], in0=sg[:], in1=B[:, :], op=ALU.mult),
                            r=[sk, kb], w=["aT"])
            pf = 0
            for m in range(KC):
                slot, key = self.wslot([(0, [[128, NJ], [1, 128]], w2[:, :, m * 128:(m + 1) * 128])], "w2")
                for tt in range(2):
                    Fp = self.bank[4 + pf]
                    kf = "bank%d" % (4 + pf)
                    pf ^= 1
                    for j in range(NJ):
                        S.add("pe", lambda e, Fp=Fp, j=j, tt=tt, slot=slot: e.matmul(
                            Fp[:, :], lhsT=fap(slot, j * 128, [[1, 128]]),
                            rhs=_aT[:, j, tt * 512:(tt + 1) * 512], start=(j == 0), stop=(j == NJ - 1)),
                            r=[key, "aT"], w=[kf])
                    S.add("act", lambda e, Fp=Fp, m=m, tt=tt: e.activation(
                        out=_fT[:, m, tt * 512:(tt + 1) * 512], in_=Fp[:, :], func=AF.Copy), r=[kf], w=["fT%d" % tt])
                    S.add("act", lambda e, Fp=Fp, m=m, tt=tt: e.activation(
                        out=_sqf[:, m, tt * 512:(tt + 1) * 512], in_=Fp[:, :], func=AF.Square), r=[kf], w=["sqf%d" % tt])
            for tt in range(2):
                self.postnorm_residual(_fT[:, :, tt * 512:(tt + 1) * 512],
                                       lambda c, tt=tt: _sqf[:, c, tt * 512:(tt + 1) * 512],
                                       k, src, dst, t0 + tt * 512, ["fT%d" % tt, "sqf%d" % tt],
                                       out_chan=("yout" if is_out else None))

    def proj_tiles(self, slot, woff, wkstride, key, ntile, banks, cb, extra_r=()):
        _hT = self.hT
        S = self.S
        for tt in range(ntile):
            bi = banks[tt % len(banks)]
            Bk = self.bank[bi]
            bk = "bank%d" % bi
            for kc in range(KC):
                S.add("pe", lambda e, Bk=Bk, kc=kc, tt=tt: e.matmul(
                    Bk[:, :], lhsT=fap(slot, woff + kc * wkstride, [[1, 128]]),
                    rhs=_hT[:, kc, tt * 512:(tt + 1) * 512], start=(kc == 0), stop=(kc == KC - 1)),
                    r=[key, "hT"] + list(extra_r), w=[bk])
            cb(tt, Bk, bk)

    def win_cols(self, l, c0, n):
        return self.w_in[l].rearrange("(kc p) n -> p kc n", p=128)[:, :, c0:c0 + n]

    def attention(self, l, g0):
        _hT = self.hT
        _attoT = self.attoT
        S = self.S
        last_grp = (g0 + T == SEQ) and not _os.environ.get("K_NOKVOUT")
        cosF = self.alloc([T], F32)
        sinS = self.alloc([T], F32)
        S.dma("sp", cosF[:], self.cosT[:, g0:g0 + T], w=["cosF"], chan="misc")
        S.dma("sp", sinS[:], self.sinT[:, g0:g0 + T], w=["sinS"], chan="misc")
        qb = self.alloc([3, T], BF16)
        kb = [self.alloc([WIN[g] + T], BF16) for g in range(3)]
        vb = [self.alloc([DIL[g] + T // 128, 128], BF16) for g in range(3)]
        ND = self.alloc([2, T], F32)
        Pt = [self.alloc([256], BF16) for i in range(2)]
        r1 = [self.alloc([512], F32) for i in range(2)]
        r2 = [self.alloc([512], F32) for i in range(2)]
        kst = [self.alloc([512], F32) for i in range(2)]
        vst = [self.alloc([512], F32) for i in range(2)]
        rden = self.alloc([T], F32)
        cnt = {"rt": 0, "vs": 0, "pt": 0, "ps": 0, "po": 0}
        for h in range(4):
            if g0 > 0 and not _os.environ.get("K_NOHISTLD"):
                for g in range(3):
                    S.dma("sp", kb[g][:, 0:WIN[g]], self.khist[g][h], r=["khist%d" % g], w=["kb%d" % g], chan="hist")
                    S.dma("sp", vb[g][:, 0:DIL[g], :], self.vhist[g][h].rearrange("p (a b) -> p a b", b=128),
                          r=["vhist%d" % g], w=["vb%d" % g], chan="hist")
            for g in range(3):
                d, span = DIL[g], WIN[g]
                cb0 = g * 1536 + h * 128
                for qk in range(2):
                    c0 = cb0 + qk * 512
                    wsrc = self.win_cols(l, c0, 128)
                    slot, key = self.wslot([
                        (0, [[128, KC], [1, 128]], wsrc),
                        (1024, [[128, KC], [1, 64]], wsrc[:, :, 64:128]),
                        (1024 + 64, [[128, KC], [1, 64]], wsrc[:, :, 0:64])], "wqk")

                    def rope_cb(tt, Bk, bk, slot=slot, key=key, qk=qk, g=g, h=h, span=span):
                        bi2 = 2 + (tt % 2)
                        B2 = self.bank[bi2]
                        b2k = "bank%d" % bi2
                        for kc in range(KC):
                            S.add("pe", lambda e, kc=kc: e.matmul(
                                B2[:, :], lhsT=fap(slot, 1024 + kc * 128, [[1, 128]]),
                                rhs=_hT[:, kc, tt * 512:(tt + 1) * 512], start=(kc == 0), stop=(kc == KC - 1)),
                                r=[key, "hT"], w=[b2k])
                        i = cnt["rt"] % 2
                        cnt["rt"] += 1
                        S.add("dve", lambda e: e.tensor_tensor(out=r1[i][:], in0=Bk[:, :], in1=cosF[:, tt * 512:(tt + 1) * 512],
                                                                 op=ALU.mult), r=[bk, "cosF"], w=["r1_%d" % i])
                        S.add("dve", lambda e: e.tensor_tensor(out=r2[i][:], in0=B2[:, :], in1=sinS[:, tt * 512:(tt + 1) * 512],
                                                                 op=ALU.mult), r=[b2k, "sinS"], w=["r2_%d" % i])
                        if qk == 0:
                            S.add("dve", lambda e: e.tensor_tensor(out=qb[:, g, tt * 512:(tt + 1) * 512], in0=r1[i][:],
                                                                      in1=r2[i][:], op=ALU.add),
                                  r=["r1_%d" % i, "r2_%d" % i], w=["qb"])
                        else:
                            S.add("dve", lambda e: e.tensor_tensor(out=kst[i][:], in0=r1[i][:], in1=r2[i][:], op=ALU.add),
                                  r=["r1_%d" % i, "r2_%d" % i], w=["kst%d" % i])
                            S.add("act", lambda e: e.activation(out=kb[g][:, span + tt * 512:span + (tt + 1) * 512],
                                                                  in_=kst[i][:], func=AF.Copy),
                                  r=["kst%d" % i], w=["kb%d" % g])
                            if last_grp:
                                lo = max(g0 + tt * 512, SEQ - span)
                                hi = g0 + (tt + 1) * 512
                                if lo < hi:
                                    S.dma("sp", self.kT_out[g][l, h, :, lo - (SEQ - span):hi - (SEQ - span)],
                                          kst[i][:, lo - (g0 + tt * 512):512], r=["kst%d" % i], w=["kTout"], chan="kvout")
                    self.proj_tiles(slot, 0, 128, key, T // 512, [0, 1], rope_cb)
                c0 = cb0 + 1024
                slot, key = self.wslot([(0, [[128, KC], [1, 128]], self.win_cols(l, c0, 128))], "wv")
                tiles = [(n, r) for n in range(T // span) for r in range(d)]
                for t4 in range(0, len(tiles), 4):
                    bi = (t4 // 4) % 2
                    Bk = self.bank[bi]
                    bk = "bank%d" % bi
                    for q4 in range(4):
                        n, r = tiles[t4 + q4]
                        for kc in range(KC):
                            S.add("pe", lambda e, Bk=Bk, kc=kc, q4=q4, n=n, r=r, d=d, span=span, slot=slot: e.matmul(
                                Bk[:, q4 * 128:(q4 + 1) * 128],
                                lhsT=fap(_hT, kc * T + n * span + r, [[d, 128]]),
                                rhs=fap(slot, kc * 128, [[1, 128]]), start=(kc == 0), stop=(kc == KC - 1)),
                                r=[key, "hT"], w=[bk])
                    ti0 = d + t4
                    S.add("act", lambda e, Bk=Bk, ti0=ti0, g=g: e.activation(
                        out=vb[g][:, ti0:ti0 + 4, :], in_=fap(Bk, 0, [[128, 4], [1, 128]]), func=AF.Copy),
                        r=[bk], w=["vb%d" % g])
                    if last_grp and tiles[t4 + 3][0] == T // span - 1:
                        i = cnt["vs"] % 2
                        cnt["vs"] += 1
                        S.add("act", lambda e, Bk=Bk, i=i: e.activation(out=vst[i][:], in_=Bk[:, :], func=AF.Copy), r=[bk], w=["vst%d" % i])
                        for q4 in range(4):
                            n, r = tiles[t4 + q4]
                            if n == T // span - 1:
                                S.dma("sp", self.vcm_out[g][l, h, r], vst[i][:, q4 * 128:(q4 + 1) * 128],
                                      r=["vst%d" % i], w=["vout"], chan="kvout")
            for g in range(3):
                d, span = DIL[g], WIN[g]
                for n in range(T // span):
                    for r in range(d):
                        has_prev = (g0 > 0) or (n > 0)
                        c_lo = 0 if has_prev else 128
                        qap = fap(qb, g * T + n * span + r, [[d, 128]])
                        kcur = fap(kb[g], span + n * span + r, [[d, 128]])
                        kprev = fap(kb[g], n * span + r, [[d, 128]])
                        vcur = vb[g][:, d + n * d + r, :]
                        vprev = vb[g][:, n * d + r, :]
                        si = 4 + cnt["ps"] % 2
                        cnt["ps"] += 1
                        oi = 6 + cnt["po"] % 2
                        cnt["po"] += 1
                        pi = cnt["pt"] % 2
                        cnt["pt"] += 1
                        Sb, Ob, P = self.bank[si], self.bank[oi], Pt[pi]
                        sk, ok, pk = "bank%d" % si, "bank%d" % oi, "Pt%d" % pi
                        if has_prev:
                            S.add("pe", lambda e, Sb=Sb, kprev=kprev, qap=qap: e.matmul(
                                Sb[:, 0:128], lhsT=kprev, rhs=qap, start=True, stop=True), r=["kb%d" % g, "qb"], w=[sk])
                        S.add("pe", lambda e, Sb=Sb, kcur=kcur, qap=qap: e.matmul(
                            Sb[:, 128:256], lhsT=kcur, rhs=qap, start=True, stop=True), r=["kb%d" % g, "qb"], w=[sk])
                        S.add("act", lambda e, Sb=Sb, P=P, c_lo=c_lo: e.activation(
                            out=P[:, c_lo:256], in_=Sb[:, c_lo:256], func=AF.Exp, scale=float(SCALE)), r=[sk], w=[pk])
                        S.add("pool", lambda e, P=P, c_lo=c_lo: e.tensor_tensor(
                            out=P[:, c_lo:256], in0=P[:, c_lo:256], in1=self.mask2[:, c_lo:256], op=ALU.mult),
                            r=[pk, "cbf"], w=[pk])
                        for half, lh in ((0, None), (1, self.ones_bf)):
                            oc = half * 128
                            if has_prev:
                                S.add("pe", lambda e, Ob=Ob, P=P, oc=oc, lh=lh, vprev=vprev: e.matmul(
                                    Ob[:, oc:oc + 128], lhsT=(vprev if lh is None else lh[:, :]), rhs=P[:, 0:128],
                                    start=True, stop=False), r=[pk, "vb%d" % g, "ones_bf"], w=[ok])
                            S.add("pe", lambda e, Ob=Ob, P=P, oc=oc, lh=lh, vcur=vcur, has_prev=has_prev: e.matmul(
                                Ob[:, oc:oc + 128], lhsT=(vcur if lh is None else lh[:, :]), rhs=P[:, 128:256],
                                start=(not has_prev), stop=True), r=[pk, "vb%d" % g, "ones_bf"], w=[ok])
                        nd = fap(ND, n * span + r, [[T, 2], [d, 128]])
                        osrc = fap(Ob, 0, [[128, 2], [1, 128]])
                        if g == 0:
                            S.add("dve", lambda e, nd=nd, osrc=osrc: e.tensor_copy(out=nd, in_=osrc), r=[ok], w=["ND"])
                        else:
                            S.add("dve", lambda e, nd=nd, osrc=osrc: e.tensor_tensor(out=nd, in0=osrc, in1=nd, op=ALU.add),
                                  r=[ok, "ND"], w=["ND"])
            S.add("dve", lambda e: e.reciprocal(out=rden[:], in_=ND[:, 1, :]), r=["ND"], w=["rden"])
            S.add("dve", lambda e, h=h: e.tensor_tensor(out=_attoT[:, h, :], in0=ND[:, 0, :], in1=rden[:], op=ALU.mult),
                  r=["ND", "rden"], w=["attoT"])
            if not last_grp:
                for g in range(3):
                    S.dma("sp", self.khist[g][h], kb[g][:, T:T + WIN[g]], r=["kb%d" % g], w=["khist%d" % g], chan="hist")
                    S.dma("sp", self.vhist[g][h].rearrange("p (a b) -> p a b", b=128),
                          vb[g][:, T // 128:T // 128 + DIL[g], :], r=["vb%d" % g], w=["vhist%d" % g], chan="hist")
        if last_grp and "kvout" not in self.out_chans:
            self.out_chans.append("kvout")
        if self.dbg and g0 == 0 and l == 0:
            S.dma("sp", self.atto_dbg, _attoT[:], r=["attoT"], w=["attodbg"], chan="dbg")
            self.out_chans.append("dbg")

    def ssd_params(self, l):
        S = self.S
        S.dma("sp", self.convw[:], self.conv_wT[l], w=["convw"], chan="misc")
        S.dma("sp", self.convb[:], self.conv_bT[l], w=["convb"], chan="misc")
        S.dma("sp", self.dtb_bc[:], self.dt_bias[l].partition_broadcast(128), w=["dtb_bc"], chan="misc")
        S.dma("sp", self.a_bc[:], self.a_log[l].partition_broadcast(128), w=["a_bc"], chan="misc")
        S.dma("sp", self.dsk_bc[:], self.d_skip[l].partition_broadcast(128), w=["dsk_bc"], chan="misc")
        S.add("act", lambda e: e.activation(out=self.a_bc[:], in_=self.a_bc[:], func=AF.Exp), r=["a_bc"], w=["a_bc"])
        S.add("dve", lambda e: e.tensor_scalar(out=self.a_bc[:], in0=self.a_bc[:], scalar1=-1.0, scalar2=None, op0=ALU.mult),
              r=["a_bc"], w=["a_bc"])
        S.add("dve", lambda e: e.memset(self.Sst[:], 0.0), w=["Sst"])
        S.add("dve", lambda e: e.memset(self.convhist[:], 0.0), w=["convhist"])

    def ssd(self, l, g0):
        _hT = self.hT
        S = self.S
        last_grp = (g0 + T == SEQ)
        NB = T // 128
        dtT = self.alloc([NB, 32], F32)
        dA2 = self.alloc([NB, 64], F32)
        acs = self.alloc([NB, 32], F32)
        eacs = self.alloc([NB, 32], F32)
        dtw = self.alloc([NB, 32], F32)
        dec = self.alloc([NB, 32], F32)
        L1 = self.alloc([T], F32)
        acsT = self.alloc([T], F32)
        R1 = self.alloc([8, 128], F32)
        raw = self.alloc([T + 8], F32)
        acc = self.alloc([T], F32)
        xTc = self.alloc([4, T], BF16)
        BT = self.alloc([T], BF16)
        CT = self.alloc([T], BF16)
        yT = self.alloc([4, T], BF16)
        Sbf = self.alloc([512], BF16)
        Xtm = [self.alloc([512], BF16) for i in range(2)]
        Xdt = [self.alloc([512], BF16) for i in range(2)]
        Xw = [self.alloc([512], BF16) for i in range(2)]
        XD = [self.alloc([512], BF16) for i in range(2)]
        Btm = [self.alloc([128], BF16) for i in range(2)]
        CBm = [self.alloc([128], BF16) for i in range(2)]
        _Lt = self.alloc([8, 128], BF16)
        Lt = [_Lt, _Lt]
        _MG = self.alloc([8, 128], BF16)
        MG = [_MG, _MG]
        _sz = self.alloc([512], F32)
        sz = [_sz, _sz]
        _t1 = self.alloc([512], F32)
        t1 = [_t1, _t1]
        _yg = self.alloc([512], F32)
        yg = [_yg, _yg]
        nssm = self.alloc([512], F32)
        yn = [self.alloc([512], BF16) for i in range(2)]
        ss = self.alloc([4], F32)
        slot, key = self.wslot([(0, [[32, KC], [1, 32]], self.win_cols(l, 9728, 32))], "wdt")
        b0 = self.bank[0]
        for blk in range(NB):
            for kc in range(KC):
                S.add("pe", lambda e, blk=blk, kc=kc, slot=slot: e.matmul(
                    b0[:, blk * 32:(blk + 1) * 32], lhsT=_hT[:, kc, blk * 128:(blk + 1) * 128],
                    rhs=fap(slot, kc * 32, [[1, 32]]), start=(kc == 0), stop=(kc == KC - 1)), r=[key, "hT"], w=["bank0"])
        S.add("dve", lambda e: e.tensor_tensor(out=dtT[:], in0=fap(b0, 0, [[32, NB], [1, 32]]),
                                                 in1=self.dtb_bc[:].unsqueeze(1).to_broadcast([128, NB, 32]), op=ALU.add),
              r=["bank0", "dtb_bc"], w=["dtT"])
        S.add("act", lambda e: e.activation(out=dtT[:], in_=dtT[:], func=AF.Exp), r=["dtT"], w=["dtT"])
        S.add("act", lambda e: e.activation(out=dtT[:], in_=dtT[:], func=AF.Ln, bias=1.0), r=["dtT"], w=["dtT"])
        if self.stop_after == "ssd_p1":
            return
        for hf in range(2):
            S.add("dve", lambda e, hf=hf: e.tensor_tensor(
                out=dA2[:, :, hf * 32:(hf + 1) * 32], in0=dtT[:],
                in1=self.a_bc[:].unsqueeze(1).to_broadcast([128, NB, 32]), op=ALU.mult), r=["dtT", "a_bc"], w=["dA2"])
        S.add("dve", lambda e: e.memset(L1[0:32, :], 1.0), w=["L1a"])
        for b4 in range(NB // 4):
            bi = 1 + b4 % 2
            Bk = self.bank[bi]
            bk = "bank%d" % bi
            for q in range(4):
                blk = b4 * 4 + q
                S.add("pe", lambda e, Bk=Bk, q=q, blk=blk: e.matmul(
                    Bk[0:64, q * 128:(q + 1) * 128], lhsT=dA2[:, blk, :], rhs=self.tri[:, :], start=True, stop=True),
                    r=["dA2", "cf32"], w=[bk])
            S.add("act", lambda e, Bk=Bk, b4=b4: e.activation(out=acsT[0:32, b4 * 512:(b4 + 1) * 512], in_=Bk[0:32, :],
                                                               func=AF.Copy), r=[bk], w=["acsT"])
            S.add("act", lambda e, Bk=Bk, b4=b4: e.activation(out=L1[32:64, b4 * 512:(b4 + 1) * 512], in_=Bk[32:64, :],
                                                               func=AF.Copy, scale=-1.0), r=[bk], w=["L1b"])
        if self.stop_after == "ssd_p2":
            return
        b3 = self.bank[3]
        for blk in range(NB):
            S.add("pe", lambda e, blk=blk: e.matmul(b3[:, blk * 32:(blk + 1) * 32], lhsT=acsT[0:32, blk * 128:(blk + 1) * 128],
                                                    rhs=self.identS[0:32, 0:32], start=True, stop=True),
                  r=["acsT", "cf32"], w=["bank3"])
        S.add("dve", lambda e: e.tensor_copy(out=acs[:], in_=fap(b3, 0, [[32, NB], [1, 32]])), r=["bank3"], w=["acs"])
        if self.stop_after == "ssd_p3":
            return
        b4_ = self.bank[4]
        BD = self.alloc([NB, 32], F32)
        S.add("dve", lambda e: e.tensor_tensor(
            out=BD[0:32], in0=fap(acsT, 127, [[128, NB]], pn=32).unsqueeze(2).to_broadcast([32, NB, 32]),
            in1=self.identS[0:32, 0:32].unsqueeze(1).to_broadcast([32, NB, 32]), op=ALU.mult), r=["acsT", "cf32"], w=["BD"])
        S.add("pe", lambda e: e.matmul(b4_[:, :], lhsT=L1[0:32, 0:128], rhs=fap(BD, 0, [[1, NB * 32]], pn=32),
                                       start=True, stop=True), r=["BD", "L1a"], w=["bank4"])
        if self.stop_after == "ssd_p4":
            return
        S.add("act", lambda e: e.activation(out=dec[:], in_=fap(b4_, 0, [[32, NB], [1, 32]]), func=AF.Exp), r=["bank4"], w=["dec"])
        if self.stop_after == "ssd_p5":
            return
        S.add("act", lambda e: e.activation(out=dtw[:], in_=fap(b4_, 0, [[32, NB], [1, 32]]), func=AF.Copy), r=["bank4"], w=["dtw"])
        S.add("pool", lambda e: e.tensor_tensor(out=dtw[:], in0=dtw[:], in1=acs[:], op=ALU.subtract), r=["dtw", "acs"], w=["dtw"])
        if self.stop_after == "ssd_p6":
            return
        S.add("act", lambda e: e.activation(out=dtw[:], in_=dtw[:], func=AF.Exp), r=["dtw"], w=["dtw"])
        S.add("dve", lambda e: e.tensor_tensor(out=dtw[:], in0=dtw[:], in1=dtT[:], op=ALU.mult), r=["dtw", "dtT"], w=["dtw"])
        if self.stop_after == "ssd_p7":
            return
        S.add("act", lambda e: e.activation(out=eacs[:], in_=acs[:], func=AF.Exp), r=["acs"], w=["eacs"])
        if self.stop_after == "ssd_pre":
            return
        for G in range(4):
            chunks = [(6656 + G * 512 + q * 128, G * 4 + q, ("x", q)) for q in range(4)]
            chunks += [(6656 + 2048 + G * 128, 16 + G, ("B", 0)), (6656 + 2560 + G * 128, 20 + G, ("C", 0))]
            for (c0, ci, (kind, q)) in chunks:
                slot, key = self.wslot([(0, [[128, KC], [1, 128]], self.win_cols(l, c0, 128))], "wx")

                def ev(tt, Bk, bk):
                    S.add("act", lambda e: e.activation(out=raw[:, 3 + tt * 512:3 + (tt + 1) * 512], in_=Bk[:, :], func=AF.Copy),
                          r=[bk], w=["raw"])
                S.add("act", lambda e, ci=ci: e.activation(out=raw[:, 0:3], in_=self.convhist[:, ci, :], func=AF.Copy),
                      r=["convhist"], w=["raw"])
                self.proj_tiles(slot, 0, 128, key, T // 512, [5, 6], ev)
                S.add("dve", lambda e, ci=ci: e.tensor_scalar(
                    out=acc[:], in0=raw[:, 3:3 + T], scalar1=self.convw[:, ci, 3:4], scalar2=self.convb[:, ci:ci + 1],
                    op0=ALU.mult, op1=ALU.add), r=["raw", "convw", "convb"], w=["acc"])
                for j in (2, 1, 0):
                    S.add("dve", lambda e, ci=ci, j=j: e.scalar_tensor_tensor(
                        out=acc[:], in0=raw[:, j:j + T], scalar=self.convw[:, ci, j:j + 1], in1=acc[:],
                        op0=ALU.mult, op1=ALU.add), r=["raw", "convw", "acc"], w=["acc"])
                dst = xTc[:, q, :] if kind == "x" else (BT[:] if kind == "B" else CT[:])
                dk = {"x": "xTc", "B": "BT", "C": "CT"}[kind]
                S.add("act", lambda e, dst=dst: e.activation(out=dst, in_=acc[:], func=AF.Silu), r=["acc"], w=[dk])
                S.add("act", lambda e, ci=ci: e.activation(out=self.convhist[:, ci, :], in_=raw[:, T:T + 3], func=AF.Copy),
                      r=["raw"], w=["convhist"])
            if self.stop_after == "ssd_conv":
                continue
            S.dma("sp", nssm[:], self.norm_ssm[l, G * 512:(G + 1) * 512].partition_broadcast(128), w=["nssm"], chan="misc")
            S.add("dve", lambda e, G=G: e.tensor_copy(
                out=R1[32:64], in_=self.identS[32:64, 8 * G:8 * G + 8].unsqueeze(2).to_broadcast([32, 8, 128])),
                r=["cf32"], w=["R1b"])
            Sg = self.Sst[:, G * 512:(G + 1) * 512]
            S.add("act", lambda e, Sg=Sg: e.activation(out=Sbf[:], in_=Sg, func=AF.Copy), r=["Sst"], w=["Sbf"])
            wz, wzk = self.wslot([(0, [[512, KC], [1, 512]], self.win_cols(l, 4608 + G * 512, 512))], "wz")
            for c in range(NB if self.stop_after != "ssd_c1" else 1):
                i = c % 2
                tk = lambda nm: ("%s" % nm) if nm in ("Lt", "MG", "sz", "t1", "yg") else "%s%d" % (nm, i)
                tok = slice(c * 128, (c + 1) * 128)
                hs = slice(8 * G, 8 * G + 8)
                trb = self.bank_bf[7]
                for q in range(4):
                    S.add("pe", lambda e, q=q, tok=tok: e.transpose(out=trb[:, q * 128:(q + 1) * 128], in_=xTc[:, q, tok],
                                                                    identity=self.ident_bf), r=["xTc", "cbf"], w=["bank7"])
                S.add("pe", lambda e, tok=tok: e.transpose(out=trb[:, 512:640], in_=BT[:, tok], identity=self.ident_bf),
                      r=["BT", "cbf"], w=["bank7"])
                S.add("act", lambda e, i=i: e.activation(out=Xtm[i][:], in_=trb[:, 0:512], func=AF.Copy), r=["bank7"], w=[tk("Xtm")])
                S.add("act", lambda e, i=i: e.activation(out=Btm[i][:], in_=trb[:, 512:640], func=AF.Copy), r=["bank7"], w=[tk("Btm")])
                bc = lambda t_, c=c, hs=hs: t_[:, c, hs].unsqueeze(2).to_broadcast([128, 8, 64])
                x3 = lambda t_: t_[:].rearrange("p (e q) -> p e q", e=8)
                S.add("pool", lambda e, i=i, bc=bc, x3=x3: e.tensor_tensor(out=x3(Xdt[i]), in0=x3(Xtm[i]), in1=bc(dtT), op=ALU.mult),
                      r=[tk("Xtm"), "dtT"], w=[tk("Xdt")])
                S.add("pool", lambda e, i=i, bc=bc, x3=x3: e.tensor_tensor(out=x3(Xw[i]), in0=x3(Xtm[i]), in1=bc(dtw), op=ALU.mult),
                      r=[tk("Xtm"), "dtw"], w=[tk("Xw")])
                S.add("dve", lambda e, i=i, x3=x3, hs=hs: e.tensor_tensor(
                    out=x3(XD[i]), in0=x3(Xtm[i]), in1=self.dsk_bc[:, hs].unsqueeze(2).to_broadcast([128, 8, 64]), op=ALU.mult),
                    r=[tk("Xtm"), "dsk_bc"], w=[tk("XD")])
                b0_ = self.bank[0]
                S.add("pe", lambda e, tok=tok: e.matmul(b0_[:, 0:128], lhsT=BT[:, tok], rhs=CT[:, tok], start=True, stop=True),
                      r=["BT", "CT"], w=["bank0"])
                S.add("dve", lambda e, i=i: e.tensor_tensor(out=CBm[i][:], in0=b0_[:, 0:128], in1=self.tri[:, :], op=ALU.mult),
                      r=["bank0", "cf32"], w=[tk("CBm")])
                S.add("dve", lambda e, tok=tok, G=G: e.tensor_tensor(
                    out=R1[0:32], in0=acsT[0:32, tok].unsqueeze(1).to_broadcast([32, 8, 128]),
                    in1=self.identS[0:32, 8 * G:8 * G + 8].unsqueeze(2).to_broadcast([32, 8, 128]), op=ALU.mult),
                    r=["acsT", "cf32"], w=["R1a"])
                for hh in range(2):
                    Bh = self.bank[1 + hh]
                    S.add("pe", lambda e, Bh=Bh: e.matmul(Bh[:, :], lhsT=self.ident_bf, rhs=self.negm4, start=True, stop=False),
                          r=["cbf"], w=["bank%d" % (1 + hh)])
                    S.add("pe", lambda e, Bh=Bh, hh=hh, tok=tok: e.matmul(
                        Bh[:, :], lhsT=L1[0:64, tok], rhs=fap(R1, hh * 512, [[1, 512]], pn=64), start=False, stop=True),
                        r=["L1a", "L1b", "R1a", "R1b"], w=["bank%d" % (1 + hh)])
                S.add("act", lambda e, i=i: e.activation(out=fap(Lt[i], 0, [[1, 1024]]), in_=self.psall[:, 512:1536], func=AF.Exp),
                      r=["bank1", "bank2"], w=[tk("Lt")])
                S.add("dve", lambda e, i=i: e.tensor_tensor(out=MG[i][:], in0=Lt[i][:],
                                                             in1=CBm[i][:].unsqueeze(1).to_broadcast([128, 8, 128]), op=ALU.mult),
                      r=[tk("Lt"), tk("CBm")], w=[tk("MG")])
                bY, bO, bC, bZ = self.bank[3], self.bank[4], self.bank[5], self.bank[6]
                S.add("pe", lambda e, i=i: e.matmul(bY[:, :], lhsT=self.ident_bf, rhs=XD[i][:], start=True, stop=False),
                      r=["cbf", tk("XD")], w=["bank3"])
                for hd in range(8):
                    S.add("pe", lambda e, i=i, hd=hd: e.matmul(
                        bY[:, hd * 64:(hd + 1) * 64], lhsT=MG[i][:, hd, :], rhs=Xdt[i][:, hd * 64:(hd + 1) * 64],
                        start=False, stop=(hd == 7), skip_group_check=True), r=[tk("MG"), tk("Xdt")], w=["bank3"])
                S.add("pe", lambda e, tok=tok: e.matmul(bO[:, :], lhsT=CT[:, tok], rhs=Sbf[:], start=True, stop=True),
                      r=["CT", "Sbf"], w=["bank4"])
                S.add("pe", lambda e, i=i: e.matmul(bC[:, :], lhsT=Btm[i][:], rhs=Xw[i][:], start=True, stop=True),
                      r=[tk("Btm"), tk("Xw")], w=["bank5"])
                for kc in range(KC):
                    S.add("pe", lambda e, kc=kc, tok=tok, wz=wz: e.matmul(
                        bZ[:, :], lhsT=_hT[:, kc, tok], rhs=fap(wz, kc * 512, [[1, 512]]), start=(kc == 0), stop=(kc == KC - 1)),
                        r=["hT", wzk], w=["bank6"])
                S.add("act", lambda e, i=i: e.activation(out=sz[i][:], in_=bZ[:, :], func=AF.Silu), r=["bank6"], w=[tk("sz")])
                S.add("dve", lambda e, i=i, bc=bc, x3=x3: e.tensor_tensor(
                    out=x3(t1[i]), in0=fap(bO, 0, [[64, 8], [1, 64]]), in1=bc(eacs), op=ALU.mult), r=["bank4", "eacs"], w=[tk("t1")])
                S.add("dve", lambda e, i=i: e.tensor_tensor(out=t1[i][:], in0=bY[:, :], in1=t1[i][:], op=ALU.add),
                      r=["bank3", tk("t1")], w=[tk("t1")])
                S.add("pool", lambda e, i=i: e.tensor_tensor(out=yg[i][:], in0=t1[i][:], in1=sz[i][:], op=ALU.mult),
                      r=[tk("t1"), tk("sz")], w=[tk("yg")])
                S.add("act", lambda e, i=i: e.activation(out=t1[i][:], in_=yg[i][:], func=AF.Square), r=[tk("yg")], w=[tk("t1")])
                S.add("dve", lambda e, i=i: e.reduce_sum(out=ss[:, 0:1], in_=t1[i][:], axis=AX.X), r=[tk("t1")], w=["ss"])
                S.add("dve", lambda e: e.tensor_scalar(out=ss[:, 1:2], in0=ss[:, 0:1], scalar1=1.0 / 512, scalar2=EPS,
                                                        op0=ALU.mult, op1=ALU.add), r=["ss"], w=["ss1"])
                S.add("act", lambda e: e.activation(out=ss[:, 2:3], in_=ss[:, 1:2], func=AF.Sqrt), r=["ss1"], w=["ss2"])
                S.add("dve", lambda e: e.reciprocal(out=ss[:, 3:4], in_=ss[:, 2:3]), r=["ss2"], w=["ss3"])
                S.add("dve", lambda e, i=i, G=G: e.scalar_tensor_tensor(
                    out=yn[i][:], in0=yg[i][:], scalar=ss[:, 3:4], in1=nssm[:],
                    op0=ALU.mult, op1=ALU.mult), r=[tk("yg"), "ss3", "nssm"], w=[tk("yn")])
                trb2 = self.bank_bf[0]
                for q in range(4):
                    S.add("pe", lambda e, i=i, q=q: e.transpose(out=trb2[:, 512 + q * 128:512 + (q + 1) * 128],
                                                                in_=yn[i][:, q * 128:(q + 1) * 128], identity=self.ident_bf),
                          r=[tk("yn"), "cbf"], w=["bank0"])
                S.add("act", lambda e, c=c: e.activation(out=fap(yT, c * 128, [[T, 4], [1, 128]]),
                                                         in_=fap(trb2, 512, [[128, 4], [1, 128]]), func=AF.Copy),
                      r=["bank0"], w=["yT"])
                S.add("dve", lambda e, Sg=Sg, bc=bc: e.tensor_tensor(
                    out=Sg.rearrange("p (e q) -> p e q", e=8), in0=Sg.rearrange("p (e q) -> p e q", e=8), in1=bc(dec),
                    op=ALU.mult), r=["Sst", "dec"], w=["Sst"])
                S.add("dve", lambda e, Sg=Sg: e.tensor_tensor(out=Sg, in0=bC[:, :], in1=Sg, op=ALU.add), r=["Sst", "bank5"], w=["Sst"])
                S.add("act", lambda e, Sg=Sg: e.activation(out=Sbf[:], in_=Sg, func=AF.Copy), r=["Sst"], w=["Sbf"])
            S.dma("sp", self.yscr[4 * G:4 * G + 4].rearrange("c p t -> p c t"), yT[:], r=["yT"], w=["yscr"], chan="yscr")
        if last_grp:
            S.dma("sp", self.ssm_out[l], self.Sst[:], r=["Sst"], w=["ssmout"], chan="stout")
            S.dma("sp", self.conv_out[l], self.convhist[:], r=["convhist"], w=["convout"], chan="stout")
            if "stout" not in self.out_chans:
                self.out_chans.append("stout")

    def tail(self, l, g0):
        _hT = self.hT
        _sqf = self.sqf
        _attoT = self.attoT
        S = self.S
        self.norm_bufs()
        ysT = self.alloc([16, 512], BF16)
        mT = self.alloc([KC, 512], BF16)
        sga = self.alloc([512], F32)
        sgs = self.alloc([512], F32)
        ta = self.alloc([512], F32)
        tsb = self.alloc([512], F32)
        oT = self.alloc([KC, 512], F32)
        wa = self.w_br_att[l].rearrange("(k p) n -> p k n", p=128)
        ws = self.w_br_ssm[l].rearrange("(k p) n -> p k n", p=128)
        wo = self.w_out[l].rearrange("(k p) n -> p k n", p=128)
        for tt in range(T // 512):
            tsl = slice(tt * 512, (tt + 1) * 512)
            S.dma("sp", ysT[:], self.yscr[:, :, tsl].rearrange("c p t -> p c t"), r=["yscr"], w=["ysT"], chan="ysT")
            for m in range(KC):
                ms = slice(m * 128, (m + 1) * 128)
                s1, k1 = self.wslot([(0, [[128, 4], [1, 128]], wa[:, :, ms]),
                                     (512, [[128, KC], [1, 128]], self.win_cols(l, 9760 + m * 128, 128)),
                                     (1536, [[128, KC], [1, 128]], self.win_cols(l, 10784 + m * 128, 128))], "wt1")
                s2, k2 = self.wslot([(0, [[128, 16], [1, 128]], ws[:, :, ms])], "wt2")
                pa_, ps_, pga, pgs = self.bank[0], self.bank[1], self.bank[2], self.bank[3]
                for kk in range(4):
                    S.add("pe", lambda e, kk=kk, tsl=tsl, s1=s1: e.matmul(pa_[:, :], lhsT=fap(s1, kk * 128, [[1, 128]]),
                                                                   rhs=_attoT[:, kk, tsl], start=(kk == 0), stop=(kk == 3)),
                          r=[k1, "attoT"], w=["bank0"])
                for kk in range(16):
                    S.add("pe", lambda e, kk=kk, s2=s2: e.matmul(ps_[:, :], lhsT=fap(s2, kk * 128, [[1, 128]]), rhs=ysT[:, kk, :],
                                                          start=(kk == 0), stop=(kk == 15)), r=[k2, "ysT"], w=["bank1"])
                for kc in range(KC):
                    S.add("pe", lambda e, kc=kc, tsl=tsl, s1=s1: e.matmul(pga[:, :], lhsT=fap(s1, 512 + kc * 128, [[1, 128]]),
                                                                   rhs=_hT[:, kc, tsl], start=(kc == 0), stop=(kc == KC - 1)),
                          r=[k1, "hT"], w=["bank2"])
                for kc in range(KC):
                    S.add("pe", lambda e, kc=kc, tsl=tsl, s1=s1: e.matmul(pgs[:, :], lhsT=fap(s1, 1536 + kc * 128, [[1, 128]]),
                                                                   rhs=_hT[:, kc, tsl], start=(kc == 0), stop=(kc == KC - 1)),
                          r=[k1, "hT"], w=["bank3"])
                S.add("act", lambda e: e.activation(out=sga[:], in_=pga[:, :], func=AF.Sigmoid), r=["bank2"], w=["sga"])
                S.add("act", lambda e: e.activation(out=sgs[:], in_=pgs[:, :], func=AF.Sigmoid), r=["bank3"], w=["sgs"])
                S.add("dve", lambda e: e.tensor_tensor(out=ta[:], in0=pa_[:, :], in1=sga[:], op=ALU.mult), r=["bank0", "sga"], w=["ta"])
                S.add("dve", lambda e: e.tensor_tensor(out=tsb[:], in0=ps_[:, :], in1=sgs[:], op=ALU.mult), r=["bank1", "sgs"], w=["tsb"])
                S.add("dve", lambda e, m=m: e.tensor_tensor(out=mT[:, m, :], in0=ta[:], in1=tsb[:], op=ALU.add),
                      r=["ta", "tsb"], w=["mT"])
            for m2 in range(KC):
                s3, k3 = self.wslot([(0, [[128, KC], [1, 128]], wo[:, :, m2 * 128:(m2 + 1) * 128])], "wo")
                bi = 4 + m2 % 2
                Bo = self.bank[bi]
                for kc in range(KC):
                    S.add("pe", lambda e, kc=kc, Bo=Bo, s3=s3: e.matmul(Bo[:, :], lhsT=fap(s3, kc * 128, [[1, 128]]), rhs=mT[:, kc, :],
                                                                 start=(kc == 0), stop=(kc == KC - 1)), r=[k3, "mT"], w=["bank%d" % bi])
                S.add("act", lambda e, Bo=Bo, m2=m2: e.activation(out=oT[:, m2, :], in_=Bo[:, :], func=AF.Copy), r=["bank%d" % bi], w=["oT"])
                S.add("act", lambda e, Bo=Bo, m2=m2: e.activation(out=_sqf[:, m2, :], in_=Bo[:, :], func=AF.Square),
                      r=["bank%d" % bi], w=["sqo"])
            self.postnorm_residual(oT[:], lambda c: _sqf[:, c, :], 1, self.xres, self.xres, g0 + tt * 512, ["oT", "sqo"])

    def s_declare(self):
        L = self.depth
        self.xsT_in = self.din("xsT_in", [128, KC, NS])
        self.scst_in = self.din("scst_in", [128, 1172])
        self.rope_s = self.din("rope_s", [NS, 256])
        self.cache = [self.din("cache%d" % g, [L, NS, WIN[g], 1024]) for g in range(3)]
        self.st_ssm = self.din("st_ssm", [L, NS, 2048, 128])
        self.st_conv = self.din("st_conv", [L, NS, 3, 3072])
        self.convw_rep = self.din("convw_rep", [L, NS, 4, 3072])
        self.convb_rep = self.din("convb_rep", [L, NS, 3072])
        self.dtb_rep = self.din("dtb_rep", [L, NS, 32])
        self.alog_rep = self.din("alog_rep", [L, NS, 32])
        self.dsk_rep = self.din("dsk_rep", [L, NS, 32])
        self.nssm_rep = self.din("nssm_rep", [L, NS, 2048])
        self.ys_out = self.dout("ys_out", [128, KC, NS])
        self.kvs_out = [self.dout("kvs_out%d" % g, [L, NS, 2, 512]) for g in range(3)]
        self.ssms_out = self.dout("ssms_out", [L, NS, 2048, 128])
        self.convs_out = self.dout("convs_out", [L, NS, 3, 3072])

    def s_consts(self):
        S = self.S
        self.xsT = S.sb("xsT", [128, KC, NS], F32)
        S.dma("sp", self.xsT[:], self.xsT_in, w=["xsT"], chan="misc")
        self.ones_f = S.sb("ones_f", [128, 4], F32)
        S.add("pool", lambda e: e.memset(self.ones_f[:], 1.0), w=["ones_f"])

    def s_load_consts(self):
        S = self.S
        self.scst = self.alloc([1172], F32)
        S.dma("sp", self.scst, self.scst_in, w=["scst"], chan="s_ld")
        self.identF = self.scst[:, 0:128]
        self.selB = lambda b: self.scst[0:NS, 128 + b * 128:128 + (b + 1) * 128]
        self.selcol = lambda b: self.scst[0:4, 640 + b * NS:640 + (b + 1) * NS]
        self.I4 = self.scst[0:4, 656:660]
        self.bdmask = self.scst[0:4, 660:1172]
        self.ropes = self.alloc([256], F32)
        S.dma("sp", self.ropes[0:NS, :], self.rope_s, w=["ropes"], chan="s_ld")

    def s_rstd(self, ps_ap, denom, rs, rkey):
        S = self.S
        S.add("act", lambda e: e.activation(out=rs, in_=ps_ap, func=AF.Copy), r=[rkey], w=["s_rs"])
        S.add("dve", lambda e: e.tensor_scalar(out=rs, in0=rs, scalar1=1.0 / denom, scalar2=EPS, op0=ALU.mult, op1=ALU.add),
              r=["s_rs"], w=["s_rs"])
        S.add("act", lambda e: e.activation(out=rs, in_=rs, func=AF.Sqrt), r=["s_rs"], w=["s_rs"])
        S.add("dve", lambda e: e.reciprocal(out=rs, in_=rs), r=["s_rs"], w=["s_rs"])

    def s_norm_stat(self, src3, srckeys):
        S = self.S
        sq = self.alloc([KC * NS], BF16)
        rs = self.alloc([NS], F32)
        st = self.bank[6]
        S.add("act", lambda e: e.activation(out=sq.rearrange("p (c n) -> p c n", n=NS), in_=src3, func=AF.Square),
              r=list(srckeys), w=["s_sq"])
        for c in range(KC):
            S.add("pe", lambda e, c=c: e.matmul(st[:, 0:NS], lhsT=self.ones_bf[:, :], rhs=sq[:, c * NS:(c + 1) * NS],
                                                  start=(c == 0), stop=(c == KC - 1)), r=["s_sq", "ones_bf"], w=["bank6"])
        self.s_rstd(st[:, 0:NS], D, rs, "bank6")
        return rs

    def s_prenorm(self, k):
        S = self.S
        rs = self.s_norm_stat(self.xsT[:], ["xsT"])
        tmp = self.alloc([KC * NS], F32)
        tmp3 = tmp.rearrange("p (c n) -> p c n", n=NS)
        hs = self.alloc([KC * NS], BF16)
        hs3 = hs.rearrange("p (c n) -> p c n", n=NS)
        S.add("dve", lambda e: e.tensor_tensor(out=tmp3, in0=self.xsT[:], in1=rs.unsqueeze(1).to_broadcast([128, KC, NS]),
                                                 op=ALU.mult), r=["xsT", "s_rs"], w=["s_tmp"])
        S.add("dve", lambda e: e.tensor_tensor(out=tmp3, in0=tmp3, in1=self.Aall[:, k, :, 1:NCOL], op=ALU.mult),
              r=["s_tmp", "Aall"], w=["s_tmp"])
        S.add("pool", lambda e: e.tensor_tensor(out=hs3, in0=tmp3, in1=self.modT[:, k * 24:k * 24 + 8, 1:NCOL], op=ALU.add),
              r=["s_tmp", "modT"], w=["s_hs"])
        return hs3

    def s_postnorm(self, f3, fkeys, k):
        S = self.S
        rs = self.s_norm_stat(f3, fkeys)
        tmp = self.alloc([KC * NS], F32)
        tmp3 = tmp.rearrange("p (c n) -> p c n", n=NS)
        S.add("dve", lambda e: e.tensor_tensor(out=tmp3, in0=f3, in1=rs.unsqueeze(1).to_broadcast([128, KC, NS]),
                                                 op=ALU.mult), r=list(fkeys) + ["s_rs"], w=["s_tmp2"])
        S.add("dve", lambda e: e.tensor_tensor(out=tmp3, in0=tmp3, in1=self.Gall[:, k, :, 1:NCOL], op=ALU.mult),
              r=["s_tmp2", "Gall"], w=["s_tmp2"])
        S.add("pool", lambda e: e.tensor_tensor(out=self.xsT[:], in0=self.xsT[:], in1=tmp3, op=ALU.add),
              r=["s_tmp2", "xsT"], w=["xsT"])

    def s_ffn(self, l, i, k):
        S = self.S
        self.phase(0)
        hs3 = self.s_prenorm(k)
        w1 = self.w_ff_in[l, i].rearrange("(kc p) n -> p kc n", p=128)
        w2 = self.w_ff_out[l, i].rearrange("(j p) n -> p j n", p=128)
        bu = self.bank[0]
        for jb in range(NJ // 2):
            slot, key = self.wslot([
                (0, [[256, KC], [1, 256]], w1[:, :, jb * 256:(jb + 1) * 256]),
                (2048, [[256, KC], [1, 256]], w1[:, :, DFF + jb * 256:DFF + (jb + 1) * 256])], "w1s")
            for jj in range(2):
                j = jb * 2 + jj
                for half in range(2):
                    col = (half * NJ + j) * NS
                    for kc in range(KC):
                        S.add("pe", lambda e, kc=kc, jj=jj, half=half, col=col, slot=slot: e.matmul(
                            bu[:, col:col + NS], lhsT=fap(slot, half * 2048 + kc * 256 + jj * 128, [[1, 128]]),
                            rhs=hs3[:, kc, :], start=(kc == 0), stop=(kc == KC - 1)), r=[key, "s_hs"], w=["bank0"])
        sg = self.alloc([NJ * NS], F32)
        up = self.alloc([NJ * NS], F32)
        aT = self.alloc([NJ * NS], BF16)
        S.add("act", lambda e: e.activation(out=sg, in_=bu[:, 0:NJ * NS], func=AF.Silu), r=["bank0"], w=["s_sg"])
        S.add("act", lambda e: e.activation(out=up, in_=bu[:, NJ * NS:2 * NJ * NS], func=AF.Copy), r=["bank0"], w=["s_up"])
        S.add("dve", lambda e: e.tensor_tensor(out=aT, in0=sg, in1=up, op=ALU.mult), r=["s_sg", "s_up"], w=["s_aT"])
        bf_ = self.bank[1]
        for m in range(KC):
            slot, key = self.wslot([(0, [[128, NJ], [1, 128]], w2[:, :, m * 128:(m + 1) * 128])], "w2s")
            for j in range(NJ):
                S.add("pe", lambda e, j=j, m=m, slot=slot: e.matmul(
                    bf_[:, m * NS:(m + 1) * NS], lhsT=fap(slot, j * 128, [[1, 128]]), rhs=aT[:, j * NS:(j + 1) * NS],
                    start=(j == 0), stop=(j == NJ - 1)), r=[key, "s_aT"], w=["bank1"])
        fT = self.alloc([KC * NS], F32)
        S.add("act", lambda e: e.activation(out=fT, in_=bf_[:, 0:KC * NS], func=AF.Copy), r=["bank1"], w=["s_fT"])
        self.s_postnorm(fT.rearrange("p (c n) -> p c n", n=NS), ["s_fT"], k)

    def s_transpose_to_fm(self, tok_ap, nch, out_tile, bank_i, rkeys, okey):
        S = self.S
        Bk = self.bank[bank_i]
        bk = "bank%d" % bank_i
        for c in range(nch):
            S.add("pe", lambda e, c=c: e.matmul(Bk[:, c * NS:(c + 1) * NS], lhsT=tok_ap[:, c * 128:(c + 1) * 128],
                                                  rhs=self.I4[0:NS, 0:NS], start=True, stop=True),
                  r=list(rkeys) + ["scst"], w=[bk])
        S.add("act", lambda e: e.activation(out=out_tile, in_=Bk[:, 0:nch * NS], func=AF.Copy), r=[bk], w=[okey])

    def s_mixer(self, l):
        S = self.S
        self.phase(0)
        hs3 = self.s_prenorm(1)
        U = self.alloc([INC], F32)
        Us = lambda a, n: U[0:NS, a:a + n]
        abr = self.alloc([D], F32)
        self.s_load_consts()
        mark = self.aptr
        nblk = (INC + 511) // 512
        for blk in range(nblk):
            c0 = blk * 512
            n = min(512, INC - c0)
            slot, key = self.wslot([(0, [[n, KC], [1, n]], self.win_cols(l, c0, n))], "wins")
            bi = blk % 2
            Bk = self.bank[bi]
            bk = "bank%d" % bi
            for kc in range(KC):
                S.add("pe", lambda e, kc=kc, n=n, Bk=Bk, slot=slot: e.matmul(
                    Bk[0:NS, 0:n], lhsT=hs3[:, kc, :], rhs=fap(slot, kc * n, [[1, n]]), start=(kc == 0), stop=(kc == KC - 1)),
                    r=[key, "s_hs"], w=[bk])
            S.add("act", lambda e, c0=c0, n=n, Bk=Bk: e.activation(out=Us(c0, n), in_=Bk[0:NS, 0:n], func=AF.Copy),
                  r=[bk], w=["U"])
        QK = self.alloc([3 * 1024], F32)
        t1 = self.alloc([1024], F32)
        t2 = self.alloc([1024], F32)
        v3 = lambda ap, d=128: ap.rearrange("p (a d) -> p a d", d=d)
        cosF = self.ropes[0:NS, 0:128]
        sinS = self.ropes[0:NS, 128:256]
        for g in range(3):
            x = v3(Us(g * 1536, 1024))
            S.add("dve", lambda e, x=x: e.tensor_tensor(out=v3(t1[0:NS, :]), in0=x, in1=cosF.unsqueeze(1).to_broadcast([NS, 8, 128]),
                                                         op=ALU.mult), r=["U", "ropes"], w=["s_t1"])
            S.add("pool", lambda e, x=x: e.tensor_tensor(out=v3(t2[0:NS, :])[:, :, 0:64], in0=x[:, :, 64:128],
                                                          in1=sinS[:, 0:64].unsqueeze(1).to_broadcast([NS, 8, 64]), op=ALU.mult),
                  r=["U", "ropes"], w=["s_t2a"])
            S.add("pool", lambda e, x=x: e.tensor_tensor(out=v3(t2[0:NS, :])[:, :, 64:128], in0=x[:, :, 0:64],
                                                          in1=sinS[:, 64:128].unsqueeze(1).to_broadcast([NS, 8, 64]), op=ALU.mult),
                  r=["U", "ropes"], w=["s_t2b"])
            S.add("dve", lambda e, g=g: e.tensor_tensor(out=QK[0:NS, g * 1024:(g + 1) * 1024], in0=t1[0:NS, :], in1=t2[0:NS, :],
                                                         op=ALU.add), r=["s_t1", "s_t2a", "s_t2b"], w=["QK"])
            S.dma("sp", self.kvs_out[g][l, :, 0, :], QK[0:NS, g * 1024 + 512:(g + 1) * 1024], r=["QK"], w=["kvs_o"], chan="s_out")
            S.dma("sp", self.kvs_out[g][l, :, 1, :], Us(g * 1536 + 1024, 512), r=["U"], w=["kvs_o"], chan="s_out")
        if "s_out" not in self.out_chans:
            self.out_chans.append("s_out")
        P0 = self.alloc([512], F32)
        s0 = self.alloc([12], F32)
        p0 = self.alloc([12], F32)
        N0 = self.alloc([512], F32)
        D0 = self.alloc([4], F32)
        for g in range(3):
            S.add("pool", lambda e, g=g: e.tensor_tensor(out=P0[0:NS, :], in0=QK[0:NS, g * 1024:g * 1024 + 512],
                                                          in1=QK[0:NS, g * 1024 + 512:(g + 1) * 1024], op=ALU.mult),
                  r=["QK"], w=["s_P0"])
            S.add("dve", lambda e, g=g: e.reduce_sum(out=s0[0:NS, g * 4:(g + 1) * 4], in_=v3(P0[0:NS, :]), axis=AX.X),
                  r=["s_P0"], w=["s_s0"])
        S.add("act", lambda e: e.activation(out=p0[0:NS, :], in_=s0[0:NS, :], func=AF.Exp, scale=float(SCALE)), r=["s_s0"], w=["s_p0"])
        for g in range(3):
            vg = v3(Us(g * 1536 + 1024, 512))
            pb = p0[0:NS, g * 4:(g + 1) * 4].unsqueeze(2).to_broadcast([NS, 4, 128])
            if g == 0:
                S.add("dve", lambda e, vg=vg, pb=pb: e.tensor_tensor(out=v3(N0[0:NS, :]), in0=vg, in1=pb, op=ALU.mult),
                      r=["U", "s_p0"], w=["s_N0"])
            else:
                S.add("pool", lambda e, vg=vg, pb=pb: e.tensor_tensor(out=v3(P0[0:NS, :]), in0=vg, in1=pb, op=ALU.mult),
                      r=["U", "s_p0"], w=["s_P0"])
                S.add("dve", lambda e: e.tensor_tensor(out=N0[0:NS, :], in0=N0[0:NS, :], in1=P0[0:NS, :], op=ALU.add),
                      r=["s_N0", "s_P0"], w=["s_N0"])
        S.add("dve", lambda e: e.tensor_tensor(out=D0[0:NS, :], in0=p0[0:NS, 0:4], in1=p0[0:NS, 4:8], op=ALU.add), r=["s_p0"], w=["s_D0"])
        S.add("dve", lambda e: e.tensor_tensor(out=D0[0:NS, :], in0=D0[0:NS, :], in1=p0[0:NS, 8:12], op=ALU.add), r=["s_p0", "s_D0"], w=["s_D0"])
        KV = [self.alloc([1024], F32) for i in range(2)]
        prod = self.alloc([512], F32)
        sc = self.alloc([4], F32)
        pp = [self.alloc([4], F32) for i in range(2)]
        numS = self.alloc([NS * 512], F32)
        denS = self.alloc([NS], F32)
        cnt = 0
        for b in range(NS):
            nb_, db_ = 4 + 2 * (b % 2), 5 + 2 * (b % 2)
            numP, denP = self.bank[nb_], self.bank[db_]
            for g in range(3):
                i = cnt % 2
                cnt += 1
                kv = KV[i]
                kvk = "s_KV%d" % i
                src = self.cache[g][l, b].rearrange("(j d) c -> j d c", d=DIL[g])[:, 0, :]
                S.dma("sp", kv, src, w=[kvk], chan=kvk)
                qi = 2 + i
                qP = self.bank[qi]
                S.add("pe", lambda e, b=b, g=g, qP=qP: e.matmul(qP[:, :], lhsT=self.selB(b), rhs=QK[0:NS, g * 1024:g * 1024 + 512],
                                                                start=True, stop=True), r=["QK", "scst"], w=["bank%d" % qi])
                S.add("dve", lambda e, kv=kv, qP=qP: e.tensor_tensor(out=prod, in0=kv[:, 0:512], in1=qP[:, :], op=ALU.mult),
                      r=[kvk, "bank%d" % qi], w=["s_prod"])
                S.add("dve", lambda e: e.reduce_sum(out=sc, in_=v3(prod), axis=AX.X), r=["s_prod"], w=["s_sc"])
                p_ = pp[i]
                pk = "s_pp%d" % i
                S.add("act", lambda e, p_=p_: e.activation(out=p_, in_=sc, func=AF.Exp, scale=float(SCALE)), r=["s_sc"], w=[pk])
                S.add("pe", lambda e, p_=p_, kv=kv, g=g, numP=numP: e.matmul(numP[0:4, :], lhsT=p_, rhs=kv[:, 512:1024],
                                                                            start=(g == 0), stop=(g == 2)),
                      r=[pk, kvk], w=["bank%d" % nb_])
                S.add("pe", lambda e, p_=p_, g=g, denP=denP: e.matmul(denP[0:4, 0:1], lhsT=p_, rhs=self.ones_f[:, 0:1],
                                                                     start=(g == 0), stop=(g == 2)),
                      r=[pk, "ones_f"], w=["bank%d" % db_])
            S.add("act", lambda e, b=b, numP=numP: e.activation(out=numS[0:4, b * 512:(b + 1) * 512], in_=numP[0:4, :], func=AF.Copy),
                  r=["bank%d" % nb_], w=["s_numS"])
            S.add("act", lambda e, b=b, denP=denP: e.activation(out=denS[0:4, b:b + 1], in_=denP[0:4, 0:1], func=AF.Copy),
                  r=["bank%d" % db_], w=["s_denS"])
        S.add("pool", lambda e: e.tensor_tensor(out=numS[0:4, :].rearrange("p (b n) -> p b n", n=512),
                                                 in0=numS[0:4, :].rearrange("p (b n) -> p b n", n=512),
                                                 in1=self.bdmask.unsqueeze(1).to_broadcast([4, NS, 512]), op=ALU.mult),
              r=["s_numS", "scst"], w=["s_numS"])
        ncP, dcP = self.bank[0], self.bank[1]
        for b in range(NS):
            S.add("pe", lambda e, b=b: e.matmul(ncP[0:NS, :], lhsT=self.selcol(b), rhs=numS[0:4, b * 512:(b + 1) * 512],
                                                  start=(b == 0), stop=(b == NS - 1)), r=["s_numS", "scst"], w=["bank0"])
        S.add("pe", lambda e: e.matmul(dcP[0:NS, 0:4], lhsT=denS[0:4, 0:NS], rhs=self.I4, start=True, stop=True),
              r=["s_denS", "scst"], w=["bank1"])
        Dt = self.alloc([4], F32)
        Nt = self.alloc([512], F32)
        atto = self.alloc([512], F32)
        S.add("act", lambda e: e.activation(out=Dt[0:NS, :], in_=dcP[0:NS, 0:4], func=AF.Copy), r=["bank1"], w=["s_Dt"])
        S.add("dve", lambda e: e.tensor_tensor(out=Dt[0:NS, :], in0=Dt[0:NS, :], in1=D0[0:NS, :], op=ALU.add), r=["s_Dt", "s_D0"], w=["s_Dt"])
        S.add("dve", lambda e: e.reciprocal(out=Dt[0:NS, :], in_=Dt[0:NS, :]), r=["s_Dt"], w=["s_Dt"])
        S.add("act", lambda e: e.activation(out=Nt[0:NS, :], in_=ncP[0:NS, :], func=AF.Copy), r=["bank0"], w=["s_Nt"])
        S.add("dve", lambda e: e.tensor_tensor(out=Nt[0:NS, :], in0=Nt[0:NS, :], in1=N0[0:NS, :], op=ALU.add), r=["s_Nt", "s_N0"], w=["s_Nt"])
        S.add("dve", lambda e: e.tensor_tensor(out=v3(atto[0:NS, :]), in0=v3(Nt[0:NS, :]),
                                                 in1=Dt[0:NS, :].unsqueeze(2).to_broadcast([NS, 4, 128]), op=ALU.mult),
              r=["s_Nt", "s_Dt"], w=["s_atto"])
        attoT = self.alloc([4 * NS], BF16)
        self.s_transpose_to_fm(atto[0:NS, :], 4, attoT, 2, ["s_atto"], "s_attoT")
        wa = self.w_br_att[l].rearrange("(k p) n -> p k n", p=128)
        for cb in range(2):
            slot, key = self.wslot([(0, [[512, 4], [1, 512]], wa[:, :, cb * 512:(cb + 1) * 512])], "was")
            Bk = self.bank[3]
            for kk in range(4):
                S.add("pe", lambda e, kk=kk, slot=slot: e.matmul(Bk[0:NS, :], lhsT=attoT[:, kk * NS:(kk + 1) * NS],
                                                                 rhs=fap(slot, kk * 512, [[1, 512]]), start=(kk == 0), stop=(kk == 3)),
                      r=[key, "s_attoT"], w=["bank3"])
            S.add("act", lambda e, cb=cb: e.activation(out=abr[0:NS, cb * 512:(cb + 1) * 512], in_=Bk[0:NS, :], func=AF.Copy),
                  r=["bank3"], w=["s_abr"])
        self.phase(mark)
        xc = self.alloc([3072], F32)
        cw = self.alloc([4 * 512], F32)
        cs = self.alloc([3 * 512], F32)
        cbi = self.alloc([512], F32)
        acc = self.alloc([512], F32)
        ct = self.alloc([512], F32)
        for q in range(6):
            c0 = q * 512
            S.dma("sp", cw[0:NS, :].rearrange("p (j n) -> p j n", n=512), self.convw_rep[l, :, :, c0:c0 + 512], w=["s_cw"], chan="s_ld")
            S.dma("sp", cs[0:NS, :].rearrange("p (j n) -> p j n", n=512), self.st_conv[l, :, :, c0:c0 + 512], w=["s_cs"], chan="s_ld")
            S.dma("sp", cbi[0:NS, :], self.convb_rep[l, :, c0:c0 + 512], w=["s_cb"], chan="s_ld")
            S.dma("sp", self.convs_out[l, :, 0:2, c0:c0 + 512], cs[0:NS, 512:1536].rearrange("p (j n) -> p j n", n=512),
                  r=["s_cs"], w=["convs_o"], chan="s_out")
            S.dma("sp", self.convs_out[l, :, 2, c0:c0 + 512], Us(6656 + c0, 512), r=["U"], w=["convs_o"], chan="s_out")
            S.add("dve", lambda e, c0=c0: e.tensor_tensor(out=acc[0:NS, :], in0=Us(6656 + c0, 512), in1=cw[0:NS, 1536:2048], op=ALU.mult),
                  r=["U", "s_cw"], w=["s_acc"])
            S.add("dve", lambda e: e.tensor_tensor(out=acc[0:NS, :], in0=acc[0:NS, :], in1=cbi[0:NS, :], op=ALU.add),
                  r=["s_acc", "s_cb"], w=["s_acc"])
            for j in range(3):
                S.add("pool", lambda e, j=j: e.tensor_tensor(out=ct[0:NS, :], in0=cs[0:NS, j * 512:(j + 1) * 512],
                                                              in1=cw[0:NS, j * 512:(j + 1) * 512], op=ALU.mult),
                      r=["s_cs", "s_cw"], w=["s_ct"])
                S.add("dve", lambda e: e.tensor_tensor(out=acc[0:NS, :], in0=acc[0:NS, :], in1=ct[0:NS, :], op=ALU.add),
                      r=["s_acc", "s_ct"], w=["s_acc"])
            S.add("act", lambda e, c0=c0: e.activation(out=xc[0:NS, c0:c0 + 512], in_=acc[0:NS, :], func=AF.Silu), r=["s_acc"], w=["s_xc"])
        dt = self.alloc([32], F32)
        av = self.alloc([32], F32)
        dA = self.alloc([32], F32)
        dsk = self.alloc([32], F32)
        S.dma("sp", dt[0:NS, :], self.dtb_rep[l], w=["s_dt"], chan="s_ld")
        S.dma("sp", av[0:NS, :], self.alog_rep[l], w=["s_av"], chan="s_ld")
        S.dma("sp", dsk[0:NS, :], self.dsk_rep[l], w=["s_dsk"], chan="s_ld")
        S.add("dve", lambda e: e.tensor_tensor(out=dt[0:NS, :], in0=dt[0:NS, :], in1=Us(9728, 32), op=ALU.add), r=["s_dt", "U"], w=["s_dt"])
        S.add("act", lambda e: e.activation(out=dt[0:NS, :], in_=dt[0:NS, :], func=AF.Exp), r=["s_dt"], w=["s_dt"])
        S.add("act", lambda e: e.activation(out=dt[0:NS, :], in_=dt[0:NS, :], func=AF.Ln, bias=1.0), r=["s_dt"], w=["s_dt"])
        S.add("act", lambda e: e.activation(out=av[0:NS, :], in_=av[0:NS, :], func=AF.Exp), r=["s_av"], w=["s_av"])
        S.add("dve", lambda e: e.tensor_tensor(out=dA[0:NS, :], in0=dt[0:NS, :], in1=av[0:NS, :], op=ALU.mult), r=["s_dt", "s_av"], w=["s_dA"])
        S.add("act", lambda e: e.activation(out=dA[0:NS, :], in_=dA[0:NS, :], func=AF.Exp, scale=-1.0), r=["s_dA"], w=["s_dA"])
        xdt = self.alloc([2048], F32)
        dAe = self.alloc([2048], F32)
        v64 = lambda ap: ap.rearrange("p (a d) -> p a d", d=64)
        S.add("dve", lambda e: e.tensor_tensor(out=v64(xdt[0:NS, :]), in0=v64(xc[0:NS, 0:2048]),
                                                 in1=dt[0:NS, :].unsqueeze(2).to_broadcast([NS, 32, 64]), op=ALU.mult),
              r=["s_xc", "s_dt"], w=["s_xdt"])
        S.add("pool", lambda e: e.tensor_copy(out=v64(dAe[0:NS, :]), in_=dA[0:NS, :].unsqueeze(2).to_broadcast([NS, 32, 64])),
              r=["s_dA"], w=["s_dAe"])
        xdtT = self.alloc([16 * NS], F32)
        dAT = self.alloc([16 * NS], F32)
        self.s_transpose_to_fm(xdt[0:NS, :], 16, xdtT, 2, ["s_xdt"], "s_xdtT")
        self.s_transpose_to_fm(dAe[0:NS, :], 16, dAT, 3, ["s_dAe"], "s_dAT")
        Hb = [self.alloc([2048], F32) for i in range(2)]
        outer = self.alloc([2048], F32)
        Bbc = self.alloc([512], F32)
        Cbc = self.alloc([512], F32)
        yTa = self.alloc([NS * 16], F32)
        xdT3 = xdtT.rearrange("p (c n) -> p c n", n=NS)
        dAT3 = dAT.rearrange("p (c n) -> p c n", n=NS)
        for b in range(NS):
            H = Hb[b % 2]
            hk = "s_H%d" % (b % 2)
            H3 = v3(H)
            S.dma("sp", H3, self.st_ssm[l, b].rearrange("(c p) n -> p c n", p=128), w=[hk], chan=hk)
            S.add("pe", lambda e, b=b: e.matmul(self.bank[4][:, :], lhsT=self.selB(b), rhs=xc[0:NS, 2048:2560], start=True, stop=True),
                  r=["s_xc", "scst"], w=["bank4"])
            S.add("pe", lambda e, b=b: e.matmul(self.bank[5][:, :], lhsT=self.selB(b), rhs=xc[0:NS, 2560:3072], start=True, stop=True),
                  r=["s_xc", "scst"], w=["bank5"])
            S.add("act", lambda e: e.activation(out=Bbc, in_=self.bank[4][:, :], func=AF.Copy), r=["bank4"], w=["s_Bbc"])
            S.add("act", lambda e: e.activation(out=Cbc, in_=self.bank[5][:, :], func=AF.Copy), r=["bank5"], w=["s_Cbc"])
            S.add("dve", lambda e, b=b, H3=H3: e.tensor_tensor(out=H3, in0=H3, in1=dAT3[:, :, b:b + 1].to_broadcast([128, 16, 128]),
                                                                op=ALU.mult), r=[hk, "s_dAT"], w=[hk])
            for G in range(4):
                S.add("pool", lambda e, b=b, G=G: e.tensor_tensor(
                    out=v3(outer)[:, 4 * G:4 * G + 4, :], in0=xdT3[:, 4 * G:4 * G + 4, b:b + 1].to_broadcast([128, 4, 128]),
                    in1=Bbc[:, G * 128:(G + 1) * 128].unsqueeze(1).to_broadcast([128, 4, 128]), op=ALU.mult),
                    r=["s_xdtT", "s_Bbc"], w=["s_outer"])
            S.add("dve", lambda e, H=H: e.tensor_tensor(out=H, in0=H, in1=outer, op=ALU.add), r=[hk, "s_outer"], w=[hk])
            S.dma("sp", self.ssms_out[l, b].rearrange("(c p) n -> p c n", p=128), H3, r=[hk], w=["ssms_o"], chan="s_out")
            for G in range(4):
                S.add("pool", lambda e, G=G, H3=H3: e.tensor_tensor(
                    out=v3(outer)[:, 4 * G:4 * G + 4, :], in0=H3[:, 4 * G:4 * G + 4, :],
                    in1=Cbc[:, G * 128:(G + 1) * 128].unsqueeze(1).to_broadcast([128, 4, 128]), op=ALU.mult),
                    r=[hk, "s_Cbc"], w=["s_outer"])
            S.add("dve", lambda e, b=b: e.reduce_sum(out=yTa[:, b * 16:(b + 1) * 16], in_=v3(outer), axis=AX.X),
                  r=["s_outer"], w=["s_yTa"])
        ytok = self.alloc([2048], F32)
        for c in range(16):
            bi = 4 + c // 4
            Bk = self.bank[bi]
            S.add("pe", lambda e, c=c, Bk=Bk: e.matmul(Bk[0:NS, (c % 4) * 128:(c % 4 + 1) * 128],
                                                       lhsT=fap(yTa, c, [[16, NS]]), rhs=self.identF, start=True, stop=True),
                  r=["s_yTa", "scst"], w=["bank%d" % bi])
            if c % 4 == 3:
                S.add("act", lambda e, bi=bi, Bk=Bk: e.activation(out=ytok[0:NS, (bi - 4) * 512:(bi - 3) * 512], in_=Bk[0:NS, :], func=AF.Copy),
                      r=["bank%d" % bi], w=["s_ytok"])
        yt = ytok[0:NS, :]
        tq = self.alloc([2048], F32)
        tqs = tq[0:NS, :]
        S.add("pool", lambda e: e.tensor_tensor(out=v64(tqs), in0=v64(xc[0:NS, 0:2048]), in1=dsk[0:NS, :].unsqueeze(2).to_broadcast([NS, 32, 64]),
                                                 op=ALU.mult), r=["s_xc", "s_dsk"], w=["s_tq"])
        S.add("dve", lambda e: e.tensor_tensor(out=yt, in0=yt, in1=tqs, op=ALU.add), r=["s_ytok", "s_tq"], w=["s_ytok"])
        S.add("act", lambda e: e.activation(out=tqs, in_=Us(4608, 2048), func=AF.Silu), r=["U", "s_ytok"], w=["s_tq"])
        S.add("dve", lambda e: e.tensor_tensor(out=yt, in0=yt, in1=tqs, op=ALU.mult), r=["s_ytok", "s_tq"], w=["s_ytok"])
        S.add("act", lambda e: e.activation(out=tqs, in_=yt, func=AF.Square), r=["s_ytok"], w=["s_tq"])
        ss = self.alloc([4], F32)
        sss = ss[0:NS, :]
        v512 = lambda ap: ap.rearrange("p (a d) -> p a d", d=512)
        S.add("dve", lambda e: e.reduce_sum(out=sss, in_=v512(tqs), axis=AX.X), r=["s_tq"], w=["s_ss"])
        S.add("dve", lambda e: e.tensor_scalar(out=sss, in0=sss, scalar1=1.0 / 512, scalar2=EPS, op0=ALU.mult, op1=ALU.add),
              r=["s_ss"], w=["s_ss"])
        S.add("act", lambda e: e.activation(out=sss, in_=sss, func=AF.Sqrt), r=["s_ss"], w=["s_ss"])
        S.add("dve", lambda e: e.reciprocal(out=sss, in_=sss), r=["s_ss"], w=["s_ss"])
        S.add("dve", lambda e: e.tensor_tensor(out=v512(yt), in0=v512(yt), in1=sss.unsqueeze(2).to_broadcast([NS, 4, 512]), op=ALU.mult),
              r=["s_ytok", "s_ss"], w=["s_ytok"])
        S.dma("sp", tqs, self.nssm_rep[l], r=["s_ss"], w=["s_tq"], chan="s_ld")
        S.add("dve", lambda e: e.tensor_tensor(out=yt, in0=yt, in1=tqs, op=ALU.mult), r=["s_ytok", "s_tq"], w=["s_ytok"])
        ysT = self.alloc([16 * NS], BF16)
        self.s_transpose_to_fm(yt, 16, ysT, 2, ["s_ytok"], "s_ysT")
        sbr = self.alloc([D], F32)
        ws = self.w_br_ssm[l].rearrange("(k p) n -> p k n", p=128)
        for cb in range(4):
            slot, key = self.wslot([(0, [[256, 16], [1, 256]], ws[:, :, cb * 256:(cb + 1) * 256])], "wss")
            Bk = self.bank[3]
            for kk in range(16):
                S.add("pe", lambda e, kk=kk, slot=slot: e.matmul(Bk[0:NS, 0:256], lhsT=ysT[:, kk * NS:(kk + 1) * NS],
                                                                 rhs=fap(slot, kk * 256, [[1, 256]]), start=(kk == 0), stop=(kk == 15)),
                      r=[key, "s_ysT"], w=["bank3"])
            S.add("act", lambda e, cb=cb: e.activation(out=sbr[0:NS, cb * 256:(cb + 1) * 256], in_=Bk[0:NS, 0:256], func=AF.Copy),
                  r=["bank3"], w=["s_sbr"])
        sga = self.alloc([D], F32)
        sgs = self.alloc([D], F32)
        S.add("act", lambda e: e.activation(out=sga[0:NS, :], in_=Us(9760, D), func=AF.Sigmoid), r=["U"], w=["s_sga"])
        S.add("act", lambda e: e.activation(out=sgs[0:NS, :], in_=Us(10784, D), func=AF.Sigmoid), r=["U"], w=["s_sgs"])
        S.add("dve", lambda e: e.tensor_tensor(out=sga[0:NS, :], in0=sga[0:NS, :], in1=abr[0:NS, :], op=ALU.mult), r=["s_sga", "s_abr"], w=["s_sga"])
        S.add("pool", lambda e: e.tensor_tensor(out=sgs[0:NS, :], in0=sgs[0:NS, :], in1=sbr[0:NS, :], op=ALU.mult), r=["s_sgs", "s_sbr"], w=["s_sgs"])
        S.add("dve", lambda e: e.tensor_tensor(out=sga[0:NS, :], in0=sga[0:NS, :], in1=sgs[0:NS, :], op=ALU.add), r=["s_sga", "s_sgs"], w=["s_sga"])
        mT = self.alloc([KC * NS], BF16)
        self.s_transpose_to_fm(sga[0:NS, :], KC, mT, 2, ["s_sga"], "s_mT")
        wo = self.w_out[l].rearrange("(k p) n -> p k n", p=128)
        Bo = self.bank[0]
        for m2 in range(KC):
            slot, key = self.wslot([(0, [[128, KC], [1, 128]], wo[:, :, m2 * 128:(m2 + 1) * 128])], "wos")
            for kc in range(KC):
                S.add("pe", lambda e, kc=kc, m2=m2, slot=slot: e.matmul(Bo[:, m2 * NS:(m2 + 1) * NS], lhsT=fap(slot, kc * 128, [[1, 128]]),
                                                                        rhs=mT[:, kc * NS:(kc + 1) * NS], start=(kc == 0), stop=(kc == KC - 1)),
                      r=[key, "s_mT"], w=["bank0"])
        oT = self.alloc([KC * NS], F32)
        S.add("act", lambda e: e.activation(out=oT, in_=Bo[:, 0:KC * NS], func=AF.Copy), r=["bank0"], w=["s_oT"])
        self.s_postnorm(oT.rearrange("p (c n) -> p c n", n=NS), ["s_oT"], 1)

    def s_finish(self):
        self.S.dma("sp", self.ys_out, self.xsT[:], r=["xsT"], w=["ys_o"], chan="s_out")
        if "s_out" not in self.out_chans:
            self.out_chans.append("s_out")

    def step(self):
        if self.nsteps is not None and self.stepi >= self.nsteps:
            return False
        self.stepi += 1
        return True

    def mixer(self, l, g0):
        sa = self.stop_after
        if not self.step():
            return
        self.phase(0)
        self.hT = self.alloc([KC, T], BF16)
        self.attoT = self.alloc([4, T], BF16)
        mark = self.aptr
        self.norm_bufs()
        self.prenorm(self.xres, g0, T, 1, 0)
        if sa == "prenorm":
            return
        self.phase(mark)
        if sa != "noattn":
            self.attention(l, g0)
        if sa == "attn":
            return
        if not self.step():
            return
        self.phase(mark)
        self.ssd(l, g0)
        if sa in ("ssd", "ssd_pre", "ssd_conv", "ssd_c1", "ssd_p1", "ssd_p2", "ssd_p3", "ssd_p4", "ssd_p5", "ssd_p6", "ssd_p7"):
            return
        if not self.step():
            return
        self.phase(mark)
        self.sqf = self.alloc([KC, 512], BF16)
        self.tail(l, g0)

    def build(self):
        self.declare()
        self.consts()
        S = self.S
        for l in range(self.depth):
            self.modulation(l)
            if self.do_sample:
                self.s_ffn(l, 0, 0)
                self.s_mixer(l)
                self.s_ffn(l, 1, 2)
            for g in range(NGRP if self.do_prompt else 0):
                g0 = g * T
                src = self.xT_in if l == 0 else self.xres
                if self.step():
                    self.ffn(l, 0, 0, src, self.xres, g0, False)
                if g == 0:
                    self.ssd_params(l)
                self.mixer(l, g0)
                if self.stop_after:
                    break
                last = (l == self.depth - 1)
                if self.step():
                    self.ffn(l, 1, 2, self.xres, self.yT_out if last else self.xres, g0, last)
            if self.stop_after:
                break
        if self.do_sample and not self.stop_after:
            self.s_finish()
        S.emit(final_waits=self.out_chans)
        return self.nc


def pm(v):
    v = np.asarray(v)
    c = v.shape[-1] // 128
    v = v.reshape(v.shape[:-1] + (c, 128))
    return np.ascontiguousarray(np.moveaxis(v, -1, 0))


_CACHE = {}


def _consts():
    j = np.arange(128)[:, None]
    i = np.arange(128)[None, :]
    ident = (j == i).astype(np.float32)
    mask2 = np.concatenate([(j >= i), (j <= i)], axis=1).astype(np.float32)
    negm = np.where(j > i, -30000.0, 0.0).astype(np.float32)
    cbf = np.concatenate([ident, mask2, np.tile(negm, (1, 4))], axis=1)
    tri = (j <= i).astype(np.float32)
    sel127 = np.zeros((128, 128), np.float32)
    sel127[127, :] = 1.0
    identS = np.zeros((128, 32), np.float32)
    identS[0:32] = np.eye(32)
    identS[32:64] = np.eye(32)
    cf32 = np.concatenate([tri, sel127, identS], axis=1)
    half = 64
    inv = (np.float32(10000.0) ** (-(np.arange(half, dtype=np.float32) / np.float32(half)))).astype(np.float32)
    pos = np.concatenate([np.arange(SEQ), [PAST]]).astype(np.float32)
    ang = (pos[None, :] * inv[:, None]).astype(np.float32)
    cos = np.cos(ang).astype(np.float32)
    sin = np.sin(ang).astype(np.float32)
    cosT = np.concatenate([cos, cos], axis=0)
    sinT = np.concatenate([-sin, sin], axis=0)
    return cbf, cf32, np.ascontiguousarray(cosT), np.ascontiguousarray(sinT)


def _sconsts():
    sc = np.zeros((128, 1172), np.float32)
    sc[:, 0:128] = np.eye(128, dtype=np.float32)
    for b in range(NS):
        sc[b, 128 + b * 128:128 + (b + 1) * 128] = 1.0
        sc[0:4, 640 + b * NS + b] = 1.0
    sc[0:4, 656:660] = np.eye(4, dtype=np.float32)
    for h in range(4):
        sc[h, 660 + h * 128:660 + (h + 1) * 128] = 1.0
    return sc


def kernel(**inp):
    f = lambda k: np.asarray(inp[k], dtype=np.float32)
    x_prompt = f("x_prompt")
    depth = 2
    if "nc" not in _CACHE:
        b = Builder(depth=depth)
        _CACHE["nc"] = b.build()
        _CACHE["b"] = b
    nc = _CACHE["nc"]
    cbf, cf32, cosT, sinT = _consts()
    shared = {}
    shared["w_mod"] = f("w_mod")
    shared["bmodT"] = np.ascontiguousarray(f("b_mod").reshape(depth, 72, 128).transpose(0, 2, 1))
    shared["gpreT"] = np.ascontiguousarray(f("norm_pre").reshape(depth, 3, KC, 128).transpose(0, 3, 1, 2))
    shared["gpostT"] = np.ascontiguousarray(f("norm_post").reshape(depth, 3, KC, 128).transpose(0, 3, 1, 2))
    shared["w_ff_in"] = f("w_ff_in")
    shared["w_ff_out"] = f("w_ff_out")
    shared["cbf_in"] = cbf
    shared["cf32_in"] = cf32
    shared["cosT"] = cosT
    shared["sinT"] = sinT
    shared["w_in"] = f("w_in")
    shared["conv_wT"] = np.ascontiguousarray(f("conv_w").reshape(depth, 4, 24, 128).transpose(0, 3, 2, 1))
    shared["conv_bT"] = np.ascontiguousarray(f("conv_b").reshape(depth, 24, 128).transpose(0, 2, 1))
    for k in ("dt_bias", "a_log", "d_skip", "norm_ssm", "w_br_att", "w_br_ssm", "w_out"):
        shared[k] = f(k)
    rep = lambda a: np.ascontiguousarray(np.broadcast_to(a[:, None], (a.shape[0], NS) + a.shape[1:]))
    shared["scst_in"] = _sconsts()
    shared["rope_s"] = np.ascontiguousarray(np.broadcast_to(
        np.concatenate([cosT[:, SEQ], sinT[:, SEQ]])[None, :], (NS, 256)))
    shared["convw_rep"] = rep(f("conv_w"))
    shared["convb_rep"] = rep(f("conv_b"))
    shared["dtb_rep"] = rep(f("dt_bias"))
    shared["alog_rep"] = rep(f("a_log"))
    shared["dsk_rep"] = rep(f("d_skip"))
    shared["nssm_rep"] = rep(f("norm_ssm"))
    caches = [f("cache_kv_g0"), f("cache_kv_g1"), f("cache_kv_g2")]
    st_ssm = f("state_ssm")
    st_conv = f("state_conv")
    x_sample = f("x_sample")
    in_maps = []
    for core in range(8):
        bidx = core % 4
        m = dict(shared)
        sl = slice(core * NS, (core + 1) * NS)
        m["xsT_in"] = np.ascontiguousarray(x_sample[sl, 0, :].T.reshape(KC, 128, NS).transpose(1, 0, 2))
        for g in range(3):
            m["cache%d" % g] = np.ascontiguousarray(caches[g][:, sl]).reshape(depth, NS, WIN[g], 1024)
        m["st_ssm"] = np.ascontiguousarray(st_ssm[:, sl]).reshape(depth, NS, 2048, 128)
        m["st_conv"] = np.ascontiguousarray(st_conv[:, sl])
        m["xT_in"] = np.ascontiguousarray(x_prompt[bidx].T).reshape(KC, 128, SEQ)
        cc = np.concatenate([f("c_prompt")[bidx:bidx + 1], f("c_sample")[core * NS:(core + 1) * NS]], axis=0)
        m["cT"] = np.ascontiguousarray(cc.T.reshape(KC, 128, NCOL).transpose(1, 0, 2))
        m = {k: v for k, v in m.items() if k in _CACHE["b"].dram_in}
        in_maps.append(m)
    res = run_bass_kernel_spmd(nc, in_maps, core_ids=list(range(8)))
    r = res.results
    B = 4
    y_prompt = np.stack([r[b]["yT_out"].reshape(D, SEQ).T for b in range(B)], axis=0)
    outs = {"y_prompt": y_prompt}
    kvp = []
    for g in range(3):
        d, keep = DIL[g], WIN[g]
        arr = np.zeros((depth, B, keep, 2, 4, 128), np.float32)
        for b in range(B):
            kT = r[b]["kT_out%d" % g]
            arr[:, b, :, 0] = kT.transpose(0, 3, 1, 2)
            vc = r[b]["vcm_out%d" % g]
            v = vc.transpose(0, 3, 2, 1, 4).reshape(depth, keep, 4, 128)
            arr[:, b, :, 1] = v
        kvp.append(arr)
    ssm_p = np.stack([r[b]["ssm_out"].transpose(0, 2, 1).reshape(depth, 32, 64, 128) for b in range(B)], axis=1)
    conv_p = np.stack([r[b]["conv_out"].transpose(0, 3, 2, 1).reshape(depth, 3, 3072) for b in range(B)], axis=1)
    y_sample = np.concatenate([r[c]["ys_out"].transpose(2, 1, 0).reshape(NS, 1, D) for c in range(8)], axis=0)
    kvs = [np.concatenate([r[c]["kvs_out%d" % g].reshape(depth, NS, 1, 2, 4, 128) for c in range(8)], axis=1) for g in range(3)]
    ssm_s = np.concatenate([r[c]["ssms_out"].reshape(depth, NS, 32, 64, 128) for c in range(8)], axis=1)
    conv_s = np.concatenate([r[c]["convs_out"] for c in range(8)], axis=1)
    return (y_prompt, y_sample, kvp[0], kvs[0], kvp[1], kvs[1], kvp[2], kvs[2], ssm_p, ssm_s, conv_p, conv_s)
```
